# Optimizing a Trainium2 kernel written in Bass

```python
import math
import jax, jax.numpy as jnp
from jax import lax
import numpy as np

D_MODEL = 1024
BATCH = 8
SEQ = 2048
DEPTH = 4

MEM_LEN = 256
DIFF_HEADS = 8
DIFF_HEAD_DIM = 64
DIFF_WIDTH = DIFF_HEADS * 2 * DIFF_HEAD_DIM
Q_BLOCK = 128
RET_HEADS = 4
RET_KEY_DIM = 128
RET_VAL_DIM = 256
RET_QK_WIDTH = RET_HEADS * RET_KEY_DIM
RET_WIDTH = RET_HEADS * RET_VAL_DIM
RET_CHUNK = 128
LRU_WIDTH = D_MODEL
LRU_BLOCKS = 8
LRU_BLOCK_W = LRU_WIDTH // LRU_BLOCKS
CONV_WIDTH = 4
LRU_C = 8.0
N_BRANCH = 3
XA_HEADS = 4
XA_HEAD_DIM = 128
XA_WIDTH = XA_HEADS * XA_HEAD_DIM
D_FF = 4 * D_MODEL
IN_WIDTHS = (DIFF_WIDTH, DIFF_WIDTH, DIFF_WIDTH, RET_QK_WIDTH, RET_QK_WIDTH, RET_WIDTH, RET_WIDTH, LRU_WIDTH, LRU_WIDTH, N_BRANCH * D_MODEL)
IN_COLS = 3 * DIFF_WIDTH + 2 * RET_QK_WIDTH + 2 * RET_WIDTH + 2 * LRU_WIDTH + N_BRANCH * D_MODEL

kernel_name = 'hybrid_diffattn_retention_rglru_gated'


def _split_points():
    pts, acc = [], 0
    for w in IN_WIDTHS[:-1]:
        acc += w
        pts.append(acc)
    return pts


def rmsnorm(x, g, eps=1e-6):
    xf = x.astype(jnp.float32)
    y = xf * lax.rsqrt(jnp.mean(xf * xf, axis=-1, keepdims=True) + eps)
    return (y * g.astype(jnp.float32)).astype(x.dtype)


def group_norm_heads(o, eps=1e-5):
    mu = jnp.mean(o, axis=-1, keepdims=True)
    var = jnp.mean(jnp.square(o - mu), axis=-1, keepdims=True)
    return (o - mu) * lax.rsqrt(var + eps)


def rotate_every_two(t):
    t1 = t[..., 0::2]
    t2 = t[..., 1::2]
    return jnp.stack((-t2, t1), axis=-1).reshape(t.shape)


def diff_attention(q, k, v, lam):
    bn, s_len = q.shape[0], q.shape[1]
    nb = s_len // Q_BLOCK
    qb = q.reshape(bn, nb, Q_BLOCK, DIFF_HEADS, 2, DIFF_HEAD_DIM).swapaxes(0, 1)
    starts = jnp.arange(nb, dtype=jnp.int32) * Q_BLOCK
    kpos = jnp.arange(s_len, dtype=jnp.int32)
    scale = DIFF_HEAD_DIM ** -0.5

    def block(args):
        q_blk, start = args
        s = jnp.einsum('bqhcd,bkhcd->bhcqk', q_blk, k).astype(jnp.float32) * scale
        qpos = start + jnp.arange(Q_BLOCK, dtype=jnp.int32)
        causal = kpos[None, :] <= qpos[:, None]
        s = jnp.where(causal, s, -jnp.inf)
        p = jax.nn.softmax(s, axis=-1)
        a = p[:, :, 0] - lam * p[:, :, 1]
        return jnp.einsum('bhqk,bkhe->bqhe', a.astype(v.dtype), v)

    o = lax.map(block, (qb, starts))
    return o.swapaxes(0, 1).reshape(bn, s_len, DIFF_HEADS, 2 * DIFF_HEAD_DIM)


def retention_chunkwise(q, k, v, log_g):
    bn, s_len = q.shape[0], q.shape[1]
    n_chunks = s_len // RET_CHUNK

    def chunk(t):
        return t.reshape(bn, n_chunks, RET_CHUNK, RET_HEADS, t.shape[-1]).transpose(0, 3, 1, 2, 4)

    q, k, v = chunk(q), chunk(k), chunk(v)
    idx = jnp.arange(RET_CHUNK, dtype=jnp.float32)
    rel = idx[:, None] - idx[None, :]
    decay = jnp.where(rel[None] >= 0, jnp.exp(jnp.maximum(rel, 0.0)[None] * log_g[:, None, None]), 0.0)
    s = jnp.einsum('bhncd,bhnmd->bhncm', q, k) * decay[:, None]
    o_intra = jnp.einsum('bhncm,bhnme->bhnce', s, v)
    w_k = jnp.exp((RET_CHUNK - 1 - idx)[None, :] * log_g[:, None])
    kv = jnp.einsum('bhncd,hc,bhnce->bhnde', k, w_k, v)
    chunk_decay = jnp.exp(RET_CHUNK * log_g)[None, :, None, None]

    def step(state, kv_n):
        return chunk_decay * state + kv_n, state

    init = jnp.zeros((bn, RET_HEADS, RET_KEY_DIM, RET_VAL_DIM), jnp.float32)
    _, prev = lax.scan(step, init, jnp.moveaxis(kv, 2, 0))
    prev = jnp.moveaxis(prev, 0, 2)
    w_q = jnp.exp((idx + 1.0)[None, :] * log_g[:, None])
    o_cross = jnp.einsum('bhncd,bhnde->bhnce', q, prev) * w_q[None, :, None, :, None]
    o = o_intra + o_cross
    return o.transpose(0, 2, 3, 1, 4).reshape(bn, s_len, RET_HEADS, RET_VAL_DIM)


def rg_lru_branch(xb, conv_w, conv_b, w_a, b_a, w_x, b_x, lam):
    bn, s_len, width = xb.shape
    xc = lax.conv_general_dilated(xb, conv_w.astype(xb.dtype), window_strides=(1,), padding=[(CONV_WIDTH - 1, 0)], dimension_numbers=('NWC', 'WIO', 'NWC'), feature_group_count=width) + conv_b
    xg = xc.reshape(bn, s_len, LRU_BLOCKS, LRU_BLOCK_W)
    r = jax.nn.sigmoid(jnp.einsum('bsnc,ncd->bsnd', xg, w_a).reshape(bn, s_len, width) + b_a)
    i = jax.nn.sigmoid(jnp.einsum('bsnc,ncd->bsnd', xg, w_x).reshape(bn, s_len, width) + b_x)
    log_a = -LRU_C * r.astype(jnp.float32) * jax.nn.softplus(-lam.astype(jnp.float32))
    a = jnp.exp(log_a)
    mult = jnp.sqrt(-jnp.expm1(2.0 * log_a))
    b = mult * (i * xc).astype(jnp.float32)

    def combine(lhs, rhs):
        a1, b1 = lhs
        a2, b2 = rhs
        return a1 * a2, a2 * b1 + b2

    _, h = lax.associative_scan(combine, (a, b), axis=1)
    return h.astype(xb.dtype)


def setup_inputs(seed: int = 0) -> dict:
    key = jax.random.key(seed)
    ks = jax.random.split(key, 32)
    f32 = jnp.float32

    def nrm(k, shape, fan_in):
        return jax.random.normal(k, shape, f32) * fan_in ** -0.5

    def gain(k, shape):
        return 1.0 + 0.01 * jax.random.normal(k, shape, f32)

    def small(k, shape, scale=0.01):
        return scale * jax.random.normal(k, shape, f32)

    u = jax.random.uniform(ks[14], (DEPTH, LRU_WIDTH), f32, minval=0.81, maxval=0.998)
    sa = jnp.sqrt(u)
    lru_lambda = jnp.log(sa) - jnp.log1p(-sa)
    return {
        'x': jax.random.normal(ks[0], (BATCH, SEQ, D_MODEL), f32),
        'mem': jax.random.normal(ks[1], (BATCH, MEM_LEN, D_MODEL), f32),
        'norm_mix': gain(ks[2], (DEPTH, D_MODEL)),
        'w_in': nrm(ks[3], (DEPTH, D_MODEL, IN_COLS), D_MODEL),
        'diff_lq1': small(ks[4], (DEPTH, DIFF_HEAD_DIM), 0.1),
        'diff_lk1': small(ks[5], (DEPTH, DIFF_HEAD_DIM), 0.1),
        'diff_lq2': small(ks[6], (DEPTH, DIFF_HEAD_DIM), 0.1),
        'diff_lk2': small(ks[7], (DEPTH, DIFF_HEAD_DIM), 0.1),
        'diff_subln': gain(ks[8], (DEPTH, 2 * DIFF_HEAD_DIM)),
        'lru_conv_w': nrm(ks[9], (DEPTH, CONV_WIDTH, 1, LRU_WIDTH), CONV_WIDTH),
        'lru_conv_b': small(ks[10], (DEPTH, LRU_WIDTH)),
        'lru_wa': nrm(ks[11], (DEPTH, LRU_BLOCKS, LRU_BLOCK_W, LRU_BLOCK_W), LRU_BLOCK_W),
        'lru_ba': small(ks[12], (DEPTH, LRU_WIDTH)),
        'lru_wx': nrm(ks[13], (DEPTH, LRU_BLOCKS, LRU_BLOCK_W, LRU_BLOCK_W), LRU_BLOCK_W),
        'lru_bx': small(ks[15], (DEPTH, LRU_WIDTH)),
        'lru_lambda': lru_lambda,
        'w_branch': nrm(ks[16], (DEPTH, N_BRANCH, DIFF_WIDTH, D_MODEL), DIFF_WIDTH),
        'w_out': nrm(ks[17], (DEPTH, D_MODEL, D_MODEL), D_MODEL),
        'norm_xattn': gain(ks[18], (DEPTH, D_MODEL)),
        'norm_mem': gain(ks[19], (DEPTH, D_MODEL)),
        'xa_wq': nrm(ks[20], (DEPTH, D_MODEL, XA_WIDTH), D_MODEL),
        'xa_wkv': nrm(ks[21], (DEPTH, D_MODEL, 2 * XA_WIDTH), D_MODEL),
        'xa_wo': nrm(ks[22], (DEPTH, XA_WIDTH, D_MODEL), XA_WIDTH),
        'norm_mlp': gain(ks[23], (DEPTH, D_MODEL)),
        'mlp_w1': nrm(ks[24], (DEPTH, D_MODEL, D_FF), D_MODEL),
        'mlp_w2': nrm(ks[25], (DEPTH, D_FF, D_MODEL), D_FF),
        'norm_final': gain(ks[26], (D_MODEL,)),
    }


def reference(x, mem, norm_mix, w_in, diff_lq1, diff_lk1, diff_lq2, diff_lk2, diff_subln, lru_conv_w, lru_conv_b, lru_wa, lru_ba, lru_wx, lru_bx, lru_lambda, w_branch, w_out, norm_xattn, norm_mem, xa_wq, xa_wkv, xa_wo, norm_mlp, mlp_w1, mlp_w2, norm_final):
    f32 = jnp.float32
    bn, s_len, _ = x.shape
    m_len = mem.shape[1]
    splits = _split_points()
    pos = jnp.arange(s_len, dtype=f32)
    angle = jnp.repeat(1.0 / (10000.0 ** jnp.linspace(0.0, 1.0, RET_KEY_DIM // 2, dtype=f32)), 2)
    phase = pos[:, None] * angle[None, :]
    cos = jnp.cos(phase)[None, :, None, :]
    sin = jnp.sin(phase)[None, :, None, :]
    log_g = jnp.log(1.0 - jnp.exp2(-5.0 - jnp.arange(RET_HEADS, dtype=f32)))

    for l in range(DEPTH):
        h = rmsnorm(x, norm_mix[l])
        u = h @ w_in[l]
        dq, dk, dv, rq, rk, rv, rg, lx, ly, gates = jnp.split(u, splits, axis=-1)

        lam_init = 0.8 - 0.6 * math.exp(-0.3 * l)
        lam = (jnp.exp(jnp.sum(diff_lq1[l].astype(f32) * diff_lk1[l].astype(f32)))
               - jnp.exp(jnp.sum(diff_lq2[l].astype(f32) * diff_lk2[l].astype(f32))) + lam_init)
        od = diff_attention(dq.reshape(bn, s_len, DIFF_HEADS, 2, DIFF_HEAD_DIM),
                            dk.reshape(bn, s_len, DIFF_HEADS, 2, DIFF_HEAD_DIM),
                            dv.reshape(bn, s_len, DIFF_HEADS, 2 * DIFF_HEAD_DIM), lam)
        od = (rmsnorm(od, diff_subln[l], eps=1e-5) * (1.0 - lam_init)).reshape(bn, s_len, DIFF_WIDTH)

        q_r = rq.reshape(bn, s_len, RET_HEADS, RET_KEY_DIM).astype(f32)
        k_r = rk.reshape(bn, s_len, RET_HEADS, RET_KEY_DIM).astype(f32)
        q_r = q_r * cos + rotate_every_two(q_r) * sin
        k_r = (k_r * cos + rotate_every_two(k_r) * sin) * RET_KEY_DIM ** -0.5
        v_r = rv.reshape(bn, s_len, RET_HEADS, RET_VAL_DIM).astype(f32)
        o_r = group_norm_heads(retention_chunkwise(q_r, k_r, v_r, log_g)).reshape(bn, s_len, RET_WIDTH)
        o_r = (jax.nn.silu(rg.astype(f32)) * o_r).astype(x.dtype)

        hl = rg_lru_branch(lx, lru_conv_w[l], lru_conv_b[l], lru_wa[l], lru_ba[l], lru_wx[l], lru_bx[l], lru_lambda[l])
        o_l = hl * jax.nn.gelu(ly)

        g = jax.nn.sigmoid(gates.reshape(bn, s_len, N_BRANCH, D_MODEL))
        merged = (g[:, :, 0] * (od @ w_branch[l, 0])
                  + g[:, :, 1] * (o_r @ w_branch[l, 1])
                  + g[:, :, 2] * (o_l @ w_branch[l, 2]))
        x = x + merged @ w_out[l]

        hx = rmsnorm(x, norm_xattn[l])
        hm = rmsnorm(mem, norm_mem[l])
        q_x = (hx @ xa_wq[l]).reshape(bn, s_len, XA_HEADS, XA_HEAD_DIM)
        kv_m = (hm @ xa_wkv[l]).reshape(bn, m_len, 2, XA_HEADS, XA_HEAD_DIM)
        s_x = jnp.einsum('bshd,bmhd->bhsm', q_x, kv_m[:, :, 0]).astype(f32) * XA_HEAD_DIM ** -0.5
        p_x = jax.nn.softmax(s_x, axis=-1)
        o_x = jnp.einsum('bhsm,bmhd->bshd', p_x.astype(x.dtype), kv_m[:, :, 1]).reshape(bn, s_len, XA_WIDTH)
        x = x + o_x @ xa_wo[l]

        hf = rmsnorm(x, norm_mlp[l])
        x = x + jnp.square(jax.nn.relu(hf @ mlp_w1[l])) @ mlp_w2[l]

    return rmsnorm(x, norm_final)
```

```python
import math
import os
import numpy as np
import concourse.bass as bass
import concourse.mybir as mybir
from concourse.bass_utils import run_bass_kernel_spmd

F32 = mybir.dt.float32
BF16 = mybir.dt.bfloat16
AF = mybir.ActivationFunctionType
ALU = mybir.AluOpType
AX = mybir.AxisListType

ENGS = ("pe", "act", "dve", "pool", "sp")

S = 2048
D = 1024
NL = 4
NV = 104
NR = 384
IN_COLS = 11264
C_DQ, C_DK, C_DV, C_RQ, C_RK, C_RV, C_RG, C_LX, C_LY, C_G = 0, 1024, 2048, 3072, 3584, 4096, 5120, 6144, 7168, 8192


class Buf:
    __slots__ = ("name", "w", "r")

    def __init__(self, name):
        self.name = name
        self.w = {}
        self.r = {}


class DmaGroup:
    __slots__ = ("sem", "total")

    def __init__(self, sem):
        self.sem = sem
        self.total = 0


class Op:
    __slots__ = ("eng", "fn", "deps", "odeps", "signal", "sig_idx", "dma", "dma_val", "ndma", "seq", "cost", "fin", "done")

    def __init__(self, eng, fn):
        self.eng = eng
        self.fn = fn
        self.deps = []
        self.odeps = []
        self.cost = 0.0
        self.fin = 0.0
        self.done = False
        self.signal = False
        self.sig_idx = 0
        self.dma = None
        self.dma_val = 0
        self.ndma = 1


class _Dummy:
    def then_inc(self, *a, **k):
        return self


class _CostProxy:
    def reset(self, eng, is_dma):
        self.eng = eng
        self.is_dma = is_dma
        self.cost = 0.0
        self.tab = None

    def __getattr__(self, name):
        def f(*args, **kwargs):
            out = kwargs.get("out", None)
            if out is None and args:
                out = args[0]
            try:
                shp = out.shape
                n = 1
                for x_ in shp[1:]:
                    n *= int(x_)
            except Exception:
                n = 64
            if self.is_dma:
                self.cost += n * 128 * 4 / 250.0
            elif self.eng == "pe":
                mult = 1.0
                lhs = kwargs.get("lhsT", kwargs.get("in_", None))
                try:
                    if lhs is not None and lhs.dtype == F32:
                        mult = 4.0
                except Exception:
                    pass
                self.cost += 64.0 + mult * n / 2.4
            elif self.eng == "act":
                fn_ = kwargs.get("func", None)
                if fn_ == AF.Sqrt:
                    self.tab = "sqrt"
                elif fn_ == AF.Ln:
                    self.tab = "ln"
                elif fn_ == AF.Exp or fn_ == AF.Tanh:
                    self.tab = "exp"
                self.cost += 230.0 + n / 1.2 + (60.0 if kwargs.get("accum_out", None) is not None else 0.0)
            else:
                k_ = 2.0 if name == "tensor_tensor_scan" else 1.0
                self.cost += 260.0 + k_ * n / 0.96
            return _Dummy()
        return f


class Prog:
    def __init__(self, nc):
        self.nc = nc
        self.ops = {e: [] for e in ENGS}
        self.eng_sem = {}
        self.nsem = 0

    def new_sem(self, name):
        self.nsem += 1
        return self.nc.alloc_semaphore(f"s_{name}_{self.nsem}")

    def dma_group(self, name):
        return DmaGroup(self.new_sem(name))

    def op(self, eng, fn, reads=(), writes=(), pwrites=(), dma=None, ndma=1):
        o = Op(eng, fn)
        self.seq = getattr(self, "seq", 0) + 1
        o.seq = self.seq
        mykey = ("dma", id(dma)) if dma is not None else eng
        deps = {}
        for b in reads:
            for d in b.w.values():
                deps[id(d)] = d
        for b in writes:
            for d in b.w.values():
                deps[id(d)] = d
            for d in b.r.values():
                deps[id(d)] = d
        for b in pwrites:
            for d in b.r.values():
                deps[id(d)] = d
            for k, d in b.w.items():
                if k != mykey:
                    deps[id(d)] = d
                else:
                    o.odeps.append(d)
        for d in deps.values():
            if d is o:
                continue
            if d.dma is None:
                if d.eng == "pe" and eng == "pe" and dma is None:
                    o.odeps.append(d)
                    continue
                d.signal = True
            o.deps.append(d)
        if dma is not None:
            o.dma = dma
            o.ndma = ndma
            dma.total += 16 * ndma
            o.dma_val = dma.total
        for b in reads:
            prev = b.r.get(mykey)
            if prev is not None and prev is not o:
                o.odeps.append(prev)
            b.r[mykey] = o
        for b in writes:
            b.w = {mykey: o}
            b.r = {}
        for b in pwrites:
            b.w[mykey] = o
        self.ops[eng].append(o)
        return o

    def schedule(self):
        prox = _CostProxy()
        tabs = {}
        curtab = [None]
        for e in ENGS:
            for o in self.ops[e]:
                if o.fn is None:
                    o.cost = 0.0
                    continue
                prox.reset(e, o.dma is not None)
                o.fn(prox)
                o.cost = prox.cost
                tabs[id(o)] = prox.tab
        W = {"pe": 48, "act": 32, "dve": 32, "pool": 1, "sp": 1}
        rem = {e: list(self.ops[e]) for e in ENGS}
        head = {e: 0 for e in ENGS}
        tfree = {e: 0.0 for e in ENGS}
        neword = {e: [] for e in ENGS}
        total = sum(len(v) for v in rem.values())
        nsched = 0
        LAT = 120.0
        while nsched < total:
            best = None
            for e in ENGS:
                lst = rem[e]
                i = head[e]
                n = len(lst)
                while i < n and lst[i].done:
                    i += 1
                head[e] = i
                cnt = 0
                j = i
                cand = None
                while j < n and cnt < W[e]:
                    o = lst[j]
                    if not o.done:
                        cnt += 1
                        ok = True
                        r = 0.0
                        for d in o.deps:
                            if not d.done:
                                ok = False
                                break
                            if d.fin + LAT > r:
                                r = d.fin + LAT
                        if ok:
                            for d in o.odeps:
                                if not d.done:
                                    ok = False
                                    break
                        if ok:
                            st = r if r > tfree[e] else tfree[e]
                            if e == "act":
                                tb_ = tabs.get(id(o))
                                if tb_ is not None and tb_ != curtab[0]:
                                    st += 1300.0
                            if cand is None or st < cand[0]:
                                cand = (st, j, o)
                                if st <= tfree[e]:
                                    break
                    j += 1
                if cand is not None and (best is None or cand[0] < best[0]):
                    best = (cand[0], e, cand[2])
            assert best is not None, "scheduler stuck"
            st, e, o = best
            o.done = True
            if e == "act" and tabs.get(id(o)) is not None:
                curtab[0] = tabs[id(o)]
            if o.dma is not None:
                tfree[e] = st + 1000.0
                o.fin = st + 2000.0 + o.cost
            else:
                o.fin = st + o.cost
                tfree[e] = o.fin
            neword[e].append(o)
            nsched += 1
        self.ops = neword
        self.sim_time = max(tfree.values())

    def emit(self):
        nc = self.nc
        if os.environ.get("KSCHED", "1") == "1":
            self.schedule()
        for e in ENGS:
            c = 0
            for o in self.ops[e]:
                if o.signal:
                    c += 1
                    o.sig_idx = c
            if self.ops[e]:
                self.eng_sem[e] = self.new_sem("eng_" + e)
            if os.environ.get("KDBG_PRINT"):
                print("ENG", e, "ops", len(self.ops[e]), "signals", c, flush=True)

        def run(e, h):
            waited = {}
            for o in self.ops[e]:
                for d in o.deps:
                    if d.dma is not None:
                        sem, val = d.dma.sem, d.dma_val
                    else:
                        sem, val = self.eng_sem[d.eng], d.sig_idx
                    k = id(sem)
                    if waited.get(k, 0) >= val:
                        continue
                    h.wait_ge(sem, val)
                    waited[k] = val
                if o.fn is None:
                    continue
                ins = o.fn(h)
                if o.dma is not None:
                    if not isinstance(ins, (list, tuple)):
                        ins = [ins]
                    assert len(ins) == o.ndma
                    for i_ in ins:
                        i_.then_inc(o.dma.sem, 16)
                elif o.signal:
                    ins.then_inc(self.eng_sem[e], 1)

        with nc.Block() as block:
            @block.tensor
            def _(h):
                run("pe", h)

            @block.scalar
            def _(h):
                run("act", h)

            @block.vector
            def _(h):
                run("dve", h)

            @block.gpsimd
            def _(h):
                run("pool", h)

            @block.sync
            def _(h):
                run("sp", h)


class Arena:
    def __init__(self, nc, name, nbytes):
        self.nbytes = nbytes
        self.t = nc.alloc_sbuf_tensor(name, [128, nbytes // 4], F32)
        self.top = 0
        self.live = []
        self.dead = []
        self.peak = 0

    def alloc(self, name, shape_free, dtype, nbufs=1):
        esz = 2 if dtype == BF16 else 4
        n = int(np.prod(shape_free))
        nb = (n * esz + 63) // 64 * 64
        start = self.top
        end = start + nb
        assert end <= self.nbytes, f"arena overflow for {name}: {end} > {self.nbytes}"
        self.top = end
        self.peak = max(self.peak, end)
        ap = self.t[:, start // 4:(start + nb) // 4]
        if dtype != F32:
            ap = ap.bitcast(dtype)
        ap = ap[:, 0:n]
        if len(shape_free) == 2:
            ap = ap.rearrange("p (a b) -> p a b", a=shape_free[0])
        elif len(shape_free) == 3:
            ap = ap.rearrange("p (a b c) -> p a b c", a=shape_free[0], b=shape_free[1])
        bufs = [Buf(f"{name}{i}") for i in range(nbufs)]
        inh = {}
        keep = []
        for (s, e, obufs) in self.dead:
            if s < end and e > start:
                for ob in obufs:
                    for d in list(ob.w.values()) + list(ob.r.values()):
                        inh[("inh", id(d))] = d
                if s >= start and e <= end:
                    continue
            keep.append((s, e, obufs))
        self.dead = keep
        for b in bufs:
            b.r.update(inh)
        self.live.append((start, end, bufs))
        return ap, bufs

    def mark(self):
        return (self.top, len(self.live))

    def release(self, mark):
        top, nlive = mark
        for item in self.live[nlive:]:
            self.dead.append(item)
        self.live = self.live[:nlive]
        self.top = top


def mm_fn(out, pairs, start=True, skip=False):
    def fn(h):
        n = len(pairs)
        ins = None
        for i, (l, r) in enumerate(pairs):
            if skip:
                ins = h.matmul(out, lhsT=l, rhs=r, start=(start and i == 0), stop=(i == n - 1), skip_group_check=True)
            else:
                ins = h.matmul(out, lhsT=l, rhs=r, start=(start and i == 0), stop=(i == n - 1))
        return ins
    return fn


def build(n_layers=NL, stop_after=None, dbg=False):
    nc = bass.Bass("TRN2", target_bir_lowering=False)

    def dram(name, shape, dt=F32, kind="ExternalInput"):
        return nc.dram_tensor(name, shape, dt, kind=kind).ap()

    x_d = dram("x", [S, D])
    mem_d = dram("mem", [256, D])
    w_in_d = dram("w_in", [NL, D, IN_COLS])
    w_br_d = dram("w_branch", [NL, 3, D, D])
    w_out_d = dram("w_out", [NL, D, D])
    wq_d = dram("xa_wq", [NL, D, 512])
    wkv_d = dram("xa_wkv", [NL, D, 1024])
    wo_d = dram("xa_wo", [NL, 512, D])
    w1_d = dram("mlp_w1", [NL, D, 4096])
    w2_d = dram("mlp_w2", [NL, 4096, D])
    wa_d = dram("lru_wa", [NL, 8, 128, 128])
    wx_d = dram("lru_wx", [NL, 8, 128, 128])
    vecs_d = dram("vecs", [128, NL * NV])
    rows_d = dram("rows", [128, NL * NR])
    cst_d = dram("cst", [128, 256])
    retc_d = dram("retc", [128, 4 * 130])
    cs_d = dram("cstab", [S, 256])
    out_d = dram("out", [S, D], kind="ExternalOutput")
    if dbg:
        dbgx_d = dram("dbgx", [128, 8 * S], kind="ExternalOutput")
        dbgo_d = dram("dbgo", [128, 8 * S], BF16, kind="ExternalOutput")

    P = Prog(nc)
    AR = Arena(nc, "arena", 204 * 1024)
    ps = nc.alloc_psum_tensor("ps", [128, 8, 512], F32)
    psb = [Buf(f"ps{i}") for i in range(8)]
    mmctr = [0]

    def mmbank():
        b = (0, 1, 2, 7)[mmctr[0] % 4]
        mmctr[0] += 1
        return b

    xT, _ = AR.alloc("xT", [8, S], F32)
    xb = [[Buf(f"x{kc}_{tt}") for tt in range(4)] for kc in range(8)]
    cstf, (cstfb,) = AR.alloc("cstf", [256], F32)
    identf = cstf[:, 0:128]
    identb, (identbb,) = AR.alloc("identb", [128], BF16)
    maskb, (maskbb,) = AR.alloc("maskb", [128], BF16)
    onesb, (onesbb,) = AR.alloc("onesb", [128], BF16)
    vecs, (vecsb,) = AR.alloc("vecs", [NL, NV], F32)
    rows1, (rowsb,) = AR.alloc("rows", [NR], F32)
    retc, (retcb,) = AR.alloc("retc", [4, 130], F32)
    lamneg, (lamnegb,) = AR.alloc("lamneg", [4], F32)
    subln_s, (sublnb,) = AR.alloc("subln", [128], F32)
    lru_sc, (lruscb,) = AR.alloc("lrusc", [8], F32)
    hT, _ = AR.alloc("hT", [8, S], BF16)
    hb = [Buf(f"h{tt}") for tt in range(4)]
    obT, _ = AR.alloc("obT", [8, S], BF16)
    ob = [[Buf(f"o{kc}_{tt}") for tt in range(4)] for kc in range(8)]
    NSLOT = 3
    wslot = []
    for i in range(NSLOT):
        ap, (b,) = AR.alloc(f"wslot{i}", [4096], BF16)
        wslot.append((ap, b, P.dma_group(f"w{i}")))
    wctr = [0]

    def next_slot():
        s = wslot[wctr[0] % NSLOT]
        wctr[0] += 1
        return s

    def wload(dsts_srcs, slot):
        ap, b, g = slot
        n = len(dsts_srcs)

        def fn(h):
            return [h.dma_start(out=d, in_=s) for d, s in dsts_srcs]
        P.op("pool", fn, writes=[b], dma=g, ndma=n)

    def wtile_cols(w2d, col0, ncols, slot, nk=8, dcol0=0, width=None):
        ap = slot[0]
        width = width or ncols
        v = ap[:, 0:nk * width].rearrange("p (k c) -> p k c", k=nk)
        src = w2d.rearrange("(k p) c -> p k c", p=128)[:, :, col0:col0 + ncols]
        return (v[:, :, dcol0:dcol0 + ncols], src)

    def sview(slot, nk, width):
        return slot[0][:, 0:nk * width].rearrange("p (k c) -> p k c", k=nk)

    gc = P.dma_group("cst")
    P.op("sp", lambda h: h.dma_start(out=cstf, in_=cst_d), writes=[cstfb], dma=gc)
    gv = P.dma_group("vecs")
    P.op("sp", lambda h: h.dma_start(out=vecs.rearrange("p l v -> p (l v)"), in_=vecs_d), writes=[vecsb], dma=gv)
    gr = P.dma_group("rows")
    grc = P.dma_group("retc")
    P.op("sp", lambda h: h.dma_start(out=retc.rearrange("p l v -> p (l v)"), in_=retc_d), writes=[retcb], dma=grc)
    P.op("dve", lambda h: h.tensor_copy(out=identb, in_=cstf[:, 0:128]), reads=[cstfb], writes=[identbb])
    P.op("dve", lambda h: h.tensor_copy(out=maskb, in_=cstf[:, 128:256]), reads=[cstfb], writes=[maskbb])
    P.op("dve", lambda h: h.memset(onesb, 1.0), writes=[onesbb])

    mk = AR.mark()
    xin, xinb = AR.alloc("xin", [2, D], F32, nbufs=2)
    gx = [P.dma_group("xin0"), P.dma_group("xin1")]
    for tb in range(16):
        s = tb % 2
        P.op("sp", lambda h, tb=tb, s=s: h.dma_start(out=xin[:, s, :], in_=x_d[tb * 128:(tb + 1) * 128, :]), writes=[xinb[s]], dma=gx[s])
        for half in range(2):
            bk = mmbank()

            def fn(h, s=s, half=half, bk=bk):
                ins = None
                for j in range(4):
                    kc = half * 4 + j
                    ins = h.transpose(out=ps[:, bk, j * 128:(j + 1) * 128], in_=xin[:, s, kc * 128:(kc + 1) * 128], identity=identf)
                return ins
            P.op("pe", fn, reads=[xinb[s], cstfb], writes=[psb[bk]])
            tt = tb // 4
            P.op("dve" if half == 0 else "act",
                 (lambda h, bk=bk, half=half, tb=tb: h.tensor_copy(out=xT[:, half * 4:half * 4 + 4, tb * 128:(tb + 1) * 128], in_=ps[:, bk, :].rearrange("p (a b) -> p a b", a=4)))
                 if half == 0 else
                 (lambda h, bk=bk, half=half, tb=tb: h.activation(out=xT[:, half * 4:half * 4 + 4, tb * 128:(tb + 1) * 128], in_=ps[:, bk, :].rearrange("p (a b) -> p a b", a=4), func=AF.Copy)),
                 reads=[psb[bk]], pwrites=[xb[half * 4 + j][tt] for j in range(4)])
    AR.release(mk)

    def norm_stage(l, gcol, eps=1e-6):
        mk = AR.mark()
        sq, sqb = AR.alloc("sq", [2, 8, 512], BF16, nbufs=2)
        rt, rtb = AR.alloc("rt", [2, 512], F32, nbufs=2)
        for tt in range(4):
            s = tt % 2
            tsl = slice(tt * 512, (tt + 1) * 512)
            P.op("act", lambda h, s=s, tsl=tsl: h.activation(out=sq[:, s], in_=xT[:, :, tsl], func=AF.Square),
                 reads=[xb[kc][tt] for kc in range(8)], writes=[sqb[s]])
            P.op("pe", mm_fn(ps[:, 7, :], [(onesb, sq[:, s, kc, :]) for kc in range(8)]), reads=[sqb[s], onesbb], writes=[psb[7]])
            P.op("act", lambda h, s=s: h.activation(out=rt[:, s], in_=ps[:, 7, :], func=AF.Sqrt, scale=1.0 / D, bias=eps_ap(eps)),
                 reads=[psb[7], epsb], writes=[rtb[s]])
            P.op("dve", lambda h, s=s: h.reciprocal(out=rt[:, s], in_=rt[:, s]), reads=[rtb[s]], writes=[rtb[s]])
            for kc in range(8):
                P.op("dve", lambda h, s=s, kc=kc, tsl=tsl: h.scalar_tensor_tensor(out=hT[:, kc, tsl], in0=xT[:, kc, tsl], scalar=vecs[:, l, gcol + kc:gcol + kc + 1],
                                                                              in1=rt[:, s], op0=ALU.mult, op1=ALU.mult),
                     reads=[xb[kc][tt], rtb[s], vecsb], writes=[hb[tt]] if kc == 0 else (), pwrites=() if kc == 0 else [hb[tt]])
        AR.release(mk)

    epst, (epsb,) = AR.alloc("epst", [8], F32)
    P.op("dve", lambda h: h.memset(epst[:, 0:1], 1e-6), writes=[epsb])
    P.op("dve", lambda h: h.memset(epst[:, 1:2], 1e-5), pwrites=[epsb])
    P.op("dve", lambda h: h.memset(epst[:, 2:3], 1.0), pwrites=[epsb])
    P.op("dve", lambda h: h.memset(epst[:, 3:4], 0.25), pwrites=[epsb])
    P.op("dve", lambda h: h.memset(epst[:, 4:5], 4e-5), pwrites=[epsb])

    def eps_ap(eps):
        return epst[:, 0:1] if eps == 1e-6 else epst[:, 1:2]

    all_x = [xb[kc][tt] for kc in range(8) for tt in range(4)]
    all_o = [ob[kc][tt] for kc in range(8) for tt in range(4)]

    def branch_A(l):
        lam_init = 0.8 - 0.6 * math.exp(-0.3 * l)
        mk = AR.mark()
        lt, (ltb,) = AR.alloc("lamt", [2, 64], F32)
        l2, (l2b,) = AR.alloc("lam2", [2], F32)
        P.op("sp", lambda h: h.dma_start(out=rows1, in_=rows_d[:, l * NR:(l + 1) * NR]), writes=[rowsb], dma=gr)
        rv = rows1[:, 0:256].rearrange("p (a b c) -> p a b c", a=2, b=2)
        P.op("dve", lambda h: h.tensor_tensor(out=lt, in0=rv[:, :, 0, :], in1=rv[:, :, 1, :], op=ALU.mult), reads=[rowsb], writes=[ltb])
        P.op("dve", lambda h: h.reduce_sum(out=l2, in_=lt, axis=AX.X), reads=[ltb], writes=[l2b])
        P.op("act", lambda h: h.activation(out=l2, in_=l2, func=AF.Exp), reads=[l2b], writes=[l2b])
        P.op("dve", lambda h: h.scalar_tensor_tensor(out=lamneg[:, 0:1], in0=l2[:, 1:2], scalar=-lam_init, in1=l2[:, 0:1], op0=ALU.add, op1=ALU.subtract),
             reads=[l2b], writes=[lamnegb])
        P.op("dve", lambda h: h.tensor_scalar(out=subln_s, in0=rows1[:, 256:384], scalar1=1.0 - lam_init, scalar2=None, op0=ALU.mult),
             reads=[rowsb], writes=[sublnb])

        qT, qTb = AR.alloc("qT", [2, S], BF16, nbufs=2)
        kT, kTb = AR.alloc("kT", [2, 2, S], BF16, nbufs=2)
        for s_ in range(2):
            P.op("dve", lambda h, s_=s_: h.memset(kT[64:128, s_, 0, :], 0.0), writes=[kTb[s_]])
            P.op("dve", lambda h, s_=s_: h.memset(kT[0:64, s_, 1, :], 0.0), pwrites=[kTb[s_]])
        vv, vvb = AR.alloc("vA", [2, 16, 129], BF16, nbufs=2)
        for s in range(2):
            P.op("dve", lambda h, s=s: h.memset(vv[:, s, :, 128:129], 1.0), writes=[vvb[s]])
        NE = 5
        et, etb = AR.alloc("eA", [NE, 512], BF16, nbufs=NE)
        rec, (recb,) = AR.alloc("recA", [2, 4], F32)
        osb, osbb = AR.alloc("osbA", [2, 4, 129], F32, nbufs=2)
        d0, d0b = osb[:, 0, :, 0:128], osbb[0]
        d1, d1b = osb[:, 1, :, 0:128], osbb[1]
        ss, (ssb,) = AR.alloc("ssA", [4], F32)
        odn, odnb = AR.alloc("odnA", [2, 4, 128], BF16, nbufs=2)
        ectr = 0
        w2d = w_in_d[l]
        for hd in range(int(os.environ.get('KDBG_HEADS', '8'))):
            s = hd % 2
            slot = next_slot()
            wload([wtile_cols(w2d, C_DQ + hd * 128, 128, slot, width=384, dcol0=0),
                   wtile_cols(w2d, C_DK + hd * 128, 128, slot, width=384, dcol0=128),
                   wtile_cols(w2d, C_DV + hd * 128, 128, slot, width=384, dcol0=256)], slot)
            wt = sview(slot, 8, 384)
            wb_ = slot[1]
            for tt in range(4):
                tsl = slice(tt * 512, (tt + 1) * 512)
                bk = mmbank()
                P.op("pe", mm_fn(ps[:, bk, :], [(wt[:, kc, 0:128], hT[:, kc, tsl]) for kc in range(8)]), reads=[wb_, hb[tt]], writes=[psb[bk]])
                P.op("act", lambda h, bk=bk, s=s, tsl=tsl: h.activation(out=qT[:, s, tsl], in_=ps[:, bk, :], func=AF.Copy, scale=0.125),
                     reads=[psb[bk]], writes=[qTb[s]] if tt == 0 else (), pwrites=() if tt == 0 else [qTb[s]])
                bk = mmbank()
                P.op("pe", mm_fn(ps[:, bk, :], [(wt[:, kc, 128:256], hT[:, kc, tsl]) for kc in range(8)]), reads=[wb_, hb[tt]], writes=[psb[bk]])
                P.op("dve", lambda h, bk=bk, s=s, tsl=tsl: h.tensor_copy(out=kT[0:64, s, 0, tsl], in_=ps[0:64, bk, :]),
                     reads=[psb[bk]], writes=[kTb[s]] if tt == 0 else (), pwrites=() if tt == 0 else [kTb[s]])
                P.op("dve", lambda h, bk=bk, s=s, tsl=tsl: h.tensor_copy(out=kT[64:128, s, 1, tsl], in_=ps[64:128, bk, :]),
                     reads=[psb[bk]], pwrites=[kTb[s]])
            for t4 in range(4):
                bk = mmbank()

                def fn(h, t4=t4, bk=bk, wt=wt):
                    ins = None
                    for j in range(4):
                        tb = t4 * 4 + j
                        for kc in range(8):
                            ins = h.matmul(ps[:, bk, j * 128:(j + 1) * 128], lhsT=hT[:, kc, tb * 128:(tb + 1) * 128], rhs=wt[:, kc, 256:384],
                                           start=(kc == 0), stop=(kc == 7))
                    return ins
                P.op("pe", fn, reads=[wb_, hb[t4]], writes=[psb[bk]])
                P.op("dve", lambda h, bk=bk, s=s, t4=t4: h.tensor_copy(out=vv[:, s, t4 * 4:(t4 + 1) * 4, 0:128], in_=ps[:, bk, :].rearrange("p (a b) -> p a b", a=4)),
                     reads=[psb[bk]], writes=[vvb[s]] if t4 == 0 else (), pwrites=() if t4 == 0 else [vvb[s]])
            accb = [psb[3], psb[4], psb[5], psb[6]]

            def emit_score(qt, kb, c, s=s):
                nonlocal ectr
                dstart = max(0, kb - 4 * qt)
                q0 = qt * 512 + dstart * 128
                nq = 512 - dstart * 128
                bk = mmbank()
                P.op("pe", mm_fn(ps[:, bk, 0:nq], [(kT[:, s, c, kb * 128:(kb + 1) * 128], qT[:, s, q0:q0 + nq])]),
                     reads=[kTb[s], qTb[s]], writes=[psb[bk]])
                e = ectr % NE
                ectr += 1
                P.op("act", lambda h, bk=bk, e=e, nq=nq: h.activation(out=et[:, e, 0:nq], in_=ps[:, bk, 0:nq], func=AF.Exp),
                     reads=[psb[bk]], writes=[etb[e]])
                if kb >= 4 * qt:
                    P.op("dve", lambda h, e=e: h.tensor_tensor(out=et[:, e, 0:128], in0=et[:, e, 0:128], in1=maskb, op=ALU.mult),
                         reads=[etb[e], maskbb], writes=[etb[e]])
                return (qt, kb, c, dstart, e)

            def emit_av(info, s=s):
                qt, kb, c, dstart, e = info

                def fn(h):
                    ins = None
                    for qb in range(dstart, 4):
                        bank = 3 + 2 * c + qb // 2
                        co = (qb % 2) * 129
                        ins = h.matmul(ps[:, bank, co:co + 129], lhsT=et[:, e, (qb - dstart) * 128:(qb - dstart + 1) * 128], rhs=vv[:, s, kb, :],
                                       start=(kb == 0 and qb % 2 == 0), stop=(kb == 4 * qt + qb), skip_group_check=True)
                    return ins
                abufs = [accb[2 * c], accb[2 * c + 1]]
                P.op("pe", fn, reads=[etb[e], vvb[s]], writes=abufs if kb == 0 else (), pwrites=() if kb == 0 else abufs)

            def post(qt, hd=hd):
                P.op("act", lambda h: h.activation(out=osb[:, 0].rearrange("p (a b) c -> p a (b c)", a=2), in_=ps[:, 3:5, 0:258], func=AF.Copy), reads=[psb[3], psb[4]], writes=[osbb[0]])
                P.op("dve", lambda h: h.tensor_copy(out=osb[:, 1].rearrange("p (a b) c -> p a (b c)", a=2), in_=ps[:, 5:7, 0:258]), reads=[psb[5], psb[6]], writes=[osbb[1]])
                P.op("dve", lambda h: h.reciprocal(out=rec[:, 0, :], in_=osb[:, 0, :, 128]), reads=[osbb[0]], writes=[recb])
                P.op("dve", lambda h: h.reciprocal(out=rec[:, 1, :], in_=osb[:, 1, :, 128]), reads=[osbb[1]], pwrites=[recb])
                P.op("dve", lambda h: h.tensor_scalar(out=rec[:, 1, :], in0=rec[:, 1, :], scalar1=lamneg[:, 0:1], scalar2=None, op0=ALU.mult),
                     reads=[recb, lamnegb], writes=[recb])
                P.op("dve", lambda h: h.tensor_tensor(out=d0, in0=d0, in1=rec[:, 0, :].unsqueeze(2).to_broadcast([128, 4, 128]), op=ALU.mult),
                     reads=[d0b, recb], writes=[d0b])
                P.op("dve", lambda h: h.tensor_tensor(out=d1, in0=d1, in1=rec[:, 1, :].unsqueeze(2).to_broadcast([128, 4, 128]), op=ALU.mult),
                     reads=[d1b, recb], writes=[d1b])
                P.op("dve", lambda h: h.tensor_tensor(out=d0, in0=d0, in1=d1, op=ALU.add), reads=[d0b, d1b], writes=[d0b])
                P.op("dve", lambda h: h.tensor_tensor(out=d1, in0=d0, in1=d0, op=ALU.mult), reads=[d0b], writes=[d1b])
                P.op("dve", lambda h: h.reduce_sum(out=ss, in_=d1, axis=AX.X), reads=[d1b], writes=[ssb])
                P.op("act", lambda h: h.activation(out=ss, in_=ss, func=AF.Sqrt, scale=1.0 / 128, bias=epst[:, 1:2]), reads=[ssb, epsb], writes=[ssb])
                P.op("dve", lambda h: h.reciprocal(out=ss, in_=ss), reads=[ssb], writes=[ssb])
                P.op("dve", lambda h: h.tensor_tensor(out=d0, in0=d0, in1=ss.unsqueeze(2).to_broadcast([128, 4, 128]), op=ALU.mult), reads=[d0b, ssb], writes=[d0b])
                os_ = qt % 2
                P.op("dve", lambda h, os_=os_: h.tensor_tensor(out=odn[:, os_], in0=d0, in1=subln_s.unsqueeze(1).to_broadcast([128, 4, 128]), op=ALU.mult),
                     reads=[d0b, sublnb], writes=[odnb[os_]])
                bk = mmbank()
                psbf = ps[:, bk, :].bitcast(BF16)

                def fn(h, os_=os_, psbf=psbf):
                    ins = None
                    for qb in range(4):
                        ins = h.transpose(out=psbf[:, qb * 128:(qb + 1) * 128], in_=odn[:, os_, qb, :], identity=identb)
                    return ins
                P.op("pe", fn, reads=[odnb[os_], identbb], writes=[psb[bk]])
                P.op("act", lambda h, psbf=psbf, hd=hd, qt=qt: h.activation(out=obT[:, hd, qt * 512:(qt + 1) * 512], in_=psbf[:, 0:512], func=AF.Copy),
                     reads=[psb[bk]], writes=[ob[hd][qt]])

            LAG = 2
            pend = []
            for qt in range(4):
                for kb in range(4 * qt + 4):
                    for c in range(2):
                        pend.append(emit_score(qt, kb, c))
                        if len(pend) > LAG:
                            x = pend.pop(0)
                            emit_av(x)
                            if x[1] == 4 * x[0] + 3 and x[2] == 1:
                                post(x[0])
            while pend:
                x = pend.pop(0)
                emit_av(x)
                if x[1] == 4 * x[0] + 3 and x[2] == 1:
                    post(x[0])
        AR.release(mk)

    def merge(l, b):
        mk = AR.mark()
        m, _ = AR.alloc("m", [8, S], BF16)
        mb = [[Buf(f"m{kc}_{tt}") for tt in range(4)] for kc in range(8)]
        tmp_ap, tmpb = m, None
        for item in AR.live[-1:]:
            for kc in range(8):
                for tt in range(4):
                    mb[kc][tt].r = dict(item[2][0].r)
            item[2].extend([mb[kc][tt] for kc in range(8) for tt in range(4)])
        gsb, gsbb = AR.alloc("gsb", [2, 512], F32, nbufs=2)
        gctr = 0
        for cg in range(2):
            sg = next_slot()
            wload([wtile_cols(w_in_d[l], C_G + b * 1024 + cg * 512, 512, sg)], sg)
            sb = next_slot()
            wload([wtile_cols(w_br_d[l, b], cg * 512, 512, sb)], sb)
            wg = sview(sg, 8, 512)
            wb = sview(sb, 8, 512)
            for dcl in range(4):
                dc = cg * 4 + dcl
                csl = slice(dcl * 128, (dcl + 1) * 128)
                for tt in range(4):
                    tsl = slice(tt * 512, (tt + 1) * 512)
                    bg = mmbank()
                    P.op("pe", mm_fn(ps[:, bg, :], [(wg[:, kc, csl], hT[:, kc, tsl]) for kc in range(8)]), reads=[sg[1], hb[tt]], writes=[psb[bg]])
                    g_ = gctr % 2
                    gctr += 1
                    P.op("act", lambda h, bg=bg, g_=g_: h.activation(out=gsb[:, g_], in_=ps[:, bg, :], func=AF.Tanh, scale=0.5), reads=[psb[bg]], writes=[gsbb[g_]])
                    bp = mmbank()
                    P.op("pe", mm_fn(ps[:, bp, :], [(wb[:, kc, csl], obT[:, kc, tsl]) for kc in range(8)]), reads=[sb[1]] + [ob[kc][tt] for kc in range(8)], writes=[psb[bp]])
                    P.op("dve", lambda h, bp=bp, g_=g_, dc=dc, tsl=tsl: h.scalar_tensor_tensor(out=m[:, dc, tsl], in0=gsb[:, g_], scalar=1.0, in1=ps[:, bp, :], op0=ALU.add, op1=ALU.mult),
                         reads=[psb[bp], gsbb[g_]], writes=[mb[dc][tt]])
        for cg in range(2):
            so = next_slot()
            wload([wtile_cols(w_out_d[l], cg * 512, 512, so)], so)
            wo = sview(so, 8, 512)
            for dcl in range(4):
                dc = cg * 4 + dcl
                csl = slice(dcl * 128, (dcl + 1) * 128)
                for tt in range(4):
                    tsl = slice(tt * 512, (tt + 1) * 512)
                    bk = mmbank()
                    P.op("pe", mm_fn(ps[:, bk, :], [(wo[:, kc, csl], m[:, kc, tsl]) for kc in range(8)]), reads=[so[1]] + [mb[kc][tt] for kc in range(8)], writes=[psb[bk]])
                    P.op("dve", lambda h, bk=bk, dc=dc, tsl=tsl: h.scalar_tensor_tensor(out=xT[:, dc, tsl], in0=ps[:, bk, :], scalar=0.5, in1=xT[:, dc, tsl], op0=ALU.mult, op1=ALU.add),
                         reads=[psb[bk], xb[dc][tt]], writes=[xb[dc][tt]])
        AR.release(mk)

    def branch_B(l):
        mk = AR.mark()
        gam = [1.0 - 2.0 ** (-5.0 - h_) for h_ in range(4)]
        qkT, qkTb = AR.alloc("qkT", [3, S], BF16, nbufs=1)
        kw, (kwb,) = AR.alloc("kw", [16, 128], BF16)
        vr, (vrb,) = AR.alloc("vr", [16, 256], BF16)
        cst, cstb = AR.alloc("cst", [2, 256], F32, nbufs=2)
        gcs = [P.dma_group("cs0"), P.dma_group("cs1")]
        qkf, qkfb = AR.alloc("qkf", [2, 256], F32, nbufs=2)
        t1, t1b = AR.alloc("t1B", [2, 256], F32, nbufs=2)
        t2, t2b = AR.alloc("t2B", [2, 256], F32, nbufs=2)
        qk3, qk3b = AR.alloc("qk3", [2, 3, 128], BF16, nbufs=2)
        st, (stb,) = AR.alloc("st", [256], F32)
        stbf, stbfb = AR.alloc("stbf", [2, 256], BF16, nbufs=2)
        sTm, sTmb = AR.alloc("sTm", [2, 128], BF16, nbufs=2)
        sgt, sgtb = AR.alloc("sgt", [2, 256], F32, nbufs=2)
        bst, bstb = AR.alloc("bst", [2, 8], F32, nbufs=2)
        on_, onb = AR.alloc("onB", [2, 256], F32, nbufs=2)
        orb, orbb = AR.alloc("orB", [2, 256], BF16, nbufs=2)
        w2d = w_in_d[l]
        for hd in range(4):
            s1 = next_slot()
            wload([wtile_cols(w2d, C_RQ + hd * 128, 128, s1, width=512, dcol0=0),
                   wtile_cols(w2d, C_RK + hd * 128, 128, s1, width=512, dcol0=128),
                   wtile_cols(w2d, C_RV + hd * 256, 256, s1, width=512, dcol0=256)], s1)
            s2 = next_slot()
            wload([wtile_cols(w2d, C_RG + hd * 256, 256, s2)], s2)
            w1 = sview(s1, 8, 512)
            wg = sview(s2, 8, 256)
            for tb in range(16):
                s = tb % 2
                tt = tb // 4
                bsl = slice(tb * 128, (tb + 1) * 128)
                P.op("sp", lambda h, s=s, bsl=bsl: h.dma_start(out=cst[:, s], in_=cs_d[bsl, :]), writes=[cstb[s]], dma=gcs[s])
                bk = mmbank()
                P.op("pe", mm_fn(ps[:, bk, :], [(hT[:, kc, bsl], w1[:, kc, :]) for kc in range(8)]), reads=[s1[1], hb[tt]], writes=[psb[bk]])
                P.op("act", lambda h, bk=bk, s=s: h.activation(out=qkf[:, s], in_=ps[:, bk, 0:256], func=AF.Copy), reads=[psb[bk]], writes=[qkfb[s]])
                P.op("act", lambda h, bk=bk, tb=tb: h.activation(out=vr[:, tb, :], in_=ps[:, bk, 256:512], func=AF.Copy), reads=[psb[bk]],
                     writes=[vrb] if tb == 0 else (), pwrites=() if tb == 0 else [vrb])
                xq = qkf[:, s].rearrange("p (a b) -> p a b", a=2)
                cosb = cst[:, s, 0:128].unsqueeze(1).to_broadcast([128, 2, 128])
                P.op("dve", lambda h, s=s, xq=xq, cosb=cosb: h.tensor_tensor(out=t1[:, s].rearrange("p (a b) -> p a b", a=2), in0=xq, in1=cosb, op=ALU.mult),
                     reads=[qkfb[s], cstb[s]], writes=[t1b[s]])
                x4 = qkf[:, s].rearrange("p (a b c) -> p a b c", a=2, c=2)
                t24 = t2[:, s].rearrange("p (a b c) -> p a b c", a=2, c=2)
                sn4 = cst[:, s, 128:256].rearrange("p (b c) -> p b c", c=2)
                P.op("dve", lambda h, x4=x4, t24=t24, sn4=sn4: h.tensor_tensor(out=t24[:, :, :, 0], in0=x4[:, :, :, 1], in1=sn4[:, :, 0].unsqueeze(1).to_broadcast([128, 2, 64]), op=ALU.mult),
                     reads=[qkfb[s], cstb[s]], writes=[t2b[s]])
                P.op("dve", lambda h, x4=x4, t24=t24, sn4=sn4: h.tensor_tensor(out=t24[:, :, :, 1], in0=x4[:, :, :, 0], in1=sn4[:, :, 1].unsqueeze(1).to_broadcast([128, 2, 64]), op=ALU.mult),
                     reads=[qkfb[s], cstb[s]], pwrites=[t2b[s]])
                P.op("dve", lambda h, s=s: h.tensor_tensor(out=t1[:, s], in0=t1[:, s], in1=t2[:, s], op=ALU.add), reads=[t1b[s], t2b[s]], writes=[t1b[s]])
                P.op("act", lambda h, s=s: h.activation(out=qk3[:, s, 0, :], in_=t1[:, s, 0:128], func=AF.Copy), reads=[t1b[s]], writes=[qk3b[s]])
                P.op("act", lambda h, s=s, hd=hd: h.activation(out=qk3[:, s, 1, :], in_=t1[:, s, 0:128], func=AF.Identity, scale=retc[:, hd, 129:130]), reads=[t1b[s], retcb], pwrites=[qk3b[s]])
                P.op("act", lambda h, s=s: h.activation(out=qk3[:, s, 2, :], in_=t1[:, s, 128:256], func=AF.Copy), reads=[t1b[s]], pwrites=[qk3b[s]])
                P.op("dve", lambda h, s=s, hd=hd, tb=tb: h.tensor_scalar(out=kw[:, tb, :], in0=t1[:, s, 128:256], scalar1=retc[:, hd, 128:129], scalar2=None, op0=ALU.mult),
                     reads=[t1b[s], retcb], writes=[kwb] if tb == 0 else (), pwrites=() if tb == 0 else [kwb])
                bk = mmbank()
                psbf = ps[:, bk, :].bitcast(BF16)

                def fn(h, s=s, psbf=psbf):
                    ins = None
                    for j in range(3):
                        ins = h.transpose(out=psbf[:, j * 128:(j + 1) * 128], in_=qk3[:, s, j, :], identity=identb)
                    return ins
                P.op("pe", fn, reads=[qk3b[s], identbb], writes=[psb[bk]])
                P.op("dve", lambda h, psbf=psbf, bsl=bsl: h.tensor_copy(out=qkT[:, :, bsl], in_=psbf[:, 0:384].rearrange("p (a b) -> p a b", a=3)),
                     reads=[psb[bk]], writes=[qkTb[0]] if tb == 0 else (), pwrites=() if tb == 0 else [qkTb[0]])
            cd = gam[hd] ** 128
            for n in range(16):
                s = n % 2
                tt = n // 4
                bsl = slice(n * 128, (n + 1) * 128)
                bk = mmbank()
                P.op("pe", mm_fn(ps[:, bk, 0:128], [(qkT[:, 2, bsl], qkT[:, 0, bsl])]), reads=[qkTb[0]], writes=[psb[bk]])
                P.op("dve", lambda h, bk=bk, s=s, hd=hd: h.tensor_tensor(out=sTm[:, s], in0=ps[:, bk, 0:128], in1=retc[:, hd, 0:128], op=ALU.mult),
                     reads=[psb[bk], retcb], writes=[sTmb[s]])
                bo = mmbank()
                pairs = [(sTm[:, s], vr[:, n, :])]
                rd = [sTmb[s], vrb]
                if n > 0:
                    pairs.append((qkT[:, 1, bsl], stbf[:, (n - 1) % 2]))
                    rd += [qkTb[0], stbfb[(n - 1) % 2]]
                P.op("pe", mm_fn(ps[:, bo, 0:256], pairs), reads=rd, writes=[psb[bo]])
                bg = mmbank()
                P.op("pe", mm_fn(ps[:, bg, 0:256], [(hT[:, kc, bsl], wg[:, kc, :]) for kc in range(8)]), reads=[s2[1], hb[tt]], writes=[psb[bg]])
                P.op("act", lambda h, bg=bg, s=s: h.activation(out=sgt[:, s], in_=ps[:, bg, 0:256], func=AF.Tanh, scale=0.5), reads=[psb[bg]], writes=[sgtb[s]])
                P.op("dve", lambda h, bg=bg, s=s: h.scalar_tensor_tensor(out=sgt[:, s], in0=sgt[:, s], scalar=1.0, in1=ps[:, bg, 0:256], op0=ALU.add, op1=ALU.mult), reads=[psb[bg], sgtb[s]], writes=[sgtb[s]])
                if n < 15:
                    bkv = mmbank()
                    P.op("pe", mm_fn(ps[:, bkv, 0:256], [(kw[:, n, :], vr[:, n, :])]), reads=[kwb, vrb], writes=[psb[bkv]])
                    if n == 0:
                        P.op("dve", lambda h, bkv=bkv: h.tensor_copy(out=st, in_=ps[:, bkv, 0:256]), reads=[psb[bkv]], writes=[stb])
                    else:
                        P.op("dve", lambda h, bkv=bkv, cd=cd: h.scalar_tensor_tensor(out=st, in0=st, scalar=cd, in1=ps[:, bkv, 0:256], op0=ALU.mult, op1=ALU.add),
                             reads=[psb[bkv], stb], writes=[stb])
                    P.op("act", lambda h, s=s: h.activation(out=stbf[:, s], in_=st, func=AF.Copy), reads=[stb], writes=[stbfb[s]])
                P.op("dve", lambda h, bo=bo, s=s: h.bn_stats(out=bst[:, s, 0:6], in_=ps[:, bo, 0:256]), reads=[psb[bo]], writes=[bstb[s]])
                P.op("dve", lambda h, s=s: h.bn_aggr(out=bst[:, s, 6:8], in_=bst[:, s, 0:6]), reads=[bstb[s]], writes=[bstb[s]])
                P.op("act", lambda h, s=s: h.activation(out=bst[:, s, 7:8], in_=bst[:, s, 7:8], func=AF.Sqrt, scale=4.0, bias=epst[:, 4:5]), reads=[bstb[s], epsb], writes=[bstb[s]])
                P.op("dve", lambda h, s=s: h.reciprocal(out=bst[:, s, 7:8], in_=bst[:, s, 7:8]), reads=[bstb[s]], writes=[bstb[s]])
                P.op("dve", lambda h, bo=bo, s=s: h.tensor_scalar(out=on_[:, s], in0=ps[:, bo, 0:256], scalar1=bst[:, s, 6:7], scalar2=bst[:, s, 7:8], op0=ALU.subtract, op1=ALU.mult),
                     reads=[psb[bo], bstb[s]], writes=[onb[s]])
                P.op("dve", lambda h, s=s: h.tensor_tensor(out=orb[:, s], in0=on_[:, s], in1=sgt[:, s], op=ALU.mult), reads=[onb[s], sgtb[s]], writes=[orbb[s]])
                bk = mmbank()
                psbf = ps[:, bk, :].bitcast(BF16)

                def fn(h, s=s, psbf=psbf):
                    ins = None
                    for j in range(2):
                        ins = h.transpose(out=psbf[:, j * 128:(j + 1) * 128], in_=orb[:, s, j * 128:(j + 1) * 128], identity=identb)
                    return ins
                P.op("pe", fn, reads=[orbb[s], identbb], writes=[psb[bk]])
                P.op("act", lambda h, psbf=psbf, hd=hd, bsl=bsl: h.activation(out=obT[:, 2 * hd:2 * hd + 2, bsl], in_=psbf[:, 0:256].rearrange("p (a b) -> p a b", a=2), func=AF.Copy),
                     reads=[psb[bk]], pwrites=[ob[2 * hd][tt], ob[2 * hd + 1][tt]])
        AR.release(mk)

    def branch_C(l):
        mk = AR.mark()
        z, (zb,) = AR.alloc("zC", [8], F32)
        z2, (z2b,) = AR.alloc("z2C", [8], F32)
        lamv = vecs[:, l, 56:64]
        P.op("dve", lambda h: h.tensor_scalar(out=z, in0=lamv, scalar1=-1.0, scalar2=None, op0=ALU.mult), reads=[vecsb], writes=[zb])
        P.op("dve", lambda h: h.tensor_tensor(out=z2, in0=z, in1=lamv, op=ALU.max), reads=[zb, vecsb], writes=[z2b])
        P.op("act", lambda h: h.activation(out=z2, in_=z2, func=AF.Exp, scale=-1.0), reads=[z2b], writes=[z2b])
        P.op("act", lambda h: h.activation(out=z2, in_=z2, func=AF.Ln, bias=epst[:, 2:3]), reads=[z2b, epsb], writes=[z2b])
        P.op("dve", lambda h: h.tensor_scalar(out=z, in0=z, scalar1=0.0, scalar2=None, op0=ALU.max), reads=[zb], writes=[zb])
        P.op("dve", lambda h: h.tensor_tensor(out=z, in0=z, in1=z2, op=ALU.add), reads=[zb, z2b], writes=[zb])
        P.op("dve", lambda h: h.tensor_scalar(out=lru_sc, in0=z, scalar1=-4.0, scalar2=None, op0=ALU.mult), reads=[zb], writes=[lruscb])
        hbias, (hbiasb,) = AR.alloc("hbias", [16], F32)
        P.op("dve", lambda h: h.tensor_scalar(out=hbias, in0=vecs[:, l, 40:56], scalar1=0.5, scalar2=None, op0=ALU.mult), reads=[vecsb], writes=[hbiasb])
        wax, (waxb,) = AR.alloc("wax", [2, 8, 128], BF16)
        gwa = P.dma_group("wax")
        P.op("pool", lambda h: [h.dma_start(out=wax[:, 0], in_=wa_d[l].rearrange("n c d -> c n d")),
                                h.dma_start(out=wax[:, 1], in_=wx_d[l].rearrange("n c d -> c n d"))], writes=[waxb], dma=gwa, ndma=2)
        lxs, (lxsb,) = AR.alloc("lxs", [3 + S], F32)
        lxtb = [Buf(f"lxs{tt}") for tt in range(4)]
        for item in AR.live[-1:]:
            for tt in range(4):
                lxtb[tt].r = dict(item[2][0].r)
            item[2].extend(lxtb)
        NB = 2
        ysb, ysbb = AR.alloc("ysb", [NB, 512], F32, nbufs=NB)
        ty_, tyb_ = AR.alloc("ty", [1, 512], F32, nbufs=1); ty = [ty_[:, 0], ty_[:, 0]]; tyb = [tyb_[0], tyb_[0]]
        sgm, sgmb = AR.alloc("sgm", [NB, 512], F32, nbufs=NB)
        xc, xcb = AR.alloc("xc", [NB, 512], F32, nbufs=NB)
        xcbf, xcbfb = AR.alloc("xcbf", [NB, 512], BF16, nbufs=NB)
        ra, rab = AR.alloc("ra", [NB, 512], F32, nbufs=NB)
        ii, iib = AR.alloc("ii", [NB, 512], F32, nbufs=NB)
        a2_, a2b_ = AR.alloc("a2", [1, 512], F32, nbufs=1); a2 = [a2_[:, 0], a2_[:, 0]]; a2b = [a2b_[0], a2b_[0]]
        hh, hhb = AR.alloc("hh", [2, 512], F32, nbufs=2)
        w2d = w_in_d[l]
        it = 0
        for c in range(8):
            slot = next_slot()
            wload([wtile_cols(w2d, C_LX + c * 128, 128, slot, width=256, dcol0=0),
                   wtile_cols(w2d, C_LY + c * 128, 128, slot, width=256, dcol0=128)], slot)
            wt = sview(slot, 8, 256)
            P.op("dve", lambda h: h.memset(lxs[:, 0:3], 0.0), pwrites=[lxtb[0]])
            for tt in range(4):
                s = it % NB
                hs = it % 2
                it += 1
                tsl = slice(tt * 512, (tt + 1) * 512)
                bk = mmbank()
                P.op("pe", mm_fn(ps[:, bk, :], [(wt[:, kc, 0:128], hT[:, kc, tsl]) for kc in range(8)]), reads=[slot[1], hb[tt]], writes=[psb[bk]])
                P.op("act", lambda h, bk=bk, tt=tt: h.activation(out=lxs[:, 3 + tt * 512:3 + (tt + 1) * 512], in_=ps[:, bk, :], func=AF.Copy),
                     reads=[psb[bk]], writes=[lxtb[tt]] if tt > 0 else (), pwrites=[lxtb[0]] if tt == 0 else ())
                bk = mmbank()
                P.op("pe", mm_fn(ps[:, bk, :], [(wt[:, kc, 128:256], hT[:, kc, tsl]) for kc in range(8)]), reads=[slot[1], hb[tt]], writes=[psb[bk]])
                P.op("act", lambda h, bk=bk, s=s: h.activation(out=ysb[:, s], in_=ps[:, bk, :], func=AF.Copy), reads=[psb[bk]], writes=[ysbb[s]])
                P.op("dve", lambda h, s=s: h.tensor_tensor(out=ty[s], in0=ysb[:, s], in1=ysb[:, s], op=ALU.mult), reads=[ysbb[s]], writes=[tyb[s]])
                P.op("dve", lambda h, s=s: h.tensor_scalar(out=ty[s], in0=ty[s], scalar1=0.044715, scalar2=1.0, op0=ALU.mult, op1=ALU.add), reads=[tyb[s]], writes=[tyb[s]])
                P.op("dve", lambda h, s=s: h.tensor_tensor(out=ty[s], in0=ty[s], in1=ysb[:, s], op=ALU.mult), reads=[tyb[s], ysbb[s]], writes=[tyb[s]])
                P.op("act", lambda h, s=s: h.activation(out=sgm[:, s], in_=ty[s], func=AF.Tanh, scale=0.7978845608028654), reads=[tyb[s]], writes=[sgmb[s]])
                P.op("dve", lambda h, s=s: h.scalar_tensor_tensor(out=sgm[:, s], in0=sgm[:, s], scalar=1.0, in1=ysb[:, s], op0=ALU.add, op1=ALU.mult), reads=[sgmb[s], ysbb[s]], writes=[sgmb[s]])
                lrd = [lxtb[tt]] + ([lxtb[tt - 1]] if tt > 0 else [])
                P.op("dve", lambda h, s=s, tt=tt, c=c: h.tensor_scalar(out=xc[:, s], in0=lxs[:, tt * 512:tt * 512 + 512], scalar1=vecs[:, l, 64 + c:65 + c], scalar2=vecs[:, l, 32 + c:33 + c],
                                                                      op0=ALU.mult, op1=ALU.add), reads=lrd + [vecsb], writes=[xcb[s]])
                for j in range(1, 4):
                    P.op("dve", lambda h, s=s, tt=tt, c=c, j=j: h.scalar_tensor_tensor(out=xc[:, s], in0=lxs[:, tt * 512 + j:tt * 512 + j + 512], scalar=vecs[:, l, 64 + j * 8 + c:65 + j * 8 + c],
                                                                                      in1=xc[:, s], op0=ALU.mult, op1=ALU.add), reads=lrd + [vecsb, xcb[s]], writes=[xcb[s]])
                P.op("act", lambda h, s=s: h.activation(out=xcbf[:, s], in_=xc[:, s], func=AF.Copy), reads=[xcb[s]], writes=[xcbfb[s]])
                bk = mmbank()
                P.op("pe", mm_fn(ps[:, bk, :], [(wax[:, 0, c, :], xcbf[:, s])]), reads=[waxb, xcbfb[s]], writes=[psb[bk]])
                P.op("act", lambda h, bk=bk, s=s, c=c: h.activation(out=ra[:, s], in_=ps[:, bk, :], func=AF.Tanh, scale=0.5, bias=hbias[:, c:c + 1]), reads=[psb[bk], hbiasb], writes=[rab[s]])
                bk = mmbank()
                P.op("pe", mm_fn(ps[:, bk, :], [(wax[:, 1, c, :], xcbf[:, s])]), reads=[waxb, xcbfb[s]], writes=[psb[bk]])
                P.op("act", lambda h, bk=bk, s=s, c=c: h.activation(out=ii[:, s], in_=ps[:, bk, :], func=AF.Tanh, scale=0.5, bias=hbias[:, 8 + c:9 + c]), reads=[psb[bk], hbiasb], writes=[iib[s]])
                P.op("act", lambda h, s=s, c=c: h.activation(out=ra[:, s], in_=ra[:, s], func=AF.Exp, scale=lru_sc[:, c:c + 1], bias=lru_sc[:, c:c + 1]), reads=[rab[s], lruscb], writes=[rab[s]])
                P.op("dve", lambda h, s=s: h.tensor_tensor(out=a2[s], in0=ra[:, s], in1=ra[:, s], op=ALU.mult), reads=[rab[s]], writes=[a2b[s]])
                P.op("act", lambda h, s=s: h.activation(out=a2[s], in_=a2[s], func=AF.Sqrt, scale=-0.25, bias=epst[:, 3:4]), reads=[a2b[s], epsb], writes=[a2b[s]])
                P.op("dve", lambda h, s=s: h.scalar_tensor_tensor(out=ii[:, s], in0=ii[:, s], scalar=1.0, in1=xc[:, s], op0=ALU.add, op1=ALU.mult), reads=[iib[s], xcb[s]], writes=[iib[s]])
                P.op("dve", lambda h, s=s: h.tensor_tensor(out=ii[:, s], in0=ii[:, s], in1=a2[s], op=ALU.mult), reads=[iib[s], a2b[s]], writes=[iib[s]])
                if tt == 0:
                    P.op("dve", lambda h, s=s, hs=hs: h.tensor_tensor_scan(out=hh[:, hs], data0=ra[:, s], data1=ii[:, s], initial=0.0, op0=ALU.mult, op1=ALU.add),
                         reads=[rab[s], iib[s]], writes=[hhb[hs]])
                else:
                    P.op("dve", lambda h, s=s, hs=hs: h.tensor_tensor_scan(out=hh[:, hs], data0=ra[:, s], data1=ii[:, s], initial=hh[:, 1 - hs, 511:512], op0=ALU.mult, op1=ALU.add),
                         reads=[rab[s], iib[s], hhb[1 - hs]], writes=[hhb[hs]])
                P.op("dve", lambda h, s=s, hs=hs, c=c, tsl=tsl: h.scalar_tensor_tensor(out=obT[:, c, tsl], in0=hh[:, hs], scalar=0.5, in1=sgm[:, s], op0=ALU.mult, op1=ALU.mult),
                     reads=[hhb[hs], sgmb[s]], writes=[ob[c][tt]])
        AR.release(mk)

    def xattn(l):
        norm_stage(l, 8)
        mk = AR.mark()
        memt, (memtb,) = AR.alloc("memt", [2, D], F32)
        gm = P.dma_group("mem")
        P.op("sp", lambda h: h.dma_start(out=memt, in_=mem_d.rearrange("(a p) d -> p a d", p=128)), writes=[memtb], dma=gm)
        memn, (memnb,) = AR.alloc("memn", [2, D], BF16)
        msq, (msqb,) = AR.alloc("msq", [D], F32)
        mss, (mssb,) = AR.alloc("mss", [2], F32)
        for a in range(2):
            P.op("act", lambda h, a=a: h.activation(out=msq, in_=memt[:, a, :], func=AF.Square, accum_out=mss[:, a:a + 1]), reads=[memtb], writes=[msqb], pwrites=[mssb])
        P.op("act", lambda h: h.activation(out=mss, in_=mss, func=AF.Sqrt, scale=1.0 / D, bias=epst[:, 0:1]), reads=[mssb, epsb], writes=[mssb])
        P.op("dve", lambda h: h.reciprocal(out=mss, in_=mss), reads=[mssb], writes=[mssb])
        for a in range(2):
            P.op("dve", lambda h, a=a: h.tensor_scalar(out=memn[:, a, :], in0=memt[:, a, :], scalar1=mss[:, a:a + 1], scalar2=None, op0=ALU.mult),
                 reads=[memtb, mssb], writes=[memnb] if a == 0 else (), pwrites=() if a == 0 else [memnb])
        memT, (memTb,) = AR.alloc("memT", [8, 256], BF16)
        for a in range(2):
            for half in range(2):
                bk = mmbank()
                psbf = ps[:, bk, :].bitcast(BF16)

                def fn(h, a=a, half=half, psbf=psbf):
                    ins = None
                    for j in range(4):
                        kc = half * 4 + j
                        ins = h.transpose(out=psbf[:, j * 128:(j + 1) * 128], in_=memn[:, a, kc * 128:(kc + 1) * 128], identity=identb)
                    return ins
                P.op("pe", fn, reads=[memnb, identbb], writes=[psb[bk]])
                for j in range(4):
                    kc = half * 4 + j
                    P.op("dve", lambda h, psbf=psbf, j=j, kc=kc, a=a: h.tensor_scalar(out=memT[:, kc, a * 128:(a + 1) * 128], in0=psbf[:, j * 128:(j + 1) * 128],
                                                                                   scalar1=vecs[:, l, 16 + kc:17 + kc], scalar2=None, op0=ALU.mult),
                         reads=[psb[bk], vecsb], pwrites=[memTb])
        kxT, (kxTb,) = AR.alloc("kxT", [4, 256], BF16)
        vx, (vxb,) = AR.alloc("vx", [2, 4, 129], BF16)
        P.op("dve", lambda h: h.memset(vx[:, :, :, 128:129], 1.0), writes=[vxb])
        sk = next_slot()
        wload([wtile_cols(wkv_d[l], 0, 512, sk)], sk)
        wk = sview(sk, 8, 512)
        for hd in range(4):
            bk = mmbank()
            P.op("pe", mm_fn(ps[:, bk, 0:256], [(wk[:, kc, hd * 128:(hd + 1) * 128], memT[:, kc, :]) for kc in range(8)]), reads=[sk[1], memTb], writes=[psb[bk]])
            P.op("act", lambda h, bk=bk, hd=hd: h.activation(out=kxT[:, hd, :], in_=ps[:, bk, 0:256], func=AF.Copy), reads=[psb[bk]], pwrites=[kxTb])
        sv = next_slot()
        wload([wtile_cols(wkv_d[l], 512, 512, sv)], sv)
        wv = sview(sv, 8, 512)
        for a in range(2):
            bk = mmbank()
            P.op("pe", mm_fn(ps[:, bk, :], [(memT[:, kc, a * 128:(a + 1) * 128], wv[:, kc, :]) for kc in range(8)]), reads=[sv[1], memTb], writes=[psb[bk]])
            P.op("act", lambda h, bk=bk, a=a: h.activation(out=vx[:, a, :, 0:128], in_=ps[:, bk, :].rearrange("p (a b) -> p a b", a=4), func=AF.Copy), reads=[psb[bk]], pwrites=[vxb])
        sq_ = next_slot()
        wload([wtile_cols(wq_d[l], 0, 512, sq_)], sq_)
        wq = sview(sq_, 8, 512)
        qx, qxb = AR.alloc("qx", [2, S], BF16, nbufs=2)
        NE = 5
        et, etb = AR.alloc("eX", [NE, 512], BF16, nbufs=NE)
        rec, (recb,) = AR.alloc("recX", [4], F32)
        ox, oxb = AR.alloc("oxX", [2, 4, 128], BF16, nbufs=2)
        ectr = 0
        oxT = obT
        for hd in range(4):
            s = hd % 2
            for tt in range(4):
                tsl = slice(tt * 512, (tt + 1) * 512)
                bk = mmbank()
                P.op("pe", mm_fn(ps[:, bk, :], [(wq[:, kc, hd * 128:(hd + 1) * 128], hT[:, kc, tsl]) for kc in range(8)]), reads=[sq_[1], hb[tt]], writes=[psb[bk]])
                P.op("act", lambda h, bk=bk, s=s, tsl=tsl: h.activation(out=qx[:, s, tsl], in_=ps[:, bk, :], func=AF.Copy, scale=128 ** -0.5),
                     reads=[psb[bk]], writes=[qxb[s]] if tt == 0 else (), pwrites=() if tt == 0 else [qxb[s]])
            def x_score(tt, a, hd=hd, s=s):
                nonlocal ectr
                tsl = slice(tt * 512, (tt + 1) * 512)
                bk = mmbank()
                P.op("pe", mm_fn(ps[:, bk, :], [(kxT[:, hd, a * 128:(a + 1) * 128], qx[:, s, tsl])]), reads=[kxTb, qxb[s]], writes=[psb[bk]])
                e = ectr % NE
                ectr += 1
                P.op("act", lambda h, bk=bk, e=e: h.activation(out=et[:, e], in_=ps[:, bk, :], func=AF.Exp), reads=[psb[bk]], writes=[etb[e]])
                return (tt, a, e)

            def x_av(info, hd=hd):
                tt, a, e = info

                def fn(h):
                    ins = None
                    for qb in range(4):
                        bank = 3 + qb // 2
                        co = (qb % 2) * 129
                        ins = h.matmul(ps[:, bank, co:co + 129], lhsT=et[:, e, qb * 128:(qb + 1) * 128], rhs=vx[:, a, hd, :],
                                       start=(a == 0 and qb % 2 == 0), stop=(a == 1), skip_group_check=True)
                    return ins
                P.op("pe", fn, reads=[etb[e], vxb], writes=[psb[3], psb[4]] if a == 0 else (), pwrites=() if a == 0 else [psb[3], psb[4]])

            def x_post(tt, hd=hd):
                tsl = slice(tt * 512, (tt + 1) * 512)
                acc0 = ps[:, 3:5, 0:258].rearrange("p a (b c) -> p a b c", b=2)
                recv = rec.rearrange("p (a b) -> p a b", a=2)
                P.op("dve", lambda h, acc0=acc0, recv=recv: h.reciprocal(out=recv, in_=acc0[:, :, :, 128]), reads=[psb[3], psb[4]], writes=[recb])
                os_ = tt % 2
                P.op("dve", lambda h, acc0=acc0, recv=recv, os_=os_: h.tensor_tensor(out=ox[:, os_].rearrange("p (a b) c -> p a b c", a=2), in0=acc0[:, :, :, 0:128],
                                                                                  in1=recv.unsqueeze(3).to_broadcast([128, 2, 2, 128]), op=ALU.mult),
                     reads=[psb[3], psb[4], recb], writes=[oxb[os_]])
                bk = mmbank()
                psbf = ps[:, bk, :].bitcast(BF16)

                def fn(h, os_=os_, psbf=psbf):
                    ins = None
                    for qb in range(4):
                        ins = h.transpose(out=psbf[:, qb * 128:(qb + 1) * 128], in_=ox[:, os_, qb, :], identity=identb)
                    return ins
                P.op("pe", fn, reads=[oxb[os_], identbb], writes=[psb[bk]])
                P.op("act", lambda h, psbf=psbf, hd=hd, tsl=tsl: h.activation(out=oxT[:, hd, tsl], in_=psbf[:, 0:512], func=AF.Copy), reads=[psb[bk]], writes=[ob[hd][tt]])

            pend = []
            for tt in range(4):
                for a in range(2):
                    pend.append(x_score(tt, a))
                    if len(pend) > 2:
                        x = pend.pop(0)
                        x_av(x)
                        if x[1] == 1:
                            x_post(x[0])
            while pend:
                x = pend.pop(0)
                x_av(x)
                if x[1] == 1:
                    x_post(x[0])
        so = next_slot()
        P.op("pool", lambda h: [h.dma_start(out=so[0].rearrange("p (k c) -> p k c", k=4), in_=wo_d[l].rearrange("(k p) c -> p k c", p=128))], writes=[so[1]], dma=so[2], ndma=1)
        wo = so[0].rearrange("p (k c) -> p k c", k=4)
        for dc in range(8):
            for tt in range(4):
                tsl = slice(tt * 512, (tt + 1) * 512)
                bk = mmbank()
                P.op("pe", mm_fn(ps[:, bk, :], [(wo[:, hd, dc * 128:(dc + 1) * 128], oxT[:, hd, tsl]) for hd in range(4)]), reads=[so[1]] + [ob[hd][tt] for hd in range(4)], writes=[psb[bk]])
                P.op("dve", lambda h, bk=bk, dc=dc, tsl=tsl: h.tensor_tensor(out=xT[:, dc, tsl], in0=xT[:, dc, tsl], in1=ps[:, bk, :], op=ALU.add),
                     reads=[psb[bk], xb[dc][tt]], writes=[xb[dc][tt]])
        AR.release(mk)

    def mlp(l):
        norm_stage(l, 24)
        mk = AR.mark()
        rl, rlb = AR.alloc("rl", [2, 512], F32, nbufs=2)
        hid = obT.rearrange("p (a b) s -> p a b s", a=2)
        hidb = [[Buf(f"hid{a}_{tt}") for tt in range(4)] for a in range(2)]
        for a in range(2):
            for tt in range(4):
                for kc in range(4):
                    for k_, d_ in ob[a * 4 + kc][tt].w.items():
                        hidb[a][tt].r[("ow", kc, k_)] = d_
                    for k_, d_ in ob[a * 4 + kc][tt].r.items():
                        hidb[a][tt].r[("or", kc, k_)] = d_
        rctr = 0
        for fg in range(8):
            a = fg % 2
            s1 = next_slot()
            wload([wtile_cols(w1_d[l], fg * 512, 512, s1)], s1)
            w1 = sview(s1, 8, 512)
            s2 = next_slot()
            P.op("pool", lambda h, s2=s2, fg=fg: [h.dma_start(out=s2[0].rearrange("p (k c) -> p k c", k=4), in_=w2_d[l, fg * 512:(fg + 1) * 512, :].rearrange("(k p) c -> p k c", p=128))],
                 writes=[s2[1]], dma=s2[2], ndma=1)
            w2 = s2[0].rearrange("p (k c) -> p k c", k=4)
            for fc in range(4):
                for tt in range(4):
                    tsl = slice(tt * 512, (tt + 1) * 512)
                    bk = mmbank()
                    P.op("pe", mm_fn(ps[:, bk, :], [(w1[:, kc, fc * 128:(fc + 1) * 128], hT[:, kc, tsl]) for kc in range(8)]), reads=[s1[1], hb[tt]], writes=[psb[bk]])
                    r_ = rctr % 2
                    rctr += 1
                    P.op("act", lambda h, bk=bk, r_=r_: h.activation(out=rl[:, r_], in_=ps[:, bk, :], func=AF.Relu), reads=[psb[bk]], writes=[rlb[r_]])
                    P.op("dve", lambda h, r_=r_, a=a, fc=fc, tsl=tsl: h.tensor_tensor(out=hid[:, a, fc, tsl], in0=rl[:, r_], in1=rl[:, r_], op=ALU.mult),
                         reads=[rlb[r_]], writes=[hidb[a][tt]] if fc == 0 else (), pwrites=() if fc == 0 else [hidb[a][tt]])
            for dc in range(8):
                for tt in range(4):
                    tsl = slice(tt * 512, (tt + 1) * 512)
                    bk = mmbank()
                    P.op("pe", mm_fn(ps[:, bk, :], [(w2[:, fc, dc * 128:(dc + 1) * 128], hid[:, a, fc, tsl]) for fc in range(4)]), reads=[s2[1], hidb[a][tt]], writes=[psb[bk]])
                    P.op("dve", lambda h, bk=bk, dc=dc, tsl=tsl: h.tensor_tensor(out=xT[:, dc, tsl], in0=xT[:, dc, tsl], in1=ps[:, bk, :], op=ALU.add),
                         reads=[psb[bk], xb[dc][tt]], writes=[xb[dc][tt]])
        for a in range(2):
            for tt in range(4):
                for kc in range(4):
                    b_ = ob[a * 4 + kc][tt]
                    for k_, d_ in hidb[a][tt].w.items():
                        b_.r[("hw", k_)] = d_
                    for k_, d_ in hidb[a][tt].r.items():
                        b_.r[("hr", k_)] = d_
        AR.release(mk)

    def final():
        mk = AR.mark()
        sq, sqb = AR.alloc("sqF", [2, 8, 512], BF16, nbufs=2)
        rt, rtb = AR.alloc("rtF", [2, 512], F32, nbufs=2)
        yf = hT.rearrange("p a s -> p (a s)").bitcast(F32).rearrange("p (a b c) -> p a b c", a=2, b=8)
        yfb = [Buf("yf0"), Buf("yf1")]
        for yb_ in yfb:
            for tt_ in range(4):
                for k_, d_ in hb[tt_].w.items():
                    yb_.r[("hw", tt_, k_)] = d_
                for k_, d_ in hb[tt_].r.items():
                    yb_.r[("hr", tt_, k_)] = d_
        osb, osbb = AR.alloc("osb", [2, D], F32, nbufs=2)
        go = [P.dma_group("out0"), P.dma_group("out1")]
        outbufs = [Buf("outd0"), Buf("outd1")]
        octr = 0
        for tt in range(4):
            s = tt % 2
            tsl = slice(tt * 512, (tt + 1) * 512)
            P.op("act", lambda h, s=s, tsl=tsl: h.activation(out=sq[:, s], in_=xT[:, :, tsl], func=AF.Square), reads=[xb[kc][tt] for kc in range(8)], writes=[sqb[s]])
            P.op("pe", mm_fn(ps[:, 7, :], [(onesb, sq[:, s, kc, :]) for kc in range(8)]), reads=[sqb[s], onesbb], writes=[psb[7]])
            P.op("act", lambda h, s=s: h.activation(out=rt[:, s], in_=ps[:, 7, :], func=AF.Sqrt, scale=1.0 / D, bias=epst[:, 0:1]), reads=[psb[7], epsb], writes=[rtb[s]])
            P.op("dve", lambda h, s=s: h.reciprocal(out=rt[:, s], in_=rt[:, s]), reads=[rtb[s]], writes=[rtb[s]])
            for kc in range(8):
                P.op("dve", lambda h, s=s, kc=kc, tsl=tsl: h.scalar_tensor_tensor(out=yf[:, s, kc, :], in0=xT[:, kc, tsl], scalar=vecs[:, 0, 96 + kc:97 + kc], in1=rt[:, s], op0=ALU.mult, op1=ALU.mult),
                     reads=[xb[kc][tt], rtb[s], vecsb], writes=[yfb[s]] if kc == 0 else (), pwrites=() if kc == 0 else [yfb[s]])
            for tbl in range(4):
                o_ = octr % 2
                octr += 1
                tb = tt * 4 + tbl
                for half in range(2):
                    bk = mmbank()

                    def fn(h, s=s, half=half, bk=bk, tbl=tbl):
                        ins = None
                        for j in range(4):
                            kc = half * 4 + j
                            ins = h.transpose(out=ps[:, bk, j * 128:(j + 1) * 128], in_=yf[:, s, kc, tbl * 128:(tbl + 1) * 128], identity=identf)
                        return ins
                    P.op("pe", fn, reads=[yfb[s], cstfb], writes=[psb[bk]])
                    if half == 0:
                        P.op("dve", lambda h, bk=bk, o_=o_: h.tensor_copy(out=osb[:, o_, 0:512], in_=ps[:, bk, :]), reads=[psb[bk]], writes=[osbb[o_]])
                    else:
                        P.op("act", lambda h, bk=bk, o_=o_: h.activation(out=osb[:, o_, 512:1024], in_=ps[:, bk, :], func=AF.Copy), reads=[psb[bk]], pwrites=[osbb[o_]])
                P.op("sp", lambda h, o_=o_, tb=tb: h.dma_start(out=out_d[tb * 128:(tb + 1) * 128, :], in_=osb[:, o_, :]), reads=[osbb[o_]], writes=[outbufs[o_]], dma=go[o_])
        P.op("sp", None, reads=outbufs)
        AR.release(mk)

    def dump():
        gd = P.dma_group("dbg")
        d1 = Buf("dbg1")
        P.op("sp", lambda h: h.dma_start(out=dbgx_d, in_=xT.rearrange("p a s -> p (a s)")), reads=all_x, writes=[d1], dma=gd)
        gd2 = P.dma_group("dbg2")
        d2 = Buf("dbg2")
        P.op("sp", lambda h: h.dma_start(out=dbgo_d, in_=obT.rearrange("p a s -> p (a s)")), reads=all_o, writes=[d2], dma=gd2)
        P.op("sp", None, reads=[d1, d2])

    done = False
    for l in range(n_layers):
        norm_stage(l, 0)
        if stop_after == ("norm", l):
            done = True
            break
        for bi, (nm, fn_) in enumerate((("A", branch_A), ("B", branch_B), ("C", branch_C))):
            fn_(l)
            if stop_after == (nm, l):
                done = True
                break
            merge(l, bi)
        if done:
            break
        if stop_after == ("mix", l):
            done = True
            break
        xattn(l)
        if stop_after == ("xattn", l):
            done = True
            break
        mlp(l)
        if stop_after == ("mlp", l):
            done = True
            break
    if dbg:
        dump()
    if not done:
        final()
    else:
        pass
    P.emit()
    nc._arena_peak = AR.peak
    return nc


def _host_consts():
    ident = np.eye(128, dtype=np.float32)
    k = np.arange(128)[:, None]
    q = np.arange(128)[None, :]
    mask = (k <= q).astype(np.float32)
    cst = np.concatenate([ident, mask], axis=1)
    log_g = np.log(1.0 - np.exp2(-5.0 - np.arange(4, dtype=np.float64)))
    idx = np.arange(128, dtype=np.float64)
    retc = np.zeros((128, 4, 130), np.float32)
    kscale = 128.0 ** -0.5
    for h in range(4):
        rel = idx[None, :] - idx[:, None]
        dec = np.where(rel >= 0, np.exp(np.maximum(rel, 0.0) * log_g[h]), 0.0) * kscale
        retc[:, h, 0:128] = dec
        retc[:, h, 128] = np.exp((127.0 - idx) * log_g[h]) * kscale
        retc[:, h, 129] = np.exp((idx + 1.0) * log_g[h])
    pos = np.arange(S, dtype=np.float32)
    angle = np.repeat((1.0 / (10000.0 ** np.linspace(0.0, 1.0, 64, dtype=np.float32))).astype(np.float32), 2)
    phase = (pos[:, None] * angle[None, :]).astype(np.float32)
    cos = np.cos(phase).astype(np.float32)
    sin = np.sin(phase).astype(np.float32)
    sgn = np.tile(np.array([-1.0, 1.0], np.float32), 64)
    cstab = np.concatenate([cos, sin * sgn[None, :]], axis=1).astype(np.float32)
    return cst, retc.reshape(128, 4 * 130), cstab


def _pack(inputs):
    f = lambda a: np.ascontiguousarray(np.asarray(a, dtype=np.float32))

    def fm(v):
        return f(v).reshape(-1, 8, 128).transpose(2, 0, 1)
    vecs = np.zeros((128, NL, NV), np.float32)
    vecs[:, :, 0:8] = fm(inputs["norm_mix"])
    vecs[:, :, 8:16] = fm(inputs["norm_xattn"])
    vecs[:, :, 16:24] = fm(inputs["norm_mem"])
    vecs[:, :, 24:32] = fm(inputs["norm_mlp"])
    vecs[:, :, 32:40] = fm(inputs["lru_conv_b"])
    vecs[:, :, 40:48] = fm(inputs["lru_ba"])
    vecs[:, :, 48:56] = fm(inputs["lru_bx"])
    vecs[:, :, 56:64] = fm(inputs["lru_lambda"])
    cw = f(inputs["lru_conv_w"]).reshape(NL, 4, 8, 128)
    vecs[:, :, 64:96] = cw.transpose(3, 0, 1, 2).reshape(128, NL, 32)
    vecs[:, :, 96:104] = np.broadcast_to(f(inputs["norm_final"]).reshape(8, 128).T[:, None, :], (128, NL, 8))
    rows = np.zeros((128, NL, NR), np.float32)
    r1 = np.concatenate([f(inputs["diff_lq1"]), f(inputs["diff_lk1"]), f(inputs["diff_lq2"]), f(inputs["diff_lk2"]), f(inputs["diff_subln"])], axis=1)
    rows[:] = r1[None, :, :]
    return vecs.reshape(128, NL * NV), rows.reshape(128, NL * NR)


_CACHE = {}


def kernel(**inputs):
    key = "full"
    if key not in _CACHE:
        _CACHE[key] = build()
    nc = _CACHE[key]
    vecs, rows = _pack(inputs)
    cst, retc, cstab = _host_consts()
    f = lambda a: np.ascontiguousarray(np.asarray(a, dtype=np.float32))
    shared = {
        "w_in": f(inputs["w_in"]), "w_branch": f(inputs["w_branch"]), "w_out": f(inputs["w_out"]),
        "xa_wq": f(inputs["xa_wq"]), "xa_wkv": f(inputs["xa_wkv"]), "xa_wo": f(inputs["xa_wo"]),
        "mlp_w1": f(inputs["mlp_w1"]), "mlp_w2": f(inputs["mlp_w2"]),
        "lru_wa": f(inputs["lru_wa"]), "lru_wx": f(inputs["lru_wx"]),
        "vecs": vecs, "rows": rows, "cst": cst, "retc": retc, "cstab": cstab,
    }
    x = f(inputs["x"])
    mem = f(inputs["mem"])
    in_maps = []
    for b in range(8):
        m = dict(shared)
        m["x"] = x[b]
        m["mem"] = mem[b]
        in_maps.append(m)
    res = run_bass_kernel_spmd(nc, in_maps, core_ids=list(range(8)))
    return np.stack([np.asarray(r["out"], dtype=np.float32) for r in res.results], axis=0)
```

```python
import math
import os
import numpy as np
import concourse.bass as bass
import concourse.mybir as mybir
from concourse.bass_utils import run_bass_kernel_spmd

F32 = mybir.dt.float32
BF16 = mybir.dt.bfloat16
AF = mybir.ActivationFunctionType
ALU = mybir.AluOpType
AX = mybir.AxisListType

ENGS = ("pe", "act", "dve", "pool", "sp")

S = 2048
D = 1024
NL = 4
NV = 104
NR = 384
IN_COLS = 11264
C_DQ, C_DK, C_DV, C_RQ, C_RK, C_RV, C_RG, C_LX, C_LY, C_G = 0, 1024, 2048, 3072, 3584, 4096, 5120, 6144, 7168, 8192


class Buf:
    __slots__ = ("name", "w", "r")

    def __init__(self, name):
        self.name = name
        self.w = {}
        self.r = {}


class DmaGroup:
    __slots__ = ("sem", "total")

    def __init__(self, sem):
        self.sem = sem
        self.total = 0


class Op:
    __slots__ = ("eng", "fn", "deps", "odeps", "signal", "sig_idx", "dma", "dma_val", "ndma", "seq", "cost", "fin", "done")

    def __init__(self, eng, fn):
        self.eng = eng
        self.fn = fn
        self.deps = []
        self.odeps = []
        self.cost = 0.0
        self.fin = 0.0
        self.done = False
        self.signal = False
        self.sig_idx = 0
        self.dma = None
        self.dma_val = 0
        self.ndma = 1


class _Dummy:
    def then_inc(self, *a, **k):
        return self


class _CostProxy:
    def reset(self, eng, is_dma):
        self.eng = eng
        self.is_dma = is_dma
        self.cost = 0.0
        self.tab = None

    def __getattr__(self, name):
        def f(*args, **kwargs):
            out = kwargs.get("out", None)
            if out is None and args:
                out = args[0]
            try:
                shp = out.shape
                n = 1
                for x_ in shp[1:]:
                    n *= int(x_)
            except Exception:
                n = 64
            if self.is_dma:
                self.cost += n * 128 * 4 / 250.0
            elif self.eng == "pe":
                mult = 1.0
                lhs = kwargs.get("lhsT", kwargs.get("in_", None))
                try:
                    if lhs is not None and lhs.dtype == F32:
                        mult = 4.0
                except Exception:
                    pass
                self.cost += 64.0 + mult * n / 2.4
            elif self.eng == "act":
                fn_ = kwargs.get("func", None)
                if fn_ == AF.Sqrt:
                    self.tab = "sqrt"
                elif fn_ == AF.Ln:
                    self.tab = "ln"
                elif fn_ == AF.Exp or fn_ == AF.Tanh:
                    self.tab = "exp"
                self.cost += 230.0 + n / 1.2 + (60.0 if kwargs.get("accum_out", None) is not None else 0.0)
            else:
                k_ = 2.0 if name == "tensor_tensor_scan" else 1.0
                self.cost += 260.0 + k_ * n / 0.96
            return _Dummy()
        return f


class Prog:
    def __init__(self, nc):
        self.nc = nc
        self.ops = {e: [] for e in ENGS}
        self.eng_sem = {}
        self.nsem = 0

    def new_sem(self, name):
        self.nsem += 1
        return self.nc.alloc_semaphore(f"s_{name}_{self.nsem}")

    def dma_group(self, name):
        return DmaGroup(self.new_sem(name))

    def op(self, eng, fn, reads=(), writes=(), pwrites=(), dma=None, ndma=1):
        o = Op(eng, fn)
        self.seq = getattr(self, "seq", 0) + 1
        o.seq = self.seq
        mykey = ("dma", id(dma)) if dma is not None else eng
        deps = {}
        for b in reads:
            for d in b.w.values():
                deps[id(d)] = d
        for b in writes:
            for d in b.w.values():
                deps[id(d)] = d
            for d in b.r.values():
                deps[id(d)] = d
        for b in pwrites:
            for d in b.r.values():
                deps[id(d)] = d
            for k, d in b.w.items():
                if k != mykey:
                    deps[id(d)] = d
                else:
                    o.odeps.append(d)
        for d in deps.values():
            if d is o:
                continue
            if d.dma is None:
                if d.eng == "pe" and eng == "pe" and dma is None:
                    o.odeps.append(d)
                    continue
                d.signal = True
            o.deps.append(d)
        if dma is not None:
            o.dma = dma
            o.ndma = ndma
            dma.total += 16 * ndma
            o.dma_val = dma.total
        for b in reads:
            prev = b.r.get(mykey)
            if prev is not None and prev is not o:
                o.odeps.append(prev)
            b.r[mykey] = o
        for b in writes:
            b.w = {mykey: o}
            b.r = {}
        for b in pwrites:
            b.w[mykey] = o
        self.ops[eng].append(o)
        return o

    def schedule(self):
        prox = _CostProxy()
        tabs = {}
        curtab = [None]
        for e in ENGS:
            for o in self.ops[e]:
                if o.fn is None:
                    o.cost = 0.0
                    continue
                prox.reset(e, o.dma is not None)
                o.fn(prox)
                o.cost = prox.cost
                tabs[id(o)] = prox.tab
        W = {"pe": 48, "act": 32, "dve": 32, "pool": 1, "sp": 1}
        rem = {e: list(self.ops[e]) for e in ENGS}
        head = {e: 0 for e in ENGS}
        tfree = {e: 0.0 for e in ENGS}
        neword = {e: [] for e in ENGS}
        total = sum(len(v) for v in rem.values())
        nsched = 0
        LAT = float(os.environ.get("KLAT", "450"))
        while nsched < total:
            best = None
            for e in ENGS:
                lst = rem[e]
                i = head[e]
                n = len(lst)
                while i < n and lst[i].done:
                    i += 1
                head[e] = i
                cnt = 0
                j = i
                cand = None
                while j < n and cnt < W[e]:
                    o = lst[j]
                    if not o.done:
                        cnt += 1
                        ok = True
                        r = 0.0
                        for d in o.deps:
                            if not d.done:
                                ok = False
                                break
                            if d.fin + LAT > r:
                                r = d.fin + LAT
                        if ok:
                            for d in o.odeps:
                                if not d.done:
                                    ok = False
                                    break
                        if ok:
                            st = r if r > tfree[e] else tfree[e]
                            if e == "act":
                                tb_ = tabs.get(id(o))
                                if tb_ is not None and tb_ != curtab[0]:
                                    st += 1300.0
                            if cand is None or st < cand[0]:
                                cand = (st, j, o)
                                if st <= tfree[e]:
                                    break
                    j += 1
                if cand is not None and (best is None or cand[0] < best[0]):
                    best = (cand[0], e, cand[2])
            assert best is not None, "scheduler stuck"
            st, e, o = best
            o.done = True
            if e == "act" and tabs.get(id(o)) is not None:
                curtab[0] = tabs[id(o)]
            if o.dma is not None:
                tfree[e] = st + 1000.0
                o.fin = st + 2000.0 + o.cost
            else:
                o.fin = st + o.cost
                tfree[e] = o.fin
            neword[e].append(o)
            nsched += 1
        self.ops = neword
        self.sim_time = max(tfree.values())

    def emit(self):
        nc = self.nc
        if os.environ.get("KSCHED", "1") == "1":
            self.schedule()
        for e in ENGS:
            c = 0
            for o in self.ops[e]:
                if o.signal:
                    c += 1
                    o.sig_idx = c
            if self.ops[e]:
                self.eng_sem[e] = self.new_sem("eng_" + e)
            if os.environ.get("KDBG_PRINT"):
                print("ENG", e, "ops", len(self.ops[e]), "signals", c, flush=True)

        def run(e, h):
            waited = {}
            for o in self.ops[e]:
                for d in o.deps:
                    if d.dma is not None:
                        sem, val = d.dma.sem, d.dma_val
                    else:
                        sem, val = self.eng_sem[d.eng], d.sig_idx
                    k = id(sem)
                    if waited.get(k, 0) >= val:
                        continue
                    h.wait_ge(sem, val)
                    waited[k] = val
                if o.fn is None:
                    continue
                ins = o.fn(h)
                if o.dma is not None:
                    if not isinstance(ins, (list, tuple)):
                        ins = [ins]
                    assert len(ins) == o.ndma
                    for i_ in ins:
                        i_.then_inc(o.dma.sem, 16)
                elif o.signal:
                    ins.then_inc(self.eng_sem[e], 1)

        with nc.Block() as block:
            @block.tensor
            def _(h):
                run("pe", h)

            @block.scalar
            def _(h):
                run("act", h)

            @block.vector
            def _(h):
                run("dve", h)

            @block.gpsimd
            def _(h):
                run("pool", h)

            @block.sync
            def _(h):
                run("sp", h)


class Arena:
    def __init__(self, nc, name, nbytes):
        self.nbytes = nbytes
        self.t = nc.alloc_sbuf_tensor(name, [128, nbytes // 4], F32)
        self.top = 0
        self.live = []
        self.dead = []
        self.peak = 0

    def alloc(self, name, shape_free, dtype, nbufs=1):
        esz = 2 if dtype == BF16 else 4
        n = int(np.prod(shape_free))
        nb = (n * esz + 63) // 64 * 64
        start = self.top
        end = start + nb
        assert end <= self.nbytes, f"arena overflow for {name}: {end} > {self.nbytes}"
        self.top = end
        self.peak = max(self.peak, end)
        ap = self.t[:, start // 4:(start + nb) // 4]
        if dtype != F32:
            ap = ap.bitcast(dtype)
        ap = ap[:, 0:n]
        if len(shape_free) == 2:
            ap = ap.rearrange("p (a b) -> p a b", a=shape_free[0])
        elif len(shape_free) == 3:
            ap = ap.rearrange("p (a b c) -> p a b c", a=shape_free[0], b=shape_free[1])
        bufs = [Buf(f"{name}{i}") for i in range(nbufs)]
        inh = {}
        keep = []
        for (s, e, obufs) in self.dead:
            if s < end and e > start:
                for ob in obufs:
                    for d in list(ob.w.values()) + list(ob.r.values()):
                        inh[("inh", id(d))] = d
                if s >= start and e <= end:
                    continue
            keep.append((s, e, obufs))
        self.dead = keep
        for b in bufs:
            b.r.update(inh)
        self.live.append((start, end, bufs))
        return ap, bufs

    def mark(self):
        return (self.top, len(self.live))

    def release(self, mark):
        top, nlive = mark
        for item in self.live[nlive:]:
            self.dead.append(item)
        self.live = self.live[:nlive]
        self.top = top


def mm_fn(out, pairs, start=True, skip=False):
    def fn(h):
        n = len(pairs)
        ins = None
        for i, (l, r) in enumerate(pairs):
            if skip:
                ins = h.matmul(out, lhsT=l, rhs=r, start=(start and i == 0), stop=(i == n - 1), skip_group_check=True)
            else:
                ins = h.matmul(out, lhsT=l, rhs=r, start=(start and i == 0), stop=(i == n - 1))
        return ins
    return fn


def build(n_layers=NL, stop_after=None, dbg=False):
    nc = bass.Bass("TRN2", target_bir_lowering=False)

    def dram(name, shape, dt=F32, kind="ExternalInput"):
        return nc.dram_tensor(name, shape, dt, kind=kind).ap()

    x_d = dram("x", [S, D])
    mem_d = dram("mem", [256, D])
    w_in_d = dram("w_in", [NL, D, IN_COLS])
    w_br_d = dram("w_branch", [NL, 3, D, D])
    w_out_d = dram("w_out", [NL, D, D])
    wq_d = dram("xa_wq", [NL, D, 512])
    wkv_d = dram("xa_wkv", [NL, D, 1024])
    wo_d = dram("xa_wo", [NL, 512, D])
    w1_d = dram("mlp_w1", [NL, D, 4096])
    w2_d = dram("mlp_w2", [NL, 4096, D])
    wa_d = dram("lru_wa", [NL, 8, 128, 128])
    wx_d = dram("lru_wx", [NL, 8, 128, 128])
    vecs_d = dram("vecs", [128, NL * NV])
    rows_d = dram("rows", [128, NL * NR])
    cst_d = dram("cst", [128, 256])
    retc_d = dram("retc", [128, 4 * 130])
    cs_d = dram("cstab", [S, 256])
    out_d = dram("out", [S, D], kind="ExternalOutput")
    if dbg:
        dbgx_d = dram("dbgx", [128, 8 * S], kind="ExternalOutput")
        dbgo_d = dram("dbgo", [128, 8 * S], BF16, kind="ExternalOutput")

    P = Prog(nc)
    AR = Arena(nc, "arena", 204 * 1024)
    ps = nc.alloc_psum_tensor("ps", [128, 8, 512], F32)
    psb = [Buf(f"ps{i}") for i in range(8)]
    mmctr = [0]

    def mmbank():
        b = (0, 1, 2, 7)[mmctr[0] % 4]
        mmctr[0] += 1
        return b

    xT, _ = AR.alloc("xT", [8, S], F32)
    xb = [[Buf(f"x{kc}_{tt}") for tt in range(4)] for kc in range(8)]
    cstf, (cstfb,) = AR.alloc("cstf", [256], F32)
    identf = cstf[:, 0:128]
    identb, (identbb,) = AR.alloc("identb", [128], BF16)
    maskb, (maskbb,) = AR.alloc("maskb", [128], BF16)
    onesb, (onesbb,) = AR.alloc("onesb", [128], BF16)
    vecs, (vecsb,) = AR.alloc("vecs", [NL, NV], F32)
    rows1, (rowsb,) = AR.alloc("rows", [NR], F32)
    retc, (retcb,) = AR.alloc("retc", [4, 130], F32)
    lamneg, (lamnegb,) = AR.alloc("lamneg", [4], F32)
    subln_s, (sublnb,) = AR.alloc("subln", [128], F32)
    lru_sc, (lruscb,) = AR.alloc("lrusc", [8], F32)
    hT, _ = AR.alloc("hT", [8, S], BF16)
    hb = [Buf(f"h{tt}") for tt in range(4)]
    obT, _ = AR.alloc("obT", [8, S], BF16)
    ob = [[Buf(f"o{kc}_{tt}") for tt in range(4)] for kc in range(8)]
    NSLOT = 3
    wslot = []
    for i in range(NSLOT):
        ap, (b,) = AR.alloc(f"wslot{i}", [4096], BF16)
        wslot.append((ap, b, P.dma_group(f"w{i}")))
    wctr = [0]

    def next_slot():
        s = wslot[wctr[0] % NSLOT]
        wctr[0] += 1
        return s

    def wload(dsts_srcs, slot):
        ap, b, g = slot
        n = len(dsts_srcs)

        def fn(h):
            return [h.dma_start(out=d, in_=s) for d, s in dsts_srcs]
        P.op("pool", fn, writes=[b], dma=g, ndma=n)

    def wtile_cols(w2d, col0, ncols, slot, nk=8, dcol0=0, width=None):
        ap = slot[0]
        width = width or ncols
        v = ap[:, 0:nk * width].rearrange("p (k c) -> p k c", k=nk)
        src = w2d.rearrange("(k p) c -> p k c", p=128)[:, :, col0:col0 + ncols]
        return (v[:, :, dcol0:dcol0 + ncols], src)

    def sview(slot, nk, width):
        return slot[0][:, 0:nk * width].rearrange("p (k c) -> p k c", k=nk)

    gc = P.dma_group("cst")
    P.op("sp", lambda h: h.dma_start(out=cstf, in_=cst_d), writes=[cstfb], dma=gc)
    gv = P.dma_group("vecs")
    P.op("sp", lambda h: h.dma_start(out=vecs.rearrange("p l v -> p (l v)"), in_=vecs_d), writes=[vecsb], dma=gv)
    gr = P.dma_group("rows")
    grc = P.dma_group("retc")
    P.op("sp", lambda h: h.dma_start(out=retc.rearrange("p l v -> p (l v)"), in_=retc_d), writes=[retcb], dma=grc)
    P.op("dve", lambda h: h.tensor_copy(out=identb, in_=cstf[:, 0:128]), reads=[cstfb], writes=[identbb])
    P.op("dve", lambda h: h.tensor_copy(out=maskb, in_=cstf[:, 128:256]), reads=[cstfb], writes=[maskbb])
    P.op("dve", lambda h: h.memset(onesb, 1.0), writes=[onesbb])

    mk = AR.mark()
    xin, xinb = AR.alloc("xin", [2, D], F32, nbufs=2)
    gx = [P.dma_group("xin0"), P.dma_group("xin1")]
    for tb in range(16):
        s = tb % 2
        P.op("sp", lambda h, tb=tb, s=s: h.dma_start(out=xin[:, s, :], in_=x_d[tb * 128:(tb + 1) * 128, :]), writes=[xinb[s]], dma=gx[s])
        for half in range(2):
            bk = mmbank()

            def fn(h, s=s, half=half, bk=bk):
                ins = None
                for j in range(4):
                    kc = half * 4 + j
                    ins = h.transpose(out=ps[:, bk, j * 128:(j + 1) * 128], in_=xin[:, s, kc * 128:(kc + 1) * 128], identity=identf)
                return ins
            P.op("pe", fn, reads=[xinb[s], cstfb], writes=[psb[bk]])
            tt = tb // 4
            P.op("dve" if half == 0 else "act",
                 (lambda h, bk=bk, half=half, tb=tb: h.tensor_copy(out=xT[:, half * 4:half * 4 + 4, tb * 128:(tb + 1) * 128], in_=ps[:, bk, :].rearrange("p (a b) -> p a b", a=4)))
                 if half == 0 else
                 (lambda h, bk=bk, half=half, tb=tb: h.activation(out=xT[:, half * 4:half * 4 + 4, tb * 128:(tb + 1) * 128], in_=ps[:, bk, :].rearrange("p (a b) -> p a b", a=4), func=AF.Copy)),
                 reads=[psb[bk]], pwrites=[xb[half * 4 + j][tt] for j in range(4)])
    AR.release(mk)

    def norm_stage(l, gcol, eps=1e-6):
        mk = AR.mark()
        sq, sqb = AR.alloc("sq", [2, 8, 512], BF16, nbufs=2)
        rt, rtb = AR.alloc("rt", [2, 512], F32, nbufs=2)
        for tt in range(4):
            s = tt % 2
            tsl = slice(tt * 512, (tt + 1) * 512)
            P.op("act", lambda h, s=s, tsl=tsl: h.activation(out=sq[:, s], in_=xT[:, :, tsl], func=AF.Square),
                 reads=[xb[kc][tt] for kc in range(8)], writes=[sqb[s]])
            P.op("pe", mm_fn(ps[:, 7, :], [(onesb, sq[:, s, kc, :]) for kc in range(8)]), reads=[sqb[s], onesbb], writes=[psb[7]])
            P.op("act", lambda h, s=s: h.activation(out=rt[:, s], in_=ps[:, 7, :], func=AF.Sqrt, scale=1.0 / D, bias=eps_ap(eps)),
                 reads=[psb[7], epsb], writes=[rtb[s]])
            P.op("dve", lambda h, s=s: h.reciprocal(out=rt[:, s], in_=rt[:, s]), reads=[rtb[s]], writes=[rtb[s]])
            for kc in range(8):
                P.op("dve", lambda h, s=s, kc=kc, tsl=tsl: h.scalar_tensor_tensor(out=hT[:, kc, tsl], in0=xT[:, kc, tsl], scalar=vecs[:, l, gcol + kc:gcol + kc + 1],
                                                                              in1=rt[:, s], op0=ALU.mult, op1=ALU.mult),
                     reads=[xb[kc][tt], rtb[s], vecsb], writes=[hb[tt]] if kc == 0 else (), pwrites=() if kc == 0 else [hb[tt]])
        AR.release(mk)

    epst, (epsb,) = AR.alloc("epst", [8], F32)
    P.op("dve", lambda h: h.memset(epst[:, 0:1], 1e-6), writes=[epsb])
    P.op("dve", lambda h: h.memset(epst[:, 1:2], 1e-5), pwrites=[epsb])
    P.op("dve", lambda h: h.memset(epst[:, 2:3], 1.0), pwrites=[epsb])
    P.op("dve", lambda h: h.memset(epst[:, 3:4], 0.25), pwrites=[epsb])
    P.op("dve", lambda h: h.memset(epst[:, 4:5], 4e-5), pwrites=[epsb])

    def eps_ap(eps):
        return epst[:, 0:1] if eps == 1e-6 else epst[:, 1:2]

    all_x = [xb[kc][tt] for kc in range(8) for tt in range(4)]
    all_o = [ob[kc][tt] for kc in range(8) for tt in range(4)]

    def branch_A(l):
        lam_init = 0.8 - 0.6 * math.exp(-0.3 * l)
        mk = AR.mark()
        lt, (ltb,) = AR.alloc("lamt", [2, 64], F32)
        l2, (l2b,) = AR.alloc("lam2", [2], F32)
        P.op("sp", lambda h: h.dma_start(out=rows1, in_=rows_d[:, l * NR:(l + 1) * NR]), writes=[rowsb], dma=gr)
        rv = rows1[:, 0:256].rearrange("p (a b c) -> p a b c", a=2, b=2)
        P.op("dve", lambda h: h.tensor_tensor(out=lt, in0=rv[:, :, 0, :], in1=rv[:, :, 1, :], op=ALU.mult), reads=[rowsb], writes=[ltb])
        P.op("dve", lambda h: h.reduce_sum(out=l2, in_=lt, axis=AX.X), reads=[ltb], writes=[l2b])
        P.op("act", lambda h: h.activation(out=l2, in_=l2, func=AF.Exp), reads=[l2b], writes=[l2b])
        P.op("dve", lambda h: h.scalar_tensor_tensor(out=lamneg[:, 0:1], in0=l2[:, 1:2], scalar=-lam_init, in1=l2[:, 0:1], op0=ALU.add, op1=ALU.subtract),
             reads=[l2b], writes=[lamnegb])
        P.op("dve", lambda h: h.tensor_scalar(out=subln_s, in0=rows1[:, 256:384], scalar1=1.0 - lam_init, scalar2=None, op0=ALU.mult),
             reads=[rowsb], writes=[sublnb])

        qT, qTb = AR.alloc("qT", [2, S], BF16, nbufs=2)
        kT, kTb = AR.alloc("kT", [2, 2, S], BF16, nbufs=2)
        for s_ in range(2):
            P.op("dve", lambda h, s_=s_: h.memset(kT[64:128, s_, 0, :], 0.0), writes=[kTb[s_]])
            P.op("dve", lambda h, s_=s_: h.memset(kT[0:64, s_, 1, :], 0.0), pwrites=[kTb[s_]])
        vv, vvb = AR.alloc("vA", [2, 16, 129], BF16, nbufs=2)
        for s in range(2):
            P.op("dve", lambda h, s=s: h.memset(vv[:, s, :, 128:129], 1.0), writes=[vvb[s]])
        NE = 5
        et, etb = AR.alloc("eA", [NE, 512], BF16, nbufs=NE)
        rec, (recb,) = AR.alloc("recA", [2, 4], F32)
        osb, osbb = AR.alloc("osbA", [2, 4, 129], F32, nbufs=2)
        d0, d0b = osb[:, 0, :, 0:128], osbb[0]
        d1, d1b = osb[:, 1, :, 0:128], osbb[1]
        ss, (ssb,) = AR.alloc("ssA", [4], F32)
        odn, odnb = AR.alloc("odnA", [2, 4, 128], BF16, nbufs=2)
        ectr = 0
        w2d = w_in_d[l]
        for hd in range(int(os.environ.get('KDBG_HEADS', '8'))):
            s = hd % 2
            slot = next_slot()
            wload([wtile_cols(w2d, C_DQ + hd * 128, 128, slot, width=384, dcol0=0),
                   wtile_cols(w2d, C_DK + hd * 128, 128, slot, width=384, dcol0=128),
                   wtile_cols(w2d, C_DV + hd * 128, 128, slot, width=384, dcol0=256)], slot)
            wt = sview(slot, 8, 384)
            wb_ = slot[1]
            for tt in range(4):
                tsl = slice(tt * 512, (tt + 1) * 512)
                bk = mmbank()
                P.op("pe", mm_fn(ps[:, bk, :], [(wt[:, kc, 0:128], hT[:, kc, tsl]) for kc in range(8)]), reads=[wb_, hb[tt]], writes=[psb[bk]])
                P.op("act", lambda h, bk=bk, s=s, tsl=tsl: h.activation(out=qT[:, s, tsl], in_=ps[:, bk, :], func=AF.Copy, scale=0.125),
                     reads=[psb[bk]], writes=[qTb[s]] if tt == 0 else (), pwrites=() if tt == 0 else [qTb[s]])
                bk = mmbank()
                P.op("pe", mm_fn(ps[:, bk, :], [(wt[:, kc, 128:256], hT[:, kc, tsl]) for kc in range(8)]), reads=[wb_, hb[tt]], writes=[psb[bk]])
                P.op("dve", lambda h, bk=bk, s=s, tsl=tsl: h.tensor_copy(out=kT[0:64, s, 0, tsl], in_=ps[0:64, bk, :]),
                     reads=[psb[bk]], writes=[kTb[s]] if tt == 0 else (), pwrites=() if tt == 0 else [kTb[s]])
                P.op("dve", lambda h, bk=bk, s=s, tsl=tsl: h.tensor_copy(out=kT[64:128, s, 1, tsl], in_=ps[64:128, bk, :]),
                     reads=[psb[bk]], pwrites=[kTb[s]])
            for t4 in range(4):
                bk = mmbank()

                def fn(h, t4=t4, bk=bk, wt=wt):
                    ins = None
                    for j in range(4):
                        tb = t4 * 4 + j
                        for kc in range(8):
                            ins = h.matmul(ps[:, bk, j * 128:(j + 1) * 128], lhsT=hT[:, kc, tb * 128:(tb + 1) * 128], rhs=wt[:, kc, 256:384],
                                           start=(kc == 0), stop=(kc == 7))
                    return ins
                P.op("pe", fn, reads=[wb_, hb[t4]], writes=[psb[bk]])
                P.op("dve", lambda h, bk=bk, s=s, t4=t4: h.tensor_copy(out=vv[:, s, t4 * 4:(t4 + 1) * 4, 0:128], in_=ps[:, bk, :].rearrange("p (a b) -> p a b", a=4)),
                     reads=[psb[bk]], writes=[vvb[s]] if t4 == 0 else (), pwrites=() if t4 == 0 else [vvb[s]])
            accb = [psb[3], psb[4], psb[5], psb[6]]

            def emit_score(qt, kb, c, s=s):
                nonlocal ectr
                dstart = max(0, kb - 4 * qt)
                q0 = qt * 512 + dstart * 128
                nq = 512 - dstart * 128
                bk = mmbank()
                P.op("pe", mm_fn(ps[:, bk, 0:nq], [(kT[:, s, c, kb * 128:(kb + 1) * 128], qT[:, s, q0:q0 + nq])]),
                     reads=[kTb[s], qTb[s]], writes=[psb[bk]])
                e = ectr % NE
                ectr += 1
                P.op("act", lambda h, bk=bk, e=e, nq=nq: h.activation(out=et[:, e, 0:nq], in_=ps[:, bk, 0:nq], func=AF.Exp),
                     reads=[psb[bk]], writes=[etb[e]])
                if kb >= 4 * qt:
                    P.op("dve", lambda h, e=e: h.tensor_tensor(out=et[:, e, 0:128], in0=et[:, e, 0:128], in1=maskb, op=ALU.mult),
                         reads=[etb[e], maskbb], writes=[etb[e]])
                return (qt, kb, c, dstart, e)

            def emit_av(info, s=s):
                qt, kb, c, dstart, e = info

                def fn(h):
                    ins = None
                    for qb in range(dstart, 4):
                        bank = 3 + 2 * c + qb // 2
                        co = (qb % 2) * 129
                        ins = h.matmul(ps[:, bank, co:co + 129], lhsT=et[:, e, (qb - dstart) * 128:(qb - dstart + 1) * 128], rhs=vv[:, s, kb, :],
                                       start=(kb == 0 and qb % 2 == 0), stop=(kb == 4 * qt + qb), skip_group_check=True)
                    return ins
                abufs = [accb[2 * c], accb[2 * c + 1]]
                P.op("pe", fn, reads=[etb[e], vvb[s]], writes=abufs if kb == 0 else (), pwrites=() if kb == 0 else abufs)

            def post(qt, hd=hd):
                P.op("act", lambda h: h.activation(out=osb[:, 0].rearrange("p (a b) c -> p a (b c)", a=2), in_=ps[:, 3:5, 0:258], func=AF.Copy), reads=[psb[3], psb[4]], writes=[osbb[0]])
                P.op("dve", lambda h: h.tensor_copy(out=osb[:, 1].rearrange("p (a b) c -> p a (b c)", a=2), in_=ps[:, 5:7, 0:258]), reads=[psb[5], psb[6]], writes=[osbb[1]])
                P.op("dve", lambda h: h.reciprocal(out=rec[:, 0, :], in_=osb[:, 0, :, 128]), reads=[osbb[0]], writes=[recb])
                P.op("dve", lambda h: h.reciprocal(out=rec[:, 1, :], in_=osb[:, 1, :, 128]), reads=[osbb[1]], pwrites=[recb])
                P.op("dve", lambda h: h.tensor_scalar(out=rec[:, 1, :], in0=rec[:, 1, :], scalar1=lamneg[:, 0:1], scalar2=None, op0=ALU.mult),
                     reads=[recb, lamnegb], writes=[recb])
                P.op("dve", lambda h: h.tensor_tensor(out=d0, in0=d0, in1=rec[:, 0, :].unsqueeze(2).to_broadcast([128, 4, 128]), op=ALU.mult),
                     reads=[d0b, recb], writes=[d0b])
                P.op("dve", lambda h: h.tensor_tensor(out=d1, in0=d1, in1=rec[:, 1, :].unsqueeze(2).to_broadcast([128, 4, 128]), op=ALU.mult),
                     reads=[d1b, recb], writes=[d1b])
                P.op("dve", lambda h: h.tensor_tensor(out=d0, in0=d0, in1=d1, op=ALU.add), reads=[d0b, d1b], writes=[d0b])
                P.op("dve", lambda h: h.tensor_tensor(out=d1, in0=d0, in1=d0, op=ALU.mult), reads=[d0b], writes=[d1b])
                P.op("dve", lambda h: h.reduce_sum(out=ss, in_=d1, axis=AX.X), reads=[d1b], writes=[ssb])
                P.op("act", lambda h: h.activation(out=ss, in_=ss, func=AF.Sqrt, scale=1.0 / 128, bias=epst[:, 1:2]), reads=[ssb, epsb], writes=[ssb])
                P.op("dve", lambda h: h.reciprocal(out=ss, in_=ss), reads=[ssb], writes=[ssb])
                P.op("dve", lambda h: h.tensor_tensor(out=d0, in0=d0, in1=ss.unsqueeze(2).to_broadcast([128, 4, 128]), op=ALU.mult), reads=[d0b, ssb], writes=[d0b])
                os_ = qt % 2
                P.op("dve", lambda h, os_=os_: h.tensor_tensor(out=odn[:, os_], in0=d0, in1=subln_s.unsqueeze(1).to_broadcast([128, 4, 128]), op=ALU.mult),
                     reads=[d0b, sublnb], writes=[odnb[os_]])
                bk = mmbank()
                psbf = ps[:, bk, :].bitcast(BF16)

                def fn(h, os_=os_, psbf=psbf):
                    ins = None
                    for qb in range(4):
                        ins = h.transpose(out=psbf[:, qb * 128:(qb + 1) * 128], in_=odn[:, os_, qb, :], identity=identb)
                    return ins
                P.op("pe", fn, reads=[odnb[os_], identbb], writes=[psb[bk]])
                P.op("act", lambda h, psbf=psbf, hd=hd, qt=qt: h.activation(out=obT[:, hd, qt * 512:(qt + 1) * 512], in_=psbf[:, 0:512], func=AF.Copy),
                     reads=[psb[bk]], writes=[ob[hd][qt]])

            LAG = 2
            pend = []
            for qt in range(4):
                for kb in range(4 * qt + 4):
                    for c in range(2):
                        pend.append(emit_score(qt, kb, c))
                        if len(pend) > LAG:
                            x = pend.pop(0)
                            emit_av(x)
                            if x[1] == 4 * x[0] + 3 and x[2] == 1:
                                post(x[0])
            while pend:
                x = pend.pop(0)
                emit_av(x)
                if x[1] == 4 * x[0] + 3 and x[2] == 1:
                    post(x[0])
        AR.release(mk)

    def merge(l, b):
        mk = AR.mark()
        m, _ = AR.alloc("m", [8, S], BF16)
        mb = [[Buf(f"m{kc}_{tt}") for tt in range(4)] for kc in range(8)]
        tmp_ap, tmpb = m, None
        for item in AR.live[-1:]:
            for kc in range(8):
                for tt in range(4):
                    mb[kc][tt].r = dict(item[2][0].r)
            item[2].extend([mb[kc][tt] for kc in range(8) for tt in range(4)])
        gsb, gsbb = AR.alloc("gsb", [2, 512], F32, nbufs=2)
        gctr = 0
        for cg in range(2):
            sg = next_slot()
            wload([wtile_cols(w_in_d[l], C_G + b * 1024 + cg * 512, 512, sg)], sg)
            sb = next_slot()
            wload([wtile_cols(w_br_d[l, b], cg * 512, 512, sb)], sb)
            wg = sview(sg, 8, 512)
            wb = sview(sb, 8, 512)
            for dcl in range(4):
                dc = cg * 4 + dcl
                csl = slice(dcl * 128, (dcl + 1) * 128)
                for tt in range(4):
                    tsl = slice(tt * 512, (tt + 1) * 512)
                    bg = mmbank()
                    P.op("pe", mm_fn(ps[:, bg, :], [(wg[:, kc, csl], hT[:, kc, tsl]) for kc in range(8)]), reads=[sg[1], hb[tt]], writes=[psb[bg]])
                    g_ = gctr % 2
                    gctr += 1
                    P.op("act", lambda h, bg=bg, g_=g_: h.activation(out=gsb[:, g_], in_=ps[:, bg, :], func=AF.Tanh, scale=0.5), reads=[psb[bg]], writes=[gsbb[g_]])
                    bp = mmbank()
                    P.op("pe", mm_fn(ps[:, bp, :], [(wb[:, kc, csl], obT[:, kc, tsl]) for kc in range(8)]), reads=[sb[1]] + [ob[kc][tt] for kc in range(8)], writes=[psb[bp]])
                    P.op("dve", lambda h, bp=bp, g_=g_, dc=dc, tsl=tsl: h.scalar_tensor_tensor(out=m[:, dc, tsl], in0=gsb[:, g_], scalar=1.0, in1=ps[:, bp, :], op0=ALU.add, op1=ALU.mult),
                         reads=[psb[bp], gsbb[g_]], writes=[mb[dc][tt]])
        for cg in range(2):
            so = next_slot()
            wload([wtile_cols(w_out_d[l], cg * 512, 512, so)], so)
            wo = sview(so, 8, 512)
            for dcl in range(4):
                dc = cg * 4 + dcl
                csl = slice(dcl * 128, (dcl + 1) * 128)
                for tt in range(4):
                    tsl = slice(tt * 512, (tt + 1) * 512)
                    bk = mmbank()
                    P.op("pe", mm_fn(ps[:, bk, :], [(wo[:, kc, csl], m[:, kc, tsl]) for kc in range(8)]), reads=[so[1]] + [mb[kc][tt] for kc in range(8)], writes=[psb[bk]])
                    P.op("dve", lambda h, bk=bk, dc=dc, tsl=tsl: h.scalar_tensor_tensor(out=xT[:, dc, tsl], in0=ps[:, bk, :], scalar=0.5, in1=xT[:, dc, tsl], op0=ALU.mult, op1=ALU.add),
                         reads=[psb[bk], xb[dc][tt]], writes=[xb[dc][tt]])
        AR.release(mk)

    def branch_B(l):
        mk = AR.mark()
        gam = [1.0 - 2.0 ** (-5.0 - h_) for h_ in range(4)]
        qkT, qkTb = AR.alloc("qkT", [3, S], BF16, nbufs=1)
        kw, (kwb,) = AR.alloc("kw", [16, 128], BF16)
        vr, (vrb,) = AR.alloc("vr", [16, 256], BF16)
        cst, cstb = AR.alloc("cst", [2, 256], F32, nbufs=2)
        gcs = [P.dma_group("cs0"), P.dma_group("cs1")]
        qkf, qkfb = AR.alloc("qkf", [2, 256], F32, nbufs=2)
        t1, t1b = AR.alloc("t1B", [2, 256], F32, nbufs=2)
        t2, t2b = AR.alloc("t2B", [2, 256], F32, nbufs=2)
        qk3, qk3b = AR.alloc("qk3", [2, 3, 128], BF16, nbufs=2)
        st, (stb,) = AR.alloc("st", [256], F32)
        stbf, stbfb = AR.alloc("stbf", [2, 256], BF16, nbufs=2)
        sTm, sTmb = AR.alloc("sTm", [2, 128], BF16, nbufs=2)
        sgt, sgtb = AR.alloc("sgt", [2, 256], F32, nbufs=2)
        bst, bstb = AR.alloc("bst", [2, 8], F32, nbufs=2)
        on_, onb = AR.alloc("onB", [2, 256], F32, nbufs=2)
        orb, orbb = AR.alloc("orB", [2, 256], BF16, nbufs=2)
        w2d = w_in_d[l]
        for hd in range(4):
            s1 = next_slot()
            wload([wtile_cols(w2d, C_RQ + hd * 128, 128, s1, width=512, dcol0=0),
                   wtile_cols(w2d, C_RK + hd * 128, 128, s1, width=512, dcol0=128),
                   wtile_cols(w2d, C_RV + hd * 256, 256, s1, width=512, dcol0=256)], s1)
            s2 = next_slot()
            wload([wtile_cols(w2d, C_RG + hd * 256, 256, s2)], s2)
            w1 = sview(s1, 8, 512)
            wg = sview(s2, 8, 256)
            for tb in range(16):
                s = tb % 2
                tt = tb // 4
                bsl = slice(tb * 128, (tb + 1) * 128)
                P.op("sp", lambda h, s=s, bsl=bsl: h.dma_start(out=cst[:, s], in_=cs_d[bsl, :]), writes=[cstb[s]], dma=gcs[s])
                bk = mmbank()
                P.op("pe", mm_fn(ps[:, bk, :], [(hT[:, kc, bsl], w1[:, kc, :]) for kc in range(8)]), reads=[s1[1], hb[tt]], writes=[psb[bk]])
                P.op("act", lambda h, bk=bk, s=s: h.activation(out=qkf[:, s], in_=ps[:, bk, 0:256], func=AF.Copy), reads=[psb[bk]], writes=[qkfb[s]])
                P.op("act", lambda h, bk=bk, tb=tb: h.activation(out=vr[:, tb, :], in_=ps[:, bk, 256:512], func=AF.Copy), reads=[psb[bk]],
                     writes=[vrb] if tb == 0 else (), pwrites=() if tb == 0 else [vrb])
                xq = qkf[:, s].rearrange("p (a b) -> p a b", a=2)
                cosb = cst[:, s, 0:128].unsqueeze(1).to_broadcast([128, 2, 128])
                P.op("dve", lambda h, s=s, xq=xq, cosb=cosb: h.tensor_tensor(out=t1[:, s].rearrange("p (a b) -> p a b", a=2), in0=xq, in1=cosb, op=ALU.mult),
                     reads=[qkfb[s], cstb[s]], writes=[t1b[s]])
                x4 = qkf[:, s].rearrange("p (a b c) -> p a b c", a=2, c=2)
                t24 = t2[:, s].rearrange("p (a b c) -> p a b c", a=2, c=2)
                sn4 = cst[:, s, 128:256].rearrange("p (b c) -> p b c", c=2)
                P.op("dve", lambda h, x4=x4, t24=t24, sn4=sn4: h.tensor_tensor(out=t24[:, :, :, 0], in0=x4[:, :, :, 1], in1=sn4[:, :, 0].unsqueeze(1).to_broadcast([128, 2, 64]), op=ALU.mult),
                     reads=[qkfb[s], cstb[s]], writes=[t2b[s]])
                P.op("dve", lambda h, x4=x4, t24=t24, sn4=sn4: h.tensor_tensor(out=t24[:, :, :, 1], in0=x4[:, :, :, 0], in1=sn4[:, :, 1].unsqueeze(1).to_broadcast([128, 2, 64]), op=ALU.mult),
                     reads=[qkfb[s], cstb[s]], pwrites=[t2b[s]])
                P.op("dve", lambda h, s=s: h.tensor_tensor(out=t1[:, s], in0=t1[:, s], in1=t2[:, s], op=ALU.add), reads=[t1b[s], t2b[s]], writes=[t1b[s]])
                P.op("act", lambda h, s=s: h.activation(out=qk3[:, s, 0, :], in_=t1[:, s, 0:128], func=AF.Copy), reads=[t1b[s]], writes=[qk3b[s]])
                P.op("act", lambda h, s=s, hd=hd: h.activation(out=qk3[:, s, 1, :], in_=t1[:, s, 0:128], func=AF.Identity, scale=retc[:, hd, 129:130]), reads=[t1b[s], retcb], pwrites=[qk3b[s]])
                P.op("act", lambda h, s=s: h.activation(out=qk3[:, s, 2, :], in_=t1[:, s, 128:256], func=AF.Copy), reads=[t1b[s]], pwrites=[qk3b[s]])
                P.op("dve", lambda h, s=s, hd=hd, tb=tb: h.tensor_scalar(out=kw[:, tb, :], in0=t1[:, s, 128:256], scalar1=retc[:, hd, 128:129], scalar2=None, op0=ALU.mult),
                     reads=[t1b[s], retcb], writes=[kwb] if tb == 0 else (), pwrites=() if tb == 0 else [kwb])
                bk = mmbank()
                psbf = ps[:, bk, :].bitcast(BF16)

                def fn(h, s=s, psbf=psbf):
                    ins = None
                    for j in range(3):
                        ins = h.transpose(out=psbf[:, j * 128:(j + 1) * 128], in_=qk3[:, s, j, :], identity=identb)
                    return ins
                P.op("pe", fn, reads=[qk3b[s], identbb], writes=[psb[bk]])
                P.op("dve", lambda h, psbf=psbf, bsl=bsl: h.tensor_copy(out=qkT[:, :, bsl], in_=psbf[:, 0:384].rearrange("p (a b) -> p a b", a=3)),
                     reads=[psb[bk]], writes=[qkTb[0]] if tb == 0 else (), pwrites=() if tb == 0 else [qkTb[0]])
            cd = gam[hd] ** 128
            for n in range(16):
                s = n % 2
                tt = n // 4
                bsl = slice(n * 128, (n + 1) * 128)
                bk = mmbank()
                P.op("pe", mm_fn(ps[:, bk, 0:128], [(qkT[:, 2, bsl], qkT[:, 0, bsl])]), reads=[qkTb[0]], writes=[psb[bk]])
                P.op("dve", lambda h, bk=bk, s=s, hd=hd: h.tensor_tensor(out=sTm[:, s], in0=ps[:, bk, 0:128], in1=retc[:, hd, 0:128], op=ALU.mult),
                     reads=[psb[bk], retcb], writes=[sTmb[s]])
                bo = mmbank()
                pairs = [(sTm[:, s], vr[:, n, :])]
                rd = [sTmb[s], vrb]
                if n > 0:
                    pairs.append((qkT[:, 1, bsl], stbf[:, (n - 1) % 2]))
                    rd += [qkTb[0], stbfb[(n - 1) % 2]]
                P.op("pe", mm_fn(ps[:, bo, 0:256], pairs), reads=rd, writes=[psb[bo]])
                bg = mmbank()
                P.op("pe", mm_fn(ps[:, bg, 0:256], [(hT[:, kc, bsl], wg[:, kc, :]) for kc in range(8)]), reads=[s2[1], hb[tt]], writes=[psb[bg]])
                P.op("act", lambda h, bg=bg, s=s: h.activation(out=sgt[:, s], in_=ps[:, bg, 0:256], func=AF.Tanh, scale=0.5), reads=[psb[bg]], writes=[sgtb[s]])
                P.op("dve", lambda h, bg=bg, s=s: h.scalar_tensor_tensor(out=sgt[:, s], in0=sgt[:, s], scalar=1.0, in1=ps[:, bg, 0:256], op0=ALU.add, op1=ALU.mult), reads=[psb[bg], sgtb[s]], writes=[sgtb[s]])
                if n < 15:
                    bkv = mmbank()
                    P.op("pe", mm_fn(ps[:, bkv, 0:256], [(kw[:, n, :], vr[:, n, :])]), reads=[kwb, vrb], writes=[psb[bkv]])
                    if n == 0:
                        P.op("dve", lambda h, bkv=bkv: h.tensor_copy(out=st, in_=ps[:, bkv, 0:256]), reads=[psb[bkv]], writes=[stb])
                    else:
                        P.op("dve", lambda h, bkv=bkv, cd=cd: h.scalar_tensor_tensor(out=st, in0=st, scalar=cd, in1=ps[:, bkv, 0:256], op0=ALU.mult, op1=ALU.add),
                             reads=[psb[bkv], stb], writes=[stb])
                    P.op("act", lambda h, s=s: h.activation(out=stbf[:, s], in_=st, func=AF.Copy), reads=[stb], writes=[stbfb[s]])
                P.op("dve", lambda h, bo=bo, s=s: h.bn_stats(out=bst[:, s, 0:6], in_=ps[:, bo, 0:256]), reads=[psb[bo]], writes=[bstb[s]])
                P.op("dve", lambda h, s=s: h.bn_aggr(out=bst[:, s, 6:8], in_=bst[:, s, 0:6]), reads=[bstb[s]], writes=[bstb[s]])
                P.op("act", lambda h, s=s: h.activation(out=bst[:, s, 7:8], in_=bst[:, s, 7:8], func=AF.Sqrt, scale=4.0, bias=epst[:, 4:5]), reads=[bstb[s], epsb], writes=[bstb[s]])
                P.op("dve", lambda h, s=s: h.reciprocal(out=bst[:, s, 7:8], in_=bst[:, s, 7:8]), reads=[bstb[s]], writes=[bstb[s]])
                P.op("dve", lambda h, bo=bo, s=s: h.tensor_scalar(out=on_[:, s], in0=ps[:, bo, 0:256], scalar1=bst[:, s, 6:7], scalar2=bst[:, s, 7:8], op0=ALU.subtract, op1=ALU.mult),
                     reads=[psb[bo], bstb[s]], writes=[onb[s]])
                P.op("dve", lambda h, s=s: h.tensor_tensor(out=orb[:, s], in0=on_[:, s], in1=sgt[:, s], op=ALU.mult), reads=[onb[s], sgtb[s]], writes=[orbb[s]])
                bk = mmbank()
                psbf = ps[:, bk, :].bitcast(BF16)

                def fn(h, s=s, psbf=psbf):
                    ins = None
                    for j in range(2):
                        ins = h.transpose(out=psbf[:, j * 128:(j + 1) * 128], in_=orb[:, s, j * 128:(j + 1) * 128], identity=identb)
                    return ins
                P.op("pe", fn, reads=[orbb[s], identbb], writes=[psb[bk]])
                P.op("act", lambda h, psbf=psbf, hd=hd, bsl=bsl: h.activation(out=obT[:, 2 * hd:2 * hd + 2, bsl], in_=psbf[:, 0:256].rearrange("p (a b) -> p a b", a=2), func=AF.Copy),
                     reads=[psb[bk]], pwrites=[ob[2 * hd][tt], ob[2 * hd + 1][tt]])
        AR.release(mk)

    def branch_C(l):
        mk = AR.mark()
        z, (zb,) = AR.alloc("zC", [8], F32)
        z2, (z2b,) = AR.alloc("z2C", [8], F32)
        lamv = vecs[:, l, 56:64]
        P.op("dve", lambda h: h.tensor_scalar(out=z, in0=lamv, scalar1=-1.0, scalar2=None, op0=ALU.mult), reads=[vecsb], writes=[zb])
        P.op("dve", lambda h: h.tensor_tensor(out=z2, in0=z, in1=lamv, op=ALU.max), reads=[zb, vecsb], writes=[z2b])
        P.op("act", lambda h: h.activation(out=z2, in_=z2, func=AF.Exp, scale=-1.0), reads=[z2b], writes=[z2b])
        P.op("act", lambda h: h.activation(out=z2, in_=z2, func=AF.Ln, bias=epst[:, 2:3]), reads=[z2b, epsb], writes=[z2b])
        P.op("dve", lambda h: h.tensor_scalar(out=z, in0=z, scalar1=0.0, scalar2=None, op0=ALU.max), reads=[zb], writes=[zb])
        P.op("dve", lambda h: h.tensor_tensor(out=z, in0=z, in1=z2, op=ALU.add), reads=[zb, z2b], writes=[zb])
        P.op("dve", lambda h: h.tensor_scalar(out=lru_sc, in0=z, scalar1=-4.0, scalar2=None, op0=ALU.mult), reads=[zb], writes=[lruscb])
        hbias, (hbiasb,) = AR.alloc("hbias", [16], F32)
        P.op("dve", lambda h: h.tensor_scalar(out=hbias, in0=vecs[:, l, 40:56], scalar1=0.5, scalar2=None, op0=ALU.mult), reads=[vecsb], writes=[hbiasb])
        wax, (waxb,) = AR.alloc("wax", [2, 8, 128], BF16)
        gwa = P.dma_group("wax")
        P.op("pool", lambda h: [h.dma_start(out=wax[:, 0], in_=wa_d[l].rearrange("n c d -> c n d")),
                                h.dma_start(out=wax[:, 1], in_=wx_d[l].rearrange("n c d -> c n d"))], writes=[waxb], dma=gwa, ndma=2)
        lxs, (lxsb,) = AR.alloc("lxs", [3 + S], F32)
        lxtb = [Buf(f"lxs{tt}") for tt in range(4)]
        for item in AR.live[-1:]:
            for tt in range(4):
                lxtb[tt].r = dict(item[2][0].r)
            item[2].extend(lxtb)
        NB = 2
        ysb, ysbb = AR.alloc("ysb", [NB, 512], F32, nbufs=NB)
        ty_, tyb_ = AR.alloc("ty", [1, 512], F32, nbufs=1); ty = [ty_[:, 0], ty_[:, 0]]; tyb = [tyb_[0], tyb_[0]]
        sgm, sgmb = AR.alloc("sgm", [NB, 512], F32, nbufs=NB)
        xc, xcb = AR.alloc("xc", [NB, 512], F32, nbufs=NB)
        xcbf, xcbfb = AR.alloc("xcbf", [NB, 512], BF16, nbufs=NB)
        ra, rab = AR.alloc("ra", [NB, 512], F32, nbufs=NB)
        ii, iib = AR.alloc("ii", [NB, 512], F32, nbufs=NB)
        a2_, a2b_ = AR.alloc("a2", [1, 512], F32, nbufs=1); a2 = [a2_[:, 0], a2_[:, 0]]; a2b = [a2b_[0], a2b_[0]]
        hh, hhb = AR.alloc("hh", [2, 512], F32, nbufs=2)
        w2d = w_in_d[l]
        it = 0
        for c in range(8):
            slot = next_slot()
            wload([wtile_cols(w2d, C_LX + c * 128, 128, slot, width=256, dcol0=0),
                   wtile_cols(w2d, C_LY + c * 128, 128, slot, width=256, dcol0=128)], slot)
            wt = sview(slot, 8, 256)
            P.op("dve", lambda h: h.memset(lxs[:, 0:3], 0.0), pwrites=[lxtb[0]])
            for tt in range(4):
                s = it % NB
                hs = it % 2
                it += 1
                tsl = slice(tt * 512, (tt + 1) * 512)
                bk = mmbank()
                P.op("pe", mm_fn(ps[:, bk, :], [(wt[:, kc, 0:128], hT[:, kc, tsl]) for kc in range(8)]), reads=[slot[1], hb[tt]], writes=[psb[bk]])
                P.op("act", lambda h, bk=bk, tt=tt: h.activation(out=lxs[:, 3 + tt * 512:3 + (tt + 1) * 512], in_=ps[:, bk, :], func=AF.Copy),
                     reads=[psb[bk]], writes=[lxtb[tt]] if tt > 0 else (), pwrites=[lxtb[0]] if tt == 0 else ())
                bk = mmbank()
                P.op("pe", mm_fn(ps[:, bk, :], [(wt[:, kc, 128:256], hT[:, kc, tsl]) for kc in range(8)]), reads=[slot[1], hb[tt]], writes=[psb[bk]])
                P.op("act", lambda h, bk=bk, s=s: h.activation(out=ysb[:, s], in_=ps[:, bk, :], func=AF.Copy), reads=[psb[bk]], writes=[ysbb[s]])
                P.op("dve", lambda h, s=s: h.tensor_tensor(out=ty[s], in0=ysb[:, s], in1=ysb[:, s], op=ALU.mult), reads=[ysbb[s]], writes=[tyb[s]])
                P.op("dve", lambda h, s=s: h.tensor_scalar(out=ty[s], in0=ty[s], scalar1=0.044715, scalar2=1.0, op0=ALU.mult, op1=ALU.add), reads=[tyb[s]], writes=[tyb[s]])
                P.op("dve", lambda h, s=s: h.tensor_tensor(out=ty[s], in0=ty[s], in1=ysb[:, s], op=ALU.mult), reads=[tyb[s], ysbb[s]], writes=[tyb[s]])
                P.op("act", lambda h, s=s: h.activation(out=sgm[:, s], in_=ty[s], func=AF.Tanh, scale=0.7978845608028654), reads=[tyb[s]], writes=[sgmb[s]])
                P.op("dve", lambda h, s=s: h.scalar_tensor_tensor(out=sgm[:, s], in0=sgm[:, s], scalar=1.0, in1=ysb[:, s], op0=ALU.add, op1=ALU.mult), reads=[sgmb[s], ysbb[s]], writes=[sgmb[s]])
                lrd = [lxtb[tt]] + ([lxtb[tt - 1]] if tt > 0 else [])
                P.op("dve", lambda h, s=s, tt=tt, c=c: h.tensor_scalar(out=xc[:, s], in0=lxs[:, tt * 512:tt * 512 + 512], scalar1=vecs[:, l, 64 + c:65 + c], scalar2=vecs[:, l, 32 + c:33 + c],
                                                                      op0=ALU.mult, op1=ALU.add), reads=lrd + [vecsb], writes=[xcb[s]])
                for j in range(1, 4):
                    P.op("dve", lambda h, s=s, tt=tt, c=c, j=j: h.scalar_tensor_tensor(out=xc[:, s], in0=lxs[:, tt * 512 + j:tt * 512 + j + 512], scalar=vecs[:, l, 64 + j * 8 + c:65 + j * 8 + c],
                                                                                      in1=xc[:, s], op0=ALU.mult, op1=ALU.add), reads=lrd + [vecsb, xcb[s]], writes=[xcb[s]])
                P.op("act", lambda h, s=s: h.activation(out=xcbf[:, s], in_=xc[:, s], func=AF.Copy), reads=[xcb[s]], writes=[xcbfb[s]])
                bk = mmbank()
                P.op("pe", mm_fn(ps[:, bk, :], [(wax[:, 0, c, :], xcbf[:, s])]), reads=[waxb, xcbfb[s]], writes=[psb[bk]])
                P.op("act", lambda h, bk=bk, s=s, c=c: h.activation(out=ra[:, s], in_=ps[:, bk, :], func=AF.Tanh, scale=0.5, bias=hbias[:, c:c + 1]), reads=[psb[bk], hbiasb], writes=[rab[s]])
                bk = mmbank()
                P.op("pe", mm_fn(ps[:, bk, :], [(wax[:, 1, c, :], xcbf[:, s])]), reads=[waxb, xcbfb[s]], writes=[psb[bk]])
                P.op("act", lambda h, bk=bk, s=s, c=c: h.activation(out=ii[:, s], in_=ps[:, bk, :], func=AF.Tanh, scale=0.5, bias=hbias[:, 8 + c:9 + c]), reads=[psb[bk], hbiasb], writes=[iib[s]])
                P.op("act", lambda h, s=s, c=c: h.activation(out=ra[:, s], in_=ra[:, s], func=AF.Exp, scale=lru_sc[:, c:c + 1], bias=lru_sc[:, c:c + 1]), reads=[rab[s], lruscb], writes=[rab[s]])
                P.op("dve", lambda h, s=s: h.tensor_tensor(out=a2[s], in0=ra[:, s], in1=ra[:, s], op=ALU.mult), reads=[rab[s]], writes=[a2b[s]])
                P.op("act", lambda h, s=s: h.activation(out=a2[s], in_=a2[s], func=AF.Sqrt, scale=-0.25, bias=epst[:, 3:4]), reads=[a2b[s], epsb], writes=[a2b[s]])
                P.op("dve", lambda h, s=s: h.scalar_tensor_tensor(out=ii[:, s], in0=ii[:, s], scalar=1.0, in1=xc[:, s], op0=ALU.add, op1=ALU.mult), reads=[iib[s], xcb[s]], writes=[iib[s]])
                P.op("dve", lambda h, s=s: h.tensor_tensor(out=ii[:, s], in0=ii[:, s], in1=a2[s], op=ALU.mult), reads=[iib[s], a2b[s]], writes=[iib[s]])
                if tt == 0:
                    P.op("dve", lambda h, s=s, hs=hs: h.tensor_tensor_scan(out=hh[:, hs], data0=ra[:, s], data1=ii[:, s], initial=0.0, op0=ALU.mult, op1=ALU.add),
                         reads=[rab[s], iib[s]], writes=[hhb[hs]])
                else:
                    P.op("dve", lambda h, s=s, hs=hs: h.tensor_tensor_scan(out=hh[:, hs], data0=ra[:, s], data1=ii[:, s], initial=hh[:, 1 - hs, 511:512], op0=ALU.mult, op1=ALU.add),
                         reads=[rab[s], iib[s], hhb[1 - hs]], writes=[hhb[hs]])
                P.op("dve", lambda h, s=s, hs=hs, c=c, tsl=tsl: h.scalar_tensor_tensor(out=obT[:, c, tsl], in0=hh[:, hs], scalar=0.5, in1=sgm[:, s], op0=ALU.mult, op1=ALU.mult),
                     reads=[hhb[hs], sgmb[s]], writes=[ob[c][tt]])
        AR.release(mk)

    def xattn(l):
        norm_stage(l, 8)
        mk = AR.mark()
        memt, (memtb,) = AR.alloc("memt", [2, D], F32)
        gm = P.dma_group("mem")
        P.op("sp", lambda h: h.dma_start(out=memt, in_=mem_d.rearrange("(a p) d -> p a d", p=128)), writes=[memtb], dma=gm)
        memn, (memnb,) = AR.alloc("memn", [2, D], BF16)
        msq, (msqb,) = AR.alloc("msq", [D], F32)
        mss, (mssb,) = AR.alloc("mss", [2], F32)
        for a in range(2):
            P.op("act", lambda h, a=a: h.activation(out=msq, in_=memt[:, a, :], func=AF.Square, accum_out=mss[:, a:a + 1]), reads=[memtb], writes=[msqb], pwrites=[mssb])
        P.op("act", lambda h: h.activation(out=mss, in_=mss, func=AF.Sqrt, scale=1.0 / D, bias=epst[:, 0:1]), reads=[mssb, epsb], writes=[mssb])
        P.op("dve", lambda h: h.reciprocal(out=mss, in_=mss), reads=[mssb], writes=[mssb])
        for a in range(2):
            P.op("dve", lambda h, a=a: h.tensor_scalar(out=memn[:, a, :], in0=memt[:, a, :], scalar1=mss[:, a:a + 1], scalar2=None, op0=ALU.mult),
                 reads=[memtb, mssb], writes=[memnb] if a == 0 else (), pwrites=() if a == 0 else [memnb])
        memT, (memTb,) = AR.alloc("memT", [8, 256], BF16)
        for a in range(2):
            for half in range(2):
                bk = mmbank()
                psbf = ps[:, bk, :].bitcast(BF16)

                def fn(h, a=a, half=half, psbf=psbf):
                    ins = None
                    for j in range(4):
                        kc = half * 4 + j
                        ins = h.transpose(out=psbf[:, j * 128:(j + 1) * 128], in_=memn[:, a, kc * 128:(kc + 1) * 128], identity=identb)
                    return ins
                P.op("pe", fn, reads=[memnb, identbb], writes=[psb[bk]])
                for j in range(4):
                    kc = half * 4 + j
                    P.op("dve", lambda h, psbf=psbf, j=j, kc=kc, a=a: h.tensor_scalar(out=memT[:, kc, a * 128:(a + 1) * 128], in0=psbf[:, j * 128:(j + 1) * 128],
                                                                                   scalar1=vecs[:, l, 16 + kc:17 + kc], scalar2=None, op0=ALU.mult),
                         reads=[psb[bk], vecsb], pwrites=[memTb])
        kxT, (kxTb,) = AR.alloc("kxT", [4, 256], BF16)
        vx, (vxb,) = AR.alloc("vx", [2, 4, 129], BF16)
        P.op("dve", lambda h: h.memset(vx[:, :, :, 128:129], 1.0), writes=[vxb])
        sk = next_slot()
        wload([wtile_cols(wkv_d[l], 0, 512, sk)], sk)
        wk = sview(sk, 8, 512)
        for hd in range(4):
            bk = mmbank()
            P.op("pe", mm_fn(ps[:, bk, 0:256], [(wk[:, kc, hd * 128:(hd + 1) * 128], memT[:, kc, :]) for kc in range(8)]), reads=[sk[1], memTb], writes=[psb[bk]])
            P.op("act", lambda h, bk=bk, hd=hd: h.activation(out=kxT[:, hd, :], in_=ps[:, bk, 0:256], func=AF.Copy), reads=[psb[bk]], pwrites=[kxTb])
        sv = next_slot()
        wload([wtile_cols(wkv_d[l], 512, 512, sv)], sv)
        wv = sview(sv, 8, 512)
        for a in range(2):
            bk = mmbank()
            P.op("pe", mm_fn(ps[:, bk, :], [(memT[:, kc, a * 128:(a + 1) * 128], wv[:, kc, :]) for kc in range(8)]), reads=[sv[1], memTb], writes=[psb[bk]])
            P.op("act", lambda h, bk=bk, a=a: h.activation(out=vx[:, a, :, 0:128], in_=ps[:, bk, :].rearrange("p (a b) -> p a b", a=4), func=AF.Copy), reads=[psb[bk]], pwrites=[vxb])
        sq_ = next_slot()
        wload([wtile_cols(wq_d[l], 0, 512, sq_)], sq_)
        wq = sview(sq_, 8, 512)
        qx, qxb = AR.alloc("qx", [2, S], BF16, nbufs=2)
        NE = 5
        et, etb = AR.alloc("eX", [NE, 512], BF16, nbufs=NE)
        rec, (recb,) = AR.alloc("recX", [4], F32)
        ox, oxb = AR.alloc("oxX", [2, 4, 128], BF16, nbufs=2)
        ectr = 0
        oxT = obT
        for hd in range(4):
            s = hd % 2
            for tt in range(4):
                tsl = slice(tt * 512, (tt + 1) * 512)
                bk = mmbank()
                P.op("pe", mm_fn(ps[:, bk, :], [(wq[:, kc, hd * 128:(hd + 1) * 128], hT[:, kc, tsl]) for kc in range(8)]), reads=[sq_[1], hb[tt]], writes=[psb[bk]])
                P.op("act", lambda h, bk=bk, s=s, tsl=tsl: h.activation(out=qx[:, s, tsl], in_=ps[:, bk, :], func=AF.Copy, scale=128 ** -0.5),
                     reads=[psb[bk]], writes=[qxb[s]] if tt == 0 else (), pwrites=() if tt == 0 else [qxb[s]])
            def x_score(tt, a, hd=hd, s=s):
                nonlocal ectr
                tsl = slice(tt * 512, (tt + 1) * 512)
                bk = mmbank()
                P.op("pe", mm_fn(ps[:, bk, :], [(kxT[:, hd, a * 128:(a + 1) * 128], qx[:, s, tsl])]), reads=[kxTb, qxb[s]], writes=[psb[bk]])
                e = ectr % NE
                ectr += 1
                P.op("act", lambda h, bk=bk, e=e: h.activation(out=et[:, e], in_=ps[:, bk, :], func=AF.Exp), reads=[psb[bk]], writes=[etb[e]])
                return (tt, a, e)

            def x_av(info, hd=hd):
                tt, a, e = info

                def fn(h):
                    ins = None
                    for qb in range(4):
                        bank = 3 + qb // 2
                        co = (qb % 2) * 129
                        ins = h.matmul(ps[:, bank, co:co + 129], lhsT=et[:, e, qb * 128:(qb + 1) * 128], rhs=vx[:, a, hd, :],
                                       start=(a == 0 and qb % 2 == 0), stop=(a == 1), skip_group_check=True)
                    return ins
                P.op("pe", fn, reads=[etb[e], vxb], writes=[psb[3], psb[4]] if a == 0 else (), pwrites=() if a == 0 else [psb[3], psb[4]])

            def x_post(tt, hd=hd):
                tsl = slice(tt * 512, (tt + 1) * 512)
                acc0 = ps[:, 3:5, 0:258].rearrange("p a (b c) -> p a b c", b=2)
                recv = rec.rearrange("p (a b) -> p a b", a=2)
                P.op("dve", lambda h, acc0=acc0, recv=recv: h.reciprocal(out=recv, in_=acc0[:, :, :, 128]), reads=[psb[3], psb[4]], writes=[recb])
                os_ = tt % 2
                P.op("dve", lambda h, acc0=acc0, recv=recv, os_=os_: h.tensor_tensor(out=ox[:, os_].rearrange("p (a b) c -> p a b c", a=2), in0=acc0[:, :, :, 0:128],
                                                                                  in1=recv.unsqueeze(3).to_broadcast([128, 2, 2, 128]), op=ALU.mult),
                     reads=[psb[3], psb[4], recb], writes=[oxb[os_]])
                bk = mmbank()
                psbf = ps[:, bk, :].bitcast(BF16)

                def fn(h, os_=os_, psbf=psbf):
                    ins = None
                    for qb in range(4):
                        ins = h.transpose(out=psbf[:, qb * 128:(qb + 1) * 128], in_=ox[:, os_, qb, :], identity=identb)
                    return ins
                P.op("pe", fn, reads=[oxb[os_], identbb], writes=[psb[bk]])
                P.op("act", lambda h, psbf=psbf, hd=hd, tsl=tsl: h.activation(out=oxT[:, hd, tsl], in_=psbf[:, 0:512], func=AF.Copy), reads=[psb[bk]], writes=[ob[hd][tt]])

            pend = []
            for tt in range(4):
                for a in range(2):
                    pend.append(x_score(tt, a))
                    if len(pend) > 2:
                        x = pend.pop(0)
                        x_av(x)
                        if x[1] == 1:
                            x_post(x[0])
            while pend:
                x = pend.pop(0)
                x_av(x)
                if x[1] == 1:
                    x_post(x[0])
        so = next_slot()
        P.op("pool", lambda h: [h.dma_start(out=so[0].rearrange("p (k c) -> p k c", k=4), in_=wo_d[l].rearrange("(k p) c -> p k c", p=128))], writes=[so[1]], dma=so[2], ndma=1)
        wo = so[0].rearrange("p (k c) -> p k c", k=4)
        for dc in range(8):
            for tt in range(4):
                tsl = slice(tt * 512, (tt + 1) * 512)
                bk = mmbank()
                P.op("pe", mm_fn(ps[:, bk, :], [(wo[:, hd, dc * 128:(dc + 1) * 128], oxT[:, hd, tsl]) for hd in range(4)]), reads=[so[1]] + [ob[hd][tt] for hd in range(4)], writes=[psb[bk]])
                P.op("dve", lambda h, bk=bk, dc=dc, tsl=tsl: h.tensor_tensor(out=xT[:, dc, tsl], in0=xT[:, dc, tsl], in1=ps[:, bk, :], op=ALU.add),
                     reads=[psb[bk], xb[dc][tt]], writes=[xb[dc][tt]])
        AR.release(mk)

    def mlp(l):
        norm_stage(l, 24)
        mk = AR.mark()
        rl, rlb = AR.alloc("rl", [2, 512], F32, nbufs=2)
        hid = obT.rearrange("p (a b) s -> p a b s", a=2)
        hidb = [[Buf(f"hid{a}_{tt}") for tt in range(4)] for a in range(2)]
        for a in range(2):
            for tt in range(4):
                for kc in range(4):
                    for k_, d_ in ob[a * 4 + kc][tt].w.items():
                        hidb[a][tt].r[("ow", kc, k_)] = d_
                    for k_, d_ in ob[a * 4 + kc][tt].r.items():
                        hidb[a][tt].r[("or", kc, k_)] = d_
        rctr = 0
        for fg in range(8):
            a = fg % 2
            s1 = next_slot()
            wload([wtile_cols(w1_d[l], fg * 512, 512, s1)], s1)
            w1 = sview(s1, 8, 512)
            s2 = next_slot()
            P.op("pool", lambda h, s2=s2, fg=fg: [h.dma_start(out=s2[0].rearrange("p (k c) -> p k c", k=4), in_=w2_d[l, fg * 512:(fg + 1) * 512, :].rearrange("(k p) c -> p k c", p=128))],
                 writes=[s2[1]], dma=s2[2], ndma=1)
            w2 = s2[0].rearrange("p (k c) -> p k c", k=4)
            for fc in range(4):
                for tt in range(4):
                    tsl = slice(tt * 512, (tt + 1) * 512)
                    bk = mmbank()
                    P.op("pe", mm_fn(ps[:, bk, :], [(w1[:, kc, fc * 128:(fc + 1) * 128], hT[:, kc, tsl]) for kc in range(8)]), reads=[s1[1], hb[tt]], writes=[psb[bk]])
                    r_ = rctr % 2
                    rctr += 1
                    P.op("act", lambda h, bk=bk, r_=r_: h.activation(out=rl[:, r_], in_=ps[:, bk, :], func=AF.Relu), reads=[psb[bk]], writes=[rlb[r_]])
                    P.op("dve", lambda h, r_=r_, a=a, fc=fc, tsl=tsl: h.tensor_tensor(out=hid[:, a, fc, tsl], in0=rl[:, r_], in1=rl[:, r_], op=ALU.mult),
                         reads=[rlb[r_]], writes=[hidb[a][tt]] if fc == 0 else (), pwrites=() if fc == 0 else [hidb[a][tt]])
            for dc in range(8):
                for tt in range(4):
                    tsl = slice(tt * 512, (tt + 1) * 512)
                    bk = mmbank()
                    P.op("pe", mm_fn(ps[:, bk, :], [(w2[:, fc, dc * 128:(dc + 1) * 128], hid[:, a, fc, tsl]) for fc in range(4)]), reads=[s2[1], hidb[a][tt]], writes=[psb[bk]])
                    P.op("dve", lambda h, bk=bk, dc=dc, tsl=tsl: h.tensor_tensor(out=xT[:, dc, tsl], in0=xT[:, dc, tsl], in1=ps[:, bk, :], op=ALU.add),
                         reads=[psb[bk], xb[dc][tt]], writes=[xb[dc][tt]])
        for a in range(2):
            for tt in range(4):
                for kc in range(4):
                    b_ = ob[a * 4 + kc][tt]
                    for k_, d_ in hidb[a][tt].w.items():
                        b_.r[("hw", k_)] = d_
                    for k_, d_ in hidb[a][tt].r.items():
                        b_.r[("hr", k_)] = d_
        AR.release(mk)

    def final():
        mk = AR.mark()
        sq, sqb = AR.alloc("sqF", [2, 8, 512], BF16, nbufs=2)
        rt, rtb = AR.alloc("rtF", [2, 512], F32, nbufs=2)
        yf = hT.rearrange("p a s -> p (a s)").bitcast(F32).rearrange("p (a b c) -> p a b c", a=2, b=8)
        yfb = [Buf("yf0"), Buf("yf1")]
        for yb_ in yfb:
            for tt_ in range(4):
                for k_, d_ in hb[tt_].w.items():
                    yb_.r[("hw", tt_, k_)] = d_
                for k_, d_ in hb[tt_].r.items():
                    yb_.r[("hr", tt_, k_)] = d_
        osb, osbb = AR.alloc("osb", [2, D], F32, nbufs=2)
        go = [P.dma_group("out0"), P.dma_group("out1")]
        outbufs = [Buf("outd0"), Buf("outd1")]
        octr = 0
        for tt in range(4):
            s = tt % 2
            tsl = slice(tt * 512, (tt + 1) * 512)
            P.op("act", lambda h, s=s, tsl=tsl: h.activation(out=sq[:, s], in_=xT[:, :, tsl], func=AF.Square), reads=[xb[kc][tt] for kc in range(8)], writes=[sqb[s]])
            P.op("pe", mm_fn(ps[:, 7, :], [(onesb, sq[:, s, kc, :]) for kc in range(8)]), reads=[sqb[s], onesbb], writes=[psb[7]])
            P.op("act", lambda h, s=s: h.activation(out=rt[:, s], in_=ps[:, 7, :], func=AF.Sqrt, scale=1.0 / D, bias=epst[:, 0:1]), reads=[psb[7], epsb], writes=[rtb[s]])
            P.op("dve", lambda h, s=s: h.reciprocal(out=rt[:, s], in_=rt[:, s]), reads=[rtb[s]], writes=[rtb[s]])
            for kc in range(8):
                P.op("dve", lambda h, s=s, kc=kc, tsl=tsl: h.scalar_tensor_tensor(out=yf[:, s, kc, :], in0=xT[:, kc, tsl], scalar=vecs[:, 0, 96 + kc:97 + kc], in1=rt[:, s], op0=ALU.mult, op1=ALU.mult),
                     reads=[xb[kc][tt], rtb[s], vecsb], writes=[yfb[s]] if kc == 0 else (), pwrites=() if kc == 0 else [yfb[s]])
            for tbl in range(4):
                o_ = octr % 2
                octr += 1
                tb = tt * 4 + tbl
                for half in range(2):
                    bk = mmbank()

                    def fn(h, s=s, half=half, bk=bk, tbl=tbl):
                        ins = None
                        for j in range(4):
                            kc = half * 4 + j
                            ins = h.transpose(out=ps[:, bk, j * 128:(j + 1) * 128], in_=yf[:, s, kc, tbl * 128:(tbl + 1) * 128], identity=identf)
                        return ins
                    P.op("pe", fn, reads=[yfb[s], cstfb], writes=[psb[bk]])
                    if half == 0:
                        P.op("dve", lambda h, bk=bk, o_=o_: h.tensor_copy(out=osb[:, o_, 0:512], in_=ps[:, bk, :]), reads=[psb[bk]], writes=[osbb[o_]])
                    else:
                        P.op("act", lambda h, bk=bk, o_=o_: h.activation(out=osb[:, o_, 512:1024], in_=ps[:, bk, :], func=AF.Copy), reads=[psb[bk]], pwrites=[osbb[o_]])
                P.op("sp", lambda h, o_=o_, tb=tb: h.dma_start(out=out_d[tb * 128:(tb + 1) * 128, :], in_=osb[:, o_, :]), reads=[osbb[o_]], writes=[outbufs[o_]], dma=go[o_])
        P.op("sp", None, reads=outbufs)
        AR.release(mk)

    def dump():
        gd = P.dma_group("dbg")
        d1 = Buf("dbg1")
        P.op("sp", lambda h: h.dma_start(out=dbgx_d, in_=xT.rearrange("p a s -> p (a s)")), reads=all_x, writes=[d1], dma=gd)
        gd2 = P.dma_group("dbg2")
        d2 = Buf("dbg2")
        P.op("sp", lambda h: h.dma_start(out=dbgo_d, in_=obT.rearrange("p a s -> p (a s)")), reads=all_o, writes=[d2], dma=gd2)
        P.op("sp", None, reads=[d1, d2])

    done = False
    for l in range(n_layers):
        norm_stage(l, 0)
        if stop_after == ("norm", l):
            done = True
            break
        for bi, (nm, fn_) in enumerate((("A", branch_A), ("B", branch_B), ("C", branch_C))):
            fn_(l)
            if stop_after == (nm, l):
                done = True
                break
            merge(l, bi)
        if done:
            break
        if stop_after == ("mix", l):
            done = True
            break
        xattn(l)
        if stop_after == ("xattn", l):
            done = True
            break
        mlp(l)
        if stop_after == ("mlp", l):
            done = True
            break
    if dbg:
        dump()
    if not done:
        final()
    else:
        pass
    P.emit()
    nc._arena_peak = AR.peak
    return nc


def _host_consts():
    ident = np.eye(128, dtype=np.float32)
    k = np.arange(128)[:, None]
    q = np.arange(128)[None, :]
    mask = (k <= q).astype(np.float32)
    cst = np.concatenate([ident, mask], axis=1)
    log_g = np.log(1.0 - np.exp2(-5.0 - np.arange(4, dtype=np.float64)))
    idx = np.arange(128, dtype=np.float64)
    retc = np.zeros((128, 4, 130), np.float32)
    kscale = 128.0 ** -0.5
    for h in range(4):
        rel = idx[None, :] - idx[:, None]
        dec = np.where(rel >= 0, np.exp(np.maximum(rel, 0.0) * log_g[h]), 0.0) * kscale
        retc[:, h, 0:128] = dec
        retc[:, h, 128] = np.exp((127.0 - idx) * log_g[h]) * kscale
        retc[:, h, 129] = np.exp((idx + 1.0) * log_g[h])
    pos = np.arange(S, dtype=np.float32)
    angle = np.repeat((1.0 / (10000.0 ** np.linspace(0.0, 1.0, 64, dtype=np.float32))).astype(np.float32), 2)
    phase = (pos[:, None] * angle[None, :]).astype(np.float32)
    cos = np.cos(phase).astype(np.float32)
    sin = np.sin(phase).astype(np.float32)
    sgn = np.tile(np.array([-1.0, 1.0], np.float32), 64)
    cstab = np.concatenate([cos, sin * sgn[None, :]], axis=1).astype(np.float32)
    return cst, retc.reshape(128, 4 * 130), cstab


def _pack(inputs):
    f = lambda a: np.ascontiguousarray(np.asarray(a, dtype=np.float32))

    def fm(v):
        return f(v).reshape(-1, 8, 128).transpose(2, 0, 1)
    vecs = np.zeros((128, NL, NV), np.float32)
    vecs[:, :, 0:8] = fm(inputs["norm_mix"])
    vecs[:, :, 8:16] = fm(inputs["norm_xattn"])
    vecs[:, :, 16:24] = fm(inputs["norm_mem"])
    vecs[:, :, 24:32] = fm(inputs["norm_mlp"])
    vecs[:, :, 32:40] = fm(inputs["lru_conv_b"])
    vecs[:, :, 40:48] = fm(inputs["lru_ba"])
    vecs[:, :, 48:56] = fm(inputs["lru_bx"])
    vecs[:, :, 56:64] = fm(inputs["lru_lambda"])
    cw = f(inputs["lru_conv_w"]).reshape(NL, 4, 8, 128)
    vecs[:, :, 64:96] = cw.transpose(3, 0, 1, 2).reshape(128, NL, 32)
    vecs[:, :, 96:104] = np.broadcast_to(f(inputs["norm_final"]).reshape(8, 128).T[:, None, :], (128, NL, 8))
    rows = np.zeros((128, NL, NR), np.float32)
    r1 = np.concatenate([f(inputs["diff_lq1"]), f(inputs["diff_lk1"]), f(inputs["diff_lq2"]), f(inputs["diff_lk2"]), f(inputs["diff_subln"])], axis=1)
    rows[:] = r1[None, :, :]
    return vecs.reshape(128, NL * NV), rows.reshape(128, NL * NR)


_CACHE = {}


def kernel(**inputs):
    key = "full"
    if key not in _CACHE:
        _CACHE[key] = build()
    nc = _CACHE[key]
    vecs, rows = _pack(inputs)
    cst, retc, cstab = _host_consts()
    f = lambda a: np.ascontiguousarray(np.asarray(a, dtype=np.float32))
    shared = {
        "w_in": f(inputs["w_in"]), "w_branch": f(inputs["w_branch"]), "w_out": f(inputs["w_out"]),
        "xa_wq": f(inputs["xa_wq"]), "xa_wkv": f(inputs["xa_wkv"]), "xa_wo": f(inputs["xa_wo"]),
        "mlp_w1": f(inputs["mlp_w1"]), "mlp_w2": f(inputs["mlp_w2"]),
        "lru_wa": f(inputs["lru_wa"]), "lru_wx": f(inputs["lru_wx"]),
        "vecs": vecs, "rows": rows, "cst": cst, "retc": retc, "cstab": cstab,
    }
    x = f(inputs["x"])
    mem = f(inputs["mem"])
    in_maps = []
    for b in range(8):
        m = dict(shared)
        m["x"] = x[b]
        m["mem"] = mem[b]
        in_maps.append(m)
    res = run_bass_kernel_spmd(nc, in_maps, core_ids=list(range(8)))
    return np.stack([np.asarray(r["out"], dtype=np.float32) for r in res.results], axis=0)
```

```python
import math
import os
import numpy as np
import concourse.bass as bass
import concourse.mybir as mybir
from concourse.bass_utils import run_bass_kernel_spmd

F32 = mybir.dt.float32
BF16 = mybir.dt.bfloat16
AF = mybir.ActivationFunctionType
ALU = mybir.AluOpType
AX = mybir.AxisListType

ENGS = ("pe", "act", "dve", "pool", "sp")

S = 2048
D = 1024
NL = 4
NV = 104
NR = 384
IN_COLS = 11264
C_DQ, C_DK, C_DV, C_RQ, C_RK, C_RV, C_RG, C_LX, C_LY, C_G = 0, 1024, 2048, 3072, 3584, 4096, 5120, 6144, 7168, 8192


class Buf:
    __slots__ = ("name", "w", "r")

    def __init__(self, name):
        self.name = name
        self.w = {}
        self.r = {}


class DmaGroup:
    __slots__ = ("sem", "total")

    def __init__(self, sem):
        self.sem = sem
        self.total = 0


class Op:
    __slots__ = ("eng", "fn", "deps", "odeps", "signal", "sig_idx", "dma", "dma_val", "ndma", "seq", "cost", "fin", "done", "defer")

    def __init__(self, eng, fn):
        self.eng = eng
        self.fn = fn
        self.deps = []
        self.odeps = []
        self.cost = 0.0
        self.fin = 0.0
        self.done = False
        self.defer = 0.0
        self.signal = False
        self.sig_idx = 0
        self.dma = None
        self.dma_val = 0
        self.ndma = 1


class _Dummy:
    def then_inc(self, *a, **k):
        return self


class _CostProxy:
    def reset(self, eng, is_dma):
        self.eng = eng
        self.is_dma = is_dma
        self.cost = 0.0
        self.tab = None

    def __getattr__(self, name):
        def f(*args, **kwargs):
            out = kwargs.get("out", None)
            if out is None and args:
                out = args[0]
            try:
                shp = out.shape
                n = 1
                for x_ in shp[1:]:
                    n *= int(x_)
            except Exception:
                n = 64
            if self.is_dma:
                self.cost += n * 128 * 4 / 250.0
            elif self.eng == "pe":
                mult = 1.0
                lhs = kwargs.get("lhsT", kwargs.get("in_", None))
                try:
                    if lhs is not None and lhs.dtype == F32:
                        mult = 4.0
                except Exception:
                    pass
                self.cost += 64.0 + mult * n / 2.4
            elif self.eng == "act":
                fn_ = kwargs.get("func", None)
                if fn_ == AF.Sqrt:
                    self.tab = "sqrt"
                elif fn_ == AF.Ln:
                    self.tab = "ln"
                elif fn_ == AF.Exp or fn_ == AF.Tanh:
                    self.tab = "exp"
                self.cost += 230.0 + n / 1.2 + (60.0 if kwargs.get("accum_out", None) is not None else 0.0)
            else:
                k_ = 2.0 if name == "tensor_tensor_scan" else 1.0
                self.cost += 260.0 + k_ * n / 0.96
            return _Dummy()
        return f


class Prog:
    def __init__(self, nc):
        self.nc = nc
        self.ops = {e: [] for e in ENGS}
        self.eng_sem = {}
        self.nsem = 0

    def new_sem(self, name):
        self.nsem += 1
        return self.nc.alloc_semaphore(f"s_{name}_{self.nsem}")

    def dma_group(self, name):
        return DmaGroup(self.new_sem(name))

    def op(self, eng, fn, reads=(), writes=(), pwrites=(), dma=None, ndma=1, defer=0.0):
        o = Op(eng, fn)
        o.defer = defer
        self.seq = getattr(self, "seq", 0) + 1
        o.seq = self.seq
        mykey = ("dma", id(dma)) if dma is not None else eng
        deps = {}
        for b in reads:
            for d in b.w.values():
                deps[id(d)] = d
        for b in writes:
            for d in b.w.values():
                deps[id(d)] = d
            for d in b.r.values():
                deps[id(d)] = d
        for b in pwrites:
            for d in b.r.values():
                deps[id(d)] = d
            for k, d in b.w.items():
                if k != mykey:
                    deps[id(d)] = d
                else:
                    o.odeps.append(d)
        for d in deps.values():
            if d is o:
                continue
            if d.dma is None:
                if d.eng == "pe" and eng == "pe" and dma is None:
                    o.odeps.append(d)
                    continue
                d.signal = True
            o.deps.append(d)
        if dma is not None:
            o.dma = dma
            o.ndma = ndma
            dma.total += 16 * ndma
            o.dma_val = dma.total
        for b in reads:
            prev = b.r.get(mykey)
            if prev is not None and prev is not o:
                o.odeps.append(prev)
            b.r[mykey] = o
        for b in writes:
            b.w = {mykey: o}
            b.r = {}
        for b in pwrites:
            b.w[mykey] = o
        self.ops[eng].append(o)
        return o

    def schedule(self):
        prox = _CostProxy()
        tabs = {}
        curtab = [None]
        for e in ENGS:
            for o in self.ops[e]:
                if o.fn is None:
                    o.cost = 0.0
                    continue
                prox.reset(e, o.dma is not None)
                o.fn(prox)
                o.cost = prox.cost
                tabs[id(o)] = prox.tab
        W = {"pe": 48, "act": 32, "dve": 32, "pool": 1, "sp": 1}
        rem = {e: list(self.ops[e]) for e in ENGS}
        head = {e: 0 for e in ENGS}
        tfree = {e: 0.0 for e in ENGS}
        neword = {e: [] for e in ENGS}
        total = sum(len(v) for v in rem.values())
        nsched = 0
        LAT = 120.0
        while nsched < total:
            best = None
            for e in ENGS:
                lst = rem[e]
                i = head[e]
                n = len(lst)
                while i < n and lst[i].done:
                    i += 1
                head[e] = i
                cnt = 0
                j = i
                cand = None
                while j < n and cnt < W[e]:
                    o = lst[j]
                    if not o.done:
                        cnt += 1
                        ok = True
                        r = 0.0
                        for d in o.deps:
                            if not d.done:
                                ok = False
                                break
                            if d.fin + LAT > r:
                                r = d.fin + LAT
                        if ok:
                            for d in o.odeps:
                                if not d.done:
                                    ok = False
                                    break
                        if ok:
                            r += o.defer
                            st = r if r > tfree[e] else tfree[e]
                            if e == "act":
                                tb_ = tabs.get(id(o))
                                if tb_ is not None and tb_ != curtab[0]:
                                    st += 1300.0
                            if cand is None or st < cand[0]:
                                cand = (st, j, o)
                                if st <= tfree[e]:
                                    break
                    j += 1
                if cand is not None and (best is None or cand[0] < best[0]):
                    best = (cand[0], e, cand[2])
            assert best is not None, "scheduler stuck"
            st, e, o = best
            o.done = True
            if e == "act" and tabs.get(id(o)) is not None:
                curtab[0] = tabs[id(o)]
            if o.dma is not None:
                tfree[e] = st + 1000.0
                o.fin = st + 2000.0 + o.cost
            else:
                o.fin = st + o.cost
                tfree[e] = o.fin
            neword[e].append(o)
            nsched += 1
        self.ops = neword
        self.sim_time = max(tfree.values())

    def emit(self):
        nc = self.nc
        if os.environ.get("KSCHED", "1") == "1":
            self.schedule()
        for e in ENGS:
            c = 0
            for o in self.ops[e]:
                if o.signal:
                    c += 1
                    o.sig_idx = c
            if self.ops[e]:
                self.eng_sem[e] = self.new_sem("eng_" + e)
            if os.environ.get("KDBG_PRINT"):
                print("ENG", e, "ops", len(self.ops[e]), "signals", c, flush=True)

        def run(e, h):
            waited = {}
            for o in self.ops[e]:
                for d in o.deps:
                    if d.dma is not None:
                        sem, val = d.dma.sem, d.dma_val
                    else:
                        sem, val = self.eng_sem[d.eng], d.sig_idx
                    k = id(sem)
                    if waited.get(k, 0) >= val:
                        continue
                    h.wait_ge(sem, val)
                    waited[k] = val
                if o.fn is None:
                    continue
                ins = o.fn(h)
                if o.dma is not None:
                    if not isinstance(ins, (list, tuple)):
                        ins = [ins]
                    assert len(ins) == o.ndma
                    for i_ in ins:
                        i_.then_inc(o.dma.sem, 16)
                elif o.signal:
                    ins.then_inc(self.eng_sem[e], 1)

        with nc.Block() as block:
            @block.tensor
            def _(h):
                run("pe", h)

            @block.scalar
            def _(h):
                run("act", h)

            @block.vector
            def _(h):
                run("dve", h)

            @block.gpsimd
            def _(h):
                run("pool", h)

            @block.sync
            def _(h):
                run("sp", h)


class Arena:
    def __init__(self, nc, name, nbytes):
        self.nbytes = nbytes
        self.t = nc.alloc_sbuf_tensor(name, [128, nbytes // 4], F32)
        self.top = 0
        self.live = []
        self.dead = []
        self.peak = 0

    def alloc(self, name, shape_free, dtype, nbufs=1):
        esz = 2 if dtype == BF16 else 4
        n = int(np.prod(shape_free))
        nb = (n * esz + 63) // 64 * 64
        start = self.top
        end = start + nb
        assert end <= self.nbytes, f"arena overflow for {name}: {end} > {self.nbytes}"
        self.top = end
        self.peak = max(self.peak, end)
        ap = self.t[:, start // 4:(start + nb) // 4]
        if dtype != F32:
            ap = ap.bitcast(dtype)
        ap = ap[:, 0:n]
        if len(shape_free) == 2:
            ap = ap.rearrange("p (a b) -> p a b", a=shape_free[0])
        elif len(shape_free) == 3:
            ap = ap.rearrange("p (a b c) -> p a b c", a=shape_free[0], b=shape_free[1])
        bufs = [Buf(f"{name}{i}") for i in range(nbufs)]
        inh = {}
        keep = []
        for (s, e, obufs) in self.dead:
            if s < end and e > start:
                for ob in obufs:
                    for d in list(ob.w.values()) + list(ob.r.values()):
                        inh[("inh", id(d))] = d
                if s >= start and e <= end:
                    continue
            keep.append((s, e, obufs))
        self.dead = keep
        for b in bufs:
            b.r.update(inh)
        self.live.append((start, end, bufs))
        return ap, bufs

    def mark(self):
        return (self.top, len(self.live))

    def release(self, mark):
        top, nlive = mark
        for item in self.live[nlive:]:
            self.dead.append(item)
        self.live = self.live[:nlive]
        self.top = top


def mm_fn(out, pairs, start=True, skip=False):
    def fn(h):
        n = len(pairs)
        ins = None
        for i, (l, r) in enumerate(pairs):
            if skip:
                ins = h.matmul(out, lhsT=l, rhs=r, start=(start and i == 0), stop=(i == n - 1), skip_group_check=True)
            else:
                ins = h.matmul(out, lhsT=l, rhs=r, start=(start and i == 0), stop=(i == n - 1))
        return ins
    return fn


def build(n_layers=NL, stop_after=None, dbg=False):
    nc = bass.Bass("TRN2", target_bir_lowering=False)

    def dram(name, shape, dt=F32, kind="ExternalInput"):
        return nc.dram_tensor(name, shape, dt, kind=kind).ap()

    x_d = dram("x", [S, D])
    mem_d = dram("mem", [256, D])
    w_in_d = dram("w_in", [NL, D, IN_COLS])
    w_br_d = dram("w_branch", [NL, 3, D, D])
    w_out_d = dram("w_out", [NL, D, D])
    wq_d = dram("xa_wq", [NL, D, 512])
    wkv_d = dram("xa_wkv", [NL, D, 1024])
    wo_d = dram("xa_wo", [NL, 512, D])
    w1_d = dram("mlp_w1", [NL, D, 4096])
    w2_d = dram("mlp_w2", [NL, 4096, D])
    wa_d = dram("lru_wa", [NL, 8, 128, 128])
    wx_d = dram("lru_wx", [NL, 8, 128, 128])
    vecs_d = dram("vecs", [128, NL * NV])
    rows_d = dram("rows", [128, NL * NR])
    cst_d = dram("cst", [128, 256])
    retc_d = dram("retc", [128, 4 * 130])
    cs_d = dram("cstab", [S, 256])
    out_d = dram("out", [S, D], kind="ExternalOutput")
    if dbg:
        dbgx_d = dram("dbgx", [128, 8 * S], kind="ExternalOutput")
        dbgo_d = dram("dbgo", [128, 8 * S], BF16, kind="ExternalOutput")

    P = Prog(nc)
    AR = Arena(nc, "arena", 204 * 1024)
    ps = nc.alloc_psum_tensor("ps", [128, 8, 512], F32)
    psb = [Buf(f"ps{i}") for i in range(8)]
    mmctr = [0]

    def mmbank():
        b = (0, 1, 2, 7)[mmctr[0] % 4]
        mmctr[0] += 1
        return b

    xT, _ = AR.alloc("xT", [8, S], F32)
    xb = [[Buf(f"x{kc}_{tt}") for tt in range(4)] for kc in range(8)]
    cstf, (cstfb,) = AR.alloc("cstf", [256], F32)
    identf = cstf[:, 0:128]
    identb, (identbb,) = AR.alloc("identb", [128], BF16)
    maskb, (maskbb,) = AR.alloc("maskb", [128], BF16)
    onesb, (onesbb,) = AR.alloc("onesb", [128], BF16)
    vecs, (vecsb,) = AR.alloc("vecs", [NL, NV], F32)
    rows1, (rowsb,) = AR.alloc("rows", [NR], F32)
    retc, (retcb,) = AR.alloc("retc", [4, 130], F32)
    lamneg, (lamnegb,) = AR.alloc("lamneg", [4], F32)
    subln_s, (sublnb,) = AR.alloc("subln", [128], F32)
    lru_sc, (lruscb,) = AR.alloc("lrusc", [8], F32)
    hT, _ = AR.alloc("hT", [8, S], BF16)
    hb = [Buf(f"h{tt}") for tt in range(4)]
    obT, _ = AR.alloc("obT", [8, S], BF16)
    ob = [[Buf(f"o{kc}_{tt}") for tt in range(4)] for kc in range(8)]
    NSLOT = 3
    wslot = []
    for i in range(NSLOT):
        ap, (b,) = AR.alloc(f"wslot{i}", [4096], BF16)
        wslot.append((ap, b, P.dma_group(f"w{i}")))
    wctr = [0]

    def next_slot():
        s = wslot[wctr[0] % NSLOT]
        wctr[0] += 1
        return s

    def wload(dsts_srcs, slot):
        ap, b, g = slot
        n = len(dsts_srcs)

        def fn(h):
            return [h.dma_start(out=d, in_=s) for d, s in dsts_srcs]
        P.op("pool", fn, writes=[b], dma=g, ndma=n)

    def wtile_cols(w2d, col0, ncols, slot, nk=8, dcol0=0, width=None):
        ap = slot[0]
        width = width or ncols
        v = ap[:, 0:nk * width].rearrange("p (k c) -> p k c", k=nk)
        src = w2d.rearrange("(k p) c -> p k c", p=128)[:, :, col0:col0 + ncols]
        return (v[:, :, dcol0:dcol0 + ncols], src)

    def sview(slot, nk, width):
        return slot[0][:, 0:nk * width].rearrange("p (k c) -> p k c", k=nk)

    gc = P.dma_group("cst")
    P.op("sp", lambda h: h.dma_start(out=cstf, in_=cst_d), writes=[cstfb], dma=gc)
    gv = P.dma_group("vecs")
    P.op("sp", lambda h: h.dma_start(out=vecs.rearrange("p l v -> p (l v)"), in_=vecs_d), writes=[vecsb], dma=gv)
    gr = P.dma_group("rows")
    grc = P.dma_group("retc")
    P.op("sp", lambda h: h.dma_start(out=retc.rearrange("p l v -> p (l v)"), in_=retc_d), writes=[retcb], dma=grc)
    P.op("dve", lambda h: h.tensor_copy(out=identb, in_=cstf[:, 0:128]), reads=[cstfb], writes=[identbb])
    P.op("dve", lambda h: h.tensor_copy(out=maskb, in_=cstf[:, 128:256]), reads=[cstfb], writes=[maskbb])
    P.op("dve", lambda h: h.memset(onesb, 1.0), writes=[onesbb])

    mk = AR.mark()
    xin, xinb = AR.alloc("xin", [2, D], F32, nbufs=2)
    gx = [P.dma_group("xin0"), P.dma_group("xin1")]
    for tb in range(16):
        s = tb % 2
        P.op("sp", lambda h, tb=tb, s=s: h.dma_start(out=xin[:, s, :], in_=x_d[tb * 128:(tb + 1) * 128, :]), writes=[xinb[s]], dma=gx[s])
        for half in range(2):
            bk = mmbank()

            def fn(h, s=s, half=half, bk=bk):
                ins = None
                for j in range(4):
                    kc = half * 4 + j
                    ins = h.transpose(out=ps[:, bk, j * 128:(j + 1) * 128], in_=xin[:, s, kc * 128:(kc + 1) * 128], identity=identf)
                return ins
            P.op("pe", fn, reads=[xinb[s], cstfb], writes=[psb[bk]])
            tt = tb // 4
            P.op("dve" if half == 0 else "act",
                 (lambda h, bk=bk, half=half, tb=tb: h.tensor_copy(out=xT[:, half * 4:half * 4 + 4, tb * 128:(tb + 1) * 128], in_=ps[:, bk, :].rearrange("p (a b) -> p a b", a=4)))
                 if half == 0 else
                 (lambda h, bk=bk, half=half, tb=tb: h.activation(out=xT[:, half * 4:half * 4 + 4, tb * 128:(tb + 1) * 128], in_=ps[:, bk, :].rearrange("p (a b) -> p a b", a=4), func=AF.Copy)),
                 reads=[psb[bk]], pwrites=[xb[half * 4 + j][tt] for j in range(4)])
    AR.release(mk)

    def norm_stage(l, gcol, eps=1e-6):
        mk = AR.mark()
        sq, sqb = AR.alloc("sq", [2, 8, 512], BF16, nbufs=2)
        rt, rtb = AR.alloc("rt", [2, 512], F32, nbufs=2)
        for tt in range(4):
            s = tt % 2
            tsl = slice(tt * 512, (tt + 1) * 512)
            P.op("act", lambda h, s=s, tsl=tsl: h.activation(out=sq[:, s], in_=xT[:, :, tsl], func=AF.Square),
                 reads=[xb[kc][tt] for kc in range(8)], writes=[sqb[s]])
            P.op("pe", mm_fn(ps[:, 7, :], [(onesb, sq[:, s, kc, :]) for kc in range(8)]), reads=[sqb[s], onesbb], writes=[psb[7]])
            P.op("act", lambda h, s=s: h.activation(out=rt[:, s], in_=ps[:, 7, :], func=AF.Sqrt, scale=1.0 / D, bias=eps_ap(eps)),
                 reads=[psb[7], epsb], writes=[rtb[s]])
            P.op("dve", lambda h, s=s: h.reciprocal(out=rt[:, s], in_=rt[:, s]), reads=[rtb[s]], writes=[rtb[s]])
            for kc in range(8):
                P.op("dve", lambda h, s=s, kc=kc, tsl=tsl: h.scalar_tensor_tensor(out=hT[:, kc, tsl], in0=xT[:, kc, tsl], scalar=vecs[:, l, gcol + kc:gcol + kc + 1],
                                                                              in1=rt[:, s], op0=ALU.mult, op1=ALU.mult),
                     reads=[xb[kc][tt], rtb[s], vecsb], writes=[hb[tt]] if kc == 0 else (), pwrites=() if kc == 0 else [hb[tt]])
        AR.release(mk)

    epst, (epsb,) = AR.alloc("epst", [8], F32)
    P.op("dve", lambda h: h.memset(epst[:, 0:1], 1e-6), writes=[epsb])
    P.op("dve", lambda h: h.memset(epst[:, 1:2], 1e-5), pwrites=[epsb])
    P.op("dve", lambda h: h.memset(epst[:, 2:3], 1.0), pwrites=[epsb])
    P.op("dve", lambda h: h.memset(epst[:, 3:4], 0.25), pwrites=[epsb])
    P.op("dve", lambda h: h.memset(epst[:, 4:5], 4e-5), pwrites=[epsb])

    def eps_ap(eps):
        return epst[:, 0:1] if eps == 1e-6 else epst[:, 1:2]

    all_x = [xb[kc][tt] for kc in range(8) for tt in range(4)]
    all_o = [ob[kc][tt] for kc in range(8) for tt in range(4)]

    def branch_A(l):
        lam_init = 0.8 - 0.6 * math.exp(-0.3 * l)
        mk = AR.mark()
        lt, (ltb,) = AR.alloc("lamt", [2, 64], F32)
        l2, (l2b,) = AR.alloc("lam2", [2], F32)
        P.op("sp", lambda h: h.dma_start(out=rows1, in_=rows_d[:, l * NR:(l + 1) * NR]), writes=[rowsb], dma=gr)
        rv = rows1[:, 0:256].rearrange("p (a b c) -> p a b c", a=2, b=2)
        P.op("dve", lambda h: h.tensor_tensor(out=lt, in0=rv[:, :, 0, :], in1=rv[:, :, 1, :], op=ALU.mult), reads=[rowsb], writes=[ltb])
        P.op("dve", lambda h: h.reduce_sum(out=l2, in_=lt, axis=AX.X), reads=[ltb], writes=[l2b])
        P.op("act", lambda h: h.activation(out=l2, in_=l2, func=AF.Exp), reads=[l2b], writes=[l2b])
        P.op("dve", lambda h: h.scalar_tensor_tensor(out=lamneg[:, 0:1], in0=l2[:, 1:2], scalar=-lam_init, in1=l2[:, 0:1], op0=ALU.add, op1=ALU.subtract),
             reads=[l2b], writes=[lamnegb])
        P.op("dve", lambda h: h.tensor_scalar(out=subln_s, in0=rows1[:, 256:384], scalar1=1.0 - lam_init, scalar2=None, op0=ALU.mult),
             reads=[rowsb], writes=[sublnb])

        qT, qTb = AR.alloc("qT", [2, S], BF16, nbufs=2)
        kT, kTb = AR.alloc("kT", [2, 2, S], BF16, nbufs=2)
        for s_ in range(2):
            P.op("dve", lambda h, s_=s_: h.memset(kT[64:128, s_, 0, :], 0.0), writes=[kTb[s_]])
            P.op("dve", lambda h, s_=s_: h.memset(kT[0:64, s_, 1, :], 0.0), pwrites=[kTb[s_]])
        vv, vvb = AR.alloc("vA", [2, 16, 129], BF16, nbufs=2)
        for s in range(2):
            P.op("dve", lambda h, s=s: h.memset(vv[:, s, :, 128:129], 1.0), writes=[vvb[s]])
        NE = 5
        et, etb = AR.alloc("eA", [NE, 512], BF16, nbufs=NE)
        rec, (recb,) = AR.alloc("recA", [2, 4], F32)
        osb, osbb = AR.alloc("osbA", [2, 4, 129], F32, nbufs=2)
        d0, d0b = osb[:, 0, :, 0:128], osbb[0]
        d1, d1b = osb[:, 1, :, 0:128], osbb[1]
        ss, (ssb,) = AR.alloc("ssA", [4], F32)
        odn, odnb = AR.alloc("odnA", [2, 4, 128], BF16, nbufs=2)
        ectr = 0
        w2d = w_in_d[l]
        for hd in range(int(os.environ.get('KDBG_HEADS', '8'))):
            s = hd % 2
            slot = next_slot()
            wload([wtile_cols(w2d, C_DQ + hd * 128, 128, slot, width=384, dcol0=0),
                   wtile_cols(w2d, C_DK + hd * 128, 128, slot, width=384, dcol0=128),
                   wtile_cols(w2d, C_DV + hd * 128, 128, slot, width=384, dcol0=256)], slot)
            wt = sview(slot, 8, 384)
            wb_ = slot[1]
            for tt in range(4):
                tsl = slice(tt * 512, (tt + 1) * 512)
                bk = mmbank()
                P.op("pe", mm_fn(ps[:, bk, :], [(wt[:, kc, 0:128], hT[:, kc, tsl]) for kc in range(8)]), reads=[wb_, hb[tt]], writes=[psb[bk]])
                P.op("act", lambda h, bk=bk, s=s, tsl=tsl: h.activation(out=qT[:, s, tsl], in_=ps[:, bk, :], func=AF.Copy, scale=0.125),
                     reads=[psb[bk]], writes=[qTb[s]] if tt == 0 else (), pwrites=() if tt == 0 else [qTb[s]])
                bk = mmbank()
                P.op("pe", mm_fn(ps[:, bk, :], [(wt[:, kc, 128:256], hT[:, kc, tsl]) for kc in range(8)]), reads=[wb_, hb[tt]], writes=[psb[bk]])
                P.op("dve", lambda h, bk=bk, s=s, tsl=tsl: h.tensor_copy(out=kT[0:64, s, 0, tsl], in_=ps[0:64, bk, :]),
                     reads=[psb[bk]], writes=[kTb[s]] if tt == 0 else (), pwrites=() if tt == 0 else [kTb[s]])
                P.op("dve", lambda h, bk=bk, s=s, tsl=tsl: h.tensor_copy(out=kT[64:128, s, 1, tsl], in_=ps[64:128, bk, :]),
                     reads=[psb[bk]], pwrites=[kTb[s]])
            for t4 in range(4):
                bk = mmbank()

                def fn(h, t4=t4, bk=bk, wt=wt):
                    ins = None
                    for j in range(4):
                        tb = t4 * 4 + j
                        for kc in range(8):
                            ins = h.matmul(ps[:, bk, j * 128:(j + 1) * 128], lhsT=hT[:, kc, tb * 128:(tb + 1) * 128], rhs=wt[:, kc, 256:384],
                                           start=(kc == 0), stop=(kc == 7))
                    return ins
                P.op("pe", fn, reads=[wb_, hb[t4]], writes=[psb[bk]])
                P.op("dve", lambda h, bk=bk, s=s, t4=t4: h.tensor_copy(out=vv[:, s, t4 * 4:(t4 + 1) * 4, 0:128], in_=ps[:, bk, :].rearrange("p (a b) -> p a b", a=4)),
                     reads=[psb[bk]], writes=[vvb[s]] if t4 == 0 else (), pwrites=() if t4 == 0 else [vvb[s]])
            accb = [psb[3], psb[4], psb[5], psb[6]]

            def emit_score(qt, kb, c, s=s):
                nonlocal ectr
                dstart = max(0, kb - 4 * qt)
                q0 = qt * 512 + dstart * 128
                nq = 512 - dstart * 128
                bk = mmbank()
                P.op("pe", mm_fn(ps[:, bk, 0:nq], [(kT[:, s, c, kb * 128:(kb + 1) * 128], qT[:, s, q0:q0 + nq])]),
                     reads=[kTb[s], qTb[s]], writes=[psb[bk]])
                e = ectr % NE
                ectr += 1
                P.op("act", lambda h, bk=bk, e=e, nq=nq: h.activation(out=et[:, e, 0:nq], in_=ps[:, bk, 0:nq], func=AF.Exp),
                     reads=[psb[bk]], writes=[etb[e]])
                if kb >= 4 * qt:
                    P.op("dve", lambda h, e=e: h.tensor_tensor(out=et[:, e, 0:128], in0=et[:, e, 0:128], in1=maskb, op=ALU.mult),
                         reads=[etb[e], maskbb], writes=[etb[e]])
                return (qt, kb, c, dstart, e)

            def emit_av(info, s=s):
                qt, kb, c, dstart, e = info

                def fn(h):
                    ins = None
                    for qb in range(dstart, 4):
                        bank = 3 + 2 * c + qb // 2
                        co = (qb % 2) * 129
                        ins = h.matmul(ps[:, bank, co:co + 129], lhsT=et[:, e, (qb - dstart) * 128:(qb - dstart + 1) * 128], rhs=vv[:, s, kb, :],
                                       start=(kb == 0 and qb % 2 == 0), stop=(kb == 4 * qt + qb), skip_group_check=True)
                    return ins
                abufs = [accb[2 * c], accb[2 * c + 1]]
                P.op("pe", fn, reads=[etb[e], vvb[s]], writes=abufs if kb == 0 else (), pwrites=() if kb == 0 else abufs)

            def post(qt, hd=hd):
                P.op("act", lambda h: h.activation(out=osb[:, 0].rearrange("p (a b) c -> p a (b c)", a=2), in_=ps[:, 3:5, 0:258], func=AF.Copy), reads=[psb[3], psb[4]], writes=[osbb[0]])
                P.op("dve", lambda h: h.tensor_copy(out=osb[:, 1].rearrange("p (a b) c -> p a (b c)", a=2), in_=ps[:, 5:7, 0:258]), reads=[psb[5], psb[6]], writes=[osbb[1]])
                P.op("dve", lambda h: h.reciprocal(out=rec[:, 0, :], in_=osb[:, 0, :, 128]), reads=[osbb[0]], writes=[recb])
                P.op("dve", lambda h: h.reciprocal(out=rec[:, 1, :], in_=osb[:, 1, :, 128]), reads=[osbb[1]], pwrites=[recb])
                P.op("dve", lambda h: h.tensor_scalar(out=rec[:, 1, :], in0=rec[:, 1, :], scalar1=lamneg[:, 0:1], scalar2=None, op0=ALU.mult),
                     reads=[recb, lamnegb], writes=[recb])
                P.op("dve", lambda h: h.tensor_tensor(out=d0, in0=d0, in1=rec[:, 0, :].unsqueeze(2).to_broadcast([128, 4, 128]), op=ALU.mult),
                     reads=[d0b, recb], writes=[d0b])
                P.op("dve", lambda h: h.tensor_tensor(out=d1, in0=d1, in1=rec[:, 1, :].unsqueeze(2).to_broadcast([128, 4, 128]), op=ALU.mult),
                     reads=[d1b, recb], writes=[d1b])
                P.op("dve", lambda h: h.tensor_tensor(out=d0, in0=d0, in1=d1, op=ALU.add), reads=[d0b, d1b], writes=[d0b])
                P.op("dve", lambda h: h.tensor_tensor(out=d1, in0=d0, in1=d0, op=ALU.mult), reads=[d0b], writes=[d1b])
                P.op("dve", lambda h: h.reduce_sum(out=ss, in_=d1, axis=AX.X), reads=[d1b], writes=[ssb])
                P.op("act", lambda h: h.activation(out=ss, in_=ss, func=AF.Sqrt, scale=1.0 / 128, bias=epst[:, 1:2]), reads=[ssb, epsb], writes=[ssb])
                P.op("dve", lambda h: h.reciprocal(out=ss, in_=ss), reads=[ssb], writes=[ssb])
                P.op("dve", lambda h: h.tensor_tensor(out=d0, in0=d0, in1=ss.unsqueeze(2).to_broadcast([128, 4, 128]), op=ALU.mult), reads=[d0b, ssb], writes=[d0b])
                os_ = qt % 2
                P.op("dve", lambda h, os_=os_: h.tensor_tensor(out=odn[:, os_], in0=d0, in1=subln_s.unsqueeze(1).to_broadcast([128, 4, 128]), op=ALU.mult),
                     reads=[d0b, sublnb], writes=[odnb[os_]])
                bk = mmbank()
                psbf = ps[:, bk, :].bitcast(BF16)

                def fn(h, os_=os_, psbf=psbf):
                    ins = None
                    for qb in range(4):
                        ins = h.transpose(out=psbf[:, qb * 128:(qb + 1) * 128], in_=odn[:, os_, qb, :], identity=identb)
                    return ins
                P.op("pe", fn, reads=[odnb[os_], identbb], writes=[psb[bk]], defer=6000.0)
                P.op("act", lambda h, psbf=psbf, hd=hd, qt=qt: h.activation(out=obT[:, hd, qt * 512:(qt + 1) * 512], in_=psbf[:, 0:512], func=AF.Copy),
                     reads=[psb[bk]], writes=[ob[hd][qt]])

            LAG = 2
            pend = []
            for qt in range(4):
                for kb in range(4 * qt + 4):
                    for c in range(2):
                        pend.append(emit_score(qt, kb, c))
                        if len(pend) > LAG:
                            x = pend.pop(0)
                            emit_av(x)
                            if x[1] == 4 * x[0] + 3 and x[2] == 1:
                                post(x[0])
            while pend:
                x = pend.pop(0)
                emit_av(x)
                if x[1] == 4 * x[0] + 3 and x[2] == 1:
                    post(x[0])
        AR.release(mk)

    def merge(l, b):
        mk = AR.mark()
        m, _ = AR.alloc("m", [8, S], BF16)
        mb = [[Buf(f"m{kc}_{tt}") for tt in range(4)] for kc in range(8)]
        tmp_ap, tmpb = m, None
        for item in AR.live[-1:]:
            for kc in range(8):
                for tt in range(4):
                    mb[kc][tt].r = dict(item[2][0].r)
            item[2].extend([mb[kc][tt] for kc in range(8) for tt in range(4)])
        gsb, gsbb = AR.alloc("gsb", [2, 512], F32, nbufs=2)
        gctr = 0
        for cg in range(2):
            sg = next_slot()
            wload([wtile_cols(w_in_d[l], C_G + b * 1024 + cg * 512, 512, sg)], sg)
            sb = next_slot()
            wload([wtile_cols(w_br_d[l, b], cg * 512, 512, sb)], sb)
            wg = sview(sg, 8, 512)
            wb = sview(sb, 8, 512)
            for dcl in range(4):
                dc = cg * 4 + dcl
                csl = slice(dcl * 128, (dcl + 1) * 128)
                for tt in range(4):
                    tsl = slice(tt * 512, (tt + 1) * 512)
                    bg = mmbank()
                    P.op("pe", mm_fn(ps[:, bg, :], [(wg[:, kc, csl], hT[:, kc, tsl]) for kc in range(8)]), reads=[sg[1], hb[tt]], writes=[psb[bg]])
                    g_ = gctr % 2
                    gctr += 1
                    P.op("act", lambda h, bg=bg, g_=g_: h.activation(out=gsb[:, g_], in_=ps[:, bg, :], func=AF.Tanh, scale=0.5), reads=[psb[bg]], writes=[gsbb[g_]])
                    bp = mmbank()
                    P.op("pe", mm_fn(ps[:, bp, :], [(wb[:, kc, csl], obT[:, kc, tsl]) for kc in range(8)]), reads=[sb[1]] + [ob[kc][tt] for kc in range(8)], writes=[psb[bp]])
                    P.op("dve", lambda h, bp=bp, g_=g_, dc=dc, tsl=tsl: h.scalar_tensor_tensor(out=m[:, dc, tsl], in0=gsb[:, g_], scalar=1.0, in1=ps[:, bp, :], op0=ALU.add, op1=ALU.mult),
                         reads=[psb[bp], gsbb[g_]], writes=[mb[dc][tt]])
        for cg in range(2):
            so = next_slot()
            wload([wtile_cols(w_out_d[l], cg * 512, 512, so)], so)
            wo = sview(so, 8, 512)
            for dcl in range(4):
                dc = cg * 4 + dcl
                csl = slice(dcl * 128, (dcl + 1) * 128)
                for tt in range(4):
                    tsl = slice(tt * 512, (tt + 1) * 512)
                    bk = mmbank()
                    P.op("pe", mm_fn(ps[:, bk, :], [(wo[:, kc, csl], m[:, kc, tsl]) for kc in range(8)]), reads=[so[1]] + [mb[kc][tt] for kc in range(8)], writes=[psb[bk]])
                    P.op("dve", lambda h, bk=bk, dc=dc, tsl=tsl: h.scalar_tensor_tensor(out=xT[:, dc, tsl], in0=ps[:, bk, :], scalar=0.5, in1=xT[:, dc, tsl], op0=ALU.mult, op1=ALU.add),
                         reads=[psb[bk], xb[dc][tt]], writes=[xb[dc][tt]])
        AR.release(mk)

    def branch_B(l):
        mk = AR.mark()
        gam = [1.0 - 2.0 ** (-5.0 - h_) for h_ in range(4)]
        qkT, qkTb = AR.alloc("qkT", [3, S], BF16, nbufs=1)
        kw, (kwb,) = AR.alloc("kw", [16, 128], BF16)
        vr, (vrb,) = AR.alloc("vr", [16, 256], BF16)
        cst, cstb = AR.alloc("cst", [2, 256], F32, nbufs=2)
        gcs = [P.dma_group("cs0"), P.dma_group("cs1")]
        qkf, qkfb = AR.alloc("qkf", [2, 256], F32, nbufs=2)
        t1, t1b = AR.alloc("t1B", [2, 256], F32, nbufs=2)
        t2, t2b = AR.alloc("t2B", [2, 256], F32, nbufs=2)
        qk3, qk3b = AR.alloc("qk3", [2, 3, 128], BF16, nbufs=2)
        st, (stb,) = AR.alloc("st", [256], F32)
        stbf, stbfb = AR.alloc("stbf", [2, 256], BF16, nbufs=2)
        sTm, sTmb = AR.alloc("sTm", [2, 128], BF16, nbufs=2)
        sgt, sgtb = AR.alloc("sgt", [2, 256], F32, nbufs=2)
        bst, bstb = AR.alloc("bst", [2, 8], F32, nbufs=2)
        on_, onb = AR.alloc("onB", [2, 256], F32, nbufs=2)
        orb, orbb = AR.alloc("orB", [2, 256], BF16, nbufs=2)
        w2d = w_in_d[l]
        for hd in range(4):
            s1 = next_slot()
            wload([wtile_cols(w2d, C_RQ + hd * 128, 128, s1, width=512, dcol0=0),
                   wtile_cols(w2d, C_RK + hd * 128, 128, s1, width=512, dcol0=128),
                   wtile_cols(w2d, C_RV + hd * 256, 256, s1, width=512, dcol0=256)], s1)
            s2 = next_slot()
            wload([wtile_cols(w2d, C_RG + hd * 256, 256, s2)], s2)
            w1 = sview(s1, 8, 512)
            wg = sview(s2, 8, 256)
            for tb in range(16):
                s = tb % 2
                tt = tb // 4
                bsl = slice(tb * 128, (tb + 1) * 128)
                P.op("sp", lambda h, s=s, bsl=bsl: h.dma_start(out=cst[:, s], in_=cs_d[bsl, :]), writes=[cstb[s]], dma=gcs[s])
                bk = mmbank()
                P.op("pe", mm_fn(ps[:, bk, :], [(hT[:, kc, bsl], w1[:, kc, :]) for kc in range(8)]), reads=[s1[1], hb[tt]], writes=[psb[bk]])
                P.op("act", lambda h, bk=bk, s=s: h.activation(out=qkf[:, s], in_=ps[:, bk, 0:256], func=AF.Copy), reads=[psb[bk]], writes=[qkfb[s]])
                P.op("act", lambda h, bk=bk, tb=tb: h.activation(out=vr[:, tb, :], in_=ps[:, bk, 256:512], func=AF.Copy), reads=[psb[bk]],
                     writes=[vrb] if tb == 0 else (), pwrites=() if tb == 0 else [vrb])
                xq = qkf[:, s].rearrange("p (a b) -> p a b", a=2)
                cosb = cst[:, s, 0:128].unsqueeze(1).to_broadcast([128, 2, 128])
                P.op("dve", lambda h, s=s, xq=xq, cosb=cosb: h.tensor_tensor(out=t1[:, s].rearrange("p (a b) -> p a b", a=2), in0=xq, in1=cosb, op=ALU.mult),
                     reads=[qkfb[s], cstb[s]], writes=[t1b[s]])
                x4 = qkf[:, s].rearrange("p (a b c) -> p a b c", a=2, c=2)
                t24 = t2[:, s].rearrange("p (a b c) -> p a b c", a=2, c=2)
                sn4 = cst[:, s, 128:256].rearrange("p (b c) -> p b c", c=2)
                P.op("dve", lambda h, x4=x4, t24=t24, sn4=sn4: h.tensor_tensor(out=t24[:, :, :, 0], in0=x4[:, :, :, 1], in1=sn4[:, :, 0].unsqueeze(1).to_broadcast([128, 2, 64]), op=ALU.mult),
                     reads=[qkfb[s], cstb[s]], writes=[t2b[s]])
                P.op("dve", lambda h, x4=x4, t24=t24, sn4=sn4: h.tensor_tensor(out=t24[:, :, :, 1], in0=x4[:, :, :, 0], in1=sn4[:, :, 1].unsqueeze(1).to_broadcast([128, 2, 64]), op=ALU.mult),
                     reads=[qkfb[s], cstb[s]], pwrites=[t2b[s]])
                P.op("dve", lambda h, s=s: h.tensor_tensor(out=t1[:, s], in0=t1[:, s], in1=t2[:, s], op=ALU.add), reads=[t1b[s], t2b[s]], writes=[t1b[s]])
                P.op("act", lambda h, s=s: h.activation(out=qk3[:, s, 0, :], in_=t1[:, s, 0:128], func=AF.Copy), reads=[t1b[s]], writes=[qk3b[s]])
                P.op("act", lambda h, s=s, hd=hd: h.activation(out=qk3[:, s, 1, :], in_=t1[:, s, 0:128], func=AF.Identity, scale=retc[:, hd, 129:130]), reads=[t1b[s], retcb], pwrites=[qk3b[s]])
                P.op("act", lambda h, s=s: h.activation(out=qk3[:, s, 2, :], in_=t1[:, s, 128:256], func=AF.Copy), reads=[t1b[s]], pwrites=[qk3b[s]])
                P.op("dve", lambda h, s=s, hd=hd, tb=tb: h.tensor_scalar(out=kw[:, tb, :], in0=t1[:, s, 128:256], scalar1=retc[:, hd, 128:129], scalar2=None, op0=ALU.mult),
                     reads=[t1b[s], retcb], writes=[kwb] if tb == 0 else (), pwrites=() if tb == 0 else [kwb])
                bk = mmbank()
                psbf = ps[:, bk, :].bitcast(BF16)

                def fn(h, s=s, psbf=psbf):
                    ins = None
                    for j in range(3):
                        ins = h.transpose(out=psbf[:, j * 128:(j + 1) * 128], in_=qk3[:, s, j, :], identity=identb)
                    return ins
                P.op("pe", fn, reads=[qk3b[s], identbb], writes=[psb[bk]])
                P.op("dve", lambda h, psbf=psbf, bsl=bsl: h.tensor_copy(out=qkT[:, :, bsl], in_=psbf[:, 0:384].rearrange("p (a b) -> p a b", a=3)),
                     reads=[psb[bk]], writes=[qkTb[0]] if tb == 0 else (), pwrites=() if tb == 0 else [qkTb[0]])
            cd = gam[hd] ** 128
            for n in range(16):
                s = n % 2
                tt = n // 4
                bsl = slice(n * 128, (n + 1) * 128)
                bk = mmbank()
                P.op("pe", mm_fn(ps[:, bk, 0:128], [(qkT[:, 2, bsl], qkT[:, 0, bsl])]), reads=[qkTb[0]], writes=[psb[bk]])
                P.op("dve", lambda h, bk=bk, s=s, hd=hd: h.tensor_tensor(out=sTm[:, s], in0=ps[:, bk, 0:128], in1=retc[:, hd, 0:128], op=ALU.mult),
                     reads=[psb[bk], retcb], writes=[sTmb[s]])
                bo = mmbank()
                pairs = [(sTm[:, s], vr[:, n, :])]
                rd = [sTmb[s], vrb]
                if n > 0:
                    pairs.append((qkT[:, 1, bsl], stbf[:, (n - 1) % 2]))
                    rd += [qkTb[0], stbfb[(n - 1) % 2]]
                P.op("pe", mm_fn(ps[:, bo, 0:256], pairs), reads=rd, writes=[psb[bo]])
                bg = mmbank()
                P.op("pe", mm_fn(ps[:, bg, 0:256], [(hT[:, kc, bsl], wg[:, kc, :]) for kc in range(8)]), reads=[s2[1], hb[tt]], writes=[psb[bg]])
                P.op("act", lambda h, bg=bg, s=s: h.activation(out=sgt[:, s], in_=ps[:, bg, 0:256], func=AF.Tanh, scale=0.5), reads=[psb[bg]], writes=[sgtb[s]])
                P.op("dve", lambda h, bg=bg, s=s: h.scalar_tensor_tensor(out=sgt[:, s], in0=sgt[:, s], scalar=1.0, in1=ps[:, bg, 0:256], op0=ALU.add, op1=ALU.mult), reads=[psb[bg], sgtb[s]], writes=[sgtb[s]])
                if n < 15:
                    bkv = mmbank()
                    P.op("pe", mm_fn(ps[:, bkv, 0:256], [(kw[:, n, :], vr[:, n, :])]), reads=[kwb, vrb], writes=[psb[bkv]])
                    if n == 0:
                        P.op("dve", lambda h, bkv=bkv: h.tensor_copy(out=st, in_=ps[:, bkv, 0:256]), reads=[psb[bkv]], writes=[stb])
                    else:
                        P.op("dve", lambda h, bkv=bkv, cd=cd: h.scalar_tensor_tensor(out=st, in0=st, scalar=cd, in1=ps[:, bkv, 0:256], op0=ALU.mult, op1=ALU.add),
                             reads=[psb[bkv], stb], writes=[stb])
                    P.op("act", lambda h, s=s: h.activation(out=stbf[:, s], in_=st, func=AF.Copy), reads=[stb], writes=[stbfb[s]])
                P.op("dve", lambda h, bo=bo, s=s: h.bn_stats(out=bst[:, s, 0:6], in_=ps[:, bo, 0:256]), reads=[psb[bo]], writes=[bstb[s]])
                P.op("dve", lambda h, s=s: h.bn_aggr(out=bst[:, s, 6:8], in_=bst[:, s, 0:6]), reads=[bstb[s]], writes=[bstb[s]])
                P.op("act", lambda h, s=s: h.activation(out=bst[:, s, 7:8], in_=bst[:, s, 7:8], func=AF.Sqrt, scale=4.0, bias=epst[:, 4:5]), reads=[bstb[s], epsb], writes=[bstb[s]])
                P.op("dve", lambda h, s=s: h.reciprocal(out=bst[:, s, 7:8], in_=bst[:, s, 7:8]), reads=[bstb[s]], writes=[bstb[s]])
                P.op("dve", lambda h, bo=bo, s=s: h.tensor_scalar(out=on_[:, s], in0=ps[:, bo, 0:256], scalar1=bst[:, s, 6:7], scalar2=bst[:, s, 7:8], op0=ALU.subtract, op1=ALU.mult),
                     reads=[psb[bo], bstb[s]], writes=[onb[s]])
                P.op("dve", lambda h, s=s: h.tensor_tensor(out=orb[:, s], in0=on_[:, s], in1=sgt[:, s], op=ALU.mult), reads=[onb[s], sgtb[s]], writes=[orbb[s]])
                bk = mmbank()
                psbf = ps[:, bk, :].bitcast(BF16)

                def fn(h, s=s, psbf=psbf):
                    ins = None
                    for j in range(2):
                        ins = h.transpose(out=psbf[:, j * 128:(j + 1) * 128], in_=orb[:, s, j * 128:(j + 1) * 128], identity=identb)
                    return ins
                P.op("pe", fn, reads=[orbb[s], identbb], writes=[psb[bk]], defer=4000.0)
                P.op("act", lambda h, psbf=psbf, hd=hd, bsl=bsl: h.activation(out=obT[:, 2 * hd:2 * hd + 2, bsl], in_=psbf[:, 0:256].rearrange("p (a b) -> p a b", a=2), func=AF.Copy),
                     reads=[psb[bk]], pwrites=[ob[2 * hd][tt], ob[2 * hd + 1][tt]])
        AR.release(mk)

    def branch_C(l):
        mk = AR.mark()
        z, (zb,) = AR.alloc("zC", [8], F32)
        z2, (z2b,) = AR.alloc("z2C", [8], F32)
        lamv = vecs[:, l, 56:64]
        P.op("dve", lambda h: h.tensor_scalar(out=z, in0=lamv, scalar1=-1.0, scalar2=None, op0=ALU.mult), reads=[vecsb], writes=[zb])
        P.op("dve", lambda h: h.tensor_tensor(out=z2, in0=z, in1=lamv, op=ALU.max), reads=[zb, vecsb], writes=[z2b])
        P.op("act", lambda h: h.activation(out=z2, in_=z2, func=AF.Exp, scale=-1.0), reads=[z2b], writes=[z2b])
        P.op("act", lambda h: h.activation(out=z2, in_=z2, func=AF.Ln, bias=epst[:, 2:3]), reads=[z2b, epsb], writes=[z2b])
        P.op("dve", lambda h: h.tensor_scalar(out=z, in0=z, scalar1=0.0, scalar2=None, op0=ALU.max), reads=[zb], writes=[zb])
        P.op("dve", lambda h: h.tensor_tensor(out=z, in0=z, in1=z2, op=ALU.add), reads=[zb, z2b], writes=[zb])
        P.op("dve", lambda h: h.tensor_scalar(out=lru_sc, in0=z, scalar1=-4.0, scalar2=None, op0=ALU.mult), reads=[zb], writes=[lruscb])
        hbias, (hbiasb,) = AR.alloc("hbias", [16], F32)
        P.op("dve", lambda h: h.tensor_scalar(out=hbias, in0=vecs[:, l, 40:56], scalar1=0.5, scalar2=None, op0=ALU.mult), reads=[vecsb], writes=[hbiasb])
        wax, (waxb,) = AR.alloc("wax", [2, 8, 128], BF16)
        gwa = P.dma_group("wax")
        P.op("pool", lambda h: [h.dma_start(out=wax[:, 0], in_=wa_d[l].rearrange("n c d -> c n d")),
                                h.dma_start(out=wax[:, 1], in_=wx_d[l].rearrange("n c d -> c n d"))], writes=[waxb], dma=gwa, ndma=2)
        lxs, (lxsb,) = AR.alloc("lxs", [3 + S], F32)
        lxtb = [Buf(f"lxs{tt}") for tt in range(4)]
        for item in AR.live[-1:]:
            for tt in range(4):
                lxtb[tt].r = dict(item[2][0].r)
            item[2].extend(lxtb)
        NB = 2
        ysb, ysbb = AR.alloc("ysb", [NB, 512], F32, nbufs=NB)
        ty_, tyb_ = AR.alloc("ty", [1, 512], F32, nbufs=1); ty = [ty_[:, 0], ty_[:, 0]]; tyb = [tyb_[0], tyb_[0]]
        sgm, sgmb = AR.alloc("sgm", [NB, 512], F32, nbufs=NB)
        xc, xcb = AR.alloc("xc", [NB, 512], F32, nbufs=NB)
        xcbf, xcbfb = AR.alloc("xcbf", [NB, 512], BF16, nbufs=NB)
        ra, rab = AR.alloc("ra", [NB, 512], F32, nbufs=NB)
        ii, iib = AR.alloc("ii", [NB, 512], F32, nbufs=NB)
        a2_, a2b_ = AR.alloc("a2", [1, 512], F32, nbufs=1); a2 = [a2_[:, 0], a2_[:, 0]]; a2b = [a2b_[0], a2b_[0]]
        hh, hhb = AR.alloc("hh", [2, 512], F32, nbufs=2)
        w2d = w_in_d[l]
        it = 0
        for c in range(8):
            slot = next_slot()
            wload([wtile_cols(w2d, C_LX + c * 128, 128, slot, width=256, dcol0=0),
                   wtile_cols(w2d, C_LY + c * 128, 128, slot, width=256, dcol0=128)], slot)
            wt = sview(slot, 8, 256)
            P.op("dve", lambda h: h.memset(lxs[:, 0:3], 0.0), pwrites=[lxtb[0]])
            for tt in range(4):
                s = it % NB
                hs = it % 2
                it += 1
                tsl = slice(tt * 512, (tt + 1) * 512)
                bk = mmbank()
                P.op("pe", mm_fn(ps[:, bk, :], [(wt[:, kc, 0:128], hT[:, kc, tsl]) for kc in range(8)]), reads=[slot[1], hb[tt]], writes=[psb[bk]])
                P.op("act", lambda h, bk=bk, tt=tt: h.activation(out=lxs[:, 3 + tt * 512:3 + (tt + 1) * 512], in_=ps[:, bk, :], func=AF.Copy),
                     reads=[psb[bk]], writes=[lxtb[tt]] if tt > 0 else (), pwrites=[lxtb[0]] if tt == 0 else ())
                bk = mmbank()
                P.op("pe", mm_fn(ps[:, bk, :], [(wt[:, kc, 128:256], hT[:, kc, tsl]) for kc in range(8)]), reads=[slot[1], hb[tt]], writes=[psb[bk]])
                P.op("act", lambda h, bk=bk, s=s: h.activation(out=ysb[:, s], in_=ps[:, bk, :], func=AF.Copy), reads=[psb[bk]], writes=[ysbb[s]])
                P.op("dve", lambda h, s=s: h.tensor_tensor(out=ty[s], in0=ysb[:, s], in1=ysb[:, s], op=ALU.mult), reads=[ysbb[s]], writes=[tyb[s]])
                P.op("dve", lambda h, s=s: h.tensor_scalar(out=ty[s], in0=ty[s], scalar1=0.044715, scalar2=1.0, op0=ALU.mult, op1=ALU.add), reads=[tyb[s]], writes=[tyb[s]])
                P.op("dve", lambda h, s=s: h.tensor_tensor(out=ty[s], in0=ty[s], in1=ysb[:, s], op=ALU.mult), reads=[tyb[s], ysbb[s]], writes=[tyb[s]])
                P.op("act", lambda h, s=s: h.activation(out=sgm[:, s], in_=ty[s], func=AF.Tanh, scale=0.7978845608028654), reads=[tyb[s]], writes=[sgmb[s]])
                P.op("dve", lambda h, s=s: h.scalar_tensor_tensor(out=sgm[:, s], in0=sgm[:, s], scalar=1.0, in1=ysb[:, s], op0=ALU.add, op1=ALU.mult), reads=[sgmb[s], ysbb[s]], writes=[sgmb[s]])
                lrd = [lxtb[tt]] + ([lxtb[tt - 1]] if tt > 0 else [])
                P.op("dve", lambda h, s=s, tt=tt, c=c: h.tensor_scalar(out=xc[:, s], in0=lxs[:, tt * 512:tt * 512 + 512], scalar1=vecs[:, l, 64 + c:65 + c], scalar2=vecs[:, l, 32 + c:33 + c],
                                                                      op0=ALU.mult, op1=ALU.add), reads=lrd + [vecsb], writes=[xcb[s]])
                for j in range(1, 4):
                    P.op("dve", lambda h, s=s, tt=tt, c=c, j=j: h.scalar_tensor_tensor(out=xc[:, s], in0=lxs[:, tt * 512 + j:tt * 512 + j + 512], scalar=vecs[:, l, 64 + j * 8 + c:65 + j * 8 + c],
                                                                                      in1=xc[:, s], op0=ALU.mult, op1=ALU.add), reads=lrd + [vecsb, xcb[s]], writes=[xcb[s]])
                P.op("act", lambda h, s=s: h.activation(out=xcbf[:, s], in_=xc[:, s], func=AF.Copy), reads=[xcb[s]], writes=[xcbfb[s]])
                bk = mmbank()
                P.op("pe", mm_fn(ps[:, bk, :], [(wax[:, 0, c, :], xcbf[:, s])]), reads=[waxb, xcbfb[s]], writes=[psb[bk]])
                P.op("act", lambda h, bk=bk, s=s, c=c: h.activation(out=ra[:, s], in_=ps[:, bk, :], func=AF.Tanh, scale=0.5, bias=hbias[:, c:c + 1]), reads=[psb[bk], hbiasb], writes=[rab[s]])
                bk = mmbank()
                P.op("pe", mm_fn(ps[:, bk, :], [(wax[:, 1, c, :], xcbf[:, s])]), reads=[waxb, xcbfb[s]], writes=[psb[bk]])
                P.op("act", lambda h, bk=bk, s=s, c=c: h.activation(out=ii[:, s], in_=ps[:, bk, :], func=AF.Tanh, scale=0.5, bias=hbias[:, 8 + c:9 + c]), reads=[psb[bk], hbiasb], writes=[iib[s]])
                P.op("act", lambda h, s=s, c=c: h.activation(out=ra[:, s], in_=ra[:, s], func=AF.Exp, scale=lru_sc[:, c:c + 1], bias=lru_sc[:, c:c + 1]), reads=[rab[s], lruscb], writes=[rab[s]])
                P.op("dve", lambda h, s=s: h.tensor_tensor(out=a2[s], in0=ra[:, s], in1=ra[:, s], op=ALU.mult), reads=[rab[s]], writes=[a2b[s]])
                P.op("act", lambda h, s=s: h.activation(out=a2[s], in_=a2[s], func=AF.Sqrt, scale=-0.25, bias=epst[:, 3:4]), reads=[a2b[s], epsb], writes=[a2b[s]])
                P.op("dve", lambda h, s=s: h.scalar_tensor_tensor(out=ii[:, s], in0=ii[:, s], scalar=1.0, in1=xc[:, s], op0=ALU.add, op1=ALU.mult), reads=[iib[s], xcb[s]], writes=[iib[s]])
                P.op("dve", lambda h, s=s: h.tensor_tensor(out=ii[:, s], in0=ii[:, s], in1=a2[s], op=ALU.mult), reads=[iib[s], a2b[s]], writes=[iib[s]])
                if tt == 0:
                    P.op("dve", lambda h, s=s, hs=hs: h.tensor_tensor_scan(out=hh[:, hs], data0=ra[:, s], data1=ii[:, s], initial=0.0, op0=ALU.mult, op1=ALU.add),
                         reads=[rab[s], iib[s]], writes=[hhb[hs]])
                else:
                    P.op("dve", lambda h, s=s, hs=hs: h.tensor_tensor_scan(out=hh[:, hs], data0=ra[:, s], data1=ii[:, s], initial=hh[:, 1 - hs, 511:512], op0=ALU.mult, op1=ALU.add),
                         reads=[rab[s], iib[s], hhb[1 - hs]], writes=[hhb[hs]])
                P.op("dve", lambda h, s=s, hs=hs, c=c, tsl=tsl: h.scalar_tensor_tensor(out=obT[:, c, tsl], in0=hh[:, hs], scalar=0.5, in1=sgm[:, s], op0=ALU.mult, op1=ALU.mult),
                     reads=[hhb[hs], sgmb[s]], writes=[ob[c][tt]])
        AR.release(mk)

    def xattn(l):
        norm_stage(l, 8)
        mk = AR.mark()
        memt, (memtb,) = AR.alloc("memt", [2, D], F32)
        gm = P.dma_group("mem")
        P.op("sp", lambda h: h.dma_start(out=memt, in_=mem_d.rearrange("(a p) d -> p a d", p=128)), writes=[memtb], dma=gm)
        memn, (memnb,) = AR.alloc("memn", [2, D], BF16)
        msq, (msqb,) = AR.alloc("msq", [D], F32)
        mss, (mssb,) = AR.alloc("mss", [2], F32)
        for a in range(2):
            P.op("act", lambda h, a=a: h.activation(out=msq, in_=memt[:, a, :], func=AF.Square, accum_out=mss[:, a:a + 1]), reads=[memtb], writes=[msqb], pwrites=[mssb])
        P.op("act", lambda h: h.activation(out=mss, in_=mss, func=AF.Sqrt, scale=1.0 / D, bias=epst[:, 0:1]), reads=[mssb, epsb], writes=[mssb])
        P.op("dve", lambda h: h.reciprocal(out=mss, in_=mss), reads=[mssb], writes=[mssb])
        for a in range(2):
            P.op("dve", lambda h, a=a: h.tensor_scalar(out=memn[:, a, :], in0=memt[:, a, :], scalar1=mss[:, a:a + 1], scalar2=None, op0=ALU.mult),
                 reads=[memtb, mssb], writes=[memnb] if a == 0 else (), pwrites=() if a == 0 else [memnb])
        memT, (memTb,) = AR.alloc("memT", [8, 256], BF16)
        for a in range(2):
            for half in range(2):
                bk = mmbank()
                psbf = ps[:, bk, :].bitcast(BF16)

                def fn(h, a=a, half=half, psbf=psbf):
                    ins = None
                    for j in range(4):
                        kc = half * 4 + j
                        ins = h.transpose(out=psbf[:, j * 128:(j + 1) * 128], in_=memn[:, a, kc * 128:(kc + 1) * 128], identity=identb)
                    return ins
                P.op("pe", fn, reads=[memnb, identbb], writes=[psb[bk]])
                for j in range(4):
                    kc = half * 4 + j
                    P.op("dve", lambda h, psbf=psbf, j=j, kc=kc, a=a: h.tensor_scalar(out=memT[:, kc, a * 128:(a + 1) * 128], in0=psbf[:, j * 128:(j + 1) * 128],
                                                                                   scalar1=vecs[:, l, 16 + kc:17 + kc], scalar2=None, op0=ALU.mult),
                         reads=[psb[bk], vecsb], pwrites=[memTb])
        kxT, (kxTb,) = AR.alloc("kxT", [4, 256], BF16)
        vx, (vxb,) = AR.alloc("vx", [2, 4, 129], BF16)
        P.op("dve", lambda h: h.memset(vx[:, :, :, 128:129], 1.0), writes=[vxb])
        sk = next_slot()
        wload([wtile_cols(wkv_d[l], 0, 512, sk)], sk)
        wk = sview(sk, 8, 512)
        for hd in range(4):
            bk = mmbank()
            P.op("pe", mm_fn(ps[:, bk, 0:256], [(wk[:, kc, hd * 128:(hd + 1) * 128], memT[:, kc, :]) for kc in range(8)]), reads=[sk[1], memTb], writes=[psb[bk]])
            P.op("act", lambda h, bk=bk, hd=hd: h.activation(out=kxT[:, hd, :], in_=ps[:, bk, 0:256], func=AF.Copy), reads=[psb[bk]], pwrites=[kxTb])
        sv = next_slot()
        wload([wtile_cols(wkv_d[l], 512, 512, sv)], sv)
        wv = sview(sv, 8, 512)
        for a in range(2):
            bk = mmbank()
            P.op("pe", mm_fn(ps[:, bk, :], [(memT[:, kc, a * 128:(a + 1) * 128], wv[:, kc, :]) for kc in range(8)]), reads=[sv[1], memTb], writes=[psb[bk]])
            P.op("act", lambda h, bk=bk, a=a: h.activation(out=vx[:, a, :, 0:128], in_=ps[:, bk, :].rearrange("p (a b) -> p a b", a=4), func=AF.Copy), reads=[psb[bk]], pwrites=[vxb])
        sq_ = next_slot()
        wload([wtile_cols(wq_d[l], 0, 512, sq_)], sq_)
        wq = sview(sq_, 8, 512)
        qx, qxb = AR.alloc("qx", [2, S], BF16, nbufs=2)
        NE = 5
        et, etb = AR.alloc("eX", [NE, 512], BF16, nbufs=NE)
        rec, (recb,) = AR.alloc("recX", [4], F32)
        ox, oxb = AR.alloc("oxX", [2, 4, 128], BF16, nbufs=2)
        ectr = 0
        oxT = obT
        for hd in range(4):
            s = hd % 2
            for tt in range(4):
                tsl = slice(tt * 512, (tt + 1) * 512)
                bk = mmbank()
                P.op("pe", mm_fn(ps[:, bk, :], [(wq[:, kc, hd * 128:(hd + 1) * 128], hT[:, kc, tsl]) for kc in range(8)]), reads=[sq_[1], hb[tt]], writes=[psb[bk]])
                P.op("act", lambda h, bk=bk, s=s, tsl=tsl: h.activation(out=qx[:, s, tsl], in_=ps[:, bk, :], func=AF.Copy, scale=128 ** -0.5),
                     reads=[psb[bk]], writes=[qxb[s]] if tt == 0 else (), pwrites=() if tt == 0 else [qxb[s]])
            def x_score(tt, a, hd=hd, s=s):
                nonlocal ectr
                tsl = slice(tt * 512, (tt + 1) * 512)
                bk = mmbank()
                P.op("pe", mm_fn(ps[:, bk, :], [(kxT[:, hd, a * 128:(a + 1) * 128], qx[:, s, tsl])]), reads=[kxTb, qxb[s]], writes=[psb[bk]])
                e = ectr % NE
                ectr += 1
                P.op("act", lambda h, bk=bk, e=e: h.activation(out=et[:, e], in_=ps[:, bk, :], func=AF.Exp), reads=[psb[bk]], writes=[etb[e]])
                return (tt, a, e)

            def x_av(info, hd=hd):
                tt, a, e = info

                def fn(h):
                    ins = None
                    for qb in range(4):
                        bank = 3 + qb // 2
                        co = (qb % 2) * 129
                        ins = h.matmul(ps[:, bank, co:co + 129], lhsT=et[:, e, qb * 128:(qb + 1) * 128], rhs=vx[:, a, hd, :],
                                       start=(a == 0 and qb % 2 == 0), stop=(a == 1), skip_group_check=True)
                    return ins
                P.op("pe", fn, reads=[etb[e], vxb], writes=[psb[3], psb[4]] if a == 0 else (), pwrites=() if a == 0 else [psb[3], psb[4]])

            def x_post(tt, hd=hd):
                tsl = slice(tt * 512, (tt + 1) * 512)
                acc0 = ps[:, 3:5, 0:258].rearrange("p a (b c) -> p a b c", b=2)
                recv = rec.rearrange("p (a b) -> p a b", a=2)
                P.op("dve", lambda h, acc0=acc0, recv=recv: h.reciprocal(out=recv, in_=acc0[:, :, :, 128]), reads=[psb[3], psb[4]], writes=[recb])
                os_ = tt % 2
                P.op("dve", lambda h, acc0=acc0, recv=recv, os_=os_: h.tensor_tensor(out=ox[:, os_].rearrange("p (a b) c -> p a b c", a=2), in0=acc0[:, :, :, 0:128],
                                                                                  in1=recv.unsqueeze(3).to_broadcast([128, 2, 2, 128]), op=ALU.mult),
                     reads=[psb[3], psb[4], recb], writes=[oxb[os_]])
                bk = mmbank()
                psbf = ps[:, bk, :].bitcast(BF16)

                def fn(h, os_=os_, psbf=psbf):
                    ins = None
                    for qb in range(4):
                        ins = h.transpose(out=psbf[:, qb * 128:(qb + 1) * 128], in_=ox[:, os_, qb, :], identity=identb)
                    return ins
                P.op("pe", fn, reads=[oxb[os_], identbb], writes=[psb[bk]], defer=3000.0)
                P.op("act", lambda h, psbf=psbf, hd=hd, tsl=tsl: h.activation(out=oxT[:, hd, tsl], in_=psbf[:, 0:512], func=AF.Copy), reads=[psb[bk]], writes=[ob[hd][tt]])

            pend = []
            for tt in range(4):
                for a in range(2):
                    pend.append(x_score(tt, a))
                    if len(pend) > 2:
                        x = pend.pop(0)
                        x_av(x)
                        if x[1] == 1:
                            x_post(x[0])
            while pend:
                x = pend.pop(0)
                x_av(x)
                if x[1] == 1:
                    x_post(x[0])
        so = next_slot()
        P.op("pool", lambda h: [h.dma_start(out=so[0].rearrange("p (k c) -> p k c", k=4), in_=wo_d[l].rearrange("(k p) c -> p k c", p=128))], writes=[so[1]], dma=so[2], ndma=1)
        wo = so[0].rearrange("p (k c) -> p k c", k=4)
        for dc in range(8):
            for tt in range(4):
                tsl = slice(tt * 512, (tt + 1) * 512)
                bk = mmbank()
                P.op("pe", mm_fn(ps[:, bk, :], [(wo[:, hd, dc * 128:(dc + 1) * 128], oxT[:, hd, tsl]) for hd in range(4)]), reads=[so[1]] + [ob[hd][tt] for hd in range(4)], writes=[psb[bk]])
                P.op("dve", lambda h, bk=bk, dc=dc, tsl=tsl: h.tensor_tensor(out=xT[:, dc, tsl], in0=xT[:, dc, tsl], in1=ps[:, bk, :], op=ALU.add),
                     reads=[psb[bk], xb[dc][tt]], writes=[xb[dc][tt]])
        AR.release(mk)

    def mlp(l):
        norm_stage(l, 24)
        mk = AR.mark()
        rl, rlb = AR.alloc("rl", [2, 512], F32, nbufs=2)
        hid = obT.rearrange("p (a b) s -> p a b s", a=2)
        hidb = [[Buf(f"hid{a}_{tt}") for tt in range(4)] for a in range(2)]
        for a in range(2):
            for tt in range(4):
                for kc in range(4):
                    for k_, d_ in ob[a * 4 + kc][tt].w.items():
                        hidb[a][tt].r[("ow", kc, k_)] = d_
                    for k_, d_ in ob[a * 4 + kc][tt].r.items():
                        hidb[a][tt].r[("or", kc, k_)] = d_
        rctr = 0
        for fg in range(8):
            a = fg % 2
            s1 = next_slot()
            wload([wtile_cols(w1_d[l], fg * 512, 512, s1)], s1)
            w1 = sview(s1, 8, 512)
            s2 = next_slot()
            P.op("pool", lambda h, s2=s2, fg=fg: [h.dma_start(out=s2[0].rearrange("p (k c) -> p k c", k=4), in_=w2_d[l, fg * 512:(fg + 1) * 512, :].rearrange("(k p) c -> p k c", p=128))],
                 writes=[s2[1]], dma=s2[2], ndma=1)
            w2 = s2[0].rearrange("p (k c) -> p k c", k=4)
            for fc in range(4):
                for tt in range(4):
                    tsl = slice(tt * 512, (tt + 1) * 512)
                    bk = mmbank()
                    P.op("pe", mm_fn(ps[:, bk, :], [(w1[:, kc, fc * 128:(fc + 1) * 128], hT[:, kc, tsl]) for kc in range(8)]), reads=[s1[1], hb[tt]], writes=[psb[bk]])
                    r_ = rctr % 2
                    rctr += 1
                    P.op("act", lambda h, bk=bk, r_=r_: h.activation(out=rl[:, r_], in_=ps[:, bk, :], func=AF.Relu), reads=[psb[bk]], writes=[rlb[r_]])
                    P.op("dve", lambda h, r_=r_, a=a, fc=fc, tsl=tsl: h.tensor_tensor(out=hid[:, a, fc, tsl], in0=rl[:, r_], in1=rl[:, r_], op=ALU.mult),
                         reads=[rlb[r_]], writes=[hidb[a][tt]] if fc == 0 else (), pwrites=() if fc == 0 else [hidb[a][tt]])
            for dc in range(8):
                for tt in range(4):
                    tsl = slice(tt * 512, (tt + 1) * 512)
                    bk = mmbank()
                    P.op("pe", mm_fn(ps[:, bk, :], [(w2[:, fc, dc * 128:(dc + 1) * 128], hid[:, a, fc, tsl]) for fc in range(4)]), reads=[s2[1], hidb[a][tt]], writes=[psb[bk]])
                    P.op("dve", lambda h, bk=bk, dc=dc, tsl=tsl: h.tensor_tensor(out=xT[:, dc, tsl], in0=xT[:, dc, tsl], in1=ps[:, bk, :], op=ALU.add),
                         reads=[psb[bk], xb[dc][tt]], writes=[xb[dc][tt]])
        for a in range(2):
            for tt in range(4):
                for kc in range(4):
                    b_ = ob[a * 4 + kc][tt]
                    for k_, d_ in hidb[a][tt].w.items():
                        b_.r[("hw", k_)] = d_
                    for k_, d_ in hidb[a][tt].r.items():
                        b_.r[("hr", k_)] = d_
        AR.release(mk)

    def final():
        mk = AR.mark()
        sq, sqb = AR.alloc("sqF", [2, 8, 512], BF16, nbufs=2)
        rt, rtb = AR.alloc("rtF", [2, 512], F32, nbufs=2)
        yf = hT.rearrange("p a s -> p (a s)").bitcast(F32).rearrange("p (a b c) -> p a b c", a=2, b=8)
        yfb = [Buf("yf0"), Buf("yf1")]
        for yb_ in yfb:
            for tt_ in range(4):
                for k_, d_ in hb[tt_].w.items():
                    yb_.r[("hw", tt_, k_)] = d_
                for k_, d_ in hb[tt_].r.items():
                    yb_.r[("hr", tt_, k_)] = d_
        osb, osbb = AR.alloc("osb", [2, D], F32, nbufs=2)
        go = [P.dma_group("out0"), P.dma_group("out1")]
        outbufs = [Buf("outd0"), Buf("outd1")]
        octr = 0
        for tt in range(4):
            s = tt % 2
            tsl = slice(tt * 512, (tt + 1) * 512)
            P.op("act", lambda h, s=s, tsl=tsl: h.activation(out=sq[:, s], in_=xT[:, :, tsl], func=AF.Square), reads=[xb[kc][tt] for kc in range(8)], writes=[sqb[s]])
            P.op("pe", mm_fn(ps[:, 7, :], [(onesb, sq[:, s, kc, :]) for kc in range(8)]), reads=[sqb[s], onesbb], writes=[psb[7]])
            P.op("act", lambda h, s=s: h.activation(out=rt[:, s], in_=ps[:, 7, :], func=AF.Sqrt, scale=1.0 / D, bias=epst[:, 0:1]), reads=[psb[7], epsb], writes=[rtb[s]])
            P.op("dve", lambda h, s=s: h.reciprocal(out=rt[:, s], in_=rt[:, s]), reads=[rtb[s]], writes=[rtb[s]])
            for kc in range(8):
                P.op("dve", lambda h, s=s, kc=kc, tsl=tsl: h.scalar_tensor_tensor(out=yf[:, s, kc, :], in0=xT[:, kc, tsl], scalar=vecs[:, 0, 96 + kc:97 + kc], in1=rt[:, s], op0=ALU.mult, op1=ALU.mult),
                     reads=[xb[kc][tt], rtb[s], vecsb], writes=[yfb[s]] if kc == 0 else (), pwrites=() if kc == 0 else [yfb[s]])
            for tbl in range(4):
                o_ = octr % 2
                octr += 1
                tb = tt * 4 + tbl
                for half in range(2):
                    bk = mmbank()

                    def fn(h, s=s, half=half, bk=bk, tbl=tbl):
                        ins = None
                        for j in range(4):
                            kc = half * 4 + j
                            ins = h.transpose(out=ps[:, bk, j * 128:(j + 1) * 128], in_=yf[:, s, kc, tbl * 128:(tbl + 1) * 128], identity=identf)
                        return ins
                    P.op("pe", fn, reads=[yfb[s], cstfb], writes=[psb[bk]])
                    if half == 0:
                        P.op("dve", lambda h, bk=bk, o_=o_: h.tensor_copy(out=osb[:, o_, 0:512], in_=ps[:, bk, :]), reads=[psb[bk]], writes=[osbb[o_]])
                    else:
                        P.op("act", lambda h, bk=bk, o_=o_: h.activation(out=osb[:, o_, 512:1024], in_=ps[:, bk, :], func=AF.Copy), reads=[psb[bk]], pwrites=[osbb[o_]])
                P.op("sp", lambda h, o_=o_, tb=tb: h.dma_start(out=out_d[tb * 128:(tb + 1) * 128, :], in_=osb[:, o_, :]), reads=[osbb[o_]], writes=[outbufs[o_]], dma=go[o_])
        P.op("sp", None, reads=outbufs)
        AR.release(mk)

    def dump():
        gd = P.dma_group("dbg")
        d1 = Buf("dbg1")
        P.op("sp", lambda h: h.dma_start(out=dbgx_d, in_=xT.rearrange("p a s -> p (a s)")), reads=all_x, writes=[d1], dma=gd)
        gd2 = P.dma_group("dbg2")
        d2 = Buf("dbg2")
        P.op("sp", lambda h: h.dma_start(out=dbgo_d, in_=obT.rearrange("p a s -> p (a s)")), reads=all_o, writes=[d2], dma=gd2)
        P.op("sp", None, reads=[d1, d2])

    done = False
    for l in range(n_layers):
        norm_stage(l, 0)
        if stop_after == ("norm", l):
            done = True
            break
        for bi, (nm, fn_) in enumerate((("A", branch_A), ("B", branch_B), ("C", branch_C))):
            fn_(l)
            if stop_after == (nm, l):
                done = True
                break
            merge(l, bi)
        if done:
            break
        if stop_after == ("mix", l):
            done = True
            break
        xattn(l)
        if stop_after == ("xattn", l):
            done = True
            break
        mlp(l)
        if stop_after == ("mlp", l):
            done = True
            break
    if dbg:
        dump()
    if not done:
        final()
    else:
        pass
    P.emit()
    nc._arena_peak = AR.peak
    return nc


def _host_consts():
    ident = np.eye(128, dtype=np.float32)
    k = np.arange(128)[:, None]
    q = np.arange(128)[None, :]
    mask = (k <= q).astype(np.float32)
    cst = np.concatenate([ident, mask], axis=1)
    log_g = np.log(1.0 - np.exp2(-5.0 - np.arange(4, dtype=np.float64)))
    idx = np.arange(128, dtype=np.float64)
    retc = np.zeros((128, 4, 130), np.float32)
    kscale = 128.0 ** -0.5
    for h in range(4):
        rel = idx[None, :] - idx[:, None]
        dec = np.where(rel >= 0, np.exp(np.maximum(rel, 0.0) * log_g[h]), 0.0) * kscale
        retc[:, h, 0:128] = dec
        retc[:, h, 128] = np.exp((127.0 - idx) * log_g[h]) * kscale
        retc[:, h, 129] = np.exp((idx + 1.0) * log_g[h])
    pos = np.arange(S, dtype=np.float32)
    angle = np.repeat((1.0 / (10000.0 ** np.linspace(0.0, 1.0, 64, dtype=np.float32))).astype(np.float32), 2)
    phase = (pos[:, None] * angle[None, :]).astype(np.float32)
    cos = np.cos(phase).astype(np.float32)
    sin = np.sin(phase).astype(np.float32)
    sgn = np.tile(np.array([-1.0, 1.0], np.float32), 64)
    cstab = np.concatenate([cos, sin * sgn[None, :]], axis=1).astype(np.float32)
    return cst, retc.reshape(128, 4 * 130), cstab


def _pack(inputs):
    f = lambda a: np.ascontiguousarray(np.asarray(a, dtype=np.float32))

    def fm(v):
        return f(v).reshape(-1, 8, 128).transpose(2, 0, 1)
    vecs = np.zeros((128, NL, NV), np.float32)
    vecs[:, :, 0:8] = fm(inputs["norm_mix"])
    vecs[:, :, 8:16] = fm(inputs["norm_xattn"])
    vecs[:, :, 16:24] = fm(inputs["norm_mem"])
    vecs[:, :, 24:32] = fm(inputs["norm_mlp"])
    vecs[:, :, 32:40] = fm(inputs["lru_conv_b"])
    vecs[:, :, 40:48] = fm(inputs["lru_ba"])
    vecs[:, :, 48:56] = fm(inputs["lru_bx"])
    vecs[:, :, 56:64] = fm(inputs["lru_lambda"])
    cw = f(inputs["lru_conv_w"]).reshape(NL, 4, 8, 128)
    vecs[:, :, 64:96] = cw.transpose(3, 0, 1, 2).reshape(128, NL, 32)
    vecs[:, :, 96:104] = np.broadcast_to(f(inputs["norm_final"]).reshape(8, 128).T[:, None, :], (128, NL, 8))
    rows = np.zeros((128, NL, NR), np.float32)
    r1 = np.concatenate([f(inputs["diff_lq1"]), f(inputs["diff_lk1"]), f(inputs["diff_lq2"]), f(inputs["diff_lk2"]), f(inputs["diff_subln"])], axis=1)
    rows[:] = r1[None, :, :]
    return vecs.reshape(128, NL * NV), rows.reshape(128, NL * NR)


_CACHE = {}


def kernel(**inputs):
    key = "full"
    if key not in _CACHE:
        _CACHE[key] = build()
    nc = _CACHE[key]
    vecs, rows = _pack(inputs)
    cst, retc, cstab = _host_consts()
    f = lambda a: np.ascontiguousarray(np.asarray(a, dtype=np.float32))
    shared = {
        "w_in": f(inputs["w_in"]), "w_branch": f(inputs["w_branch"]), "w_out": f(inputs["w_out"]),
        "xa_wq": f(inputs["xa_wq"]), "xa_wkv": f(inputs["xa_wkv"]), "xa_wo": f(inputs["xa_wo"]),
        "mlp_w1": f(inputs["mlp_w1"]), "mlp_w2": f(inputs["mlp_w2"]),
        "lru_wa": f(inputs["lru_wa"]), "lru_wx": f(inputs["lru_wx"]),
        "vecs": vecs, "rows": rows, "cst": cst, "retc": retc, "cstab": cstab,
    }
    x = f(inputs["x"])
    mem = f(inputs["mem"])
    in_maps = []
    for b in range(8):
        m = dict(shared)
        m["x"] = x[b]
        m["mem"] = mem[b]
        in_maps.append(m)
    res = run_bass_kernel_spmd(nc, in_maps, core_ids=list(range(8)))
    return np.stack([np.asarray(r["out"], dtype=np.float32) for r in res.results], axis=0)
```

```python
import math
import os
import numpy as np
import concourse.bass as bass
import concourse.mybir as mybir
from concourse.bass_utils import run_bass_kernel_spmd

F32 = mybir.dt.float32
BF16 = mybir.dt.bfloat16
AF = mybir.ActivationFunctionType
ALU = mybir.AluOpType
AX = mybir.AxisListType

ENGS = ("pe", "act", "dve", "pool", "sp")

S = 2048
D = 1024
NL = 4
NV = 104
NR = 384
IN_COLS = 11264
C_DQ, C_DK, C_DV, C_RQ, C_RK, C_RV, C_RG, C_LX, C_LY, C_G = 0, 1024, 2048, 3072, 3584, 4096, 5120, 6144, 7168, 8192


class Buf:
    __slots__ = ("name", "w", "r")

    def __init__(self, name):
        self.name = name
        self.w = {}
        self.r = {}


class DmaGroup:
    __slots__ = ("sem", "total")

    def __init__(self, sem):
        self.sem = sem
        self.total = 0


class Op:
    __slots__ = ("eng", "fn", "deps", "odeps", "signal", "sig_idx", "dma", "dma_val", "ndma", "seq", "cost", "fin", "done")

    def __init__(self, eng, fn):
        self.eng = eng
        self.fn = fn
        self.deps = []
        self.odeps = []
        self.cost = 0.0
        self.fin = 0.0
        self.done = False
        self.signal = False
        self.sig_idx = 0
        self.dma = None
        self.dma_val = 0
        self.ndma = 1


class _Dummy:
    def then_inc(self, *a, **k):
        return self


class _CostProxy:
    def reset(self, eng, is_dma):
        self.eng = eng
        self.is_dma = is_dma
        self.cost = 0.0
        self.tab = None

    def __getattr__(self, name):
        def f(*args, **kwargs):
            out = kwargs.get("out", None)
            if out is None and args:
                out = args[0]
            try:
                shp = out.shape
                n = 1
                for x_ in shp[1:]:
                    n *= int(x_)
            except Exception:
                n = 64
            if self.is_dma:
                self.cost += n * 128 * 4 / 250.0
            elif self.eng == "pe":
                mult = 1.0
                lhs = kwargs.get("lhsT", kwargs.get("in_", None))
                try:
                    if lhs is not None and lhs.dtype == F32:
                        mult = 4.0
                except Exception:
                    pass
                self.cost += 64.0 + mult * n / 2.4
            elif self.eng == "act":
                fn_ = kwargs.get("func", None)
                if fn_ == AF.Sqrt:
                    self.tab = "sqrt"
                elif fn_ == AF.Ln:
                    self.tab = "ln"
                elif fn_ == AF.Exp or fn_ == AF.Tanh:
                    self.tab = "exp"
                self.cost += 230.0 + n / 1.2 + (60.0 if kwargs.get("accum_out", None) is not None else 0.0)
            else:
                k_ = 2.0 if name == "tensor_tensor_scan" else 1.0
                self.cost += 260.0 + k_ * n / 0.96
            return _Dummy()
        return f


class Prog:
    def __init__(self, nc):
        self.nc = nc
        self.ops = {e: [] for e in ENGS}
        self.eng_sem = {}
        self.nsem = 0

    def new_sem(self, name):
        self.nsem += 1
        return self.nc.alloc_semaphore(f"s_{name}_{self.nsem}")

    def dma_group(self, name):
        return DmaGroup(self.new_sem(name))

    def op(self, eng, fn, reads=(), writes=(), pwrites=(), dma=None, ndma=1):
        o = Op(eng, fn)
        self.seq = getattr(self, "seq", 0) + 1
        o.seq = self.seq
        mykey = ("dma", id(dma)) if dma is not None else eng
        deps = {}
        for b in reads:
            for d in b.w.values():
                deps[id(d)] = d
        for b in writes:
            for d in b.w.values():
                deps[id(d)] = d
            for d in b.r.values():
                deps[id(d)] = d
        for b in pwrites:
            for d in b.r.values():
                deps[id(d)] = d
            for k, d in b.w.items():
                if k != mykey:
                    deps[id(d)] = d
                else:
                    o.odeps.append(d)
        for d in deps.values():
            if d is o:
                continue
            if d.dma is None:
                if d.eng == "pe" and eng == "pe" and dma is None:
                    o.odeps.append(d)
                    continue
                d.signal = True
            o.deps.append(d)
        if dma is not None:
            o.dma = dma
            o.ndma = ndma
            dma.total += 16 * ndma
            o.dma_val = dma.total
        for b in reads:
            prev = b.r.get(mykey)
            if prev is not None and prev is not o:
                o.odeps.append(prev)
            b.r[mykey] = o
        for b in writes:
            b.w = {mykey: o}
            b.r = {}
        for b in pwrites:
            b.w[mykey] = o
        self.ops[eng].append(o)
        return o

    def schedule(self):
        prox = _CostProxy()
        tabs = {}
        curtab = [None]
        for e in ENGS:
            for o in self.ops[e]:
                if o.fn is None:
                    o.cost = 0.0
                    continue
                prox.reset(e, o.dma is not None)
                o.fn(prox)
                o.cost = prox.cost
                tabs[id(o)] = prox.tab
        W = {"pe": 48, "act": 32, "dve": 32, "pool": 1, "sp": 1}
        rem = {e: list(self.ops[e]) for e in ENGS}
        head = {e: 0 for e in ENGS}
        tfree = {e: 0.0 for e in ENGS}
        neword = {e: [] for e in ENGS}
        total = sum(len(v) for v in rem.values())
        nsched = 0
        LAT = 120.0
        while nsched < total:
            best = None
            for e in ENGS:
                lst = rem[e]
                i = head[e]
                n = len(lst)
                while i < n and lst[i].done:
                    i += 1
                head[e] = i
                cnt = 0
                j = i
                cand = None
                while j < n and cnt < W[e]:
                    o = lst[j]
                    if not o.done:
                        cnt += 1
                        ok = True
                        r = 0.0
                        for d in o.deps:
                            if not d.done:
                                ok = False
                                break
                            if d.fin + LAT > r:
                                r = d.fin + LAT
                        if ok:
                            for d in o.odeps:
                                if not d.done:
                                    ok = False
                                    break
                        if ok:
                            st = r if r > tfree[e] else tfree[e]
                            if e == "act":
                                tb_ = tabs.get(id(o))
                                if tb_ is not None and tb_ != curtab[0]:
                                    st += 1300.0
                            if cand is None or st < cand[0]:
                                cand = (st, j, o)
                                if st <= tfree[e]:
                                    break
                    j += 1
                if cand is not None and (best is None or cand[0] < best[0]):
                    best = (cand[0], e, cand[2])
            assert best is not None, "scheduler stuck"
            st, e, o = best
            o.done = True
            if e == "act" and tabs.get(id(o)) is not None:
                curtab[0] = tabs[id(o)]
            if o.dma is not None:
                tfree[e] = st + 1000.0
                o.fin = st + 2000.0 + o.cost
            else:
                o.fin = st + o.cost
                tfree[e] = o.fin
            neword[e].append(o)
            nsched += 1
        self.ops = neword
        self.sim_time = max(tfree.values())

    def emit(self):
        nc = self.nc
        if os.environ.get("KSCHED", "1") == "1":
            self.schedule()
        for e in ENGS:
            c = 0
            for o in self.ops[e]:
                if o.signal:
                    c += 1
                    o.sig_idx = c
            if self.ops[e]:
                self.eng_sem[e] = self.new_sem("eng_" + e)
            if os.environ.get("KDBG_PRINT"):
                print("ENG", e, "ops", len(self.ops[e]), "signals", c, flush=True)

        def run(e, h):
            waited = {}
            for o in self.ops[e]:
                for d in o.deps:
                    if d.dma is not None:
                        sem, val = d.dma.sem, d.dma_val
                    else:
                        sem, val = self.eng_sem[d.eng], d.sig_idx
                    k = id(sem)
                    if waited.get(k, 0) >= val:
                        continue
                    h.wait_ge(sem, val)
                    waited[k] = val
                if o.fn is None:
                    continue
                ins = o.fn(h)
                if o.dma is not None:
                    if not isinstance(ins, (list, tuple)):
                        ins = [ins]
                    assert len(ins) == o.ndma
                    for i_ in ins:
                        i_.then_inc(o.dma.sem, 16)
                elif o.signal:
                    ins.then_inc(self.eng_sem[e], 1)

        with nc.Block() as block:
            @block.tensor
            def _(h):
                run("pe", h)

            @block.scalar
            def _(h):
                run("act", h)

            @block.vector
            def _(h):
                run("dve", h)

            @block.gpsimd
            def _(h):
                run("pool", h)

            @block.sync
            def _(h):
                run("sp", h)


class Arena:
    def __init__(self, nc, name, nbytes):
        self.nbytes = nbytes
        self.t = nc.alloc_sbuf_tensor(name, [128, nbytes // 4], F32)
        self.top = 0
        self.live = []
        self.dead = []
        self.peak = 0

    def alloc(self, name, shape_free, dtype, nbufs=1):
        esz = 2 if dtype == BF16 else 4
        n = int(np.prod(shape_free))
        nb = (n * esz + 63) // 64 * 64
        start = self.top
        end = start + nb
        assert end <= self.nbytes, f"arena overflow for {name}: {end} > {self.nbytes}"
        self.top = end
        self.peak = max(self.peak, end)
        ap = self.t[:, start // 4:(start + nb) // 4]
        if dtype != F32:
            ap = ap.bitcast(dtype)
        ap = ap[:, 0:n]
        if len(shape_free) == 2:
            ap = ap.rearrange("p (a b) -> p a b", a=shape_free[0])
        elif len(shape_free) == 3:
            ap = ap.rearrange("p (a b c) -> p a b c", a=shape_free[0], b=shape_free[1])
        bufs = [Buf(f"{name}{i}") for i in range(nbufs)]
        inh = {}
        keep = []
        for (s, e, obufs) in self.dead:
            if s < end and e > start:
                for ob in obufs:
                    for d in list(ob.w.values()) + list(ob.r.values()):
                        inh[("inh", id(d))] = d
                if s >= start and e <= end:
                    continue
            keep.append((s, e, obufs))
        self.dead = keep
        for b in bufs:
            b.r.update(inh)
        self.live.append((start, end, bufs))
        return ap, bufs

    def mark(self):
        return (self.top, len(self.live))

    def release(self, mark):
        top, nlive = mark
        for item in self.live[nlive:]:
            self.dead.append(item)
        self.live = self.live[:nlive]
        self.top = top


def mm_fn(out, pairs, start=True, skip=False):
    def fn(h):
        n = len(pairs)
        ins = None
        for i, (l, r) in enumerate(pairs):
            if skip:
                ins = h.matmul(out, lhsT=l, rhs=r, start=(start and i == 0), stop=(i == n - 1), skip_group_check=True)
            else:
                ins = h.matmul(out, lhsT=l, rhs=r, start=(start and i == 0), stop=(i == n - 1))
        return ins
    return fn


def build(n_layers=NL, stop_after=None, dbg=False):
    nc = bass.Bass("TRN2", target_bir_lowering=False)

    def dram(name, shape, dt=F32, kind="ExternalInput"):
        return nc.dram_tensor(name, shape, dt, kind=kind).ap()

    x_d = dram("x", [S, D])
    mem_d = dram("mem", [256, D])
    w_in_d = dram("w_in", [NL, D, IN_COLS])
    w_br_d = dram("w_branch", [NL, 3, D, D])
    w_out_d = dram("w_out", [NL, D, D])
    wq_d = dram("xa_wq", [NL, D, 512])
    wkv_d = dram("xa_wkv", [NL, D, 1024])
    wo_d = dram("xa_wo", [NL, 512, D])
    w1_d = dram("mlp_w1", [NL, D, 4096])
    w2_d = dram("mlp_w2", [NL, 4096, D])
    wa_d = dram("lru_wa", [NL, 8, 128, 128])
    wx_d = dram("lru_wx", [NL, 8, 128, 128])
    vecs_d = dram("vecs", [128, NL * NV])
    rows_d = dram("rows", [128, NL * NR])
    cst_d = dram("cst", [128, 256])
    retc_d = dram("retc", [128, 4 * 130])
    cs_d = dram("cstab", [S, 256])
    out_d = dram("out", [S, D], kind="ExternalOutput")
    if dbg:
        dbgx_d = dram("dbgx", [128, 8 * S], kind="ExternalOutput")
        dbgo_d = dram("dbgo", [128, 8 * S], BF16, kind="ExternalOutput")

    P = Prog(nc)
    AR = Arena(nc, "arena", 204 * 1024)
    ps = nc.alloc_psum_tensor("ps", [128, 8, 512], F32)
    psb = [Buf(f"ps{i}") for i in range(8)]
    mmctr = [0]

    def mmbank():
        b = (0, 1, 2, 7)[mmctr[0] % 4]
        mmctr[0] += 1
        return b

    xT, _ = AR.alloc("xT", [8, S], F32)
    xb = [[Buf(f"x{kc}_{tt}") for tt in range(4)] for kc in range(8)]
    cstf, (cstfb,) = AR.alloc("cstf", [256], F32)
    identf = cstf[:, 0:128]
    identb, (identbb,) = AR.alloc("identb", [128], BF16)
    maskb, (maskbb,) = AR.alloc("maskb", [128], BF16)
    onesb, (onesbb,) = AR.alloc("onesb", [128], BF16)
    vecs, (vecsb,) = AR.alloc("vecs", [NL, NV], F32)
    rows1, (rowsb,) = AR.alloc("rows", [NR], F32)
    retc, (retcb,) = AR.alloc("retc", [4, 130], F32)
    lamneg, (lamnegb,) = AR.alloc("lamneg", [4], F32)
    subln_s, (sublnb,) = AR.alloc("subln", [128], F32)
    lru_sc, (lruscb,) = AR.alloc("lrusc", [8], F32)
    hT, _ = AR.alloc("hT", [8, S], BF16)
    hb = [Buf(f"h{tt}") for tt in range(4)]
    obT, _ = AR.alloc("obT", [8, S], BF16)
    ob = [[Buf(f"o{kc}_{tt}") for tt in range(4)] for kc in range(8)]
    NSLOT = 3
    wslot = []
    for i in range(NSLOT):
        ap, (b,) = AR.alloc(f"wslot{i}", [4096], BF16)
        wslot.append((ap, b, P.dma_group(f"w{i}")))
    wctr = [0]

    def next_slot():
        s = wslot[wctr[0] % NSLOT]
        wctr[0] += 1
        return s

    def wload(dsts_srcs, slot):
        ap, b, g = slot
        n = len(dsts_srcs)

        def fn(h):
            return [h.dma_start(out=d, in_=s) for d, s in dsts_srcs]
        P.op("pool", fn, writes=[b], dma=g, ndma=n)

    def wtile_cols(w2d, col0, ncols, slot, nk=8, dcol0=0, width=None):
        ap = slot[0]
        width = width or ncols
        v = ap[:, 0:nk * width].rearrange("p (k c) -> p k c", k=nk)
        src = w2d.rearrange("(k p) c -> p k c", p=128)[:, :, col0:col0 + ncols]
        return (v[:, :, dcol0:dcol0 + ncols], src)

    def sview(slot, nk, width):
        return slot[0][:, 0:nk * width].rearrange("p (k c) -> p k c", k=nk)

    gc = P.dma_group("cst")
    P.op("sp", lambda h: h.dma_start(out=cstf, in_=cst_d), writes=[cstfb], dma=gc)
    gv = P.dma_group("vecs")
    P.op("sp", lambda h: h.dma_start(out=vecs.rearrange("p l v -> p (l v)"), in_=vecs_d), writes=[vecsb], dma=gv)
    gr = P.dma_group("rows")
    grc = P.dma_group("retc")
    P.op("sp", lambda h: h.dma_start(out=retc.rearrange("p l v -> p (l v)"), in_=retc_d), writes=[retcb], dma=grc)
    P.op("dve", lambda h: h.tensor_copy(out=identb, in_=cstf[:, 0:128]), reads=[cstfb], writes=[identbb])
    P.op("dve", lambda h: h.tensor_copy(out=maskb, in_=cstf[:, 128:256]), reads=[cstfb], writes=[maskbb])
    P.op("dve", lambda h: h.memset(onesb, 1.0), writes=[onesbb])

    mk = AR.mark()
    xin, xinb = AR.alloc("xin", [2, D], F32, nbufs=2)
    gx = [P.dma_group("xin0"), P.dma_group("xin1")]
    for tb in range(16):
        s = tb % 2
        P.op("sp", lambda h, tb=tb, s=s: h.dma_start(out=xin[:, s, :], in_=x_d[tb * 128:(tb + 1) * 128, :]), writes=[xinb[s]], dma=gx[s])
        for half in range(2):
            bk = mmbank()

            def fn(h, s=s, half=half, bk=bk):
                ins = None
                for j in range(4):
                    kc = half * 4 + j
                    ins = h.transpose(out=ps[:, bk, j * 128:(j + 1) * 128], in_=xin[:, s, kc * 128:(kc + 1) * 128], identity=identf)
                return ins
            P.op("pe", fn, reads=[xinb[s], cstfb], writes=[psb[bk]])
            tt = tb // 4
            P.op("dve" if half == 0 else "act",
                 (lambda h, bk=bk, half=half, tb=tb: h.tensor_copy(out=xT[:, half * 4:half * 4 + 4, tb * 128:(tb + 1) * 128], in_=ps[:, bk, :].rearrange("p (a b) -> p a b", a=4)))
                 if half == 0 else
                 (lambda h, bk=bk, half=half, tb=tb: h.activation(out=xT[:, half * 4:half * 4 + 4, tb * 128:(tb + 1) * 128], in_=ps[:, bk, :].rearrange("p (a b) -> p a b", a=4), func=AF.Copy)),
                 reads=[psb[bk]], pwrites=[xb[half * 4 + j][tt] for j in range(4)])
    AR.release(mk)

    def norm_stage(l, gcol, eps=1e-6):
        mk = AR.mark()
        sq, sqb = AR.alloc("sq", [2, 8, 512], BF16, nbufs=2)
        rt, rtb = AR.alloc("rt", [2, 512], F32, nbufs=2)
        for tt in range(4):
            s = tt % 2
            tsl = slice(tt * 512, (tt + 1) * 512)
            P.op("act", lambda h, s=s, tsl=tsl: h.activation(out=sq[:, s], in_=xT[:, :, tsl], func=AF.Square),
                 reads=[xb[kc][tt] for kc in range(8)], writes=[sqb[s]])
            P.op("pe", mm_fn(ps[:, 7, :], [(onesb, sq[:, s, kc, :]) for kc in range(8)]), reads=[sqb[s], onesbb], writes=[psb[7]])
            P.op("act", lambda h, s=s: h.activation(out=rt[:, s], in_=ps[:, 7, :], func=AF.Sqrt, scale=1.0 / D, bias=eps_ap(eps)),
                 reads=[psb[7], epsb], writes=[rtb[s]])
            P.op("dve", lambda h, s=s: h.reciprocal(out=rt[:, s], in_=rt[:, s]), reads=[rtb[s]], writes=[rtb[s]])
            for kc in range(8):
                P.op("dve", lambda h, s=s, kc=kc, tsl=tsl: h.scalar_tensor_tensor(out=hT[:, kc, tsl], in0=xT[:, kc, tsl], scalar=vecs[:, l, gcol + kc:gcol + kc + 1],
                                                                              in1=rt[:, s], op0=ALU.mult, op1=ALU.mult),
                     reads=[xb[kc][tt], rtb[s], vecsb], writes=[hb[tt]] if kc == 0 else (), pwrites=() if kc == 0 else [hb[tt]])
        AR.release(mk)

    epst, (epsb,) = AR.alloc("epst", [8], F32)
    P.op("dve", lambda h: h.memset(epst[:, 0:1], 1e-6), writes=[epsb])
    P.op("dve", lambda h: h.memset(epst[:, 1:2], 1e-5), pwrites=[epsb])
    P.op("dve", lambda h: h.memset(epst[:, 2:3], 1.0), pwrites=[epsb])
    P.op("dve", lambda h: h.memset(epst[:, 3:4], 0.25), pwrites=[epsb])
    P.op("dve", lambda h: h.memset(epst[:, 4:5], 4e-5), pwrites=[epsb])

    def eps_ap(eps):
        return epst[:, 0:1] if eps == 1e-6 else epst[:, 1:2]

    all_x = [xb[kc][tt] for kc in range(8) for tt in range(4)]
    all_o = [ob[kc][tt] for kc in range(8) for tt in range(4)]

    def branch_A(l):
        lam_init = 0.8 - 0.6 * math.exp(-0.3 * l)
        mk = AR.mark()
        lt, (ltb,) = AR.alloc("lamt", [2, 64], F32)
        l2, (l2b,) = AR.alloc("lam2", [2], F32)
        P.op("sp", lambda h: h.dma_start(out=rows1, in_=rows_d[:, l * NR:(l + 1) * NR]), writes=[rowsb], dma=gr)
        rv = rows1[:, 0:256].rearrange("p (a b c) -> p a b c", a=2, b=2)
        P.op("dve", lambda h: h.tensor_tensor(out=lt, in0=rv[:, :, 0, :], in1=rv[:, :, 1, :], op=ALU.mult), reads=[rowsb], writes=[ltb])
        P.op("dve", lambda h: h.reduce_sum(out=l2, in_=lt, axis=AX.X), reads=[ltb], writes=[l2b])
        P.op("act", lambda h: h.activation(out=l2, in_=l2, func=AF.Exp), reads=[l2b], writes=[l2b])
        P.op("dve", lambda h: h.scalar_tensor_tensor(out=lamneg[:, 0:1], in0=l2[:, 1:2], scalar=-lam_init, in1=l2[:, 0:1], op0=ALU.add, op1=ALU.subtract),
             reads=[l2b], writes=[lamnegb])
        P.op("dve", lambda h: h.tensor_scalar(out=subln_s, in0=rows1[:, 256:384], scalar1=1.0 - lam_init, scalar2=None, op0=ALU.mult),
             reads=[rowsb], writes=[sublnb])

        qT, qTb = AR.alloc("qT", [2, S], BF16, nbufs=2)
        kT, kTb = AR.alloc("kT", [2, 2, S], BF16, nbufs=2)
        for s_ in range(2):
            P.op("dve", lambda h, s_=s_: h.memset(kT[64:128, s_, 0, :], 0.0), writes=[kTb[s_]])
            P.op("dve", lambda h, s_=s_: h.memset(kT[0:64, s_, 1, :], 0.0), pwrites=[kTb[s_]])
        vv, vvb = AR.alloc("vA", [2, 16, 129], BF16, nbufs=2)
        for s in range(2):
            P.op("dve", lambda h, s=s: h.memset(vv[:, s, :, 128:129], 1.0), writes=[vvb[s]])
        NE = 5
        et, etb = AR.alloc("eA", [NE, 512], BF16, nbufs=NE)
        rec, (recb,) = AR.alloc("recA", [2, 4], F32)
        osb, osbb = AR.alloc("osbA", [2, 4, 129], F32, nbufs=2)
        d0, d0b = osb[:, 0, :, 0:128], osbb[0]
        d1, d1b = osb[:, 1, :, 0:128], osbb[1]
        ss, (ssb,) = AR.alloc("ssA", [4], F32)
        odn, odnb = AR.alloc("odnA", [2, 4, 128], BF16, nbufs=2)
        ectr = 0
        w2d = w_in_d[l]
        pending_tr = []

        def emit_tr(hd_, qt_, os__):
            bk = mmbank()
            psbf = ps[:, bk, :].bitcast(BF16)

            def fn(h):
                ins = None
                for qb in range(4):
                    ins = h.transpose(out=psbf[:, qb * 128:(qb + 1) * 128], in_=odn[:, os__, qb, :], identity=identb)
                return ins
            P.op("pe", fn, reads=[odnb[os__], identbb], writes=[psb[bk]])
            P.op("act", lambda h: h.activation(out=obT[:, hd_, qt_ * 512:(qt_ + 1) * 512], in_=psbf[:, 0:512], func=AF.Copy),
                 reads=[psb[bk]], writes=[ob[hd_][qt_]])

        for hd in range(int(os.environ.get('KDBG_HEADS', '8'))):
            s = hd % 2
            slot = next_slot()
            wload([wtile_cols(w2d, C_DQ + hd * 128, 128, slot, width=384, dcol0=0),
                   wtile_cols(w2d, C_DK + hd * 128, 128, slot, width=384, dcol0=128),
                   wtile_cols(w2d, C_DV + hd * 128, 128, slot, width=384, dcol0=256)], slot)
            wt = sview(slot, 8, 384)
            wb_ = slot[1]
            for tt in range(4):
                tsl = slice(tt * 512, (tt + 1) * 512)
                bk = mmbank()
                P.op("pe", mm_fn(ps[:, bk, :], [(wt[:, kc, 0:128], hT[:, kc, tsl]) for kc in range(8)]), reads=[wb_, hb[tt]], writes=[psb[bk]])
                P.op("act", lambda h, bk=bk, s=s, tsl=tsl: h.activation(out=qT[:, s, tsl], in_=ps[:, bk, :], func=AF.Copy, scale=0.125),
                     reads=[psb[bk]], writes=[qTb[s]] if tt == 0 else (), pwrites=() if tt == 0 else [qTb[s]])
                bk = mmbank()
                P.op("pe", mm_fn(ps[:, bk, :], [(wt[:, kc, 128:256], hT[:, kc, tsl]) for kc in range(8)]), reads=[wb_, hb[tt]], writes=[psb[bk]])
                P.op("dve", lambda h, bk=bk, s=s, tsl=tsl: h.tensor_copy(out=kT[0:64, s, 0, tsl], in_=ps[0:64, bk, :]),
                     reads=[psb[bk]], writes=[kTb[s]] if tt == 0 else (), pwrites=() if tt == 0 else [kTb[s]])
                P.op("dve", lambda h, bk=bk, s=s, tsl=tsl: h.tensor_copy(out=kT[64:128, s, 1, tsl], in_=ps[64:128, bk, :]),
                     reads=[psb[bk]], pwrites=[kTb[s]])
            for t4 in range(4):
                bk = mmbank()

                def fn(h, t4=t4, bk=bk, wt=wt):
                    ins = None
                    for j in range(4):
                        tb = t4 * 4 + j
                        for kc in range(8):
                            ins = h.matmul(ps[:, bk, j * 128:(j + 1) * 128], lhsT=hT[:, kc, tb * 128:(tb + 1) * 128], rhs=wt[:, kc, 256:384],
                                           start=(kc == 0), stop=(kc == 7))
                    return ins
                P.op("pe", fn, reads=[wb_, hb[t4]], writes=[psb[bk]])
                P.op("dve", lambda h, bk=bk, s=s, t4=t4: h.tensor_copy(out=vv[:, s, t4 * 4:(t4 + 1) * 4, 0:128], in_=ps[:, bk, :].rearrange("p (a b) -> p a b", a=4)),
                     reads=[psb[bk]], writes=[vvb[s]] if t4 == 0 else (), pwrites=() if t4 == 0 else [vvb[s]])
            accb = [psb[3], psb[4], psb[5], psb[6]]

            def emit_score(qt, kb, c, s=s):
                nonlocal ectr
                dstart = max(0, kb - 4 * qt)
                q0 = qt * 512 + dstart * 128
                nq = 512 - dstart * 128
                bk = mmbank()
                P.op("pe", mm_fn(ps[:, bk, 0:nq], [(kT[:, s, c, kb * 128:(kb + 1) * 128], qT[:, s, q0:q0 + nq])]),
                     reads=[kTb[s], qTb[s]], writes=[psb[bk]])
                e = ectr % NE
                ectr += 1
                P.op("act", lambda h, bk=bk, e=e, nq=nq: h.activation(out=et[:, e, 0:nq], in_=ps[:, bk, 0:nq], func=AF.Exp),
                     reads=[psb[bk]], writes=[etb[e]])
                if kb >= 4 * qt:
                    P.op("dve", lambda h, e=e: h.tensor_tensor(out=et[:, e, 0:128], in0=et[:, e, 0:128], in1=maskb, op=ALU.mult),
                         reads=[etb[e], maskbb], writes=[etb[e]])
                return (qt, kb, c, dstart, e)

            def emit_av(info, s=s):
                qt, kb, c, dstart, e = info

                def fn(h):
                    ins = None
                    for qb in range(dstart, 4):
                        bank = 3 + 2 * c + qb // 2
                        co = (qb % 2) * 129
                        ins = h.matmul(ps[:, bank, co:co + 129], lhsT=et[:, e, (qb - dstart) * 128:(qb - dstart + 1) * 128], rhs=vv[:, s, kb, :],
                                       start=(kb == 0 and qb % 2 == 0), stop=(kb == 4 * qt + qb), skip_group_check=True)
                    return ins
                abufs = [accb[2 * c], accb[2 * c + 1]]
                P.op("pe", fn, reads=[etb[e], vvb[s]], writes=abufs if kb == 0 else (), pwrites=() if kb == 0 else abufs)

            def post(qt, hd=hd):
                if pending_tr:
                    emit_tr(*pending_tr.pop(0))
                P.op("act", lambda h: h.activation(out=osb[:, 0].rearrange("p (a b) c -> p a (b c)", a=2), in_=ps[:, 3:5, 0:258], func=AF.Copy), reads=[psb[3], psb[4]], writes=[osbb[0]])
                P.op("dve", lambda h: h.tensor_copy(out=osb[:, 1].rearrange("p (a b) c -> p a (b c)", a=2), in_=ps[:, 5:7, 0:258]), reads=[psb[5], psb[6]], writes=[osbb[1]])
                P.op("dve", lambda h: h.reciprocal(out=rec[:, 0, :], in_=osb[:, 0, :, 128]), reads=[osbb[0]], writes=[recb])
                P.op("dve", lambda h: h.reciprocal(out=rec[:, 1, :], in_=osb[:, 1, :, 128]), reads=[osbb[1]], pwrites=[recb])
                P.op("dve", lambda h: h.tensor_scalar(out=rec[:, 1, :], in0=rec[:, 1, :], scalar1=lamneg[:, 0:1], scalar2=None, op0=ALU.mult),
                     reads=[recb, lamnegb], writes=[recb])
                P.op("dve", lambda h: h.tensor_tensor(out=d0, in0=d0, in1=rec[:, 0, :].unsqueeze(2).to_broadcast([128, 4, 128]), op=ALU.mult),
                     reads=[d0b, recb], writes=[d0b])
                P.op("dve", lambda h: h.tensor_tensor(out=d1, in0=d1, in1=rec[:, 1, :].unsqueeze(2).to_broadcast([128, 4, 128]), op=ALU.mult),
                     reads=[d1b, recb], writes=[d1b])
                P.op("dve", lambda h: h.tensor_tensor(out=d0, in0=d0, in1=d1, op=ALU.add), reads=[d0b, d1b], writes=[d0b])
                P.op("dve", lambda h: h.tensor_tensor(out=d1, in0=d0, in1=d0, op=ALU.mult), reads=[d0b], writes=[d1b])
                P.op("dve", lambda h: h.reduce_sum(out=ss, in_=d1, axis=AX.X), reads=[d1b], writes=[ssb])
                P.op("act", lambda h: h.activation(out=ss, in_=ss, func=AF.Sqrt, scale=1.0 / 128, bias=epst[:, 1:2]), reads=[ssb, epsb], writes=[ssb])
                P.op("dve", lambda h: h.reciprocal(out=ss, in_=ss), reads=[ssb], writes=[ssb])
                P.op("dve", lambda h: h.tensor_tensor(out=d0, in0=d0, in1=ss.unsqueeze(2).to_broadcast([128, 4, 128]), op=ALU.mult), reads=[d0b, ssb], writes=[d0b])
                os_ = qt % 2
                P.op("dve", lambda h, os_=os_: h.tensor_tensor(out=odn[:, os_], in0=d0, in1=subln_s.unsqueeze(1).to_broadcast([128, 4, 128]), op=ALU.mult),
                     reads=[d0b, sublnb], writes=[odnb[os_]])
                pending_tr.append((hd, qt, os_))

            LAG = 2
            pend = []
            for qt in range(4):
                for kb in range(4 * qt + 4):
                    for c in range(2):
                        pend.append(emit_score(qt, kb, c))
                        if len(pend) > LAG:
                            x = pend.pop(0)
                            emit_av(x)
                            if x[1] == 4 * x[0] + 3 and x[2] == 1:
                                post(x[0])
            while pend:
                x = pend.pop(0)
                emit_av(x)
                if x[1] == 4 * x[0] + 3 and x[2] == 1:
                    post(x[0])
        while pending_tr:
            emit_tr(*pending_tr.pop(0))
        AR.release(mk)

    def merge(l, b):
        mk = AR.mark()
        m, _ = AR.alloc("m", [8, S], BF16)
        mb = [[Buf(f"m{kc}_{tt}") for tt in range(4)] for kc in range(8)]
        tmp_ap, tmpb = m, None
        for item in AR.live[-1:]:
            for kc in range(8):
                for tt in range(4):
                    mb[kc][tt].r = dict(item[2][0].r)
            item[2].extend([mb[kc][tt] for kc in range(8) for tt in range(4)])
        gsb, gsbb = AR.alloc("gsb", [2, 512], F32, nbufs=2)
        gctr = 0
        for cg in range(2):
            sg = next_slot()
            wload([wtile_cols(w_in_d[l], C_G + b * 1024 + cg * 512, 512, sg)], sg)
            sb = next_slot()
            wload([wtile_cols(w_br_d[l, b], cg * 512, 512, sb)], sb)
            wg = sview(sg, 8, 512)
            wb = sview(sb, 8, 512)
            for dcl in range(4):
                dc = cg * 4 + dcl
                csl = slice(dcl * 128, (dcl + 1) * 128)
                for tt in range(4):
                    tsl = slice(tt * 512, (tt + 1) * 512)
                    bg = mmbank()
                    P.op("pe", mm_fn(ps[:, bg, :], [(wg[:, kc, csl], hT[:, kc, tsl]) for kc in range(8)]), reads=[sg[1], hb[tt]], writes=[psb[bg]])
                    g_ = gctr % 2
                    gctr += 1
                    P.op("act", lambda h, bg=bg, g_=g_: h.activation(out=gsb[:, g_], in_=ps[:, bg, :], func=AF.Tanh, scale=0.5), reads=[psb[bg]], writes=[gsbb[g_]])
                    bp = mmbank()
                    P.op("pe", mm_fn(ps[:, bp, :], [(wb[:, kc, csl], obT[:, kc, tsl]) for kc in range(8)]), reads=[sb[1]] + [ob[kc][tt] for kc in range(8)], writes=[psb[bp]])
                    P.op("dve", lambda h, bp=bp, g_=g_, dc=dc, tsl=tsl: h.scalar_tensor_tensor(out=m[:, dc, tsl], in0=gsb[:, g_], scalar=1.0, in1=ps[:, bp, :], op0=ALU.add, op1=ALU.mult),
                         reads=[psb[bp], gsbb[g_]], writes=[mb[dc][tt]])
        for cg in range(2):
            so = next_slot()
            wload([wtile_cols(w_out_d[l], cg * 512, 512, so)], so)
            wo = sview(so, 8, 512)
            for dcl in range(4):
                dc = cg * 4 + dcl
                csl = slice(dcl * 128, (dcl + 1) * 128)
                for tt in range(4):
                    tsl = slice(tt * 512, (tt + 1) * 512)
                    bk = mmbank()
                    P.op("pe", mm_fn(ps[:, bk, :], [(wo[:, kc, csl], m[:, kc, tsl]) for kc in range(8)]), reads=[so[1]] + [mb[kc][tt] for kc in range(8)], writes=[psb[bk]])
                    P.op("dve", lambda h, bk=bk, dc=dc, tsl=tsl: h.scalar_tensor_tensor(out=xT[:, dc, tsl], in0=ps[:, bk, :], scalar=0.5, in1=xT[:, dc, tsl], op0=ALU.mult, op1=ALU.add),
                         reads=[psb[bk], xb[dc][tt]], writes=[xb[dc][tt]])
        AR.release(mk)

    def branch_B(l):
        mk = AR.mark()
        gam = [1.0 - 2.0 ** (-5.0 - h_) for h_ in range(4)]
        qkT, qkTb = AR.alloc("qkT", [3, S], BF16, nbufs=1)
        kw, (kwb,) = AR.alloc("kw", [16, 128], BF16)
        vr, (vrb,) = AR.alloc("vr", [16, 256], BF16)
        cst, cstb = AR.alloc("cst", [2, 256], F32, nbufs=2)
        gcs = [P.dma_group("cs0"), P.dma_group("cs1")]
        qkf, qkfb = AR.alloc("qkf", [2, 256], F32, nbufs=2)
        t1, t1b = AR.alloc("t1B", [2, 256], F32, nbufs=2)
        t2, t2b = AR.alloc("t2B", [2, 256], F32, nbufs=2)
        qk3, qk3b = AR.alloc("qk3", [2, 3, 128], BF16, nbufs=2)
        st, (stb,) = AR.alloc("st", [256], F32)
        stbf, stbfb = AR.alloc("stbf", [2, 256], BF16, nbufs=2)
        sTm, sTmb = AR.alloc("sTm", [2, 128], BF16, nbufs=2)
        sgt, sgtb = AR.alloc("sgt", [2, 256], F32, nbufs=2)
        bst, bstb = AR.alloc("bst", [2, 8], F32, nbufs=2)
        on_, onb = AR.alloc("onB", [2, 256], F32, nbufs=2)
        orb, orbb = AR.alloc("orB", [2, 256], BF16, nbufs=2)
        w2d = w_in_d[l]
        for hd in range(4):
            s1 = next_slot()
            wload([wtile_cols(w2d, C_RQ + hd * 128, 128, s1, width=512, dcol0=0),
                   wtile_cols(w2d, C_RK + hd * 128, 128, s1, width=512, dcol0=128),
                   wtile_cols(w2d, C_RV + hd * 256, 256, s1, width=512, dcol0=256)], s1)
            s2 = next_slot()
            wload([wtile_cols(w2d, C_RG + hd * 256, 256, s2)], s2)
            w1 = sview(s1, 8, 512)
            wg = sview(s2, 8, 256)
            for tb in range(16):
                s = tb % 2
                tt = tb // 4
                bsl = slice(tb * 128, (tb + 1) * 128)
                P.op("sp", lambda h, s=s, bsl=bsl: h.dma_start(out=cst[:, s], in_=cs_d[bsl, :]), writes=[cstb[s]], dma=gcs[s])
                bk = mmbank()
                P.op("pe", mm_fn(ps[:, bk, :], [(hT[:, kc, bsl], w1[:, kc, :]) for kc in range(8)]), reads=[s1[1], hb[tt]], writes=[psb[bk]])
                P.op("act", lambda h, bk=bk, s=s: h.activation(out=qkf[:, s], in_=ps[:, bk, 0:256], func=AF.Copy), reads=[psb[bk]], writes=[qkfb[s]])
                P.op("act", lambda h, bk=bk, tb=tb: h.activation(out=vr[:, tb, :], in_=ps[:, bk, 256:512], func=AF.Copy), reads=[psb[bk]],
                     writes=[vrb] if tb == 0 else (), pwrites=() if tb == 0 else [vrb])
                xq = qkf[:, s].rearrange("p (a b) -> p a b", a=2)
                cosb = cst[:, s, 0:128].unsqueeze(1).to_broadcast([128, 2, 128])
                P.op("dve", lambda h, s=s, xq=xq, cosb=cosb: h.tensor_tensor(out=t1[:, s].rearrange("p (a b) -> p a b", a=2), in0=xq, in1=cosb, op=ALU.mult),
                     reads=[qkfb[s], cstb[s]], writes=[t1b[s]])
                x4 = qkf[:, s].rearrange("p (a b c) -> p a b c", a=2, c=2)
                t24 = t2[:, s].rearrange("p (a b c) -> p a b c", a=2, c=2)
                sn4 = cst[:, s, 128:256].rearrange("p (b c) -> p b c", c=2)
                P.op("dve", lambda h, x4=x4, t24=t24, sn4=sn4: h.tensor_tensor(out=t24[:, :, :, 0], in0=x4[:, :, :, 1], in1=sn4[:, :, 0].unsqueeze(1).to_broadcast([128, 2, 64]), op=ALU.mult),
                     reads=[qkfb[s], cstb[s]], writes=[t2b[s]])
                P.op("dve", lambda h, x4=x4, t24=t24, sn4=sn4: h.tensor_tensor(out=t24[:, :, :, 1], in0=x4[:, :, :, 0], in1=sn4[:, :, 1].unsqueeze(1).to_broadcast([128, 2, 64]), op=ALU.mult),
                     reads=[qkfb[s], cstb[s]], pwrites=[t2b[s]])
                P.op("dve", lambda h, s=s: h.tensor_tensor(out=t1[:, s], in0=t1[:, s], in1=t2[:, s], op=ALU.add), reads=[t1b[s], t2b[s]], writes=[t1b[s]])
                P.op("act", lambda h, s=s: h.activation(out=qk3[:, s, 0, :], in_=t1[:, s, 0:128], func=AF.Copy), reads=[t1b[s]], writes=[qk3b[s]])
                P.op("act", lambda h, s=s, hd=hd: h.activation(out=qk3[:, s, 1, :], in_=t1[:, s, 0:128], func=AF.Identity, scale=retc[:, hd, 129:130]), reads=[t1b[s], retcb], pwrites=[qk3b[s]])
                P.op("act", lambda h, s=s: h.activation(out=qk3[:, s, 2, :], in_=t1[:, s, 128:256], func=AF.Copy), reads=[t1b[s]], pwrites=[qk3b[s]])
                P.op("dve", lambda h, s=s, hd=hd, tb=tb: h.tensor_scalar(out=kw[:, tb, :], in0=t1[:, s, 128:256], scalar1=retc[:, hd, 128:129], scalar2=None, op0=ALU.mult),
                     reads=[t1b[s], retcb], writes=[kwb] if tb == 0 else (), pwrites=() if tb == 0 else [kwb])
                bk = mmbank()
                psbf = ps[:, bk, :].bitcast(BF16)

                def fn(h, s=s, psbf=psbf):
                    ins = None
                    for j in range(3):
                        ins = h.transpose(out=psbf[:, j * 128:(j + 1) * 128], in_=qk3[:, s, j, :], identity=identb)
                    return ins
                P.op("pe", fn, reads=[qk3b[s], identbb], writes=[psb[bk]])
                P.op("dve", lambda h, psbf=psbf, bsl=bsl: h.tensor_copy(out=qkT[:, :, bsl], in_=psbf[:, 0:384].rearrange("p (a b) -> p a b", a=3)),
                     reads=[psb[bk]], writes=[qkTb[0]] if tb == 0 else (), pwrites=() if tb == 0 else [qkTb[0]])
            cd = gam[hd] ** 128
            for n in range(16):
                s = n % 2
                tt = n // 4
                bsl = slice(n * 128, (n + 1) * 128)
                bk = mmbank()
                P.op("pe", mm_fn(ps[:, bk, 0:128], [(qkT[:, 2, bsl], qkT[:, 0, bsl])]), reads=[qkTb[0]], writes=[psb[bk]])
                P.op("dve", lambda h, bk=bk, s=s, hd=hd: h.tensor_tensor(out=sTm[:, s], in0=ps[:, bk, 0:128], in1=retc[:, hd, 0:128], op=ALU.mult),
                     reads=[psb[bk], retcb], writes=[sTmb[s]])
                bo = mmbank()
                pairs = [(sTm[:, s], vr[:, n, :])]
                rd = [sTmb[s], vrb]
                if n > 0:
                    pairs.append((qkT[:, 1, bsl], stbf[:, (n - 1) % 2]))
                    rd += [qkTb[0], stbfb[(n - 1) % 2]]
                P.op("pe", mm_fn(ps[:, bo, 0:256], pairs), reads=rd, writes=[psb[bo]])
                bg = mmbank()
                P.op("pe", mm_fn(ps[:, bg, 0:256], [(hT[:, kc, bsl], wg[:, kc, :]) for kc in range(8)]), reads=[s2[1], hb[tt]], writes=[psb[bg]])
                P.op("act", lambda h, bg=bg, s=s: h.activation(out=sgt[:, s], in_=ps[:, bg, 0:256], func=AF.Tanh, scale=0.5), reads=[psb[bg]], writes=[sgtb[s]])
                P.op("dve", lambda h, bg=bg, s=s: h.scalar_tensor_tensor(out=sgt[:, s], in0=sgt[:, s], scalar=1.0, in1=ps[:, bg, 0:256], op0=ALU.add, op1=ALU.mult), reads=[psb[bg], sgtb[s]], writes=[sgtb[s]])
                if n < 15:
                    bkv = mmbank()
                    P.op("pe", mm_fn(ps[:, bkv, 0:256], [(kw[:, n, :], vr[:, n, :])]), reads=[kwb, vrb], writes=[psb[bkv]])
                    if n == 0:
                        P.op("dve", lambda h, bkv=bkv: h.tensor_copy(out=st, in_=ps[:, bkv, 0:256]), reads=[psb[bkv]], writes=[stb])
                    else:
                        P.op("dve", lambda h, bkv=bkv, cd=cd: h.scalar_tensor_tensor(out=st, in0=st, scalar=cd, in1=ps[:, bkv, 0:256], op0=ALU.mult, op1=ALU.add),
                             reads=[psb[bkv], stb], writes=[stb])
                    P.op("act", lambda h, s=s: h.activation(out=stbf[:, s], in_=st, func=AF.Copy), reads=[stb], writes=[stbfb[s]])
                P.op("dve", lambda h, bo=bo, s=s: h.bn_stats(out=bst[:, s, 0:6], in_=ps[:, bo, 0:256]), reads=[psb[bo]], writes=[bstb[s]])
                P.op("dve", lambda h, s=s: h.bn_aggr(out=bst[:, s, 6:8], in_=bst[:, s, 0:6]), reads=[bstb[s]], writes=[bstb[s]])
                P.op("act", lambda h, s=s: h.activation(out=bst[:, s, 7:8], in_=bst[:, s, 7:8], func=AF.Sqrt, scale=4.0, bias=epst[:, 4:5]), reads=[bstb[s], epsb], writes=[bstb[s]])
                P.op("dve", lambda h, s=s: h.reciprocal(out=bst[:, s, 7:8], in_=bst[:, s, 7:8]), reads=[bstb[s]], writes=[bstb[s]])
                P.op("dve", lambda h, bo=bo, s=s: h.tensor_scalar(out=on_[:, s], in0=ps[:, bo, 0:256], scalar1=bst[:, s, 6:7], scalar2=bst[:, s, 7:8], op0=ALU.subtract, op1=ALU.mult),
                     reads=[psb[bo], bstb[s]], writes=[onb[s]])
                P.op("dve", lambda h, s=s: h.tensor_tensor(out=orb[:, s], in0=on_[:, s], in1=sgt[:, s], op=ALU.mult), reads=[onb[s], sgtb[s]], writes=[orbb[s]])
                bk = mmbank()
                psbf = ps[:, bk, :].bitcast(BF16)

                def fn(h, s=s, psbf=psbf):
                    ins = None
                    for j in range(2):
                        ins = h.transpose(out=psbf[:, j * 128:(j + 1) * 128], in_=orb[:, s, j * 128:(j + 1) * 128], identity=identb)
                    return ins
                P.op("pe", fn, reads=[orbb[s], identbb], writes=[psb[bk]])
                P.op("act", lambda h, psbf=psbf, hd=hd, bsl=bsl: h.activation(out=obT[:, 2 * hd:2 * hd + 2, bsl], in_=psbf[:, 0:256].rearrange("p (a b) -> p a b", a=2), func=AF.Copy),
                     reads=[psb[bk]], pwrites=[ob[2 * hd][tt], ob[2 * hd + 1][tt]])
        AR.release(mk)

    def branch_C(l):
        mk = AR.mark()
        z, (zb,) = AR.alloc("zC", [8], F32)
        z2, (z2b,) = AR.alloc("z2C", [8], F32)
        lamv = vecs[:, l, 56:64]
        P.op("dve", lambda h: h.tensor_scalar(out=z, in0=lamv, scalar1=-1.0, scalar2=None, op0=ALU.mult), reads=[vecsb], writes=[zb])
        P.op("dve", lambda h: h.tensor_tensor(out=z2, in0=z, in1=lamv, op=ALU.max), reads=[zb, vecsb], writes=[z2b])
        P.op("act", lambda h: h.activation(out=z2, in_=z2, func=AF.Exp, scale=-1.0), reads=[z2b], writes=[z2b])
        P.op("act", lambda h: h.activation(out=z2, in_=z2, func=AF.Ln, bias=epst[:, 2:3]), reads=[z2b, epsb], writes=[z2b])
        P.op("dve", lambda h: h.tensor_scalar(out=z, in0=z, scalar1=0.0, scalar2=None, op0=ALU.max), reads=[zb], writes=[zb])
        P.op("dve", lambda h: h.tensor_tensor(out=z, in0=z, in1=z2, op=ALU.add), reads=[zb, z2b], writes=[zb])
        P.op("dve", lambda h: h.tensor_scalar(out=lru_sc, in0=z, scalar1=-4.0, scalar2=None, op0=ALU.mult), reads=[zb], writes=[lruscb])
        hbias, (hbiasb,) = AR.alloc("hbias", [16], F32)
        P.op("dve", lambda h: h.tensor_scalar(out=hbias, in0=vecs[:, l, 40:56], scalar1=0.5, scalar2=None, op0=ALU.mult), reads=[vecsb], writes=[hbiasb])
        wax, (waxb,) = AR.alloc("wax", [2, 8, 128], BF16)
        gwa = P.dma_group("wax")
        P.op("pool", lambda h: [h.dma_start(out=wax[:, 0], in_=wa_d[l].rearrange("n c d -> c n d")),
                                h.dma_start(out=wax[:, 1], in_=wx_d[l].rearrange("n c d -> c n d"))], writes=[waxb], dma=gwa, ndma=2)
        lxs, (lxsb,) = AR.alloc("lxs", [3 + S], F32)
        lxtb = [Buf(f"lxs{tt}") for tt in range(4)]
        for item in AR.live[-1:]:
            for tt in range(4):
                lxtb[tt].r = dict(item[2][0].r)
            item[2].extend(lxtb)
        NB = 2
        ysb, ysbb = AR.alloc("ysb", [NB, 512], F32, nbufs=NB)
        ty_, tyb_ = AR.alloc("ty", [1, 512], F32, nbufs=1); ty = [ty_[:, 0], ty_[:, 0]]; tyb = [tyb_[0], tyb_[0]]
        sgm, sgmb = AR.alloc("sgm", [NB, 512], F32, nbufs=NB)
        xc, xcb = AR.alloc("xc", [NB, 512], F32, nbufs=NB)
        xcbf, xcbfb = AR.alloc("xcbf", [NB, 512], BF16, nbufs=NB)
        ra, rab = AR.alloc("ra", [NB, 512], F32, nbufs=NB)
        ii, iib = AR.alloc("ii", [NB, 512], F32, nbufs=NB)
        a2_, a2b_ = AR.alloc("a2", [1, 512], F32, nbufs=1); a2 = [a2_[:, 0], a2_[:, 0]]; a2b = [a2b_[0], a2b_[0]]
        hh, hhb = AR.alloc("hh", [2, 512], F32, nbufs=2)
        w2d = w_in_d[l]
        it = 0
        for c in range(8):
            slot = next_slot()
            wload([wtile_cols(w2d, C_LX + c * 128, 128, slot, width=256, dcol0=0),
                   wtile_cols(w2d, C_LY + c * 128, 128, slot, width=256, dcol0=128)], slot)
            wt = sview(slot, 8, 256)
            P.op("dve", lambda h: h.memset(lxs[:, 0:3], 0.0), pwrites=[lxtb[0]])
            for tt in range(4):
                s = it % NB
                hs = it % 2
                it += 1
                tsl = slice(tt * 512, (tt + 1) * 512)
                bk = mmbank()
                P.op("pe", mm_fn(ps[:, bk, :], [(wt[:, kc, 0:128], hT[:, kc, tsl]) for kc in range(8)]), reads=[slot[1], hb[tt]], writes=[psb[bk]])
                P.op("act", lambda h, bk=bk, tt=tt: h.activation(out=lxs[:, 3 + tt * 512:3 + (tt + 1) * 512], in_=ps[:, bk, :], func=AF.Copy),
                     reads=[psb[bk]], writes=[lxtb[tt]] if tt > 0 else (), pwrites=[lxtb[0]] if tt == 0 else ())
                bk = mmbank()
                P.op("pe", mm_fn(ps[:, bk, :], [(wt[:, kc, 128:256], hT[:, kc, tsl]) for kc in range(8)]), reads=[slot[1], hb[tt]], writes=[psb[bk]])
                P.op("act", lambda h, bk=bk, s=s: h.activation(out=ysb[:, s], in_=ps[:, bk, :], func=AF.Copy), reads=[psb[bk]], writes=[ysbb[s]])
                P.op("dve", lambda h, s=s: h.tensor_tensor(out=ty[s], in0=ysb[:, s], in1=ysb[:, s], op=ALU.mult), reads=[ysbb[s]], writes=[tyb[s]])
                P.op("dve", lambda h, s=s: h.tensor_scalar(out=ty[s], in0=ty[s], scalar1=0.044715, scalar2=1.0, op0=ALU.mult, op1=ALU.add), reads=[tyb[s]], writes=[tyb[s]])
                P.op("dve", lambda h, s=s: h.tensor_tensor(out=ty[s], in0=ty[s], in1=ysb[:, s], op=ALU.mult), reads=[tyb[s], ysbb[s]], writes=[tyb[s]])
                P.op("act", lambda h, s=s: h.activation(out=sgm[:, s], in_=ty[s], func=AF.Tanh, scale=0.7978845608028654), reads=[tyb[s]], writes=[sgmb[s]])
                P.op("dve", lambda h, s=s: h.scalar_tensor_tensor(out=sgm[:, s], in0=sgm[:, s], scalar=1.0, in1=ysb[:, s], op0=ALU.add, op1=ALU.mult), reads=[sgmb[s], ysbb[s]], writes=[sgmb[s]])
                lrd = [lxtb[tt]] + ([lxtb[tt - 1]] if tt > 0 else [])
                P.op("dve", lambda h, s=s, tt=tt, c=c: h.tensor_scalar(out=xc[:, s], in0=lxs[:, tt * 512:tt * 512 + 512], scalar1=vecs[:, l, 64 + c:65 + c], scalar2=vecs[:, l, 32 + c:33 + c],
                                                                      op0=ALU.mult, op1=ALU.add), reads=lrd + [vecsb], writes=[xcb[s]])
                for j in range(1, 4):
                    P.op("dve", lambda h, s=s, tt=tt, c=c, j=j: h.scalar_tensor_tensor(out=xc[:, s], in0=lxs[:, tt * 512 + j:tt * 512 + j + 512], scalar=vecs[:, l, 64 + j * 8 + c:65 + j * 8 + c],
                                                                                      in1=xc[:, s], op0=ALU.mult, op1=ALU.add), reads=lrd + [vecsb, xcb[s]], writes=[xcb[s]])
                P.op("act", lambda h, s=s: h.activation(out=xcbf[:, s], in_=xc[:, s], func=AF.Copy), reads=[xcb[s]], writes=[xcbfb[s]])
                bk = mmbank()
                P.op("pe", mm_fn(ps[:, bk, :], [(wax[:, 0, c, :], xcbf[:, s])]), reads=[waxb, xcbfb[s]], writes=[psb[bk]])
                P.op("act", lambda h, bk=bk, s=s, c=c: h.activation(out=ra[:, s], in_=ps[:, bk, :], func=AF.Tanh, scale=0.5, bias=hbias[:, c:c + 1]), reads=[psb[bk], hbiasb], writes=[rab[s]])
                bk = mmbank()
                P.op("pe", mm_fn(ps[:, bk, :], [(wax[:, 1, c, :], xcbf[:, s])]), reads=[waxb, xcbfb[s]], writes=[psb[bk]])
                P.op("act", lambda h, bk=bk, s=s, c=c: h.activation(out=ii[:, s], in_=ps[:, bk, :], func=AF.Tanh, scale=0.5, bias=hbias[:, 8 + c:9 + c]), reads=[psb[bk], hbiasb], writes=[iib[s]])
                P.op("act", lambda h, s=s, c=c: h.activation(out=ra[:, s], in_=ra[:, s], func=AF.Exp, scale=lru_sc[:, c:c + 1], bias=lru_sc[:, c:c + 1]), reads=[rab[s], lruscb], writes=[rab[s]])
                P.op("dve", lambda h, s=s: h.tensor_tensor(out=a2[s], in0=ra[:, s], in1=ra[:, s], op=ALU.mult), reads=[rab[s]], writes=[a2b[s]])
                P.op("act", lambda h, s=s: h.activation(out=a2[s], in_=a2[s], func=AF.Sqrt, scale=-0.25, bias=epst[:, 3:4]), reads=[a2b[s], epsb], writes=[a2b[s]])
                P.op("dve", lambda h, s=s: h.scalar_tensor_tensor(out=ii[:, s], in0=ii[:, s], scalar=1.0, in1=xc[:, s], op0=ALU.add, op1=ALU.mult), reads=[iib[s], xcb[s]], writes=[iib[s]])
                P.op("dve", lambda h, s=s: h.tensor_tensor(out=ii[:, s], in0=ii[:, s], in1=a2[s], op=ALU.mult), reads=[iib[s], a2b[s]], writes=[iib[s]])
                if tt == 0:
                    P.op("dve", lambda h, s=s, hs=hs: h.tensor_tensor_scan(out=hh[:, hs], data0=ra[:, s], data1=ii[:, s], initial=0.0, op0=ALU.mult, op1=ALU.add),
                         reads=[rab[s], iib[s]], writes=[hhb[hs]])
                else:
                    P.op("dve", lambda h, s=s, hs=hs: h.tensor_tensor_scan(out=hh[:, hs], data0=ra[:, s], data1=ii[:, s], initial=hh[:, 1 - hs, 511:512], op0=ALU.mult, op1=ALU.add),
                         reads=[rab[s], iib[s], hhb[1 - hs]], writes=[hhb[hs]])
                P.op("dve", lambda h, s=s, hs=hs, c=c, tsl=tsl: h.scalar_tensor_tensor(out=obT[:, c, tsl], in0=hh[:, hs], scalar=0.5, in1=sgm[:, s], op0=ALU.mult, op1=ALU.mult),
                     reads=[hhb[hs], sgmb[s]], writes=[ob[c][tt]])
        AR.release(mk)

    def xattn(l):
        norm_stage(l, 8)
        mk = AR.mark()
        memt, (memtb,) = AR.alloc("memt", [2, D], F32)
        gm = P.dma_group("mem")
        P.op("sp", lambda h: h.dma_start(out=memt, in_=mem_d.rearrange("(a p) d -> p a d", p=128)), writes=[memtb], dma=gm)
        memn, (memnb,) = AR.alloc("memn", [2, D], BF16)
        msq, (msqb,) = AR.alloc("msq", [D], F32)
        mss, (mssb,) = AR.alloc("mss", [2], F32)
        for a in range(2):
            P.op("act", lambda h, a=a: h.activation(out=msq, in_=memt[:, a, :], func=AF.Square, accum_out=mss[:, a:a + 1]), reads=[memtb], writes=[msqb], pwrites=[mssb])
        P.op("act", lambda h: h.activation(out=mss, in_=mss, func=AF.Sqrt, scale=1.0 / D, bias=epst[:, 0:1]), reads=[mssb, epsb], writes=[mssb])
        P.op("dve", lambda h: h.reciprocal(out=mss, in_=mss), reads=[mssb], writes=[mssb])
        for a in range(2):
            P.op("dve", lambda h, a=a: h.tensor_scalar(out=memn[:, a, :], in0=memt[:, a, :], scalar1=mss[:, a:a + 1], scalar2=None, op0=ALU.mult),
                 reads=[memtb, mssb], writes=[memnb] if a == 0 else (), pwrites=() if a == 0 else [memnb])
        memT, (memTb,) = AR.alloc("memT", [8, 256], BF16)
        for a in range(2):
            for half in range(2):
                bk = mmbank()
                psbf = ps[:, bk, :].bitcast(BF16)

                def fn(h, a=a, half=half, psbf=psbf):
                    ins = None
                    for j in range(4):
                        kc = half * 4 + j
                        ins = h.transpose(out=psbf[:, j * 128:(j + 1) * 128], in_=memn[:, a, kc * 128:(kc + 1) * 128], identity=identb)
                    return ins
                P.op("pe", fn, reads=[memnb, identbb], writes=[psb[bk]])
                for j in range(4):
                    kc = half * 4 + j
                    P.op("dve", lambda h, psbf=psbf, j=j, kc=kc, a=a: h.tensor_scalar(out=memT[:, kc, a * 128:(a + 1) * 128], in0=psbf[:, j * 128:(j + 1) * 128],
                                                                                   scalar1=vecs[:, l, 16 + kc:17 + kc], scalar2=None, op0=ALU.mult),
                         reads=[psb[bk], vecsb], pwrites=[memTb])
        kxT, (kxTb,) = AR.alloc("kxT", [4, 256], BF16)
        vx, (vxb,) = AR.alloc("vx", [2, 4, 129], BF16)
        P.op("dve", lambda h: h.memset(vx[:, :, :, 128:129], 1.0), writes=[vxb])
        sk = next_slot()
        wload([wtile_cols(wkv_d[l], 0, 512, sk)], sk)
        wk = sview(sk, 8, 512)
        for hd in range(4):
            bk = mmbank()
            P.op("pe", mm_fn(ps[:, bk, 0:256], [(wk[:, kc, hd * 128:(hd + 1) * 128], memT[:, kc, :]) for kc in range(8)]), reads=[sk[1], memTb], writes=[psb[bk]])
            P.op("act", lambda h, bk=bk, hd=hd: h.activation(out=kxT[:, hd, :], in_=ps[:, bk, 0:256], func=AF.Copy), reads=[psb[bk]], pwrites=[kxTb])
        sv = next_slot()
        wload([wtile_cols(wkv_d[l], 512, 512, sv)], sv)
        wv = sview(sv, 8, 512)
        for a in range(2):
            bk = mmbank()
            P.op("pe", mm_fn(ps[:, bk, :], [(memT[:, kc, a * 128:(a + 1) * 128], wv[:, kc, :]) for kc in range(8)]), reads=[sv[1], memTb], writes=[psb[bk]])
            P.op("act", lambda h, bk=bk, a=a: h.activation(out=vx[:, a, :, 0:128], in_=ps[:, bk, :].rearrange("p (a b) -> p a b", a=4), func=AF.Copy), reads=[psb[bk]], pwrites=[vxb])
        sq_ = next_slot()
        wload([wtile_cols(wq_d[l], 0, 512, sq_)], sq_)
        wq = sview(sq_, 8, 512)
        qx, qxb = AR.alloc("qx", [2, S], BF16, nbufs=2)
        NE = 5
        et, etb = AR.alloc("eX", [NE, 512], BF16, nbufs=NE)
        rec, (recb,) = AR.alloc("recX", [4], F32)
        ox, oxb = AR.alloc("oxX", [2, 4, 128], BF16, nbufs=2)
        ectr = 0
        oxT = obT
        for hd in range(4):
            s = hd % 2
            for tt in range(4):
                tsl = slice(tt * 512, (tt + 1) * 512)
                bk = mmbank()
                P.op("pe", mm_fn(ps[:, bk, :], [(wq[:, kc, hd * 128:(hd + 1) * 128], hT[:, kc, tsl]) for kc in range(8)]), reads=[sq_[1], hb[tt]], writes=[psb[bk]])
                P.op("act", lambda h, bk=bk, s=s, tsl=tsl: h.activation(out=qx[:, s, tsl], in_=ps[:, bk, :], func=AF.Copy, scale=128 ** -0.5),
                     reads=[psb[bk]], writes=[qxb[s]] if tt == 0 else (), pwrites=() if tt == 0 else [qxb[s]])
            def x_score(tt, a, hd=hd, s=s):
                nonlocal ectr
                tsl = slice(tt * 512, (tt + 1) * 512)
                bk = mmbank()
                P.op("pe", mm_fn(ps[:, bk, :], [(kxT[:, hd, a * 128:(a + 1) * 128], qx[:, s, tsl])]), reads=[kxTb, qxb[s]], writes=[psb[bk]])
                e = ectr % NE
                ectr += 1
                P.op("act", lambda h, bk=bk, e=e: h.activation(out=et[:, e], in_=ps[:, bk, :], func=AF.Exp), reads=[psb[bk]], writes=[etb[e]])
                return (tt, a, e)

            def x_av(info, hd=hd):
                tt, a, e = info

                def fn(h):
                    ins = None
                    for qb in range(4):
                        bank = 3 + qb // 2
                        co = (qb % 2) * 129
                        ins = h.matmul(ps[:, bank, co:co + 129], lhsT=et[:, e, qb * 128:(qb + 1) * 128], rhs=vx[:, a, hd, :],
                                       start=(a == 0 and qb % 2 == 0), stop=(a == 1), skip_group_check=True)
                    return ins
                P.op("pe", fn, reads=[etb[e], vxb], writes=[psb[3], psb[4]] if a == 0 else (), pwrites=() if a == 0 else [psb[3], psb[4]])

            def x_post(tt, hd=hd):
                tsl = slice(tt * 512, (tt + 1) * 512)
                acc0 = ps[:, 3:5, 0:258].rearrange("p a (b c) -> p a b c", b=2)
                recv = rec.rearrange("p (a b) -> p a b", a=2)
                P.op("dve", lambda h, acc0=acc0, recv=recv: h.reciprocal(out=recv, in_=acc0[:, :, :, 128]), reads=[psb[3], psb[4]], writes=[recb])
                os_ = tt % 2
                P.op("dve", lambda h, acc0=acc0, recv=recv, os_=os_: h.tensor_tensor(out=ox[:, os_].rearrange("p (a b) c -> p a b c", a=2), in0=acc0[:, :, :, 0:128],
                                                                                  in1=recv.unsqueeze(3).to_broadcast([128, 2, 2, 128]), op=ALU.mult),
                     reads=[psb[3], psb[4], recb], writes=[oxb[os_]])
                bk = mmbank()
                psbf = ps[:, bk, :].bitcast(BF16)

                def fn(h, os_=os_, psbf=psbf):
                    ins = None
                    for qb in range(4):
                        ins = h.transpose(out=psbf[:, qb * 128:(qb + 1) * 128], in_=ox[:, os_, qb, :], identity=identb)
                    return ins
                P.op("pe", fn, reads=[oxb[os_], identbb], writes=[psb[bk]])
                P.op("act", lambda h, psbf=psbf, hd=hd, tsl=tsl: h.activation(out=oxT[:, hd, tsl], in_=psbf[:, 0:512], func=AF.Copy), reads=[psb[bk]], writes=[ob[hd][tt]])

            pend = []
            for tt in range(4):
                for a in range(2):
                    pend.append(x_score(tt, a))
                    if len(pend) > 2:
                        x = pend.pop(0)
                        x_av(x)
                        if x[1] == 1:
                            x_post(x[0])
            while pend:
                x = pend.pop(0)
                x_av(x)
                if x[1] == 1:
                    x_post(x[0])
        so = next_slot()
        P.op("pool", lambda h: [h.dma_start(out=so[0].rearrange("p (k c) -> p k c", k=4), in_=wo_d[l].rearrange("(k p) c -> p k c", p=128))], writes=[so[1]], dma=so[2], ndma=1)
        wo = so[0].rearrange("p (k c) -> p k c", k=4)
        for dc in range(8):
            for tt in range(4):
                tsl = slice(tt * 512, (tt + 1) * 512)
                bk = mmbank()
                P.op("pe", mm_fn(ps[:, bk, :], [(wo[:, hd, dc * 128:(dc + 1) * 128], oxT[:, hd, tsl]) for hd in range(4)]), reads=[so[1]] + [ob[hd][tt] for hd in range(4)], writes=[psb[bk]])
                P.op("dve", lambda h, bk=bk, dc=dc, tsl=tsl: h.tensor_tensor(out=xT[:, dc, tsl], in0=xT[:, dc, tsl], in1=ps[:, bk, :], op=ALU.add),
                     reads=[psb[bk], xb[dc][tt]], writes=[xb[dc][tt]])
        AR.release(mk)

    def mlp(l):
        norm_stage(l, 24)
        mk = AR.mark()
        rl, rlb = AR.alloc("rl", [2, 512], F32, nbufs=2)
        hid = obT.rearrange("p (a b) s -> p a b s", a=2)
        hidb = [[Buf(f"hid{a}_{tt}") for tt in range(4)] for a in range(2)]
        for a in range(2):
            for tt in range(4):
                for kc in range(4):
                    for k_, d_ in ob[a * 4 + kc][tt].w.items():
                        hidb[a][tt].r[("ow", kc, k_)] = d_
                    for k_, d_ in ob[a * 4 + kc][tt].r.items():
                        hidb[a][tt].r[("or", kc, k_)] = d_
        rctr = 0
        for fg in range(8):
            a = fg % 2
            s1 = next_slot()
            wload([wtile_cols(w1_d[l], fg * 512, 512, s1)], s1)
            w1 = sview(s1, 8, 512)
            s2 = next_slot()
            P.op("pool", lambda h, s2=s2, fg=fg: [h.dma_start(out=s2[0].rearrange("p (k c) -> p k c", k=4), in_=w2_d[l, fg * 512:(fg + 1) * 512, :].rearrange("(k p) c -> p k c", p=128))],
                 writes=[s2[1]], dma=s2[2], ndma=1)
            w2 = s2[0].rearrange("p (k c) -> p k c", k=4)
            for fc in range(4):
                for tt in range(4):
                    tsl = slice(tt * 512, (tt + 1) * 512)
                    bk = mmbank()
                    P.op("pe", mm_fn(ps[:, bk, :], [(w1[:, kc, fc * 128:(fc + 1) * 128], hT[:, kc, tsl]) for kc in range(8)]), reads=[s1[1], hb[tt]], writes=[psb[bk]])
                    r_ = rctr % 2
                    rctr += 1
                    P.op("act", lambda h, bk=bk, r_=r_: h.activation(out=rl[:, r_], in_=ps[:, bk, :], func=AF.Relu), reads=[psb[bk]], writes=[rlb[r_]])
                    P.op("dve", lambda h, r_=r_, a=a, fc=fc, tsl=tsl: h.tensor_tensor(out=hid[:, a, fc, tsl], in0=rl[:, r_], in1=rl[:, r_], op=ALU.mult),
                         reads=[rlb[r_]], writes=[hidb[a][tt]] if fc == 0 else (), pwrites=() if fc == 0 else [hidb[a][tt]])
            for dc in range(8):
                for tt in range(4):
                    tsl = slice(tt * 512, (tt + 1) * 512)
                    bk = mmbank()
                    P.op("pe", mm_fn(ps[:, bk, :], [(w2[:, fc, dc * 128:(dc + 1) * 128], hid[:, a, fc, tsl]) for fc in range(4)]), reads=[s2[1], hidb[a][tt]], writes=[psb[bk]])
                    P.op("dve", lambda h, bk=bk, dc=dc, tsl=tsl: h.tensor_tensor(out=xT[:, dc, tsl], in0=xT[:, dc, tsl], in1=ps[:, bk, :], op=ALU.add),
                         reads=[psb[bk], xb[dc][tt]], writes=[xb[dc][tt]])
        for a in range(2):
            for tt in range(4):
                for kc in range(4):
                    b_ = ob[a * 4 + kc][tt]
                    for k_, d_ in hidb[a][tt].w.items():
                        b_.r[("hw", k_)] = d_
                    for k_, d_ in hidb[a][tt].r.items():
                        b_.r[("hr", k_)] = d_
        AR.release(mk)

    def final():
        mk = AR.mark()
        sq, sqb = AR.alloc("sqF", [2, 8, 512], BF16, nbufs=2)
        rt, rtb = AR.alloc("rtF", [2, 512], F32, nbufs=2)
        yf = hT.rearrange("p a s -> p (a s)").bitcast(F32).rearrange("p (a b c) -> p a b c", a=2, b=8)
        yfb = [Buf("yf0"), Buf("yf1")]
        for yb_ in yfb:
            for tt_ in range(4):
                for k_, d_ in hb[tt_].w.items():
                    yb_.r[("hw", tt_, k_)] = d_
                for k_, d_ in hb[tt_].r.items():
                    yb_.r[("hr", tt_, k_)] = d_
        osb, osbb = AR.alloc("osb", [2, D], F32, nbufs=2)
        go = [P.dma_group("out0"), P.dma_group("out1")]
        outbufs = [Buf("outd0"), Buf("outd1")]
        octr = 0
        for tt in range(4):
            s = tt % 2
            tsl = slice(tt * 512, (tt + 1) * 512)
            P.op("act", lambda h, s=s, tsl=tsl: h.activation(out=sq[:, s], in_=xT[:, :, tsl], func=AF.Square), reads=[xb[kc][tt] for kc in range(8)], writes=[sqb[s]])
            P.op("pe", mm_fn(ps[:, 7, :], [(onesb, sq[:, s, kc, :]) for kc in range(8)]), reads=[sqb[s], onesbb], writes=[psb[7]])
            P.op("act", lambda h, s=s: h.activation(out=rt[:, s], in_=ps[:, 7, :], func=AF.Sqrt, scale=1.0 / D, bias=epst[:, 0:1]), reads=[psb[7], epsb], writes=[rtb[s]])
            P.op("dve", lambda h, s=s: h.reciprocal(out=rt[:, s], in_=rt[:, s]), reads=[rtb[s]], writes=[rtb[s]])
            for kc in range(8):
                P.op("dve", lambda h, s=s, kc=kc, tsl=tsl: h.scalar_tensor_tensor(out=yf[:, s, kc, :], in0=xT[:, kc, tsl], scalar=vecs[:, 0, 96 + kc:97 + kc], in1=rt[:, s], op0=ALU.mult, op1=ALU.mult),
                     reads=[xb[kc][tt], rtb[s], vecsb], writes=[yfb[s]] if kc == 0 else (), pwrites=() if kc == 0 else [yfb[s]])
            for tbl in range(4):
                o_ = octr % 2
                octr += 1
                tb = tt * 4 + tbl
                for half in range(2):
                    bk = mmbank()

                    def fn(h, s=s, half=half, bk=bk, tbl=tbl):
                        ins = None
                        for j in range(4):
                            kc = half * 4 + j
                            ins = h.transpose(out=ps[:, bk, j * 128:(j + 1) * 128], in_=yf[:, s, kc, tbl * 128:(tbl + 1) * 128], identity=identf)
                        return ins
                    P.op("pe", fn, reads=[yfb[s], cstfb], writes=[psb[bk]])
                    if half == 0:
                        P.op("dve", lambda h, bk=bk, o_=o_: h.tensor_copy(out=osb[:, o_, 0:512], in_=ps[:, bk, :]), reads=[psb[bk]], writes=[osbb[o_]])
                    else:
                        P.op("act", lambda h, bk=bk, o_=o_: h.activation(out=osb[:, o_, 512:1024], in_=ps[:, bk, :], func=AF.Copy), reads=[psb[bk]], pwrites=[osbb[o_]])
                P.op("sp", lambda h, o_=o_, tb=tb: h.dma_start(out=out_d[tb * 128:(tb + 1) * 128, :], in_=osb[:, o_, :]), reads=[osbb[o_]], writes=[outbufs[o_]], dma=go[o_])
        P.op("sp", None, reads=outbufs)
        AR.release(mk)

    def dump():
        gd = P.dma_group("dbg")
        d1 = Buf("dbg1")
        P.op("sp", lambda h: h.dma_start(out=dbgx_d, in_=xT.rearrange("p a s -> p (a s)")), reads=all_x, writes=[d1], dma=gd)
        gd2 = P.dma_group("dbg2")
        d2 = Buf("dbg2")
        P.op("sp", lambda h: h.dma_start(out=dbgo_d, in_=obT.rearrange("p a s -> p (a s)")), reads=all_o, writes=[d2], dma=gd2)
        P.op("sp", None, reads=[d1, d2])

    done = False
    for l in range(n_layers):
        norm_stage(l, 0)
        if stop_after == ("norm", l):
            done = True
            break
        for bi, (nm, fn_) in enumerate((("A", branch_A), ("B", branch_B), ("C", branch_C))):
            fn_(l)
            if stop_after == (nm, l):
                done = True
                break
            merge(l, bi)
        if done:
            break
        if stop_after == ("mix", l):
            done = True
            break
        xattn(l)
        if stop_after == ("xattn", l):
            done = True
            break
        mlp(l)
        if stop_after == ("mlp", l):
            done = True
            break
    if dbg:
        dump()
    if not done:
        final()
    else:
        pass
    P.emit()
    nc._arena_peak = AR.peak
    return nc


def _host_consts():
    ident = np.eye(128, dtype=np.float32)
    k = np.arange(128)[:, None]
    q = np.arange(128)[None, :]
    mask = (k <= q).astype(np.float32)
    cst = np.concatenate([ident, mask], axis=1)
    log_g = np.log(1.0 - np.exp2(-5.0 - np.arange(4, dtype=np.float64)))
    idx = np.arange(128, dtype=np.float64)
    retc = np.zeros((128, 4, 130), np.float32)
    kscale = 128.0 ** -0.5
    for h in range(4):
        rel = idx[None, :] - idx[:, None]
        dec = np.where(rel >= 0, np.exp(np.maximum(rel, 0.0) * log_g[h]), 0.0) * kscale
        retc[:, h, 0:128] = dec
        retc[:, h, 128] = np.exp((127.0 - idx) * log_g[h]) * kscale
        retc[:, h, 129] = np.exp((idx + 1.0) * log_g[h])
    pos = np.arange(S, dtype=np.float32)
    angle = np.repeat((1.0 / (10000.0 ** np.linspace(0.0, 1.0, 64, dtype=np.float32))).astype(np.float32), 2)
    phase = (pos[:, None] * angle[None, :]).astype(np.float32)
    cos = np.cos(phase).astype(np.float32)
    sin = np.sin(phase).astype(np.float32)
    sgn = np.tile(np.array([-1.0, 1.0], np.float32), 64)
    cstab = np.concatenate([cos, sin * sgn[None, :]], axis=1).astype(np.float32)
    return cst, retc.reshape(128, 4 * 130), cstab


def _pack(inputs):
    f = lambda a: np.ascontiguousarray(np.asarray(a, dtype=np.float32))

    def fm(v):
        return f(v).reshape(-1, 8, 128).transpose(2, 0, 1)
    vecs = np.zeros((128, NL, NV), np.float32)
    vecs[:, :, 0:8] = fm(inputs["norm_mix"])
    vecs[:, :, 8:16] = fm(inputs["norm_xattn"])
    vecs[:, :, 16:24] = fm(inputs["norm_mem"])
    vecs[:, :, 24:32] = fm(inputs["norm_mlp"])
    vecs[:, :, 32:40] = fm(inputs["lru_conv_b"])
    vecs[:, :, 40:48] = fm(inputs["lru_ba"])
    vecs[:, :, 48:56] = fm(inputs["lru_bx"])
    vecs[:, :, 56:64] = fm(inputs["lru_lambda"])
    cw = f(inputs["lru_conv_w"]).reshape(NL, 4, 8, 128)
    vecs[:, :, 64:96] = cw.transpose(3, 0, 1, 2).reshape(128, NL, 32)
    vecs[:, :, 96:104] = np.broadcast_to(f(inputs["norm_final"]).reshape(8, 128).T[:, None, :], (128, NL, 8))
    rows = np.zeros((128, NL, NR), np.float32)
    r1 = np.concatenate([f(inputs["diff_lq1"]), f(inputs["diff_lk1"]), f(inputs["diff_lq2"]), f(inputs["diff_lk2"]), f(inputs["diff_subln"])], axis=1)
    rows[:] = r1[None, :, :]
    return vecs.reshape(128, NL * NV), rows.reshape(128, NL * NR)


_CACHE = {}


def kernel(**inputs):
    key = "full"
    if key not in _CACHE:
        _CACHE[key] = build()
    nc = _CACHE[key]
    vecs, rows = _pack(inputs)
    cst, retc, cstab = _host_consts()
    f = lambda a: np.ascontiguousarray(np.asarray(a, dtype=np.float32))
    shared = {
        "w_in": f(inputs["w_in"]), "w_branch": f(inputs["w_branch"]), "w_out": f(inputs["w_out"]),
        "xa_wq": f(inputs["xa_wq"]), "xa_wkv": f(inputs["xa_wkv"]), "xa_wo": f(inputs["xa_wo"]),
        "mlp_w1": f(inputs["mlp_w1"]), "mlp_w2": f(inputs["mlp_w2"]),
        "lru_wa": f(inputs["lru_wa"]), "lru_wx": f(inputs["lru_wx"]),
        "vecs": vecs, "rows": rows, "cst": cst, "retc": retc, "cstab": cstab,
    }
    x = f(inputs["x"])
    mem = f(inputs["mem"])
    in_maps = []
    for b in range(8):
        m = dict(shared)
        m["x"] = x[b]
        m["mem"] = mem[b]
        in_maps.append(m)
    res = run_bass_kernel_spmd(nc, in_maps, core_ids=list(range(8)))
    return np.stack([np.asarray(r["out"], dtype=np.float32) for r in res.results], axis=0)
```

```python
import math
import os
import numpy as np
import concourse.bass as bass
import concourse.mybir as mybir
from concourse.bass_utils import run_bass_kernel_spmd

F32 = mybir.dt.float32
BF16 = mybir.dt.bfloat16
AF = mybir.ActivationFunctionType
ALU = mybir.AluOpType
AX = mybir.AxisListType

ENGS = ("pe", "act", "dve", "pool", "sp")

S = 2048
D = 1024
NL = 4
NV = 104
NR = 384
IN_COLS = 11264
C_DQ, C_DK, C_DV, C_RQ, C_RK, C_RV, C_RG, C_LX, C_LY, C_G = 0, 1024, 2048, 3072, 3584, 4096, 5120, 6144, 7168, 8192


class Buf:
    __slots__ = ("name", "w", "r")

    def __init__(self, name):
        self.name = name
        self.w = {}
        self.r = {}


class DmaGroup:
    __slots__ = ("sem", "total")

    def __init__(self, sem):
        self.sem = sem
        self.total = 0


class Op:
    __slots__ = ("eng", "fn", "deps", "odeps", "signal", "sig_idx", "dma", "dma_val", "ndma", "seq", "cost", "fin", "done")

    def __init__(self, eng, fn):
        self.eng = eng
        self.fn = fn
        self.deps = []
        self.odeps = []
        self.cost = 0.0
        self.fin = 0.0
        self.done = False
        self.signal = False
        self.sig_idx = 0
        self.dma = None
        self.dma_val = 0
        self.ndma = 1


class _Dummy:
    def then_inc(self, *a, **k):
        return self


class _CostProxy:
    def reset(self, eng, is_dma):
        self.eng = eng
        self.is_dma = is_dma
        self.cost = 0.0
        self.tab = None

    def __getattr__(self, name):
        def f(*args, **kwargs):
            out = kwargs.get("out", None)
            if out is None and args:
                out = args[0]
            try:
                shp = out.shape
                n = 1
                for x_ in shp[1:]:
                    n *= int(x_)
            except Exception:
                n = 64
            if self.is_dma:
                self.cost += n * 128 * 4 / 250.0
            elif self.eng == "pe":
                mult = 1.0
                lhs = kwargs.get("lhsT", kwargs.get("in_", None))
                try:
                    if lhs is not None and lhs.dtype == F32:
                        mult = 4.0
                except Exception:
                    pass
                self.cost += 64.0 + mult * n / 2.4
            elif self.eng == "act":
                fn_ = kwargs.get("func", None)
                if fn_ == AF.Sqrt:
                    self.tab = "sqrt"
                elif fn_ == AF.Ln:
                    self.tab = "ln"
                elif fn_ == AF.Exp or fn_ == AF.Tanh:
                    self.tab = "exp"
                self.cost += 230.0 + n / 1.2 + (60.0 if kwargs.get("accum_out", None) is not None else 0.0)
            else:
                k_ = 2.0 if name == "tensor_tensor_scan" else 1.0
                self.cost += 260.0 + k_ * n / 0.96
            return _Dummy()
        return f


class Prog:
    def __init__(self, nc):
        self.nc = nc
        self.ops = {e: [] for e in ENGS}
        self.eng_sem = {}
        self.nsem = 0

    def new_sem(self, name):
        self.nsem += 1
        return self.nc.alloc_semaphore(f"s_{name}_{self.nsem}")

    def dma_group(self, name):
        return DmaGroup(self.new_sem(name))

    def op(self, eng, fn, reads=(), writes=(), pwrites=(), dma=None, ndma=1):
        o = Op(eng, fn)
        self.seq = getattr(self, "seq", 0) + 1
        o.seq = self.seq
        mykey = ("dma", id(dma)) if dma is not None else eng
        deps = {}
        for b in reads:
            for d in b.w.values():
                deps[id(d)] = d
        for b in writes:
            for d in b.w.values():
                deps[id(d)] = d
            for d in b.r.values():
                deps[id(d)] = d
        for b in pwrites:
            for d in b.r.values():
                deps[id(d)] = d
            for k, d in b.w.items():
                if k != mykey:
                    deps[id(d)] = d
                else:
                    o.odeps.append(d)
        for d in deps.values():
            if d is o:
                continue
            if d.dma is None:
                if d.eng == "pe" and eng == "pe" and dma is None:
                    o.odeps.append(d)
                    continue
                d.signal = True
            o.deps.append(d)
        if dma is not None:
            o.dma = dma
            o.ndma = ndma
            dma.total += 16 * ndma
            o.dma_val = dma.total
        for b in reads:
            prev = b.r.get(mykey)
            if prev is not None and prev is not o:
                o.odeps.append(prev)
            b.r[mykey] = o
        for b in writes:
            b.w = {mykey: o}
            b.r = {}
        for b in pwrites:
            b.w[mykey] = o
        self.ops[eng].append(o)
        return o

    def schedule(self):
        prox = _CostProxy()
        tabs = {}
        curtab = [None]
        for e in ENGS:
            for o in self.ops[e]:
                if o.fn is None:
                    o.cost = 0.0
                    continue
                prox.reset(e, o.dma is not None)
                o.fn(prox)
                o.cost = prox.cost
                tabs[id(o)] = prox.tab
        W = {"pe": 48, "act": 32, "dve": 32, "pool": 1, "sp": 1}
        rem = {e: list(self.ops[e]) for e in ENGS}
        head = {e: 0 for e in ENGS}
        tfree = {e: 0.0 for e in ENGS}
        neword = {e: [] for e in ENGS}
        total = sum(len(v) for v in rem.values())
        nsched = 0
        LAT = 120.0
        while nsched < total:
            best = None
            for e in ENGS:
                lst = rem[e]
                i = head[e]
                n = len(lst)
                while i < n and lst[i].done:
                    i += 1
                head[e] = i
                cnt = 0
                j = i
                cand = None
                while j < n and cnt < W[e]:
                    o = lst[j]
                    if not o.done:
                        cnt += 1
                        ok = True
                        r = 0.0
                        for d in o.deps:
                            if not d.done:
                                ok = False
                                break
                            if d.fin + LAT > r:
                                r = d.fin + LAT
                        if ok:
                            for d in o.odeps:
                                if not d.done:
                                    ok = False
                                    break
                        if ok:
                            st = r if r > tfree[e] else tfree[e]
                            if e == "act":
                                tb_ = tabs.get(id(o))
                                if tb_ is not None and tb_ != curtab[0]:
                                    st += 1300.0
                            if cand is None or st < cand[0]:
                                cand = (st, j, o)
                                if st <= tfree[e]:
                                    break
                    j += 1
                if cand is not None and (best is None or cand[0] < best[0]):
                    best = (cand[0], e, cand[2])
            assert best is not None, "scheduler stuck"
            st, e, o = best
            o.done = True
            if e == "act" and tabs.get(id(o)) is not None:
                curtab[0] = tabs[id(o)]
            if o.dma is not None:
                tfree[e] = st + 1000.0
                o.fin = st + 2000.0 + o.cost
            else:
                o.fin = st + o.cost
                tfree[e] = o.fin
            neword[e].append(o)
            nsched += 1
        self.ops = neword
        self.sim_time = max(tfree.values())

    def emit(self):
        nc = self.nc
        if os.environ.get("KSCHED", "1") == "1":
            self.schedule()
        for e in ENGS:
            c = 0
            for o in self.ops[e]:
                if o.signal:
                    c += 1
                    o.sig_idx = c
            if self.ops[e]:
                self.eng_sem[e] = self.new_sem("eng_" + e)
            if os.environ.get("KDBG_PRINT"):
                print("ENG", e, "ops", len(self.ops[e]), "signals", c, flush=True)

        def run(e, h):
            waited = {}
            for o in self.ops[e]:
                for d in o.deps:
                    if d.dma is not None:
                        sem, val = d.dma.sem, d.dma_val
                    else:
                        sem, val = self.eng_sem[d.eng], d.sig_idx
                    k = id(sem)
                    if waited.get(k, 0) >= val:
                        continue
                    h.wait_ge(sem, val)
                    waited[k] = val
                if o.fn is None:
                    continue
                ins = o.fn(h)
                if o.dma is not None:
                    if not isinstance(ins, (list, tuple)):
                        ins = [ins]
                    assert len(ins) == o.ndma
                    for i_ in ins:
                        i_.then_inc(o.dma.sem, 16)
                elif o.signal:
                    ins.then_inc(self.eng_sem[e], 1)

        with nc.Block() as block:
            @block.tensor
            def _(h):
                run("pe", h)

            @block.scalar
            def _(h):
                run("act", h)

            @block.vector
            def _(h):
                run("dve", h)

            @block.gpsimd
            def _(h):
                run("pool", h)

            @block.sync
            def _(h):
                run("sp", h)


class Arena:
    def __init__(self, nc, name, nbytes):
        self.nbytes = nbytes
        self.t = nc.alloc_sbuf_tensor(name, [128, nbytes // 4], F32)
        self.top = 0
        self.live = []
        self.dead = []
        self.peak = 0

    def alloc(self, name, shape_free, dtype, nbufs=1):
        esz = 2 if dtype == BF16 else 4
        n = int(np.prod(shape_free))
        nb = (n * esz + 63) // 64 * 64
        start = self.top
        end = start + nb
        assert end <= self.nbytes, f"arena overflow for {name}: {end} > {self.nbytes}"
        self.top = end
        self.peak = max(self.peak, end)
        ap = self.t[:, start // 4:(start + nb) // 4]
        if dtype != F32:
            ap = ap.bitcast(dtype)
        ap = ap[:, 0:n]
        if len(shape_free) == 2:
            ap = ap.rearrange("p (a b) -> p a b", a=shape_free[0])
        elif len(shape_free) == 3:
            ap = ap.rearrange("p (a b c) -> p a b c", a=shape_free[0], b=shape_free[1])
        bufs = [Buf(f"{name}{i}") for i in range(nbufs)]
        inh = {}
        keep = []
        for (s, e, obufs) in self.dead:
            if s < end and e > start:
                for ob in obufs:
                    for d in list(ob.w.values()) + list(ob.r.values()):
                        inh[("inh", id(d))] = d
                if s >= start and e <= end:
                    continue
            keep.append((s, e, obufs))
        self.dead = keep
        for b in bufs:
            b.r.update(inh)
        self.live.append((start, end, bufs))
        return ap, bufs

    def mark(self):
        return (self.top, len(self.live))

    def release(self, mark):
        top, nlive = mark
        for item in self.live[nlive:]:
            self.dead.append(item)
        self.live = self.live[:nlive]
        self.top = top


def mm_fn(out, pairs, start=True, skip=False):
    def fn(h):
        n = len(pairs)
        ins = None
        for i, (l, r) in enumerate(pairs):
            if skip:
                ins = h.matmul(out, lhsT=l, rhs=r, start=(start and i == 0), stop=(i == n - 1), skip_group_check=True)
            else:
                ins = h.matmul(out, lhsT=l, rhs=r, start=(start and i == 0), stop=(i == n - 1))
        return ins
    return fn


def build(n_layers=NL, stop_after=None, dbg=False):
    nc = bass.Bass("TRN2", target_bir_lowering=False)

    def dram(name, shape, dt=F32, kind="ExternalInput"):
        return nc.dram_tensor(name, shape, dt, kind=kind).ap()

    x_d = dram("x", [S, D])
    mem_d = dram("mem", [256, D])
    w_in_d = dram("w_in", [NL, D, IN_COLS])
    w_br_d = dram("w_branch", [NL, 3, D, D])
    w_out_d = dram("w_out", [NL, D, D])
    wq_d = dram("xa_wq", [NL, D, 512])
    wkv_d = dram("xa_wkv", [NL, D, 1024])
    wo_d = dram("xa_wo", [NL, 512, D])
    w1_d = dram("mlp_w1", [NL, D, 4096])
    w2_d = dram("mlp_w2", [NL, 4096, D])
    wa_d = dram("lru_wa", [NL, 8, 128, 128])
    wx_d = dram("lru_wx", [NL, 8, 128, 128])
    vecs_d = dram("vecs", [128, NL * NV])
    rows_d = dram("rows", [128, NL * NR])
    cst_d = dram("cst", [128, 256])
    retc_d = dram("retc", [128, 4 * 130])
    cs_d = dram("cstab", [S, 256])
    out_d = dram("out", [S, D], kind="ExternalOutput")
    if dbg:
        dbgx_d = dram("dbgx", [128, 8 * S], kind="ExternalOutput")
        dbgo_d = dram("dbgo", [128, 8 * S], BF16, kind="ExternalOutput")

    P = Prog(nc)
    AR = Arena(nc, "arena", 204 * 1024)
    ps = nc.alloc_psum_tensor("ps", [128, 8, 512], F32)
    psb = [Buf(f"ps{i}") for i in range(8)]
    mmctr = [0]

    def mmbank():
        b = (0, 1, 2, 7)[mmctr[0] % 4]
        mmctr[0] += 1
        return b

    xT, _ = AR.alloc("xT", [8, S], F32)
    xb = [[Buf(f"x{kc}_{tt}") for tt in range(4)] for kc in range(8)]
    cstf, (cstfb,) = AR.alloc("cstf", [256], F32)
    identf = cstf[:, 0:128]
    identb, (identbb,) = AR.alloc("identb", [128], BF16)
    maskb, (maskbb,) = AR.alloc("maskb", [128], BF16)
    onesb, (onesbb,) = AR.alloc("onesb", [128], BF16)
    vecs, (vecsb,) = AR.alloc("vecs", [NL, NV], F32)
    rows1, (rowsb,) = AR.alloc("rows", [NR], F32)
    retc, (retcb,) = AR.alloc("retc", [4, 130], F32)
    lamneg, (lamnegb,) = AR.alloc("lamneg", [4], F32)
    subln_s, (sublnb,) = AR.alloc("subln", [128], F32)
    lru_sc, (lruscb,) = AR.alloc("lrusc", [8], F32)
    hT, _ = AR.alloc("hT", [8, S], BF16)
    hb = [Buf(f"h{tt}") for tt in range(4)]
    obT, _ = AR.alloc("obT", [8, S], BF16)
    ob = [[Buf(f"o{kc}_{tt}") for tt in range(4)] for kc in range(8)]
    NSLOT = 3
    wslot = []
    for i in range(NSLOT):
        ap, (b,) = AR.alloc(f"wslot{i}", [4096], BF16)
        wslot.append((ap, b, P.dma_group(f"w{i}")))
    wctr = [0]

    def next_slot():
        s = wslot[wctr[0] % NSLOT]
        wctr[0] += 1
        return s

    def wload(dsts_srcs, slot):
        ap, b, g = slot
        n = len(dsts_srcs)

        def fn(h):
            return [h.dma_start(out=d, in_=s) for d, s in dsts_srcs]
        P.op("pool", fn, writes=[b], dma=g, ndma=n)

    def wtile_cols(w2d, col0, ncols, slot, nk=8, dcol0=0, width=None):
        ap = slot[0]
        width = width or ncols
        v = ap[:, 0:nk * width].rearrange("p (k c) -> p k c", k=nk)
        src = w2d.rearrange("(k p) c -> p k c", p=128)[:, :, col0:col0 + ncols]
        return (v[:, :, dcol0:dcol0 + ncols], src)

    def sview(slot, nk, width):
        return slot[0][:, 0:nk * width].rearrange("p (k c) -> p k c", k=nk)

    gc = P.dma_group("cst")
    P.op("sp", lambda h: h.dma_start(out=cstf, in_=cst_d), writes=[cstfb], dma=gc)
    gv = P.dma_group("vecs")
    P.op("sp", lambda h: h.dma_start(out=vecs.rearrange("p l v -> p (l v)"), in_=vecs_d), writes=[vecsb], dma=gv)
    gr = P.dma_group("rows")
    grc = P.dma_group("retc")
    P.op("sp", lambda h: h.dma_start(out=retc.rearrange("p l v -> p (l v)"), in_=retc_d), writes=[retcb], dma=grc)
    P.op("dve", lambda h: h.tensor_copy(out=identb, in_=cstf[:, 0:128]), reads=[cstfb], writes=[identbb])
    P.op("dve", lambda h: h.tensor_copy(out=maskb, in_=cstf[:, 128:256]), reads=[cstfb], writes=[maskbb])
    P.op("dve", lambda h: h.memset(onesb, 1.0), writes=[onesbb])

    mk = AR.mark()
    xin, xinb = AR.alloc("xin", [2, D], F32, nbufs=2)
    gx = [P.dma_group("xin0"), P.dma_group("xin1")]
    for tb in range(16):
        s = tb % 2
        P.op("sp", lambda h, tb=tb, s=s: h.dma_start(out=xin[:, s, :], in_=x_d[tb * 128:(tb + 1) * 128, :]), writes=[xinb[s]], dma=gx[s])
        for half in range(2):
            bk = mmbank()

            def fn(h, s=s, half=half, bk=bk):
                ins = None
                for j in range(4):
                    kc = half * 4 + j
                    ins = h.transpose(out=ps[:, bk, j * 128:(j + 1) * 128], in_=xin[:, s, kc * 128:(kc + 1) * 128], identity=identf)
                return ins
            P.op("pe", fn, reads=[xinb[s], cstfb], writes=[psb[bk]])
            tt = tb // 4
            P.op("dve" if half == 0 else "act",
                 (lambda h, bk=bk, half=half, tb=tb: h.tensor_copy(out=xT[:, half * 4:half * 4 + 4, tb * 128:(tb + 1) * 128], in_=ps[:, bk, :].rearrange("p (a b) -> p a b", a=4)))
                 if half == 0 else
                 (lambda h, bk=bk, half=half, tb=tb: h.activation(out=xT[:, half * 4:half * 4 + 4, tb * 128:(tb + 1) * 128], in_=ps[:, bk, :].rearrange("p (a b) -> p a b", a=4), func=AF.Copy)),
                 reads=[psb[bk]], pwrites=[xb[half * 4 + j][tt] for j in range(4)])
    AR.release(mk)

    def norm_stage(l, gcol, eps=1e-6):
        mk = AR.mark()
        sq, sqb = AR.alloc("sq", [2, 8, 512], BF16, nbufs=2)
        rt, rtb = AR.alloc("rt", [2, 512], F32, nbufs=2)
        for tt in range(4):
            s = tt % 2
            tsl = slice(tt * 512, (tt + 1) * 512)
            P.op("act", lambda h, s=s, tsl=tsl: h.activation(out=sq[:, s], in_=xT[:, :, tsl], func=AF.Square),
                 reads=[xb[kc][tt] for kc in range(8)], writes=[sqb[s]])
            P.op("pe", mm_fn(ps[:, 7, :], [(onesb, sq[:, s, kc, :]) for kc in range(8)]), reads=[sqb[s], onesbb], writes=[psb[7]])
            P.op("act", lambda h, s=s: h.activation(out=rt[:, s], in_=ps[:, 7, :], func=AF.Sqrt, scale=1.0 / D, bias=eps_ap(eps)),
                 reads=[psb[7], epsb], writes=[rtb[s]])
            P.op("dve", lambda h, s=s: h.reciprocal(out=rt[:, s], in_=rt[:, s]), reads=[rtb[s]], writes=[rtb[s]])
            for kc in range(8):
                P.op("dve", lambda h, s=s, kc=kc, tsl=tsl: h.scalar_tensor_tensor(out=hT[:, kc, tsl], in0=xT[:, kc, tsl], scalar=vecs[:, l, gcol + kc:gcol + kc + 1],
                                                                              in1=rt[:, s], op0=ALU.mult, op1=ALU.mult),
                     reads=[xb[kc][tt], rtb[s], vecsb], writes=[hb[tt]] if kc == 0 else (), pwrites=() if kc == 0 else [hb[tt]])
        AR.release(mk)

    epst, (epsb,) = AR.alloc("epst", [8], F32)
    P.op("dve", lambda h: h.memset(epst[:, 0:1], 1e-6), writes=[epsb])
    P.op("dve", lambda h: h.memset(epst[:, 1:2], 1e-5), pwrites=[epsb])
    P.op("dve", lambda h: h.memset(epst[:, 2:3], 1.0), pwrites=[epsb])
    P.op("dve", lambda h: h.memset(epst[:, 3:4], 0.25), pwrites=[epsb])
    P.op("dve", lambda h: h.memset(epst[:, 4:5], 4e-5), pwrites=[epsb])

    def eps_ap(eps):
        return epst[:, 0:1] if eps == 1e-6 else epst[:, 1:2]

    all_x = [xb[kc][tt] for kc in range(8) for tt in range(4)]
    all_o = [ob[kc][tt] for kc in range(8) for tt in range(4)]

    def branch_A(l):
        lam_init = 0.8 - 0.6 * math.exp(-0.3 * l)
        mk = AR.mark()
        lt, (ltb,) = AR.alloc("lamt", [2, 64], F32)
        l2, (l2b,) = AR.alloc("lam2", [2], F32)
        P.op("sp", lambda h: h.dma_start(out=rows1, in_=rows_d[:, l * NR:(l + 1) * NR]), writes=[rowsb], dma=gr)
        rv = rows1[:, 0:256].rearrange("p (a b c) -> p a b c", a=2, b=2)
        P.op("dve", lambda h: h.tensor_tensor(out=lt, in0=rv[:, :, 0, :], in1=rv[:, :, 1, :], op=ALU.mult), reads=[rowsb], writes=[ltb])
        P.op("dve", lambda h: h.reduce_sum(out=l2, in_=lt, axis=AX.X), reads=[ltb], writes=[l2b])
        P.op("act", lambda h: h.activation(out=l2, in_=l2, func=AF.Exp), reads=[l2b], writes=[l2b])
        P.op("dve", lambda h: h.scalar_tensor_tensor(out=lamneg[:, 0:1], in0=l2[:, 1:2], scalar=-lam_init, in1=l2[:, 0:1], op0=ALU.add, op1=ALU.subtract),
             reads=[l2b], writes=[lamnegb])
        P.op("dve", lambda h: h.tensor_scalar(out=subln_s, in0=rows1[:, 256:384], scalar1=1.0 - lam_init, scalar2=None, op0=ALU.mult),
             reads=[rowsb], writes=[sublnb])

        qT, qTb = AR.alloc("qT", [2, S], BF16, nbufs=2)
        kT, kTb = AR.alloc("kT", [2, 2, S], BF16, nbufs=2)
        for s_ in range(2):
            P.op("dve", lambda h, s_=s_: h.memset(kT[64:128, s_, 0, :], 0.0), writes=[kTb[s_]])
            P.op("dve", lambda h, s_=s_: h.memset(kT[0:64, s_, 1, :], 0.0), pwrites=[kTb[s_]])
        vv, vvb = AR.alloc("vA", [2, 16, 129], BF16, nbufs=2)
        for s in range(2):
            P.op("dve", lambda h, s=s: h.memset(vv[:, s, :, 128:129], 1.0), writes=[vvb[s]])
        NE = 5
        et, etb = AR.alloc("eA", [NE, 512], BF16, nbufs=NE)
        rec, (recb,) = AR.alloc("recA", [2, 4], F32)
        osb, osbb = AR.alloc("osbA", [2, 4, 129], F32, nbufs=2)
        d0, d0b = osb[:, 0, :, 0:128], osbb[0]
        d1, d1b = osb[:, 1, :, 0:128], osbb[1]
        ss, (ssb,) = AR.alloc("ssA", [4], F32)
        odn, odnb = AR.alloc("odnA", [2, 4, 128], BF16, nbufs=2)
        ectr = 0
        w2d = w_in_d[l]
        pending_tr = []

        def emit_tr(hd_, qt_, os__):
            bk = mmbank()
            psbf = ps[:, bk, :].bitcast(BF16)

            def fn(h):
                ins = None
                for qb in range(4):
                    ins = h.transpose(out=psbf[:, qb * 128:(qb + 1) * 128], in_=odn[:, os__, qb, :], identity=identb)
                return ins
            P.op("pe", fn, reads=[odnb[os__], identbb], writes=[psb[bk]])
            P.op("act", lambda h: h.activation(out=obT[:, hd_, qt_ * 512:(qt_ + 1) * 512], in_=psbf[:, 0:512], func=AF.Copy),
                 reads=[psb[bk]], writes=[ob[hd_][qt_]])

        for hd in range(int(os.environ.get('KDBG_HEADS', '8'))):
            s = hd % 2
            slot = next_slot()
            wload([wtile_cols(w2d, C_DQ + hd * 128, 128, slot, width=384, dcol0=0),
                   wtile_cols(w2d, C_DK + hd * 128, 128, slot, width=384, dcol0=128),
                   wtile_cols(w2d, C_DV + hd * 128, 128, slot, width=384, dcol0=256)], slot)
            wt = sview(slot, 8, 384)
            wb_ = slot[1]
            for tt in range(4):
                tsl = slice(tt * 512, (tt + 1) * 512)
                bk = mmbank()
                P.op("pe", mm_fn(ps[:, bk, :], [(wt[:, kc, 0:128], hT[:, kc, tsl]) for kc in range(8)]), reads=[wb_, hb[tt]], writes=[psb[bk]])
                P.op("act", lambda h, bk=bk, s=s, tsl=tsl: h.activation(out=qT[:, s, tsl], in_=ps[:, bk, :], func=AF.Copy, scale=0.125),
                     reads=[psb[bk]], writes=[qTb[s]] if tt == 0 else (), pwrites=() if tt == 0 else [qTb[s]])
                bk = mmbank()
                P.op("pe", mm_fn(ps[:, bk, :], [(wt[:, kc, 128:256], hT[:, kc, tsl]) for kc in range(8)]), reads=[wb_, hb[tt]], writes=[psb[bk]])
                P.op("dve", lambda h, bk=bk, s=s, tsl=tsl: h.tensor_copy(out=kT[0:64, s, 0, tsl], in_=ps[0:64, bk, :]),
                     reads=[psb[bk]], writes=[kTb[s]] if tt == 0 else (), pwrites=() if tt == 0 else [kTb[s]])
                P.op("dve", lambda h, bk=bk, s=s, tsl=tsl: h.tensor_copy(out=kT[64:128, s, 1, tsl], in_=ps[64:128, bk, :]),
                     reads=[psb[bk]], pwrites=[kTb[s]])
            for t4 in range(4):
                bk = mmbank()

                def fn(h, t4=t4, bk=bk, wt=wt):
                    ins = None
                    for j in range(4):
                        tb = t4 * 4 + j
                        for kc in range(8):
                            ins = h.matmul(ps[:, bk, j * 128:(j + 1) * 128], lhsT=hT[:, kc, tb * 128:(tb + 1) * 128], rhs=wt[:, kc, 256:384],
                                           start=(kc == 0), stop=(kc == 7))
                    return ins
                P.op("pe", fn, reads=[wb_, hb[t4]], writes=[psb[bk]])
                P.op("dve", lambda h, bk=bk, s=s, t4=t4: h.tensor_copy(out=vv[:, s, t4 * 4:(t4 + 1) * 4, 0:128], in_=ps[:, bk, :].rearrange("p (a b) -> p a b", a=4)),
                     reads=[psb[bk]], writes=[vvb[s]] if t4 == 0 else (), pwrites=() if t4 == 0 else [vvb[s]])
            accb = [psb[3], psb[4], psb[5], psb[6]]

            def emit_score(qt, kb, c, s=s):
                nonlocal ectr
                dstart = max(0, kb - 4 * qt)
                q0 = qt * 512 + dstart * 128
                nq = 512 - dstart * 128
                bk = mmbank()
                P.op("pe", mm_fn(ps[:, bk, 0:nq], [(kT[:, s, c, kb * 128:(kb + 1) * 128], qT[:, s, q0:q0 + nq])]),
                     reads=[kTb[s], qTb[s]], writes=[psb[bk]])
                e = ectr % NE
                ectr += 1
                P.op("act", lambda h, bk=bk, e=e, nq=nq: h.activation(out=et[:, e, 0:nq], in_=ps[:, bk, 0:nq], func=AF.Exp),
                     reads=[psb[bk]], writes=[etb[e]])
                if kb >= 4 * qt:
                    P.op("dve", lambda h, e=e: h.tensor_tensor(out=et[:, e, 0:128], in0=et[:, e, 0:128], in1=maskb, op=ALU.mult),
                         reads=[etb[e], maskbb], writes=[etb[e]])
                return (qt, kb, c, dstart, e)

            def emit_av(info, s=s):
                qt, kb, c, dstart, e = info

                def fn(h):
                    ins = None
                    for qb in range(dstart, 4):
                        bank = 3 + 2 * c + qb // 2
                        co = (qb % 2) * 129
                        ins = h.matmul(ps[:, bank, co:co + 129], lhsT=et[:, e, (qb - dstart) * 128:(qb - dstart + 1) * 128], rhs=vv[:, s, kb, :],
                                       start=(kb == 0 and qb % 2 == 0), stop=(kb == 4 * qt + qb), skip_group_check=True)
                    return ins
                abufs = [accb[2 * c], accb[2 * c + 1]]
                P.op("pe", fn, reads=[etb[e], vvb[s]], writes=abufs if kb == 0 else (), pwrites=() if kb == 0 else abufs)

            def post(qt, hd=hd):
                if pending_tr:
                    emit_tr(*pending_tr.pop(0))
                P.op("act", lambda h: h.activation(out=osb[:, 0].rearrange("p (a b) c -> p a (b c)", a=2), in_=ps[:, 3:5, 0:258], func=AF.Copy), reads=[psb[3], psb[4]], writes=[osbb[0]])
                P.op("dve", lambda h: h.tensor_copy(out=osb[:, 1].rearrange("p (a b) c -> p a (b c)", a=2), in_=ps[:, 5:7, 0:258]), reads=[psb[5], psb[6]], writes=[osbb[1]])
                P.op("dve", lambda h: h.reciprocal(out=rec[:, 0, :], in_=osb[:, 0, :, 128]), reads=[osbb[0]], writes=[recb])
                P.op("dve", lambda h: h.reciprocal(out=rec[:, 1, :], in_=osb[:, 1, :, 128]), reads=[osbb[1]], pwrites=[recb])
                P.op("dve", lambda h: h.tensor_scalar(out=rec[:, 1, :], in0=rec[:, 1, :], scalar1=lamneg[:, 0:1], scalar2=None, op0=ALU.mult),
                     reads=[recb, lamnegb], writes=[recb])
                P.op("dve", lambda h: h.tensor_tensor(out=d0, in0=d0, in1=rec[:, 0, :].unsqueeze(2).to_broadcast([128, 4, 128]), op=ALU.mult),
                     reads=[d0b, recb], writes=[d0b])
                P.op("dve", lambda h: h.tensor_tensor(out=d1, in0=d1, in1=rec[:, 1, :].unsqueeze(2).to_broadcast([128, 4, 128]), op=ALU.mult),
                     reads=[d1b, recb], writes=[d1b])
                P.op("dve", lambda h: h.tensor_tensor(out=d0, in0=d0, in1=d1, op=ALU.add), reads=[d0b, d1b], writes=[d0b])
                P.op("dve", lambda h: h.tensor_tensor(out=d1, in0=d0, in1=d0, op=ALU.mult), reads=[d0b], writes=[d1b])
                P.op("dve", lambda h: h.reduce_sum(out=ss, in_=d1, axis=AX.X), reads=[d1b], writes=[ssb])
                P.op("act", lambda h: h.activation(out=ss, in_=ss, func=AF.Sqrt, scale=1.0 / 128, bias=epst[:, 1:2]), reads=[ssb, epsb], writes=[ssb])
                P.op("dve", lambda h: h.reciprocal(out=ss, in_=ss), reads=[ssb], writes=[ssb])
                P.op("dve", lambda h: h.tensor_tensor(out=d0, in0=d0, in1=ss.unsqueeze(2).to_broadcast([128, 4, 128]), op=ALU.mult), reads=[d0b, ssb], writes=[d0b])
                os_ = qt % 2
                P.op("dve", lambda h, os_=os_: h.tensor_tensor(out=odn[:, os_], in0=d0, in1=subln_s.unsqueeze(1).to_broadcast([128, 4, 128]), op=ALU.mult),
                     reads=[d0b, sublnb], writes=[odnb[os_]])
                pending_tr.append((hd, qt, os_))

            LAG = 2
            pend = []
            for qt in range(4):
                for kb in range(4 * qt + 4):
                    for c in range(2):
                        pend.append(emit_score(qt, kb, c))
                        if len(pend) > LAG:
                            x = pend.pop(0)
                            emit_av(x)
                            if x[1] == 4 * x[0] + 3 and x[2] == 1:
                                post(x[0])
            while pend:
                x = pend.pop(0)
                emit_av(x)
                if x[1] == 4 * x[0] + 3 and x[2] == 1:
                    post(x[0])
        while pending_tr:
            emit_tr(*pending_tr.pop(0))
        AR.release(mk)

    def merge(l, b):
        mk = AR.mark()
        m, _ = AR.alloc("m", [8, S], BF16)
        mb = [[Buf(f"m{kc}_{tt}") for tt in range(4)] for kc in range(8)]
        tmp_ap, tmpb = m, None
        for item in AR.live[-1:]:
            for kc in range(8):
                for tt in range(4):
                    mb[kc][tt].r = dict(item[2][0].r)
            item[2].extend([mb[kc][tt] for kc in range(8) for tt in range(4)])
        gsb, gsbb = AR.alloc("gsb", [2, 512], F32, nbufs=2)
        gctr = 0
        for cg in range(2):
            sg = next_slot()
            wload([wtile_cols(w_in_d[l], C_G + b * 1024 + cg * 512, 512, sg)], sg)
            sb = next_slot()
            wload([wtile_cols(w_br_d[l, b], cg * 512, 512, sb)], sb)
            wg = sview(sg, 8, 512)
            wb = sview(sb, 8, 512)
            for dcl in range(4):
                dc = cg * 4 + dcl
                csl = slice(dcl * 128, (dcl + 1) * 128)
                for tt in range(4):
                    tsl = slice(tt * 512, (tt + 1) * 512)
                    bg = mmbank()
                    P.op("pe", mm_fn(ps[:, bg, :], [(wg[:, kc, csl], hT[:, kc, tsl]) for kc in range(8)]), reads=[sg[1], hb[tt]], writes=[psb[bg]])
                    g_ = gctr % 2
                    gctr += 1
                    P.op("act", lambda h, bg=bg, g_=g_: h.activation(out=gsb[:, g_], in_=ps[:, bg, :], func=AF.Tanh, scale=0.5), reads=[psb[bg]], writes=[gsbb[g_]])
                    bp = mmbank()
                    P.op("pe", mm_fn(ps[:, bp, :], [(wb[:, kc, csl], obT[:, kc, tsl]) for kc in range(8)]), reads=[sb[1]] + [ob[kc][tt] for kc in range(8)], writes=[psb[bp]])
                    P.op("dve", lambda h, bp=bp, g_=g_, dc=dc, tsl=tsl: h.scalar_tensor_tensor(out=m[:, dc, tsl], in0=gsb[:, g_], scalar=1.0, in1=ps[:, bp, :], op0=ALU.add, op1=ALU.mult),
                         reads=[psb[bp], gsbb[g_]], writes=[mb[dc][tt]])
        for cg in range(2):
            so = next_slot()
            wload([wtile_cols(w_out_d[l], cg * 512, 512, so)], so)
            wo = sview(so, 8, 512)
            for dcl in range(4):
                dc = cg * 4 + dcl
                csl = slice(dcl * 128, (dcl + 1) * 128)
                for tt in range(4):
                    tsl = slice(tt * 512, (tt + 1) * 512)
                    bk = mmbank()
                    P.op("pe", mm_fn(ps[:, bk, :], [(wo[:, kc, csl], m[:, kc, tsl]) for kc in range(8)]), reads=[so[1]] + [mb[kc][tt] for kc in range(8)], writes=[psb[bk]])
                    P.op("dve", lambda h, bk=bk, dc=dc, tsl=tsl: h.scalar_tensor_tensor(out=xT[:, dc, tsl], in0=ps[:, bk, :], scalar=0.5, in1=xT[:, dc, tsl], op0=ALU.mult, op1=ALU.add),
                         reads=[psb[bk], xb[dc][tt]], writes=[xb[dc][tt]])
        AR.release(mk)

    def branch_B(l):
        mk = AR.mark()
        gam = [1.0 - 2.0 ** (-5.0 - h_) for h_ in range(4)]
        qkT, qkTb = AR.alloc("qkT", [3, S], BF16, nbufs=1)
        kw, (kwb,) = AR.alloc("kw", [16, 128], BF16)
        vr, (vrb,) = AR.alloc("vr", [16, 256], BF16)
        cst, cstb = AR.alloc("cst", [2, 256], F32, nbufs=2)
        gcs = [P.dma_group("cs0"), P.dma_group("cs1")]
        qkf, qkfb = AR.alloc("qkf", [2, 256], F32, nbufs=2)
        t1, t1b = AR.alloc("t1B", [2, 256], F32, nbufs=2)
        t2, t2b = AR.alloc("t2B", [2, 256], F32, nbufs=2)
        qk3, qk3b = AR.alloc("qk3", [2, 3, 128], BF16, nbufs=2)
        st, (stb,) = AR.alloc("st", [256], F32)
        stbf, stbfb = AR.alloc("stbf", [2, 256], BF16, nbufs=2)
        sTm, sTmb = AR.alloc("sTm", [2, 128], BF16, nbufs=2)
        sgt, sgtb = AR.alloc("sgt", [2, 256], F32, nbufs=2)
        bst, bstb = AR.alloc("bst", [2, 8], F32, nbufs=2)
        on_, onb = AR.alloc("onB", [2, 256], F32, nbufs=2)
        orb, orbb = AR.alloc("orB", [2, 256], BF16, nbufs=2)
        w2d = w_in_d[l]
        for hd in range(4):
            s1 = next_slot()
            wload([wtile_cols(w2d, C_RQ + hd * 128, 128, s1, width=512, dcol0=0),
                   wtile_cols(w2d, C_RK + hd * 128, 128, s1, width=512, dcol0=128),
                   wtile_cols(w2d, C_RV + hd * 256, 256, s1, width=512, dcol0=256)], s1)
            s2 = next_slot()
            wload([wtile_cols(w2d, C_RG + hd * 256, 256, s2)], s2)
            w1 = sview(s1, 8, 512)
            wg = sview(s2, 8, 256)
            pend1 = []

            def tr1(s_, bsl_, tb_):
                bk_ = mmbank()
                psbf_ = ps[:, bk_, :].bitcast(BF16)

                def fn(h):
                    ins = None
                    for j in range(3):
                        ins = h.transpose(out=psbf_[:, j * 128:(j + 1) * 128], in_=qk3[:, s_, j, :], identity=identb)
                    return ins
                P.op("pe", fn, reads=[qk3b[s_], identbb], writes=[psb[bk_]])
                P.op("dve", lambda h: h.tensor_copy(out=qkT[:, :, bsl_], in_=psbf_[:, 0:384].rearrange("p (a b) -> p a b", a=3)),
                     reads=[psb[bk_]], writes=[qkTb[0]] if tb_ == 0 else (), pwrites=() if tb_ == 0 else [qkTb[0]])

            for tb in range(16):
                s = tb % 2
                tt = tb // 4
                bsl = slice(tb * 128, (tb + 1) * 128)
                P.op("sp", lambda h, s=s, bsl=bsl: h.dma_start(out=cst[:, s], in_=cs_d[bsl, :]), writes=[cstb[s]], dma=gcs[s])
                bk = mmbank()
                P.op("pe", mm_fn(ps[:, bk, :], [(hT[:, kc, bsl], w1[:, kc, :]) for kc in range(8)]), reads=[s1[1], hb[tt]], writes=[psb[bk]])
                if pend1:
                    tr1(*pend1.pop(0))
                P.op("act", lambda h, bk=bk, s=s: h.activation(out=qkf[:, s], in_=ps[:, bk, 0:256], func=AF.Copy), reads=[psb[bk]], writes=[qkfb[s]])
                P.op("act", lambda h, bk=bk, tb=tb: h.activation(out=vr[:, tb, :], in_=ps[:, bk, 256:512], func=AF.Copy), reads=[psb[bk]],
                     writes=[vrb] if tb == 0 else (), pwrites=() if tb == 0 else [vrb])
                xq = qkf[:, s].rearrange("p (a b) -> p a b", a=2)
                cosb = cst[:, s, 0:128].unsqueeze(1).to_broadcast([128, 2, 128])
                P.op("dve", lambda h, s=s, xq=xq, cosb=cosb: h.tensor_tensor(out=t1[:, s].rearrange("p (a b) -> p a b", a=2), in0=xq, in1=cosb, op=ALU.mult),
                     reads=[qkfb[s], cstb[s]], writes=[t1b[s]])
                x4 = qkf[:, s].rearrange("p (a b c) -> p a b c", a=2, c=2)
                t24 = t2[:, s].rearrange("p (a b c) -> p a b c", a=2, c=2)
                sn4 = cst[:, s, 128:256].rearrange("p (b c) -> p b c", c=2)
                P.op("dve", lambda h, x4=x4, t24=t24, sn4=sn4: h.tensor_tensor(out=t24[:, :, :, 0], in0=x4[:, :, :, 1], in1=sn4[:, :, 0].unsqueeze(1).to_broadcast([128, 2, 64]), op=ALU.mult),
                     reads=[qkfb[s], cstb[s]], writes=[t2b[s]])
                P.op("dve", lambda h, x4=x4, t24=t24, sn4=sn4: h.tensor_tensor(out=t24[:, :, :, 1], in0=x4[:, :, :, 0], in1=sn4[:, :, 1].unsqueeze(1).to_broadcast([128, 2, 64]), op=ALU.mult),
                     reads=[qkfb[s], cstb[s]], pwrites=[t2b[s]])
                P.op("dve", lambda h, s=s: h.tensor_tensor(out=t1[:, s], in0=t1[:, s], in1=t2[:, s], op=ALU.add), reads=[t1b[s], t2b[s]], writes=[t1b[s]])
                P.op("act", lambda h, s=s: h.activation(out=qk3[:, s, 0, :], in_=t1[:, s, 0:128], func=AF.Copy), reads=[t1b[s]], writes=[qk3b[s]])
                P.op("act", lambda h, s=s, hd=hd: h.activation(out=qk3[:, s, 1, :], in_=t1[:, s, 0:128], func=AF.Identity, scale=retc[:, hd, 129:130]), reads=[t1b[s], retcb], pwrites=[qk3b[s]])
                P.op("act", lambda h, s=s: h.activation(out=qk3[:, s, 2, :], in_=t1[:, s, 128:256], func=AF.Copy), reads=[t1b[s]], pwrites=[qk3b[s]])
                P.op("dve", lambda h, s=s, hd=hd, tb=tb: h.tensor_scalar(out=kw[:, tb, :], in0=t1[:, s, 128:256], scalar1=retc[:, hd, 128:129], scalar2=None, op0=ALU.mult),
                     reads=[t1b[s], retcb], writes=[kwb] if tb == 0 else (), pwrites=() if tb == 0 else [kwb])
                pend1.append((s, bsl, tb))
            while pend1:
                tr1(*pend1.pop(0))
            cd = gam[hd] ** 128
            pend2 = []

            def tr2(s_, bsl_, tt_, hd_=hd):
                bk_ = mmbank()
                psbf_ = ps[:, bk_, :].bitcast(BF16)

                def fn(h):
                    ins = None
                    for j in range(2):
                        ins = h.transpose(out=psbf_[:, j * 128:(j + 1) * 128], in_=orb[:, s_, j * 128:(j + 1) * 128], identity=identb)
                    return ins
                P.op("pe", fn, reads=[orbb[s_], identbb], writes=[psb[bk_]])
                P.op("act", lambda h: h.activation(out=obT[:, 2 * hd_:2 * hd_ + 2, bsl_], in_=psbf_[:, 0:256].rearrange("p (a b) -> p a b", a=2), func=AF.Copy),
                     reads=[psb[bk_]], pwrites=[ob[2 * hd_][tt_], ob[2 * hd_ + 1][tt_]])

            for n in range(16):
                s = n % 2
                tt = n // 4
                bsl = slice(n * 128, (n + 1) * 128)
                bk = mmbank()
                P.op("pe", mm_fn(ps[:, bk, 0:128], [(qkT[:, 2, bsl], qkT[:, 0, bsl])]), reads=[qkTb[0]], writes=[psb[bk]])
                P.op("dve", lambda h, bk=bk, s=s, hd=hd: h.tensor_tensor(out=sTm[:, s], in0=ps[:, bk, 0:128], in1=retc[:, hd, 0:128], op=ALU.mult),
                     reads=[psb[bk], retcb], writes=[sTmb[s]])
                bo = mmbank()
                pairs = [(sTm[:, s], vr[:, n, :])]
                rd = [sTmb[s], vrb]
                if n > 0:
                    pairs.append((qkT[:, 1, bsl], stbf[:, (n - 1) % 2]))
                    rd += [qkTb[0], stbfb[(n - 1) % 2]]
                P.op("pe", mm_fn(ps[:, bo, 0:256], pairs), reads=rd, writes=[psb[bo]])
                bg = mmbank()
                P.op("pe", mm_fn(ps[:, bg, 0:256], [(hT[:, kc, bsl], wg[:, kc, :]) for kc in range(8)]), reads=[s2[1], hb[tt]], writes=[psb[bg]])
                P.op("act", lambda h, bg=bg, s=s: h.activation(out=sgt[:, s], in_=ps[:, bg, 0:256], func=AF.Tanh, scale=0.5), reads=[psb[bg]], writes=[sgtb[s]])
                P.op("dve", lambda h, bg=bg, s=s: h.scalar_tensor_tensor(out=sgt[:, s], in0=sgt[:, s], scalar=1.0, in1=ps[:, bg, 0:256], op0=ALU.add, op1=ALU.mult), reads=[psb[bg], sgtb[s]], writes=[sgtb[s]])
                if n < 15:
                    bkv = mmbank()
                    P.op("pe", mm_fn(ps[:, bkv, 0:256], [(kw[:, n, :], vr[:, n, :])]), reads=[kwb, vrb], writes=[psb[bkv]])
                    if n == 0:
                        P.op("dve", lambda h, bkv=bkv: h.tensor_copy(out=st, in_=ps[:, bkv, 0:256]), reads=[psb[bkv]], writes=[stb])
                    else:
                        P.op("dve", lambda h, bkv=bkv, cd=cd: h.scalar_tensor_tensor(out=st, in0=st, scalar=cd, in1=ps[:, bkv, 0:256], op0=ALU.mult, op1=ALU.add),
                             reads=[psb[bkv], stb], writes=[stb])
                    P.op("act", lambda h, s=s: h.activation(out=stbf[:, s], in_=st, func=AF.Copy), reads=[stb], writes=[stbfb[s]])
                P.op("dve", lambda h, bo=bo, s=s: h.bn_stats(out=bst[:, s, 0:6], in_=ps[:, bo, 0:256]), reads=[psb[bo]], writes=[bstb[s]])
                P.op("dve", lambda h, s=s: h.bn_aggr(out=bst[:, s, 6:8], in_=bst[:, s, 0:6]), reads=[bstb[s]], writes=[bstb[s]])
                P.op("act", lambda h, s=s: h.activation(out=bst[:, s, 7:8], in_=bst[:, s, 7:8], func=AF.Sqrt, scale=4.0, bias=epst[:, 4:5]), reads=[bstb[s], epsb], writes=[bstb[s]])
                P.op("dve", lambda h, s=s: h.reciprocal(out=bst[:, s, 7:8], in_=bst[:, s, 7:8]), reads=[bstb[s]], writes=[bstb[s]])
                P.op("dve", lambda h, bo=bo, s=s: h.tensor_scalar(out=on_[:, s], in0=ps[:, bo, 0:256], scalar1=bst[:, s, 6:7], scalar2=bst[:, s, 7:8], op0=ALU.subtract, op1=ALU.mult),
                     reads=[psb[bo], bstb[s]], writes=[onb[s]])
                P.op("dve", lambda h, s=s: h.tensor_tensor(out=orb[:, s], in0=on_[:, s], in1=sgt[:, s], op=ALU.mult), reads=[onb[s], sgtb[s]], writes=[orbb[s]])
                if pend2:
                    tr2(*pend2.pop(0))
                pend2.append((s, bsl, tt))
            while pend2:
                tr2(*pend2.pop(0))
        AR.release(mk)

    def branch_C(l):
        mk = AR.mark()
        z, (zb,) = AR.alloc("zC", [8], F32)
        z2, (z2b,) = AR.alloc("z2C", [8], F32)
        lamv = vecs[:, l, 56:64]
        P.op("dve", lambda h: h.tensor_scalar(out=z, in0=lamv, scalar1=-1.0, scalar2=None, op0=ALU.mult), reads=[vecsb], writes=[zb])
        P.op("dve", lambda h: h.tensor_tensor(out=z2, in0=z, in1=lamv, op=ALU.max), reads=[zb, vecsb], writes=[z2b])
        P.op("act", lambda h: h.activation(out=z2, in_=z2, func=AF.Exp, scale=-1.0), reads=[z2b], writes=[z2b])
        P.op("act", lambda h: h.activation(out=z2, in_=z2, func=AF.Ln, bias=epst[:, 2:3]), reads=[z2b, epsb], writes=[z2b])
        P.op("dve", lambda h: h.tensor_scalar(out=z, in0=z, scalar1=0.0, scalar2=None, op0=ALU.max), reads=[zb], writes=[zb])
        P.op("dve", lambda h: h.tensor_tensor(out=z, in0=z, in1=z2, op=ALU.add), reads=[zb, z2b], writes=[zb])
        P.op("dve", lambda h: h.tensor_scalar(out=lru_sc, in0=z, scalar1=-4.0, scalar2=None, op0=ALU.mult), reads=[zb], writes=[lruscb])
        hbias, (hbiasb,) = AR.alloc("hbias", [16], F32)
        P.op("dve", lambda h: h.tensor_scalar(out=hbias, in0=vecs[:, l, 40:56], scalar1=0.5, scalar2=None, op0=ALU.mult), reads=[vecsb], writes=[hbiasb])
        wax, (waxb,) = AR.alloc("wax", [2, 8, 128], BF16)
        gwa = P.dma_group("wax")
        P.op("pool", lambda h: [h.dma_start(out=wax[:, 0], in_=wa_d[l].rearrange("n c d -> c n d")),
                                h.dma_start(out=wax[:, 1], in_=wx_d[l].rearrange("n c d -> c n d"))], writes=[waxb], dma=gwa, ndma=2)
        lxs, (lxsb,) = AR.alloc("lxs", [3 + S], F32)
        lxtb = [Buf(f"lxs{tt}") for tt in range(4)]
        for item in AR.live[-1:]:
            for tt in range(4):
                lxtb[tt].r = dict(item[2][0].r)
            item[2].extend(lxtb)
        NB = 2
        ysb, ysbb = AR.alloc("ysb", [NB, 512], F32, nbufs=NB)
        ty_, tyb_ = AR.alloc("ty", [1, 512], F32, nbufs=1); ty = [ty_[:, 0], ty_[:, 0]]; tyb = [tyb_[0], tyb_[0]]
        sgm, sgmb = AR.alloc("sgm", [NB, 512], F32, nbufs=NB)
        xc, xcb = AR.alloc("xc", [NB, 512], F32, nbufs=NB)
        xcbf, xcbfb = AR.alloc("xcbf", [NB, 512], BF16, nbufs=NB)
        ra, rab = AR.alloc("ra", [NB, 512], F32, nbufs=NB)
        ii, iib = AR.alloc("ii", [NB, 512], F32, nbufs=NB)
        a2_, a2b_ = AR.alloc("a2", [1, 512], F32, nbufs=1); a2 = [a2_[:, 0], a2_[:, 0]]; a2b = [a2b_[0], a2b_[0]]
        hh, hhb = AR.alloc("hh", [2, 512], F32, nbufs=2)
        w2d = w_in_d[l]
        it = 0
        for c in range(8):
            slot = next_slot()
            wload([wtile_cols(w2d, C_LX + c * 128, 128, slot, width=256, dcol0=0),
                   wtile_cols(w2d, C_LY + c * 128, 128, slot, width=256, dcol0=128)], slot)
            wt = sview(slot, 8, 256)
            P.op("dve", lambda h: h.memset(lxs[:, 0:3], 0.0), pwrites=[lxtb[0]])
            for tt in range(4):
                s = it % NB
                hs = it % 2
                it += 1
                tsl = slice(tt * 512, (tt + 1) * 512)
                bk = mmbank()
                P.op("pe", mm_fn(ps[:, bk, :], [(wt[:, kc, 0:128], hT[:, kc, tsl]) for kc in range(8)]), reads=[slot[1], hb[tt]], writes=[psb[bk]])
                P.op("act", lambda h, bk=bk, tt=tt: h.activation(out=lxs[:, 3 + tt * 512:3 + (tt + 1) * 512], in_=ps[:, bk, :], func=AF.Copy),
                     reads=[psb[bk]], writes=[lxtb[tt]] if tt > 0 else (), pwrites=[lxtb[0]] if tt == 0 else ())
                bk = mmbank()
                P.op("pe", mm_fn(ps[:, bk, :], [(wt[:, kc, 128:256], hT[:, kc, tsl]) for kc in range(8)]), reads=[slot[1], hb[tt]], writes=[psb[bk]])
                P.op("act", lambda h, bk=bk, s=s: h.activation(out=ysb[:, s], in_=ps[:, bk, :], func=AF.Copy), reads=[psb[bk]], writes=[ysbb[s]])
                P.op("dve", lambda h, s=s: h.tensor_tensor(out=ty[s], in0=ysb[:, s], in1=ysb[:, s], op=ALU.mult), reads=[ysbb[s]], writes=[tyb[s]])
                P.op("dve", lambda h, s=s: h.tensor_scalar(out=ty[s], in0=ty[s], scalar1=0.044715, scalar2=1.0, op0=ALU.mult, op1=ALU.add), reads=[tyb[s]], writes=[tyb[s]])
                P.op("dve", lambda h, s=s: h.tensor_tensor(out=ty[s], in0=ty[s], in1=ysb[:, s], op=ALU.mult), reads=[tyb[s], ysbb[s]], writes=[tyb[s]])
                P.op("act", lambda h, s=s: h.activation(out=sgm[:, s], in_=ty[s], func=AF.Tanh, scale=0.7978845608028654), reads=[tyb[s]], writes=[sgmb[s]])
                P.op("dve", lambda h, s=s: h.scalar_tensor_tensor(out=sgm[:, s], in0=sgm[:, s], scalar=1.0, in1=ysb[:, s], op0=ALU.add, op1=ALU.mult), reads=[sgmb[s], ysbb[s]], writes=[sgmb[s]])
                lrd = [lxtb[tt]] + ([lxtb[tt - 1]] if tt > 0 else [])
                P.op("dve", lambda h, s=s, tt=tt, c=c: h.tensor_scalar(out=xc[:, s], in0=lxs[:, tt * 512:tt * 512 + 512], scalar1=vecs[:, l, 64 + c:65 + c], scalar2=vecs[:, l, 32 + c:33 + c],
                                                                      op0=ALU.mult, op1=ALU.add), reads=lrd + [vecsb], writes=[xcb[s]])
                for j in range(1, 4):
                    P.op("dve", lambda h, s=s, tt=tt, c=c, j=j: h.scalar_tensor_tensor(out=xc[:, s], in0=lxs[:, tt * 512 + j:tt * 512 + j + 512], scalar=vecs[:, l, 64 + j * 8 + c:65 + j * 8 + c],
                                                                                      in1=xc[:, s], op0=ALU.mult, op1=ALU.add), reads=lrd + [vecsb, xcb[s]], writes=[xcb[s]])
                P.op("act", lambda h, s=s: h.activation(out=xcbf[:, s], in_=xc[:, s], func=AF.Copy), reads=[xcb[s]], writes=[xcbfb[s]])
                bk = mmbank()
                P.op("pe", mm_fn(ps[:, bk, :], [(wax[:, 0, c, :], xcbf[:, s])]), reads=[waxb, xcbfb[s]], writes=[psb[bk]])
                P.op("act", lambda h, bk=bk, s=s, c=c: h.activation(out=ra[:, s], in_=ps[:, bk, :], func=AF.Tanh, scale=0.5, bias=hbias[:, c:c + 1]), reads=[psb[bk], hbiasb], writes=[rab[s]])
                bk = mmbank()
                P.op("pe", mm_fn(ps[:, bk, :], [(wax[:, 1, c, :], xcbf[:, s])]), reads=[waxb, xcbfb[s]], writes=[psb[bk]])
                P.op("act", lambda h, bk=bk, s=s, c=c: h.activation(out=ii[:, s], in_=ps[:, bk, :], func=AF.Tanh, scale=0.5, bias=hbias[:, 8 + c:9 + c]), reads=[psb[bk], hbiasb], writes=[iib[s]])
                P.op("act", lambda h, s=s, c=c: h.activation(out=ra[:, s], in_=ra[:, s], func=AF.Exp, scale=lru_sc[:, c:c + 1], bias=lru_sc[:, c:c + 1]), reads=[rab[s], lruscb], writes=[rab[s]])
                P.op("dve", lambda h, s=s: h.tensor_tensor(out=a2[s], in0=ra[:, s], in1=ra[:, s], op=ALU.mult), reads=[rab[s]], writes=[a2b[s]])
                P.op("act", lambda h, s=s: h.activation(out=a2[s], in_=a2[s], func=AF.Sqrt, scale=-0.25, bias=epst[:, 3:4]), reads=[a2b[s], epsb], writes=[a2b[s]])
                P.op("dve", lambda h, s=s: h.scalar_tensor_tensor(out=ii[:, s], in0=ii[:, s], scalar=1.0, in1=xc[:, s], op0=ALU.add, op1=ALU.mult), reads=[iib[s], xcb[s]], writes=[iib[s]])
                P.op("dve", lambda h, s=s: h.tensor_tensor(out=ii[:, s], in0=ii[:, s], in1=a2[s], op=ALU.mult), reads=[iib[s], a2b[s]], writes=[iib[s]])
                if tt == 0:
                    P.op("dve", lambda h, s=s, hs=hs: h.tensor_tensor_scan(out=hh[:, hs], data0=ra[:, s], data1=ii[:, s], initial=0.0, op0=ALU.mult, op1=ALU.add),
                         reads=[rab[s], iib[s]], writes=[hhb[hs]])
                else:
                    P.op("dve", lambda h, s=s, hs=hs: h.tensor_tensor_scan(out=hh[:, hs], data0=ra[:, s], data1=ii[:, s], initial=hh[:, 1 - hs, 511:512], op0=ALU.mult, op1=ALU.add),
                         reads=[rab[s], iib[s], hhb[1 - hs]], writes=[hhb[hs]])
                P.op("dve", lambda h, s=s, hs=hs, c=c, tsl=tsl: h.scalar_tensor_tensor(out=obT[:, c, tsl], in0=hh[:, hs], scalar=0.5, in1=sgm[:, s], op0=ALU.mult, op1=ALU.mult),
                     reads=[hhb[hs], sgmb[s]], writes=[ob[c][tt]])
        AR.release(mk)

    def xattn(l):
        norm_stage(l, 8)
        mk = AR.mark()
        memt, (memtb,) = AR.alloc("memt", [2, D], F32)
        gm = P.dma_group("mem")
        P.op("sp", lambda h: h.dma_start(out=memt, in_=mem_d.rearrange("(a p) d -> p a d", p=128)), writes=[memtb], dma=gm)
        memn, (memnb,) = AR.alloc("memn", [2, D], BF16)
        msq, (msqb,) = AR.alloc("msq", [D], F32)
        mss, (mssb,) = AR.alloc("mss", [2], F32)
        for a in range(2):
            P.op("act", lambda h, a=a: h.activation(out=msq, in_=memt[:, a, :], func=AF.Square, accum_out=mss[:, a:a + 1]), reads=[memtb], writes=[msqb], pwrites=[mssb])
        P.op("act", lambda h: h.activation(out=mss, in_=mss, func=AF.Sqrt, scale=1.0 / D, bias=epst[:, 0:1]), reads=[mssb, epsb], writes=[mssb])
        P.op("dve", lambda h: h.reciprocal(out=mss, in_=mss), reads=[mssb], writes=[mssb])
        for a in range(2):
            P.op("dve", lambda h, a=a: h.tensor_scalar(out=memn[:, a, :], in0=memt[:, a, :], scalar1=mss[:, a:a + 1], scalar2=None, op0=ALU.mult),
                 reads=[memtb, mssb], writes=[memnb] if a == 0 else (), pwrites=() if a == 0 else [memnb])
        memT, (memTb,) = AR.alloc("memT", [8, 256], BF16)
        for a in range(2):
            for half in range(2):
                bk = mmbank()
                psbf = ps[:, bk, :].bitcast(BF16)

                def fn(h, a=a, half=half, psbf=psbf):
                    ins = None
                    for j in range(4):
                        kc = half * 4 + j
                        ins = h.transpose(out=psbf[:, j * 128:(j + 1) * 128], in_=memn[:, a, kc * 128:(kc + 1) * 128], identity=identb)
                    return ins
                P.op("pe", fn, reads=[memnb, identbb], writes=[psb[bk]])
                for j in range(4):
                    kc = half * 4 + j
                    P.op("dve", lambda h, psbf=psbf, j=j, kc=kc, a=a: h.tensor_scalar(out=memT[:, kc, a * 128:(a + 1) * 128], in0=psbf[:, j * 128:(j + 1) * 128],
                                                                                   scalar1=vecs[:, l, 16 + kc:17 + kc], scalar2=None, op0=ALU.mult),
                         reads=[psb[bk], vecsb], pwrites=[memTb])
        kxT, (kxTb,) = AR.alloc("kxT", [4, 256], BF16)
        vx, (vxb,) = AR.alloc("vx", [2, 4, 129], BF16)
        P.op("dve", lambda h: h.memset(vx[:, :, :, 128:129], 1.0), writes=[vxb])
        sk = next_slot()
        wload([wtile_cols(wkv_d[l], 0, 512, sk)], sk)
        wk = sview(sk, 8, 512)
        for hd in range(4):
            bk = mmbank()
            P.op("pe", mm_fn(ps[:, bk, 0:256], [(wk[:, kc, hd * 128:(hd + 1) * 128], memT[:, kc, :]) for kc in range(8)]), reads=[sk[1], memTb], writes=[psb[bk]])
            P.op("act", lambda h, bk=bk, hd=hd: h.activation(out=kxT[:, hd, :], in_=ps[:, bk, 0:256], func=AF.Copy), reads=[psb[bk]], pwrites=[kxTb])
        sv = next_slot()
        wload([wtile_cols(wkv_d[l], 512, 512, sv)], sv)
        wv = sview(sv, 8, 512)
        for a in range(2):
            bk = mmbank()
            P.op("pe", mm_fn(ps[:, bk, :], [(memT[:, kc, a * 128:(a + 1) * 128], wv[:, kc, :]) for kc in range(8)]), reads=[sv[1], memTb], writes=[psb[bk]])
            P.op("act", lambda h, bk=bk, a=a: h.activation(out=vx[:, a, :, 0:128], in_=ps[:, bk, :].rearrange("p (a b) -> p a b", a=4), func=AF.Copy), reads=[psb[bk]], pwrites=[vxb])
        sq_ = next_slot()
        wload([wtile_cols(wq_d[l], 0, 512, sq_)], sq_)
        wq = sview(sq_, 8, 512)
        qx, qxb = AR.alloc("qx", [2, S], BF16, nbufs=2)
        NE = 5
        et, etb = AR.alloc("eX", [NE, 512], BF16, nbufs=NE)
        rec, (recb,) = AR.alloc("recX", [4], F32)
        ox, oxb = AR.alloc("oxX", [2, 4, 128], BF16, nbufs=2)
        ectr = 0
        oxT = obT
        for hd in range(4):
            s = hd % 2
            for tt in range(4):
                tsl = slice(tt * 512, (tt + 1) * 512)
                bk = mmbank()
                P.op("pe", mm_fn(ps[:, bk, :], [(wq[:, kc, hd * 128:(hd + 1) * 128], hT[:, kc, tsl]) for kc in range(8)]), reads=[sq_[1], hb[tt]], writes=[psb[bk]])
                P.op("act", lambda h, bk=bk, s=s, tsl=tsl: h.activation(out=qx[:, s, tsl], in_=ps[:, bk, :], func=AF.Copy, scale=128 ** -0.5),
                     reads=[psb[bk]], writes=[qxb[s]] if tt == 0 else (), pwrites=() if tt == 0 else [qxb[s]])
            def x_score(tt, a, hd=hd, s=s):
                nonlocal ectr
                tsl = slice(tt * 512, (tt + 1) * 512)
                bk = mmbank()
                P.op("pe", mm_fn(ps[:, bk, :], [(kxT[:, hd, a * 128:(a + 1) * 128], qx[:, s, tsl])]), reads=[kxTb, qxb[s]], writes=[psb[bk]])
                e = ectr % NE
                ectr += 1
                P.op("act", lambda h, bk=bk, e=e: h.activation(out=et[:, e], in_=ps[:, bk, :], func=AF.Exp), reads=[psb[bk]], writes=[etb[e]])
                return (tt, a, e)

            def x_av(info, hd=hd):
                tt, a, e = info

                def fn(h):
                    ins = None
                    for qb in range(4):
                        bank = 3 + qb // 2
                        co = (qb % 2) * 129
                        ins = h.matmul(ps[:, bank, co:co + 129], lhsT=et[:, e, qb * 128:(qb + 1) * 128], rhs=vx[:, a, hd, :],
                                       start=(a == 0 and qb % 2 == 0), stop=(a == 1), skip_group_check=True)
                    return ins
                P.op("pe", fn, reads=[etb[e], vxb], writes=[psb[3], psb[4]] if a == 0 else (), pwrites=() if a == 0 else [psb[3], psb[4]])

            def x_post(tt, hd=hd):
                tsl = slice(tt * 512, (tt + 1) * 512)
                acc0 = ps[:, 3:5, 0:258].rearrange("p a (b c) -> p a b c", b=2)
                recv = rec.rearrange("p (a b) -> p a b", a=2)
                P.op("dve", lambda h, acc0=acc0, recv=recv: h.reciprocal(out=recv, in_=acc0[:, :, :, 128]), reads=[psb[3], psb[4]], writes=[recb])
                os_ = tt % 2
                P.op("dve", lambda h, acc0=acc0, recv=recv, os_=os_: h.tensor_tensor(out=ox[:, os_].rearrange("p (a b) c -> p a b c", a=2), in0=acc0[:, :, :, 0:128],
                                                                                  in1=recv.unsqueeze(3).to_broadcast([128, 2, 2, 128]), op=ALU.mult),
                     reads=[psb[3], psb[4], recb], writes=[oxb[os_]])
                bk = mmbank()
                psbf = ps[:, bk, :].bitcast(BF16)

                def fn(h, os_=os_, psbf=psbf):
                    ins = None
                    for qb in range(4):
                        ins = h.transpose(out=psbf[:, qb * 128:(qb + 1) * 128], in_=ox[:, os_, qb, :], identity=identb)
                    return ins
                P.op("pe", fn, reads=[oxb[os_], identbb], writes=[psb[bk]])
                P.op("act", lambda h, psbf=psbf, hd=hd, tsl=tsl: h.activation(out=oxT[:, hd, tsl], in_=psbf[:, 0:512], func=AF.Copy), reads=[psb[bk]], writes=[ob[hd][tt]])

            pend = []
            for tt in range(4):
                for a in range(2):
                    pend.append(x_score(tt, a))
                    if len(pend) > 2:
                        x = pend.pop(0)
                        x_av(x)
                        if x[1] == 1:
                            x_post(x[0])
            while pend:
                x = pend.pop(0)
                x_av(x)
                if x[1] == 1:
                    x_post(x[0])
        so = next_slot()
        P.op("pool", lambda h: [h.dma_start(out=so[0].rearrange("p (k c) -> p k c", k=4), in_=wo_d[l].rearrange("(k p) c -> p k c", p=128))], writes=[so[1]], dma=so[2], ndma=1)
        wo = so[0].rearrange("p (k c) -> p k c", k=4)
        for dc in range(8):
            for tt in range(4):
                tsl = slice(tt * 512, (tt + 1) * 512)
                bk = mmbank()
                P.op("pe", mm_fn(ps[:, bk, :], [(wo[:, hd, dc * 128:(dc + 1) * 128], oxT[:, hd, tsl]) for hd in range(4)]), reads=[so[1]] + [ob[hd][tt] for hd in range(4)], writes=[psb[bk]])
                P.op("dve", lambda h, bk=bk, dc=dc, tsl=tsl: h.tensor_tensor(out=xT[:, dc, tsl], in0=xT[:, dc, tsl], in1=ps[:, bk, :], op=ALU.add),
                     reads=[psb[bk], xb[dc][tt]], writes=[xb[dc][tt]])
        AR.release(mk)

    def mlp(l):
        norm_stage(l, 24)
        mk = AR.mark()
        rl, rlb = AR.alloc("rl", [2, 512], F32, nbufs=2)
        hid = obT.rearrange("p (a b) s -> p a b s", a=2)
        hidb = [[Buf(f"hid{a}_{tt}") for tt in range(4)] for a in range(2)]
        for a in range(2):
            for tt in range(4):
                for kc in range(4):
                    for k_, d_ in ob[a * 4 + kc][tt].w.items():
                        hidb[a][tt].r[("ow", kc, k_)] = d_
                    for k_, d_ in ob[a * 4 + kc][tt].r.items():
                        hidb[a][tt].r[("or", kc, k_)] = d_
        rctr = 0
        for fg in range(8):
            a = fg % 2
            s1 = next_slot()
            wload([wtile_cols(w1_d[l], fg * 512, 512, s1)], s1)
            w1 = sview(s1, 8, 512)
            s2 = next_slot()
            P.op("pool", lambda h, s2=s2, fg=fg: [h.dma_start(out=s2[0].rearrange("p (k c) -> p k c", k=4), in_=w2_d[l, fg * 512:(fg + 1) * 512, :].rearrange("(k p) c -> p k c", p=128))],
                 writes=[s2[1]], dma=s2[2], ndma=1)
            w2 = s2[0].rearrange("p (k c) -> p k c", k=4)
            for fc in range(4):
                for tt in range(4):
                    tsl = slice(tt * 512, (tt + 1) * 512)
                    bk = mmbank()
                    P.op("pe", mm_fn(ps[:, bk, :], [(w1[:, kc, fc * 128:(fc + 1) * 128], hT[:, kc, tsl]) for kc in range(8)]), reads=[s1[1], hb[tt]], writes=[psb[bk]])
                    r_ = rctr % 2
                    rctr += 1
                    P.op("act", lambda h, bk=bk, r_=r_: h.activation(out=rl[:, r_], in_=ps[:, bk, :], func=AF.Relu), reads=[psb[bk]], writes=[rlb[r_]])
                    P.op("dve", lambda h, r_=r_, a=a, fc=fc, tsl=tsl: h.tensor_tensor(out=hid[:, a, fc, tsl], in0=rl[:, r_], in1=rl[:, r_], op=ALU.mult),
                         reads=[rlb[r_]], writes=[hidb[a][tt]] if fc == 0 else (), pwrites=() if fc == 0 else [hidb[a][tt]])
            for dc in range(8):
                for tt in range(4):
                    tsl = slice(tt * 512, (tt + 1) * 512)
                    bk = mmbank()
                    P.op("pe", mm_fn(ps[:, bk, :], [(w2[:, fc, dc * 128:(dc + 1) * 128], hid[:, a, fc, tsl]) for fc in range(4)]), reads=[s2[1], hidb[a][tt]], writes=[psb[bk]])
                    P.op("dve", lambda h, bk=bk, dc=dc, tsl=tsl: h.tensor_tensor(out=xT[:, dc, tsl], in0=xT[:, dc, tsl], in1=ps[:, bk, :], op=ALU.add),
                         reads=[psb[bk], xb[dc][tt]], writes=[xb[dc][tt]])
        for a in range(2):
            for tt in range(4):
                for kc in range(4):
                    b_ = ob[a * 4 + kc][tt]
                    for k_, d_ in hidb[a][tt].w.items():
                        b_.r[("hw", k_)] = d_
                    for k_, d_ in hidb[a][tt].r.items():
                        b_.r[("hr", k_)] = d_
        AR.release(mk)

    def final():
        mk = AR.mark()
        sq, sqb = AR.alloc("sqF", [2, 8, 512], BF16, nbufs=2)
        rt, rtb = AR.alloc("rtF", [2, 512], F32, nbufs=2)
        yf = hT.rearrange("p a s -> p (a s)").bitcast(F32).rearrange("p (a b c) -> p a b c", a=2, b=8)
        yfb = [Buf("yf0"), Buf("yf1")]
        for yb_ in yfb:
            for tt_ in range(4):
                for k_, d_ in hb[tt_].w.items():
                    yb_.r[("hw", tt_, k_)] = d_
                for k_, d_ in hb[tt_].r.items():
                    yb_.r[("hr", tt_, k_)] = d_
        osb, osbb = AR.alloc("osb", [2, D], F32, nbufs=2)
        go = [P.dma_group("out0"), P.dma_group("out1")]
        outbufs = [Buf("outd0"), Buf("outd1")]
        octr = 0
        for tt in range(4):
            s = tt % 2
            tsl = slice(tt * 512, (tt + 1) * 512)
            P.op("act", lambda h, s=s, tsl=tsl: h.activation(out=sq[:, s], in_=xT[:, :, tsl], func=AF.Square), reads=[xb[kc][tt] for kc in range(8)], writes=[sqb[s]])
            P.op("pe", mm_fn(ps[:, 7, :], [(onesb, sq[:, s, kc, :]) for kc in range(8)]), reads=[sqb[s], onesbb], writes=[psb[7]])
            P.op("act", lambda h, s=s: h.activation(out=rt[:, s], in_=ps[:, 7, :], func=AF.Sqrt, scale=1.0 / D, bias=epst[:, 0:1]), reads=[psb[7], epsb], writes=[rtb[s]])
            P.op("dve", lambda h, s=s: h.reciprocal(out=rt[:, s], in_=rt[:, s]), reads=[rtb[s]], writes=[rtb[s]])
            for kc in range(8):
                P.op("dve", lambda h, s=s, kc=kc, tsl=tsl: h.scalar_tensor_tensor(out=yf[:, s, kc, :], in0=xT[:, kc, tsl], scalar=vecs[:, 0, 96 + kc:97 + kc], in1=rt[:, s], op0=ALU.mult, op1=ALU.mult),
                     reads=[xb[kc][tt], rtb[s], vecsb], writes=[yfb[s]] if kc == 0 else (), pwrites=() if kc == 0 else [yfb[s]])
            for tbl in range(4):
                o_ = octr % 2
                octr += 1
                tb = tt * 4 + tbl
                for half in range(2):
                    bk = mmbank()

                    def fn(h, s=s, half=half, bk=bk, tbl=tbl):
                        ins = None
                        for j in range(4):
                            kc = half * 4 + j
                            ins = h.transpose(out=ps[:, bk, j * 128:(j + 1) * 128], in_=yf[:, s, kc, tbl * 128:(tbl + 1) * 128], identity=identf)
                        return ins
                    P.op("pe", fn, reads=[yfb[s], cstfb], writes=[psb[bk]])
                    if half == 0:
                        P.op("dve", lambda h, bk=bk, o_=o_: h.tensor_copy(out=osb[:, o_, 0:512], in_=ps[:, bk, :]), reads=[psb[bk]], writes=[osbb[o_]])
                    else:
                        P.op("act", lambda h, bk=bk, o_=o_: h.activation(out=osb[:, o_, 512:1024], in_=ps[:, bk, :], func=AF.Copy), reads=[psb[bk]], pwrites=[osbb[o_]])
                P.op("sp", lambda h, o_=o_, tb=tb: h.dma_start(out=out_d[tb * 128:(tb + 1) * 128, :], in_=osb[:, o_, :]), reads=[osbb[o_]], writes=[outbufs[o_]], dma=go[o_])
        P.op("sp", None, reads=outbufs)
        AR.release(mk)

    def dump():
        gd = P.dma_group("dbg")
        d1 = Buf("dbg1")
        P.op("sp", lambda h: h.dma_start(out=dbgx_d, in_=xT.rearrange("p a s -> p (a s)")), reads=all_x, writes=[d1], dma=gd)
        gd2 = P.dma_group("dbg2")
        d2 = Buf("dbg2")
        P.op("sp", lambda h: h.dma_start(out=dbgo_d, in_=obT.rearrange("p a s -> p (a s)")), reads=all_o, writes=[d2], dma=gd2)
        P.op("sp", None, reads=[d1, d2])

    done = False
    for l in range(n_layers):
        norm_stage(l, 0)
        if stop_after == ("norm", l):
            done = True
            break
        for bi, (nm, fn_) in enumerate((("A", branch_A), ("B", branch_B), ("C", branch_C))):
            fn_(l)
            if stop_after == (nm, l):
                done = True
                break
            merge(l, bi)
        if done:
            break
        if stop_after == ("mix", l):
            done = True
            break
        xattn(l)
        if stop_after == ("xattn", l):
            done = True
            break
        mlp(l)
        if stop_after == ("mlp", l):
            done = True
            break
    if dbg:
        dump()
    if not done:
        final()
    else:
        pass
    P.emit()
    nc._arena_peak = AR.peak
    return nc


def _host_consts():
    ident = np.eye(128, dtype=np.float32)
    k = np.arange(128)[:, None]
    q = np.arange(128)[None, :]
    mask = (k <= q).astype(np.float32)
    cst = np.concatenate([ident, mask], axis=1)
    log_g = np.log(1.0 - np.exp2(-5.0 - np.arange(4, dtype=np.float64)))
    idx = np.arange(128, dtype=np.float64)
    retc = np.zeros((128, 4, 130), np.float32)
    kscale = 128.0 ** -0.5
    for h in range(4):
        rel = idx[None, :] - idx[:, None]
        dec = np.where(rel >= 0, np.exp(np.maximum(rel, 0.0) * log_g[h]), 0.0) * kscale
        retc[:, h, 0:128] = dec
        retc[:, h, 128] = np.exp((127.0 - idx) * log_g[h]) * kscale
        retc[:, h, 129] = np.exp((idx + 1.0) * log_g[h])
    pos = np.arange(S, dtype=np.float32)
    angle = np.repeat((1.0 / (10000.0 ** np.linspace(0.0, 1.0, 64, dtype=np.float32))).astype(np.float32), 2)
    phase = (pos[:, None] * angle[None, :]).astype(np.float32)
    cos = np.cos(phase).astype(np.float32)
    sin = np.sin(phase).astype(np.float32)
    sgn = np.tile(np.array([-1.0, 1.0], np.float32), 64)
    cstab = np.concatenate([cos, sin * sgn[None, :]], axis=1).astype(np.float32)
    return cst, retc.reshape(128, 4 * 130), cstab


def _pack(inputs):
    f = lambda a: np.ascontiguousarray(np.asarray(a, dtype=np.float32))

    def fm(v):
        return f(v).reshape(-1, 8, 128).transpose(2, 0, 1)
    vecs = np.zeros((128, NL, NV), np.float32)
    vecs[:, :, 0:8] = fm(inputs["norm_mix"])
    vecs[:, :, 8:16] = fm(inputs["norm_xattn"])
    vecs[:, :, 16:24] = fm(inputs["norm_mem"])
    vecs[:, :, 24:32] = fm(inputs["norm_mlp"])
    vecs[:, :, 32:40] = fm(inputs["lru_conv_b"])
    vecs[:, :, 40:48] = fm(inputs["lru_ba"])
    vecs[:, :, 48:56] = fm(inputs["lru_bx"])
    vecs[:, :, 56:64] = fm(inputs["lru_lambda"])
    cw = f(inputs["lru_conv_w"]).reshape(NL, 4, 8, 128)
    vecs[:, :, 64:96] = cw.transpose(3, 0, 1, 2).reshape(128, NL, 32)
    vecs[:, :, 96:104] = np.broadcast_to(f(inputs["norm_final"]).reshape(8, 128).T[:, None, :], (128, NL, 8))
    rows = np.zeros((128, NL, NR), np.float32)
    r1 = np.concatenate([f(inputs["diff_lq1"]), f(inputs["diff_lk1"]), f(inputs["diff_lq2"]), f(inputs["diff_lk2"]), f(inputs["diff_subln"])], axis=1)
    rows[:] = r1[None, :, :]
    return vecs.reshape(128, NL * NV), rows.reshape(128, NL * NR)


_CACHE = {}


def kernel(**inputs):
    key = "full"
    if key not in _CACHE:
        _CACHE[key] = build()
    nc = _CACHE[key]
    vecs, rows = _pack(inputs)
    cst, retc, cstab = _host_consts()
    f = lambda a: np.ascontiguousarray(np.asarray(a, dtype=np.float32))
    shared = {
        "w_in": f(inputs["w_in"]), "w_branch": f(inputs["w_branch"]), "w_out": f(inputs["w_out"]),
        "xa_wq": f(inputs["xa_wq"]), "xa_wkv": f(inputs["xa_wkv"]), "xa_wo": f(inputs["xa_wo"]),
        "mlp_w1": f(inputs["mlp_w1"]), "mlp_w2": f(inputs["mlp_w2"]),
        "lru_wa": f(inputs["lru_wa"]), "lru_wx": f(inputs["lru_wx"]),
        "vecs": vecs, "rows": rows, "cst": cst, "retc": retc, "cstab": cstab,
    }
    x = f(inputs["x"])
    mem = f(inputs["mem"])
    in_maps = []
    for b in range(8):
        m = dict(shared)
        m["x"] = x[b]
        m["mem"] = mem[b]
        in_maps.append(m)
    res = run_bass_kernel_spmd(nc, in_maps, core_ids=list(range(8)))
    return np.stack([np.asarray(r["out"], dtype=np.float32) for r in res.results], axis=0)
```
